# Optimizing a Trainium2 kernel written in Bass

```python
import math
import jax, jax.numpy as jnp
from jax import lax
import numpy as np

D_MODEL = 1024
BATCH = 8
SEQ = 8192
DEPTH = 2

GRID_W = 64
CTX_LEN = 256
N_HEADS = 8
N_KV_HEADS = 2
HEAD_DIM = D_MODEL // N_HEADS
GQA_GROUP = N_HEADS // N_KV_HEADS
ROPE_THETA = 10000.0
Q_BLOCK = 128
Q_W = N_HEADS * HEAD_DIM
KV_W = N_KV_HEADS * HEAD_DIM
HY_WIDTH = D_MODEL
HY_SHORT = 3
HY_EMB = 33
HY_BANDS = (HY_EMB - 1) // 2
HY_FILTER_HIDDEN = 64
HY_TARGET = 1e-2
HY_FAST_PCT = 0.3
HY_SLOW_PCT = 1.5
HY_MAX_DECAY = math.log(HY_TARGET) / HY_FAST_PCT
HY_MIN_DECAY = math.log(HY_TARGET) / HY_SLOW_PCT
HY_SHIFT = 0.05
POOL_WIDTH = D_MODEL
POOL_WINDOWS = (2, 4, 8, 16)
POOL_GROUP = POOL_WIDTH // len(POOL_WINDOWS)
N_BRANCH = 3
D_FF = -(-8 * D_MODEL // (3 * 256)) * 256
DN_ALPHA = (2 * DEPTH) ** 0.25
DN_BETA = (8 * DEPTH) ** -0.25
EPS = 1e-6

C_Q = 0
C_K = C_Q + Q_W
C_V = C_K + KV_W
C_HY = C_V + KV_W
C_POOL = C_HY + 3 * HY_WIDTH
C_GATE = C_POOL + POOL_WIDTH
IN_WIDTH = C_GATE + N_BRANCH * D_MODEL

kernel_name = 'hybrid_hyena_gqa_pool_dit_block'


def layer_norm(x, g, b):
    xf = x.astype(jnp.float32)
    mu = jnp.mean(xf, axis=-1, keepdims=True)
    var = jnp.mean(jnp.square(xf - mu), axis=-1, keepdims=True)
    return ((xf - mu) * lax.rsqrt(var + EPS) * g + b).astype(x.dtype)


def rms_norm(x, g):
    xf = x.astype(jnp.float32)
    y = xf * lax.rsqrt(jnp.mean(jnp.square(xf), axis=-1, keepdims=True) + EPS)
    return (y * g).astype(x.dtype)


def axial_rope(x, rows, cols):
    half = HEAD_DIM // 2
    quarter = half // 2
    inv = jnp.power(ROPE_THETA, -jnp.arange(quarter, dtype=jnp.float32) / quarter)

    def rot(xh, pos):
        ang = pos.astype(jnp.float32)[:, None] * inv[None, :]
        cos = jnp.cos(ang)[None, :, None, :]
        sin = jnp.sin(ang)[None, :, None, :]
        a, b = xh[..., :quarter], xh[..., quarter:]
        return jnp.concatenate([a * cos - b * sin, b * cos + a * sin], axis=-1)

    xf = x.astype(jnp.float32)
    return jnp.concatenate([rot(xf[..., :half], rows), rot(xf[..., half:], cols)], axis=-1).astype(x.dtype)


def queries(p, q_norm_g):
    B, L = p.shape[0], p.shape[1]
    return rms_norm(p[..., C_Q:C_K].reshape(B, L, N_HEADS, HEAD_DIM), q_norm_g)


def kv_heads(pkv, k_norm_g):
    B, L = pkv.shape[0], pkv.shape[1]
    k = rms_norm(pkv[..., :KV_W].reshape(B, L, N_KV_HEADS, HEAD_DIM), k_norm_g)
    v = pkv[..., KV_W:].reshape(B, L, N_KV_HEADS, HEAD_DIM)
    return k, v


def block_attention(q, k, v):
    B, Lq = q.shape[0], q.shape[1]
    nb = Lq // Q_BLOCK
    qb = q.reshape(B, nb, Q_BLOCK, N_KV_HEADS, GQA_GROUP, HEAD_DIM).transpose(1, 0, 2, 3, 4, 5)
    kf = k.astype(jnp.float32)
    vf = v.astype(jnp.float32)
    scale = HEAD_DIM ** -0.5

    def one_block(qi):
        s = jnp.einsum('bqkgd,btkd->bkgqt', qi.astype(jnp.float32), kf) * scale
        pr = jax.nn.softmax(s, axis=-1)
        return jnp.einsum('bkgqt,btkd->bqkgd', pr, vf)

    o = lax.map(one_block, qb)
    return o.transpose(1, 0, 2, 3, 4, 5).reshape(B, Lq, Q_W).astype(q.dtype)


def hyena_kernel(L, w1, b1, freq, w2, b2, w3):
    f32 = jnp.float32
    t_idx = jnp.arange(L, dtype=f32)
    t01 = t_idx / max(L - 1, 1)
    bands = jnp.linspace(1e-4, HY_BANDS - 1, HY_BANDS, dtype=f32)
    ang = (2.0 * math.pi / L) * t_idx[:, None] * bands[None, :]
    feats = jnp.concatenate([t01[:, None], jnp.cos(ang), -jnp.sin(ang)], axis=-1)
    h = jnp.sin(freq * (feats @ w1 + b1))
    h = jnp.sin(freq * (h @ w2 + b2))
    h = (h @ w3).astype(f32)
    deltas = jnp.abs(jnp.linspace(HY_MIN_DECAY, HY_MAX_DECAY, HY_WIDTH, dtype=f32))
    window = jnp.exp(-t01[:, None] * deltas[None, :]) + HY_SHIFT
    h_fwd = h[:, :HY_WIDTH] * window
    h_bwd = h[:, HY_WIDTH:] * window
    kern = jnp.concatenate([h_fwd, jnp.zeros((1, HY_WIDTH), f32), h_bwd[:0:-1]], axis=0)
    return kern * lax.rsqrt(jnp.sum(jnp.square(kern), axis=0, keepdims=True) + EPS)


def hyena_mixer(u, lp):
    B, L, _ = u.shape
    pad = HY_SHORT // 2
    up = jnp.pad(u, ((0, 0), (pad, HY_SHORT - 1 - pad), (0, 0)))
    s = lp['hy_conv_b']
    for j in range(HY_SHORT):
        s = s + up[:, j:j + L] * lp['hy_conv_w'][j]
    v, x0, x1 = jnp.split(s, 3, axis=-1)
    z = (v * x1).astype(jnp.float32)
    kern = hyena_kernel(L, lp['hf_w1'], lp['hf_b1'], lp['hf_freq'], lp['hf_w2'], lp['hf_b2'], lp['hf_w3'])
    n = 2 * L
    zf = jnp.fft.rfft(z, n=n, axis=1)
    kf = jnp.fft.rfft(kern, n=n, axis=0)
    y = jnp.fft.irfft(zf * kf[None], n=n, axis=1)[:, :L]
    y = y + z * lp['hy_d'].astype(jnp.float32)
    return (y * x0.astype(jnp.float32)).astype(u.dtype)


def pool_mixer(u, pool_w, pool_scale):
    B, L, C = u.shape
    uf = u.astype(jnp.float32)
    cs = jnp.concatenate([jnp.zeros((B, 1, C), jnp.float32), jnp.cumsum(uf, axis=1)], axis=1)
    t = jnp.arange(L)
    outs = []
    for gi, w in enumerate(POOL_WINDOWS):
        before = w // 2
        after = w - 1 - before
        lo = jnp.clip(t - before, 0, L)
        hi = jnp.clip(t + after + 1, 0, L)
        csg = cs[..., gi * POOL_GROUP:(gi + 1) * POOL_GROUP]
        cnt = (hi - lo).astype(jnp.float32)[None, :, None]
        mean = (csg[:, hi] - csg[:, lo]) / cnt
        outs.append(mean - uf[..., gi * POOL_GROUP:(gi + 1) * POOL_GROUP])
    m = jnp.stack(outs, axis=2)
    y = jnp.einsum('blgc,gcd->blgd', m, pool_w.astype(jnp.float32)).reshape(B, L, C)
    return (y * pool_scale).astype(u.dtype)


def stream_mixer(p, q, k_all, v_all, lp):
    attn = block_attention(q, k_all, v_all)
    hy = hyena_mixer(p[..., C_HY:C_POOL], lp)
    pool = pool_mixer(p[..., C_POOL:C_GATE], lp['pool_w'], lp['pool_scale'])
    gates = jax.nn.sigmoid(p[..., C_GATE:].astype(jnp.float32)).astype(p.dtype)
    wb = lp['w_branch']
    merged = (gates[..., :D_MODEL] * (hy @ wb[0])
              + gates[..., D_MODEL:2 * D_MODEL] * (attn @ wb[1])
              + gates[..., 2 * D_MODEL:] * (pool @ wb[2]))
    return merged @ lp['w_out']


def swiglu(h, lp):
    return (jax.nn.silu(h @ lp['ffn_w1']) * (h @ lp['ffn_w3'])) @ lp['ffn_w2']


def trunk_layer(xl, xc, c, c_ctx, rows, cols, lp, last):
    mod_l = (jax.nn.silu(c) @ lp['w_ada'] + lp['b_ada'])[:, None, :]
    mod_c = jax.nn.silu(c_ctx) @ lp['w_ada'] + lp['b_ada']
    sh1_l, sc1_l, g1_l, sh2_l, sc2_l, g2_l = jnp.split(mod_l, 6, axis=-1)
    sh1_c, sc1_c, g1_c, sh2_c, sc2_c, g2_c = jnp.split(mod_c, 6, axis=-1)

    hl = xl * (1 + sc1_l) + sh1_l
    hc = xc * (1 + sc1_c) + sh1_c
    pl = hl @ lp['w_in']
    if last:
        kc, vc = kv_heads(hc @ lp['w_in'][:, C_K:C_HY], lp['k_norm_g'])
    else:
        pc = hc @ lp['w_in']
        kc, vc = kv_heads(pc[..., C_K:C_HY], lp['k_norm_g'])

    ql = axial_rope(queries(pl, lp['q_norm_g']), rows, cols)
    kl, vl = kv_heads(pl[..., C_K:C_HY], lp['k_norm_g'])
    kl = axial_rope(kl, rows, cols)
    k_all = jnp.concatenate([kc, kl], axis=1)
    v_all = jnp.concatenate([vc, vl], axis=1)
    xl = layer_norm(DN_ALPHA * xl + g1_l * stream_mixer(pl, ql, k_all, v_all, lp), lp['ln1_g'], lp['ln1_b'])
    xl = layer_norm(DN_ALPHA * xl + g2_l * swiglu(xl * (1 + sc2_l) + sh2_l, lp), lp['ln2_g'], lp['ln2_b'])

    if not last:
        qc = queries(pc, lp['q_norm_g'])
        xc = layer_norm(DN_ALPHA * xc + g1_c * stream_mixer(pc, qc, kc, vc, lp), lp['ln1_g'], lp['ln1_b'])
        xc = layer_norm(DN_ALPHA * xc + g2_c * swiglu(xc * (1 + sc2_c) + sh2_c, lp), lp['ln2_g'], lp['ln2_b'])
    return xl, xc


def setup_inputs(seed: int = 0) -> dict:
    key = jax.random.key(seed)
    ks = jax.random.split(key, 32)
    f32 = jnp.float32
    L = DEPTH
    D = D_MODEL

    def nrm(k, shape, s):
        return jax.random.normal(k, shape, f32) * s

    return {
        'x': nrm(ks[0], (BATCH, SEQ, D), 1.0),
        'c': nrm(ks[1], (BATCH, D), 1.0),
        'ctx': nrm(ks[2], (BATCH, CTX_LEN, D), 1.0),
        'c_ctx': nrm(ks[3], (D,), 1.0),
        'w_ada': nrm(ks[4], (L, D, 6 * D), 0.5 * D ** -0.5),
        'b_ada': nrm(ks[5], (L, 6 * D), 0.01),
        'w_in': nrm(ks[6], (L, D, IN_WIDTH), D ** -0.5),
        'q_norm_g': 1.0 + nrm(ks[7], (L, HEAD_DIM), 0.02),
        'k_norm_g': 1.0 + nrm(ks[8], (L, HEAD_DIM), 0.02),
        'hy_conv_w': nrm(ks[9], (L, HY_SHORT, 3 * HY_WIDTH), HY_SHORT ** -0.5),
        'hy_conv_b': nrm(ks[10], (L, 3 * HY_WIDTH), 0.02),
        'hf_w1': nrm(ks[11], (L, HY_EMB, HY_FILTER_HIDDEN), HY_EMB ** -0.5),
        'hf_b1': nrm(ks[12], (L, HY_FILTER_HIDDEN), 0.02),
        'hf_freq': 1.0 + nrm(ks[13], (L, HY_FILTER_HIDDEN), 0.02),
        'hf_w2': nrm(ks[14], (L, HY_FILTER_HIDDEN, HY_FILTER_HIDDEN), HY_FILTER_HIDDEN ** -0.5),
        'hf_b2': nrm(ks[15], (L, HY_FILTER_HIDDEN), 0.02),
        'hf_w3': nrm(ks[16], (L, HY_FILTER_HIDDEN, 2 * HY_WIDTH), HY_FILTER_HIDDEN ** -0.5),
        'hy_d': nrm(ks[17], (L, HY_WIDTH), 0.5),
        'pool_w': nrm(ks[18], (L, len(POOL_WINDOWS), POOL_GROUP, POOL_GROUP), POOL_GROUP ** -0.5),
        'pool_scale': 1.0 + nrm(ks[19], (L, POOL_WIDTH), 0.05),
        'w_branch': nrm(ks[20], (L, N_BRANCH, D, D), D ** -0.5),
        'w_out': nrm(ks[21], (L, D, D), D ** -0.5 * DN_BETA),
        'ln1_g': 1.0 + nrm(ks[22], (L, D), 0.02),
        'ln1_b': nrm(ks[23], (L, D), 0.02),
        'ln2_g': 1.0 + nrm(ks[24], (L, D), 0.02),
        'ln2_b': nrm(ks[25], (L, D), 0.02),
        'ffn_w1': nrm(ks[26], (L, D, D_FF), D ** -0.5),
        'ffn_w3': nrm(ks[27], (L, D, D_FF), D ** -0.5),
        'ffn_w2': nrm(ks[28], (L, D_FF, D), D_FF ** -0.5 * DN_BETA),
    }


def reference(x, c, ctx, c_ctx, w_ada, b_ada, w_in, q_norm_g, k_norm_g, hy_conv_w, hy_conv_b,
              hf_w1, hf_b1, hf_freq, hf_w2, hf_b2, hf_w3, hy_d, pool_w, pool_scale, w_branch, w_out,
              ln1_g, ln1_b, ln2_g, ln2_b, ffn_w1, ffn_w3, ffn_w2):
    n_tok = x.shape[1]
    ROWS = n_tok // GRID_W
    rows = jnp.repeat(jnp.arange(ROWS), GRID_W)
    cols = jnp.tile(jnp.arange(GRID_W), ROWS)
    xl, xc = x, ctx
    for l in range(DEPTH):
        lp = dict(w_ada=w_ada[l], b_ada=b_ada[l], w_in=w_in[l], q_norm_g=q_norm_g[l], k_norm_g=k_norm_g[l],
                  hy_conv_w=hy_conv_w[l], hy_conv_b=hy_conv_b[l], hf_w1=hf_w1[l], hf_b1=hf_b1[l],
                  hf_freq=hf_freq[l], hf_w2=hf_w2[l], hf_b2=hf_b2[l], hf_w3=hf_w3[l], hy_d=hy_d[l],
                  pool_w=pool_w[l], pool_scale=pool_scale[l], w_branch=w_branch[l], w_out=w_out[l],
                  ln1_g=ln1_g[l], ln1_b=ln1_b[l], ln2_g=ln2_g[l], ln2_b=ln2_b[l],
                  ffn_w1=ffn_w1[l], ffn_w3=ffn_w3[l], ffn_w2=ffn_w2[l])
        xl, xc = trunk_layer(xl, xc, c, c_ctx, rows, cols, lp, l == DEPTH - 1)
    return xl
```

```python
import contextlib
import math
import numpy as np
import ml_dtypes
import concourse.bass as bass
import concourse.mybir as mybir
from concourse.bass_utils import run_bass_kernel_spmd

F32 = mybir.dt.float32
BF16 = mybir.dt.bfloat16
AF = mybir.ActivationFunctionType
ALU = mybir.AluOpType

D = 1024
SEQ = 8192
CTX = 256
DEPTH = 2
NH = 8
HD = 128
DFF = 2816
INW = 8704
C_Q, C_K, C_V, C_HY, C_POOL, C_GATE = 0, 1024, 1280, 1536, 4608, 5632
ALPHA = (2 * DEPTH) ** 0.25
EPS = 1e-6
NFFT = 16384
POOL_WINDOWS = (2, 4, 8, 16)


class _Op:
    __slots__ = ("eng", "fn", "deps", "dma", "marked", "cnt", "sem", "semval", "prev")


class Prog:
    COMPUTE = ("pe", "act", "dve", "pool")
    QUEUES = ("sp", "act", "pool")
    ALLENG = ("pe", "act", "dve", "pool", "sp")
    SB_BASE = 24576
    SB_LIMIT = 218 * 1024

    def __init__(self, ring=8, same_engine_sync=True):
        self.nc = bass.Bass("TRN2", target_bir_lowering=False)
        self.ops = []
        self.state = {}
        self.ring = ring
        self.same = same_engine_sync
        self.dma_count = {q: 0 for q in self.QUEUES}
        self.slot_last = {}
        self.slot_val = {}
        self.last_op = {e: None for e in self.ALLENG}
        self.sb_off = self.SB_BASE
        self.sb_mark = self.SB_BASE
        self.n_alloc = 0
        self.out_dmas = []
        self.psn = 0

    def sb(self, shape, dtype, name="t"):
        nbytes = int(np.prod(shape[1:])) * mybir.dt.size(dtype)
        nbytes_al = (nbytes + 63) // 64 * 64
        self.n_alloc += 1
        h = self.nc.alloc_sbuf_tensor_at(f"{name}_{self.n_alloc}", list(shape), dtype, offset=self.sb_off)
        self.sb_off += nbytes_al
        assert self.sb_off <= self.SB_LIMIT, f"SBUF overflow {self.sb_off} ({name})"
        return h

    def sb_persist_done(self):
        self.sb_mark = self.sb_off

    def sb_reset(self):
        self.sb_off = self.sb_mark

    def _collect(self, reads, writes):
        deps = set()
        for k in reads:
            st = self.state.get(k)
            if st is not None and st[0] is not None:
                deps.add(st[0])
        for k in writes:
            st = self.state.get(k)
            if st is not None:
                if st[0] is not None:
                    deps.add(st[0])
                deps.update(st[1].values())
                deps.update(st[2])
        return deps

    def add(self, eng, fn, reads=(), writes=(), dma=False, out=False):
        pr = [k for k in reads if isinstance(k, tuple) and k[0] == "ps"]
        if pr:
            reads = [k for k in reads if not (isinstance(k, tuple) and k[0] == "ps")]
            writes = list(writes) + pr
        i = len(self.ops)
        op = _Op()
        op.eng, op.fn, op.dma, op.marked, op.cnt, op.prev = eng, fn, dma, False, 0, None
        op.deps = self._collect(reads, writes)
        if dma:
            n = self.dma_count[eng]
            self.dma_count[eng] = n + 1
            slot = (eng, n % self.ring)
            op.prev = self.slot_last.get(slot)
            self.slot_last[slot] = i
            v = self.slot_val.get(slot, 0) + 16
            self.slot_val[slot] = v
            op.sem, op.semval = slot, v
            if out:
                self.out_dmas.append(i)
        self.ops.append(op)
        for k in reads:
            st = self.state.setdefault(k, [None, {}, []])
            if dma:
                st[2].append(i)
            else:
                st[1][eng] = i
        for k in writes:
            self.state[k] = [i, {}, []]
        self.last_op[eng] = i
        return i

    def barrier(self):
        snap = [v for v in self.last_op.values() if v is not None] + list(self.slot_last.values())
        for e in self.ALLENG:
            op = _Op()
            op.eng, op.fn, op.dma, op.marked, op.cnt, op.prev = e, None, False, False, 0, None
            op.deps = set(snap)
            self.ops.append(op)
        self.state = {}

    def dma(self, q, out, in_, reads=(), writes=(), final=False):
        return self.add(q, lambda e: e.dma_start(out=out, in_=in_), reads, writes, dma=True, out=final)

    def mm(self, out, lhsT, rhs, start, stop, reads=(), writes=()):
        return self.add("pe", lambda e: e.matmul(out, lhsT=lhsT, rhs=rhs, start=start, stop=stop), reads, writes)

    def tr(self, out, in_, ident, reads=(), writes=()):
        return self.add("pe", lambda e: e.transpose(out, in_, ident), reads, writes)

    def act(self, out, in_, func, reads=(), writes=(), bias=None, scale=None):
        kw = {}
        if bias is not None:
            kw["bias"] = bias
        if scale is not None:
            kw["scale"] = scale
        return self.add("act", lambda e: e.activation(out=out, in_=in_, func=func, **kw), reads, writes)

    def tt(self, eng, out, in0, in1, op, reads=(), writes=()):
        return self.add(eng, lambda e: e.tensor_tensor(out=out, in0=in0, in1=in1, op=op), reads, writes)

    def ts(self, eng, out, in0, s1, s2, op0, op1, reads=(), writes=()):
        if op1 is None:
            return self.add(eng, lambda e: e.tensor_scalar(out=out, in0=in0, scalar1=s1, scalar2=None, op0=op0), reads, writes)
        return self.add(eng, lambda e: e.tensor_scalar(out=out, in0=in0, scalar1=s1, scalar2=s2, op0=op0, op1=op1), reads, writes)

    def stt(self, out, in0, scalar, in1, op0, op1, reads=(), writes=()):
        return self.add("dve", lambda e: e.scalar_tensor_tensor(out=out, in0=in0, scalar=scalar, in1=in1, op0=op0, op1=op1), reads, writes)

    def cp(self, eng, out, in_, reads=(), writes=()):
        if eng == "act":
            return self.add("act", lambda e: e.copy(out=out, in_=in_), reads, writes)
        return self.add(eng, lambda e: e.tensor_copy(out=out, in_=in_), reads, writes)

    def memset(self, eng, ap, val, writes=()):
        return self.add(eng, lambda e: e.memset(ap, val), (), writes)

    def emit(self):
        nc = self.nc
        ops = self.ops
        fin = _Op()
        fin.eng, fin.fn, fin.dma, fin.marked, fin.cnt, fin.prev = "sp", None, False, False, 0, None
        fin.deps = set(self.out_dmas) | set(self.slot_last.values())
        ops.append(fin)
        for op in ops:
            for d in op.deps:
                dop = ops[d]
                if dop.dma or dop.fn is None:
                    continue
                dop.marked = True
        cnt = {e: 0 for e in self.COMPUTE}
        for op in ops:
            if op.fn is not None and not op.dma:
                if op.marked:
                    cnt[op.eng] += 1
                op.cnt = cnt[op.eng]
        per_eng = {e: [] for e in self.ALLENG}
        for op in ops:
            per_eng[op.eng].append(op)
        same = self.same

        with contextlib.ExitStack() as es:
            csem = {e: es.enter_context(nc.semaphore(f"c_{e}")) for e in self.COMPUTE}
            dsem = {}
            for q in self.QUEUES:
                for r in range(self.ring):
                    dsem[(q, r)] = es.enter_context(nc.semaphore(f"d_{q}_{r}"))
            block = es.enter_context(nc.Block())

            def run(ename, e):
                waited = {}
                for op in per_eng[ename]:
                    waits = {}
                    for d in op.deps:
                        dop = ops[d]
                        if dop.dma:
                            s = ("d", dop.sem)
                            waits[s] = max(waits.get(s, 0), dop.semval)
                        else:
                            if dop.fn is None:
                                continue
                            if dop.eng == ename:
                                if op.fn is None:
                                    continue
                                if not op.dma and (ename == "pe" or not same):
                                    continue
                            s = ("c", dop.eng)
                            waits[s] = max(waits.get(s, 0), dop.cnt)
                    if op.dma and op.prev is not None:
                        p = ops[op.prev]
                        s = ("d", p.sem)
                        waits[s] = max(waits.get(s, 0), p.semval)
                    for s, v in waits.items():
                        if v <= 0 or waited.get(s, 0) >= v:
                            continue
                        e.wait_ge(dsem[s[1]] if s[0] == "d" else csem[s[1]], v)
                        waited[s] = v
                    if op.fn is None:
                        continue
                    inst = op.fn(e)
                    if op.dma:
                        inst.then_inc(dsem[op.sem], 16)
                    elif op.marked:
                        inst.then_inc(csem[ename], 1)

            block.tensor(lambda e: run("pe", e))
            block.scalar(lambda e: run("act", e))
            block.vector(lambda e: run("dve", e))
            block.gpsimd(lambda e: run("pool", e))
            block.sync(lambda e: run("sp", e))
        return nc


def _bf(a):
    return np.asarray(a, dtype=np.float32).astype(ml_dtypes.bfloat16)


def _hy_tables(L):
    f32 = np.float32
    t_idx = np.arange(L, dtype=f32)
    t01 = (t_idx / f32(max(L - 1, 1))).astype(f32)
    bands = np.linspace(1e-4, 15.0, 16, dtype=f32)
    ang = (f32(2.0 * math.pi / L) * t_idx[:, None] * bands[None, :]).astype(f32)
    feats = np.concatenate([t01[:, None], np.cos(ang), -np.sin(ang)], axis=-1).astype(f32)
    fF = np.zeros((33, 8192), f32)
    fR = np.zeros((33, 8192), f32)
    negt = np.zeros((128, 128), f32)
    mask = np.zeros((128, 128), f32)
    fF[:, :L] = feats.T
    n = np.arange(8192)
    tF = np.where(n < L, n, 0)
    negt[:64] = -np.where(n < L, t01[tF], 0).reshape(64, 128)
    mask[:64] = (n < L).astype(f32).reshape(64, 128)
    m = 8192 - n
    valid = (m >= 1) & (m <= L - 1)
    mm = np.where(valid, m, 0)
    fR[:, valid] = feats[mm[valid]].T
    negt[64:] = -np.where(valid, t01[mm], 0).reshape(64, 128)
    mask[64:] = valid.astype(f32).reshape(64, 128)
    return fF, fR, negt, mask


def _rope_tables(T, grid_w=64, ctx=False):
    f32 = np.float32
    if ctx:
        return np.ones((128, T), f32), np.zeros((128, T), f32)
    t = np.arange(T)
    rows = (t // grid_w).astype(f32)
    cols = (t % grid_w).astype(f32)
    inv = np.power(f32(10000.0), -np.arange(32, dtype=f32) / f32(32)).astype(f32)
    C = np.zeros((128, T), f32)
    S = np.zeros((128, T), f32)
    for j in range(128):
        pos = rows if j < 64 else cols
        jj = j % 64
        ang = (pos * inv[jj % 32]).astype(f32)
        C[j] = np.cos(ang)
        S[j] = -np.sin(ang) if jj < 32 else np.sin(ang)
    return C, S


_CONST_CACHE = {}


def host_consts():
    if _CONST_CACHE:
        return _CONST_CACHE
    c = {}
    c["ident"] = np.eye(128, dtype=np.float32)
    c["identb"] = _bf(np.eye(128))
    c["ropeC_l"], c["ropeS_l"] = _rope_tables(SEQ)
    c["ropeC_c"], c["ropeS_c"] = _rope_tables(CTX, ctx=True)
    a = np.arange(128, dtype=np.float64)
    th = 2 * np.pi * np.outer(a, a) / 128.0
    c["F1"] = _bf(np.stack([np.cos(th), -np.sin(th)], axis=1))
    c["I2"] = _bf(np.stack([np.cos(th)[:, :64], -np.sin(th)[:, :64]], axis=1) / NFFT)
    k1 = a[:, None, None]
    n2 = a[None, :, None]
    k2 = a[None, None, :]
    th3 = 2 * np.pi * n2 * (k1 + 128.0 * k2) / NFFT
    G = np.stack([np.cos(th3), -np.sin(th3), np.sin(th3)], axis=2)
    c["GT"] = _bf(G)
    c["HT"] = _bf(np.transpose(G, (0, 3, 2, 1)))
    for tag, L in (("l", SEQ), ("c", CTX)):
        fF, fR, negt, mask = _hy_tables(L)
        c[f"featF_{tag}"], c[f"featR_{tag}"], c[f"negt_{tag}"], c[f"mask_{tag}"] = fF, fR, negt, mask
    lo, hi = math.log(1e-2) / 1.5, math.log(1e-2) / 0.3
    c["deltas"] = np.abs(np.linspace(lo, hi, 1024, dtype=np.float32)).reshape(1, 1024).astype(np.float32)
    edge = np.zeros((4, 2, 8), np.float32)
    Tt = 4096
    for g, w in enumerate(POOL_WINDOWS):
        before, after = w // 2, w - 1 - w // 2
        for side in range(2):
            for i in range(8):
                t = i if side == 0 else Tt - 8 + i
                cnt = min(t + after + 1, Tt) - max(t - before, 0)
                edge[g, side, i] = 1.0 / cnt
    c["pedge"] = np.broadcast_to(edge.reshape(1, 64), (128, 64)).copy()
    _CONST_CACHE.update(c)
    return c


CONST_SPECS = None


def const_specs():
    c = host_consts()
    return {k: (list(v.shape), BF16 if v.dtype == ml_dtypes.bfloat16 else F32) for k, v in c.items()}


INPUT_SHAPES = {
    "x": [SEQ, D], "c": [D], "ctx": [CTX, D], "c_ctx": [D],
    "w_ada": [DEPTH, D, 6 * D], "b_ada": [DEPTH, 6 * D], "w_in": [DEPTH, D, INW],
    "q_norm_g": [DEPTH, HD], "k_norm_g": [DEPTH, HD],
    "hy_conv_w": [DEPTH, 3, 3 * D], "hy_conv_b": [DEPTH, 3 * D],
    "hf_w1": [DEPTH, 33, 64], "hf_b1": [DEPTH, 64], "hf_freq": [DEPTH, 64],
    "hf_w2": [DEPTH, 64, 64], "hf_b2": [DEPTH, 64], "hf_w3": [DEPTH, 64, 2 * D],
    "hy_d": [DEPTH, D], "pool_w": [DEPTH, 4, 256, 256], "pool_scale": [DEPTH, D],
    "w_branch": [DEPTH, 3, D, D], "w_out": [DEPTH, D, D],
    "ln1_g": [DEPTH, D], "ln1_b": [DEPTH, D], "ln2_g": [DEPTH, D], "ln2_b": [DEPTH, D],
    "ffn_w1": [DEPTH, D, DFF], "ffn_w3": [DEPTH, D, DFF], "ffn_w2": [DEPTH, DFF, D],
}


class Stream:
    pass


class Builder:
    def __init__(self, debug_outs=(), stages=None):
        self.P = Prog()
        self.nc = self.P.nc
        self.debug_outs = set(debug_outs)
        self.stages = stages
        nc = self.nc
        self.inp = {k: nc.dram_tensor(k, shp, F32, kind="ExternalInput").ap() for k, shp in INPUT_SHAPES.items()}
        self.cst = {k: nc.dram_tensor("k_" + k, shp, dt, kind="ExternalInput").ap() for k, (shp, dt) in const_specs().items()}
        self.out = nc.dram_tensor("out", [SEQ, D], F32, kind="ExternalOutput").ap()
        self.ps = [nc.alloc_psum_tensor(f"psb{i}", [128, 512], F32) for i in range(8)]
        self.scr = {}
        self._persistent()
        self._streams()

    def dram(self, name, shape, dtype):
        kind = "ExternalOutput" if name in self.debug_outs else "Internal"
        t = self.nc.dram_tensor(name, list(shape), dtype, kind=kind).ap()
        self.scr[name] = t
        return t

    def psk(self, i):
        return ("ps", i)

    def _persistent(self):
        P = self.P
        self.ident = P.sb([128, 128], F32, "ident")
        self.identb = P.sb([128, 128], BF16, "identb")
        self.ones_f = P.sb([128, 128], F32, "ones_f")
        self.ones_b = P.sb([128, 128], BF16, "ones_b")
        self.epsT = P.sb([128, 1], F32, "epsT")
        self.vecs = [P.sb([128, 192], F32, f"vecs{l}") for l in range(DEPTH)]
        self.modT = [P.sb([128, 2, 48], F32, f"modT{l}") for l in range(DEPTH)]
        self.modP = [P.sb([128, 2, 48], F32, f"modP{l}") for l in range(DEPTH)]
        self.hfv = [P.sb([64, 8], F32, f"hfv{l}") for l in range(DEPTH)]
        self.pedge = P.sb([128, 64], F32, "pedge")
        P.sb_persist_done()

    V_QG, V_QGP, V_KG, V_KGP = 0, 1, 2, 3
    V_CW = 4
    V_CB = 76
    V_PS = 100
    V_LN = 108
    V_BA = 140
    V_N = 188

    def _streams(self):
        self.XA = self.dram("XA", [D, SEQ], F32)
        self.XB = self.dram("XB", [D, SEQ], F32)
        self.XCA = self.dram("XCA", [D, CTX], F32)
        self.XCB = self.dram("XCB", [D, CTX], F32)
        self.kT_d = self.dram("kT_d", [256, SEQ + CTX], BF16)
        self.v_d = self.dram("v_d", [SEQ + CTX, 256], BF16)
        self.Bd = [self.dram(f"Bd{i}", [128, 128, 1024], BF16) for i in range(2)]
        self.Dd = [self.dram(f"Dd{i}", [128, 128, 1024], BF16) for i in range(2)]
        self.Kf = {t: [self.dram(f"Kf_{t}{i}", [128, 128, 1024], BF16) for i in range(2)] for t in ("l", "c")}
        self.st = {}
        for tag, T in (("l", SEQ), ("c", CTX)):
            s = Stream()
            s.tag, s.T = tag, T
            s.TS = 4096 if tag == "l" else 256
            s.W = 512 if tag == "l" else 256
            s.sidx = 0 if tag == "l" else 1
            s.xa = self.XA if tag == "l" else self.XCA
            s.xb = self.XB if tag == "l" else self.XCB
            s.ktok0 = CTX if tag == "l" else 0
            s.qT = self.dram(f"qT_{tag}", [D, T], BF16)
            s.z = self.dram(f"z_{tag}", [T, D], BF16)
            s.x0T = self.dram(f"x0T_{tag}", [D, T], F32)
            s.poolT = self.dram(f"poolT_{tag}", [D, T], BF16)
            s.gT = self.dram(f"gT_{tag}", [3, D, T], BF16)
            s.attnT = self.dram(f"attnT_{tag}", [D, T], BF16)
            s.y = self.dram(f"y_{tag}", [T, D], F32)
            s.ropeC = self.cst[f"ropeC_{tag}"]
            s.ropeS = self.cst[f"ropeS_{tag}"]
            self.st[tag] = s

    def stage_p0(self):
        P = self.P
        P.sb_reset()
        P.dma("sp", self.ident[:], self.cst["ident"], writes=["ident"])
        P.dma("sp", self.identb[:], self.cst["identb"], writes=["identb"])
        P.memset("dve", self.ones_f[:], 1.0, writes=["ones_f"])
        P.memset("dve", self.ones_b[:], 1.0, writes=["ones_b"])
        P.memset("dve", self.epsT[:], EPS, writes=["epsT"])
        xt = [P.sb([128, 4, D], F32, "p0x") for _ in range(2)]
        xo = [P.sb([128, 8, 512], F32, "p0o") for _ in range(2)]
        it = 0
        for src, dst, T in ((self.inp["x"], self.XA, SEQ), (self.inp["ctx"], self.XCA, CTX)):
            W = min(512, T)
            nj = W // 128
            for w in range(T // W):
                b = it % 2
                it += 1
                P.dma("sp", xt[b][:, 0:nj, :], src[w * W:(w + 1) * W, :].rearrange("(j p) d -> p j d", p=128),
                      writes=[("p0x", b)])
                for m in range(8):
                    bank = m % 4
                    for j in range(nj):
                        P.tr(self.ps[bank][:, j * 128:(j + 1) * 128], xt[b][:, j, m * 128:(m + 1) * 128], self.ident[:],
                             reads=[("p0x", b), "ident"], writes=[self.psk(bank)])
                    P.cp("act" if m % 2 else "dve", xo[b][:, m, 0:W], self.ps[bank][:, 0:W],
                         reads=[self.psk(bank)], writes=[("p0o", b, m)])
                P.dma("act", dst[:, w * W:(w + 1) * W].rearrange("(m p) t -> p m t", p=128), xo[b][:, :, 0:W],
                      reads=[("p0o", b, m) for m in range(8)])
        P.barrier()

    def stage_p1(self):
        P = self.P
        P.sb_reset()
        I = self.inp
        stg = P.sb([128, 2, 128], F32, "stg")
        stgc = P.sb([16, 128], F32, "stgc")
        stg64 = P.sb([8, 64], F32, "stg64")
        scT = P.sb([128, 8, 2], F32, "scT")
        cT = P.sb([128, 16], F32, "cT")
        wa = [P.sb([128, 8, 512], F32, "wa") for _ in range(2)]
        P.dma("sp", stgc[0:8, :], I["c"].rearrange("(m p) -> m p", p=128), writes=["stgc"])
        P.dma("sp", stgc[8:16, :], I["c_ctx"].rearrange("(m p) -> m p", p=128), writes=["stgc2"])
        P.tr(self.ps[0][:, 0:16], stgc[0:16, :], self.ident[0:16, 0:16], reads=["stgc", "stgc2", "ident"], writes=[self.psk(0)])
        P.act(cT[:], self.ps[0][:, 0:16], AF.Silu, reads=[self.psk(0)], writes=["cT"])
        for s in range(2):
            P.cp("dve", scT[:, :, s], cT[:, s * 8:(s + 1) * 8], reads=["cT"], writes=[("scT", s)])
        for l in range(DEPTH):
            rows = []
            g = I["q_norm_g"][l]
            kg = I["k_norm_g"][l]
            rows.append(("full", g))
            rows.append(("perm", g))
            rows.append(("full", kg))
            rows.append(("perm", kg))
            for j in range(3):
                for m in range(24):
                    rows.append(("full", I["hy_conv_w"][l, j, m * 128:(m + 1) * 128]))
            for m in range(24):
                rows.append(("full", I["hy_conv_b"][l, m * 128:(m + 1) * 128]))
            for m in range(8):
                rows.append(("full", I["pool_scale"][l, m * 128:(m + 1) * 128]))
            for nm in ("ln1_g", "ln1_b", "ln2_g", "ln2_b"):
                for m in range(8):
                    rows.append(("full", I[nm][l, m * 128:(m + 1) * 128]))
            for m in range(48):
                rows.append(("full", I["b_ada"][l, m * 128:(m + 1) * 128]))
            assert len(rows) == self.V_N
            def ld(r0, ap2d, n):
                grp, rr = divmod(r0, 128)
                assert rr + n <= 128
                P.dma("sp", stg[rr:rr + n, grp, :], ap2d, writes=[("stg", r0)])
                return ("stg", r0)
            keys = []
            for ri, (kind, ap) in enumerate(rows[:4]):
                grp, rr = divmod(ri, 128)
                if kind == "full":
                    P.dma("sp", stg[rr:rr + 1, grp, :], ap.rearrange("(o n) -> o n", o=1), writes=[("stg", ri)])
                else:
                    for q4, src0 in enumerate((32, 0, 96, 64)):
                        P.dma("sp", stg[rr:rr + 1, grp, q4 * 32:(q4 + 1) * 32],
                              ap[src0:src0 + 32].rearrange("(o n) -> o n", o=1), writes=[("stg", ri, q4)])
                        keys.append(("stg", ri, q4))
                keys.append(("stg", ri))
            keys.append(ld(4, I["hy_conv_w"][l].rearrange("j (m p) -> (j m) p", p=128), 72))
            keys.append(ld(76, I["hy_conv_b"][l].rearrange("(m p) -> m p", p=128), 24))
            keys.append(ld(100, I["pool_scale"][l].rearrange("(m p) -> m p", p=128), 8))
            for qi, nm in enumerate(("ln1_g", "ln1_b")):
                keys.append(ld(108 + qi * 8, I[nm][l].rearrange("(m p) -> m p", p=128), 8))
            keys.append(ld(124, I["ln2_g"][l, 0:512].rearrange("(m p) -> m p", p=128), 4))
            keys.append(ld(128, I["ln2_g"][l, 512:1024].rearrange("(m p) -> m p", p=128), 4))
            keys.append(ld(132, I["ln2_b"][l].rearrange("(m p) -> m p", p=128), 8))
            keys.append(ld(140, I["b_ada"][l].rearrange("(m p) -> m p", p=128), 48))
            P.tr(self.ps[1][:, 0:128], stg[:, 0, :], self.ident[:], reads=keys + ["ident"], writes=[self.psk(1)])
            P.tr(self.ps[1][:, 128:128 + 60], stg[0:60, 1, :], self.ident[0:60, 0:60], reads=keys + ["ident"], writes=[self.psk(1)])
            P.cp("dve", self.vecs[l][:, 0:188], self.ps[1][:, 0:188], reads=[self.psk(1)], writes=[("vecs", l)])
            for ci, nm in enumerate(("hf_b1", "hf_freq", "hf_b2")):
                P.dma("sp", stg64[ci:ci + 1, :], I[nm][l].rearrange("(o n) -> o n", o=1), writes=[("stg64", ci)])
            P.tr(self.ps[2][0:64, 0:3], stg64[0:3, :], self.ident[0:3, 0:3],
                 reads=[("stg64", ci) for ci in range(3)] + ["ident"], writes=[self.psk(2)])
            P.cp("dve", self.hfv[l][:, 0:3], self.ps[2][0:64, 0:3], reads=[self.psk(2)], writes=[("hfv", l)])
            P.tt("dve", self.hfv[l][:, 3:4], self.hfv[l][:, 0:1], self.hfv[l][:, 1:2], ALU.mult, reads=[("hfv", l)], writes=[("hfv3", l)])
            P.tt("dve", self.hfv[l][:, 4:5], self.hfv[l][:, 2:3], self.hfv[l][:, 1:2], ALU.mult, reads=[("hfv", l)], writes=[("hfv4", l)])
            bank = 3
            for cg in range(12):
                b = cg % 2
                P.dma("sp" if cg % 2 else "act", wa[b][:], I["w_ada"][l][:, cg * 512:(cg + 1) * 512].rearrange("(k p) n -> p k n", p=128),
                      writes=[("wa", b)])
                for mm in range(4):
                    m = cg * 4 + mm
                    for k in range(8):
                        P.mm(self.ps[bank][:, m * 2:m * 2 + 2], wa[b][:, k, mm * 128:(mm + 1) * 128], scT[:, k, :],
                             start=(k == 0), stop=(k == 7), reads=[("wa", b), ("scT", 0), ("scT", 1)], writes=[self.psk(bank)])
            psv = self.ps[bank][:, 0:96].rearrange("p (m s) -> p m s", s=2)
            for s in range(2):
                P.tt("dve", self.modT[l][:, s, :], psv[:, :, s], self.vecs[l][:, self.V_BA:self.V_BA + 48], ALU.add,
                     reads=[self.psk(bank), ("vecs", l)], writes=[("modT", l, s)])
                P.ts("dve", self.modP[l][:, s, :], self.modT[l][:, s, :], 1.0, None, ALU.add, None,
                     reads=[("modT", l, s)], writes=[("modP", l, s)])
            if "dbg_mod" in self.debug_outs:
                if l == 0:
                    self.dbg_mod = self.dram("dbg_mod", [DEPTH, 128, 96], F32)
                    self.dbg_vec = self.dram("dbg_vec", [DEPTH, 128, 192], F32)
                P.dma("sp", self.dbg_mod[l], self.modT[l][:].rearrange("p s m -> p (s m)"), reads=[("modT", l, 0), ("modT", l, 1)])
                P.dma("sp", self.dbg_vec[l], self.vecs[l][:], reads=[("vecs", l)])
        P.barrier()

    def _proj(self, ps_ap, wt, hT, w, W, keys_w, bank, col0=0, ncol=None):
        P = self.P
        for k in range(8):
            P.mm(ps_ap, wt[:, k, :], hT[:, k, col0 + w * W: col0 + w * W + (ncol or W)],
                 start=(k == 0), stop=(k == 7), reads=[keys_w, ("hT", k, w)], writes=[self.psk(bank)])

    def stage_a(self, l, s):
        P = self.P
        P.sb_reset()
        T, TS, W = s.T, s.TS, s.W
        NW = TS // W
        NB = TS // 128
        si = s.sidx
        vec, modT, modP = self.vecs[l], self.modT[l], self.modP[l]
        w_in = self.inp["w_in"][l]
        hT = P.sb([128, 8, TS + 16], BF16, "hT")
        wring = [P.sb([128, 8, 128], BF16, "wr") for _ in range(4)]
        wcount = [0]

        wstage = [P.sb([128, 8, 128], F32, "wst") for _ in range(3)]
        scount = [0]

        def load_w(col0, ncols=128, buf=None, key=None):
            if buf is None:
                i = wcount[0] % 4
                wcount[0] += 1
                buf, key = wring[i], ("wr", i)
            for c in range(0, ncols, 128):
                j = scount[0] % 3
                scount[0] += 1
                P.dma("sp" if j % 2 else "act", wstage[j][:], w_in[:, col0 + c:col0 + c + 128].rearrange("(k p) n -> p k n", p=128),
                      writes=[("wst", j)])
                P.cp("pool", buf[:, :, c:c + 128], wstage[j][:], reads=[("wst", j)], writes=[key if ncols == 128 else (key, c)])
            return buf, key

        mark = P.sb_off
        P.dma("sp", self.pedge[:], self.cst["pedge"], writes=["pedge"])
        for sti in range(T // TS):
            t0 = sti * TS
            P.sb_off = mark
            xs = [P.sb([128, 8, W], F32, "xs") for _ in range(2)]
            hal = P.sb([128, 8, 16], F32, "hal")
            for w in range(NW):
                b = w % 2
                P.dma("sp", xs[b][:], s.xa[:, t0 + w * W: t0 + (w + 1) * W].rearrange("(m p) t -> p m t", p=128), writes=[("xs", b)])
                for m in range(8):
                    eng = ("act", "dve", "pool")[m % 3]
                    o = hT[:, m, w * W:(w + 1) * W]
                    if eng == "act":
                        P.act(o, xs[b][:, m, :], AF.Identity, scale=modP[:, si, 8 + m:9 + m], bias=modT[:, si, m:m + 1],
                              reads=[("xs", b), ("modT", l, si), ("modP", l, si)], writes=[("hT", m, w)])
                    else:
                        P.ts(eng, o, xs[b][:, m, :], modP[:, si, 8 + m:9 + m], modT[:, si, m:m + 1], ALU.mult, ALU.add,
                             reads=[("xs", b), ("modT", l, si), ("modP", l, si)], writes=[("hT", m, w)])
            hk = []
            for side, (a, b_) in enumerate(((t0 - 8, t0), (t0 + TS, t0 + TS + 8))):
                if a >= 0 and b_ <= T:
                    P.dma("sp", hal[:, :, side * 8:(side + 1) * 8], s.xa[:, a:b_].rearrange("(m p) t -> p m t", p=128), writes=[("hal", side)])
                    for m in range(8):
                        P.ts("dve", hT[:, m, TS + side * 8: TS + side * 8 + 8], hal[:, m, side * 8:(side + 1) * 8],
                             modP[:, si, 8 + m:9 + m], modT[:, si, m:m + 1], ALU.mult, ALU.add,
                             reads=[("hal", side), ("modT", l, si), ("modP", l, si)], writes=[("hT", m, "h%d" % side)])
                else:
                    for m in range(8):
                        P.memset("dve", hT[:, m, TS + side * 8: TS + side * 8 + 8], 0.0, writes=[("hT", m, "h%d" % side)])
            P.barrier()
            if getattr(self, 'a_stop', None) == 'ph0':
                return

            def proj_halo(ps_ap, wt, wkey, bank):
                for k in range(8):
                    P.mm(ps_ap, wt[:, k, :], hT[:, k, TS:TS + 16], start=(k == 0), stop=(k == 7),
                         reads=[wkey, ("hT", k, "h0"), ("hT", k, "h1")], writes=[self.psk(bank)])

            P.sb_off = mark
            rC = P.sb([128, TS], F32, "rC")
            rS = P.sb([128, TS], F32, "rS")
            P.dma("sp", rC[:], s.ropeC[:, t0:t0 + TS], writes=["rC"])
            P.dma("act", rS[:], s.ropeS[:, t0:t0 + TS], writes=["rS"])
            wp = [P.sb([128, 8, 128], BF16, "wp") for _ in range(2)]
            sqb = [P.sb([128, W], F32, "sqb") for _ in range(2)]
            rs = [P.sb([128, W], F32, "rs") for _ in range(2)]
            t1 = [P.sb([128, W], F32, "t1") for _ in range(2)]
            t2 = [P.sb([128, W], F32, "t2") for _ in range(2)]
            qrow = [P.sb([128, TS], BF16, "qrow") for _ in range(2)]
            it = 0
            for hc in range(10):
                wq, wk = load_w(hc * 128)
                pb = hc % 2
                for q4, src0 in enumerate((32, 0, 96, 64)):
                    P.cp("pool", wp[pb][:, :, q4 * 32:(q4 + 1) * 32], wq[:, :, src0:src0 + 32], reads=[wk], writes=[("wp", pb, q4)])
                wpk = [("wp", pb, q4) for q4 in range(4)]
                gcol = self.V_QG if hc < 8 else self.V_KG
                r = hc % 2
                for w in range(NW):
                    i = it % 2
                    it += 1
                    bq, bp, bs = (0, 1, 4) if i == 0 else (2, 3, 5)
                    QL = 9
                    if QL < 2:
                        continue
                    self._proj(self.ps[bq][:, 0:W], wq, hT, w, W, wk, bq)
                    for k in range(8):
                        P.mm(self.ps[bp][:, 0:W], wp[pb][:, k, :], hT[:, k, w * W:(w + 1) * W], start=(k == 0), stop=(k == 7),
                             reads=wpk + [("hT", k, w)], writes=[self.psk(bp)])
                    if QL < 3:
                        continue
                    P.act(sqb[i][:], self.ps[bq][:, 0:W], AF.Square, reads=[self.psk(bq)], writes=[("sqb", i)])
                    P.mm(self.ps[bs][:, 0:W], self.ones_f[:], sqb[i][:], start=True, stop=True,
                         reads=["ones_f", ("sqb", i)], writes=[self.psk(bs)])
                    P.act(rs[i][:], self.ps[bs][:, 0:W], AF.Ln, scale=1.0 / 128.0, bias=self.epsT[:, 0:1],
                          reads=[self.psk(bs), "epsT"], writes=[("rs", i)])
                    P.act(rs[i][:], rs[i][:], AF.Exp, scale=-0.5, reads=[("rs", i)], writes=[("rs", i)])
                    if QL < 4:
                        continue
                    P.stt(t1[i][:], self.ps[bq][:, 0:W], vec[:, gcol:gcol + 1], rC[:, w * W:(w + 1) * W], ALU.mult, ALU.mult,
                          reads=[self.psk(bq), ("vecs", l), "rC"], writes=[("t1", i)])
                    P.stt(t2[i][:], self.ps[bp][:, 0:W], vec[:, gcol + 1:gcol + 2], rS[:, w * W:(w + 1) * W], ALU.mult, ALU.mult,
                          reads=[self.psk(bp), ("vecs", l), "rS"], writes=[("t2", i)])
                    if QL < 5:
                        continue
                    P.tt("pool", t1[i][:], t1[i][:], t2[i][:], ALU.add, reads=[("t1", i), ("t2", i)], writes=[("t1", i)])
                    P.tt("pool", qrow[r][:, w * W:(w + 1) * W], t1[i][:], rs[i][:], ALU.mult,
                         reads=[("t1", i), ("rs", i)], writes=[("qrow", r, w)])
                if hc < 8:
                    dst = s.qT[hc * 128:(hc + 1) * 128, t0:t0 + TS]
                else:
                    dst = self.kT_d[(hc - 8) * 128:(hc - 7) * 128, s.ktok0 + t0: s.ktok0 + t0 + TS]
                if QL >= 6:
                    P.dma("sp", dst, qrow[r][:], reads=[("qrow", r, w) for w in range(NW)])
            P.barrier()
            if getattr(self, 'a_stop', None) == 'qk':
                return

            P.sb_off = mark
            wv = P.sb([128, 8, 256], BF16, "wv")
            vrow = P.sb([128, NB, 256], BF16, "vrow")
            load_w(C_V, 256, wv, "wv")
            wvk = [("wv", 0), ("wv", 128)]
            for tb in range(NB):
                bank = tb % 4
                w = (tb * 128) // W
                for k in range(8):
                    P.mm(self.ps[bank][:, 0:256], hT[:, k, tb * 128:(tb + 1) * 128], wv[:, k, :], start=(k == 0), stop=(k == 7),
                         reads=wvk + [("hT", k, w)], writes=[self.psk(bank)])
                P.cp("act" if tb % 2 else "dve", vrow[:, tb, :], self.ps[bank][:, 0:256], reads=[self.psk(bank)], writes=[("vrow", tb)])
            P.dma("sp", self.v_d[s.ktok0 + t0: s.ktok0 + t0 + TS, :].rearrange("(b p) c -> p b c", p=128), vrow[:],
                  reads=[("vrow", tb) for tb in range(NB)])
            P.barrier()
            if getattr(self, 'a_stop', None) == 'v':
                return

            P.sb_off = mark
            ubuf = [P.sb([128, TS + 2], F32, "ubuf") for _ in range(2)]
            sA = P.sb([128, TS], F32, "sA")
            sB = P.sb([128, TS], F32, "sB")
            zrow = P.sb([128, TS], BF16, "zrow")
            ztile = P.sb([128, NB, 128], BF16, "ztile")
            uc = 0
            for j in range(8):
                for part, (cchunk, cm, dst, dk) in enumerate(((12 + j, j, sA, "sA"), (28 + j, 16 + j, sB, "sB"), (20 + j, 8 + j, sA, "sA"))):
                    ub = uc % 2
                    uc += 1
                    wt, wk = load_w(cchunk * 128)
                    ukeys = []
                    for w in range(NW):
                        bank = w % 4
                        self._proj(self.ps[bank][:, 0:W], wt, hT, w, W, wk, bank)
                        P.cp("act" if w % 2 else "dve", ubuf[ub][:, 1 + w * W: 1 + (w + 1) * W], self.ps[bank][:, 0:W],
                             reads=[self.psk(bank)], writes=[("ubuf", ub, w)])
                        ukeys.append(("ubuf", ub, w))
                    proj_halo(self.ps[4][:, 0:16], wt, wk, 4)
                    P.cp("dve", ubuf[ub][:, 0:TS + 2:TS + 1], self.ps[4][:, 7:9], reads=[self.psk(4)], writes=[("ubuf", ub, "h")])
                    ukeys.append(("ubuf", ub, "h"))
                    c0 = self.V_CW + cm
                    P.act(dst[:], ubuf[ub][:, 1:TS + 1], AF.Identity, scale=vec[:, c0 + 24:c0 + 25], bias=vec[:, self.V_CB + cm:self.V_CB + cm + 1],
                          reads=ukeys + [("vecs", l)], writes=[dk])
                    P.stt(dst[:], ubuf[ub][:, 0:TS], vec[:, c0:c0 + 1], dst[:], ALU.mult, ALU.add, reads=ukeys + [dk, ("vecs", l)], writes=[dk])
                    P.stt(dst[:], ubuf[ub][:, 2:TS + 2], vec[:, c0 + 48:c0 + 49], dst[:], ALU.mult, ALU.add, reads=ukeys + [dk, ("vecs", l)], writes=[dk])
                    if part == 1:
                        P.tt("pool", zrow[:], sA[:], sB[:], ALU.mult, reads=["sA", "sB"], writes=["zrow"])
                        for blk in range(NB):
                            bank = 6 + (blk // 8) % 2
                            pv = self.ps[bank][:].bitcast(BF16)
                            P.tr(pv[:, (blk % 8) * 128:(blk % 8 + 1) * 128], zrow[:, blk * 128:(blk + 1) * 128], self.identb[:],
                                 reads=["zrow", "identb"], writes=[self.psk(bank)])
                            if blk % 8 == 7 or blk == NB - 1:
                                b0 = blk - blk % 8
                                n = blk - b0 + 1
                                P.cp("act" if (blk // 8) % 2 else "dve", ztile[:, b0:b0 + n, :].rearrange("p b c -> p (b c)"), pv[:, 0:n * 128],
                                     reads=[self.psk(bank)], writes=[("ztile", b0)])
                        P.dma("sp", s.z[t0:t0 + TS, j * 128:(j + 1) * 128].rearrange("(b p) c -> p b c", p=128), ztile[:],
                              reads=[("ztile", b0) for b0 in range(0, NB, 8)])
                    if part == 2:
                        P.dma("act", s.x0T[j * 128:(j + 1) * 128, t0:t0 + TS], sA[:], reads=["sA"])
            P.barrier()
            if getattr(self, 'a_stop', None) == 'hy':
                return

            P.sb_off = mark
            n = TS + 16
            pbuf = [P.sb([128, n], F32, "pbuf") for _ in range(2)]
            A = P.sb([128, n], F32, "pA")
            Bb = P.sb([128, n], F32, "pB")
            mT = [P.sb([128, TS], BF16, "mT") for _ in range(2)]
            prow = [P.sb([128, TS], BF16, "prow") for _ in range(2)]
            pw = P.sb([128, 2, 256], BF16, "pw")
            pwf = P.sb([128, 2, 256], F32, "pwf")
            tmp8 = P.sb([128, 8], F32, "tmp8")
            pe4 = self.pedge[:].rearrange("p (g s e) -> p g s e", g=4, s=2)
            for g in range(4):
                wsz = POOL_WINDOWS[g]
                kk = g + 1
                o = 8 + wsz // 2 - 1
                P.dma("sp", pwf[:], self.inp["pool_w"][l, g].rearrange("(i p) o -> p i o", p=128), writes=["pwf"])
                P.cp("pool", pw[:], pwf[:], reads=["pwf"], writes=["pw"])
                for i in range(2):
                    wt, wk = load_w((36 + 2 * g + i) * 128)
                    pk = []
                    for w in range(NW):
                        bank = w % 4
                        self._proj(self.ps[bank][:, 0:W], wt, hT, w, W, wk, bank)
                        P.cp("act" if w % 2 else "dve", pbuf[i][:, 8 + w * W: 8 + (w + 1) * W], self.ps[bank][:, 0:W],
                             reads=[self.psk(bank)], writes=[("pbuf", i, w)])
                        pk.append(("pbuf", i, w))
                    proj_halo(self.ps[4][:, 0:16], wt, wk, 4)
                    P.cp("dve", pbuf[i][:, 0:8], self.ps[4][:, 0:8], reads=[self.psk(4)], writes=[("pbuf", i, "h0")])
                    P.cp("dve", pbuf[i][:, TS + 8:TS + 16], self.ps[4][:, 8:16], reads=[self.psk(4)], writes=[("pbuf", i, "h1")])
                    pk += [("pbuf", i, "h0"), ("pbuf", i, "h1")]
                    u = pbuf[i]
                    P.tt("pool", A[:, 1:n], u[:, 1:n], u[:, 0:n - 1], ALU.add, reads=pk, writes=["pA"])
                    R, rk = A, "pA"
                    if kk >= 2:
                        P.tt("pool", Bb[:, 3:n], A[:, 3:n], A[:, 1:n - 2], ALU.add, reads=["pA"], writes=["pB"])
                        R, rk = Bb, "pB"
                    if kk >= 3:
                        P.tt("pool", A[:, 7:n], Bb[:, 7:n], Bb[:, 3:n - 4], ALU.add, reads=["pB"], writes=["pA"])
                        R, rk = A, "pA"
                    if kk >= 4:
                        P.tt("pool", Bb[:, 15:n], A[:, 15:n], A[:, 7:n - 8], ALU.add, reads=["pA"], writes=["pB"])
                        R, rk = Bb, "pB"
                    P.stt(mT[i][:], R[:, o:o + TS], 1.0 / wsz, u[:, 8:8 + TS], ALU.mult, ALU.subtract, reads=[rk] + pk, writes=[("mT", i)])
                    if t0 == 0:
                        P.tt("dve", tmp8[:], R[:, o:o + 8], pe4[:, g, 0, :], ALU.mult, reads=[rk, "pedge"], writes=["tmp8"])
                        P.tt("dve", mT[i][:, 0:8], tmp8[:], u[:, 8:16], ALU.subtract, reads=["tmp8"] + pk, writes=[("mT", i)])
                    if t0 + TS == T:
                        P.tt("dve", tmp8[:], R[:, o + TS - 8:o + TS], pe4[:, g, 1, :], ALU.mult, reads=[rk, "pedge"], writes=["tmp8"])
                        P.tt("dve", mT[i][:, TS - 8:TS], tmp8[:], u[:, TS:TS + 8], ALU.subtract, reads=["tmp8"] + pk, writes=[("mT", i)])
                for oc in range(2):
                    for w in range(NW):
                        bank = w % 4
                        for i in range(2):
                            P.mm(self.ps[bank][:, 0:W], pw[:, i, oc * 128:(oc + 1) * 128], mT[i][:, w * W:(w + 1) * W],
                                 start=(i == 0), stop=(i == 1), reads=["pw", ("mT", i)], writes=[self.psk(bank)])
                        cidx = self.V_PS + 2 * g + oc
                        P.act(prow[oc][:, w * W:(w + 1) * W], self.ps[bank][:, 0:W], AF.Identity, scale=vec[:, cidx:cidx + 1],
                              reads=[self.psk(bank), ("vecs", l)], writes=[("prow", oc, w)])
                    P.dma("sp", s.poolT[(2 * g + oc) * 128:(2 * g + oc + 1) * 128, t0:t0 + TS], prow[oc][:],
                          reads=[("prow", oc, w) for w in range(NW)])
            P.barrier()
            if getattr(self, 'a_stop', None) == 'pool':
                return

            P.sb_off = mark
            grow = [P.sb([128, TS], BF16, "grow") for _ in range(2)]
            for gc in range(24):
                r = gc % 2
                wt, wk = load_w((44 + gc) * 128)
                for w in range(NW):
                    bank = w % 4
                    self._proj(self.ps[bank][:, 0:W], wt, hT, w, W, wk, bank)
                    P.act(grow[r][:, w * W:(w + 1) * W], self.ps[bank][:, 0:W], AF.Sigmoid, reads=[self.psk(bank)], writes=[("grow", r, w)])
                P.dma("sp", s.gT[gc // 8, (gc % 8) * 128:(gc % 8 + 1) * 128, t0:t0 + TS], grow[r][:],
                      reads=[("grow", r, w) for w in range(NW)])
            P.barrier()
            if getattr(self, 'a_stop', None) == 'gate':
                return

    def stage_b(self, l, s, NK):
        P = self.P
        P.sb_reset()
        NQ, W = s.T, s.W
        NB = NK // 128
        KT = P.sb([128, 2, NK], BF16, "KT")
        V = P.sb([128, NB, 256], BF16, "V")
        for kv in range(2):
            P.dma("sp" if kv else "act", KT[:, kv, :], self.kT_d[kv * 128:(kv + 1) * 128, 0:NK], writes=[("KT", kv)])
        vsrc = self.v_d[0:NK, :].rearrange("(b p) c -> p b c", p=128)
        vk = []
        for b0 in range(0, NB, 11):
            b1 = min(NB, b0 + 11)
            P.dma("sp", V[:, b0:b1, :], vsrc[:, b0:b1, :], writes=[("V", b0)])
            vk.append(("V", b0))
        QT = [P.sb([128, 8, W], BF16, "QT") for _ in range(2)]
        attT = [P.sb([128, 8, W], BF16, "attT") for _ in range(2)]
        pT = [P.sb([128, W], BF16, "pT") for _ in range(3)]
        rden = [P.sb([128, W], F32, "rden") for _ in range(2)]
        scale = float(HD) ** -0.5
        steps = [(qw, h, tb) for qw in range(NQ // W) for h in range(8) for tb in range(NB)]
        n = len(steps)

        def issue_S(i):
            qw, h, tb = steps[i]
            r = i % 3
            if h == 0 and tb == 0:
                P.dma("sp", QT[qw % 2][:], s.qT[:, qw * W:(qw + 1) * W].rearrange("(h p) t -> p h t", p=128), writes=[("QT", qw % 2)])
            P.mm(self.ps[r][:, 0:W], KT[:, h // 4, tb * 128:(tb + 1) * 128], QT[qw % 2][:, h, :], True, True,
                 reads=[("KT", h // 4), ("QT", qw % 2)], writes=[self.psk(r)])

        for i in range(min(2, n)):
            issue_S(i)
        for i, (qw, h, tb) in enumerate(steps):
            r = i % 3
            kv = h // 4
            ob, db = 3 + (h % 2), 5 + (h % 2)
            P.act(pT[r][:], self.ps[r][:, 0:W], AF.Exp, scale=scale, reads=[self.psk(r)], writes=[("pT", r)])
            P.mm(self.ps[ob][:, 0:W], V[:, tb, kv * 128:(kv + 1) * 128], pT[r][:], tb == 0, tb == NB - 1,
                 reads=vk + [("pT", r)], writes=[self.psk(ob)])
            P.mm(self.ps[db][:, 0:W], self.ones_b[:], pT[r][:], tb == 0, tb == NB - 1,
                 reads=[("pT", r)], writes=[self.psk(db)])
            if i + 2 < n:
                issue_S(i + 2)
            if tb == NB - 1:
                rd = rden[h % 2]
                P.add("dve", lambda e, rd=rd, db=db: e.reciprocal(out=rd[:], in_=self.ps[db][:, 0:W]), reads=[self.psk(db)], writes=[("rden", h % 2)])
                P.tt("dve", attT[qw % 2][:, h, :], self.ps[ob][:, 0:W], rd[:], ALU.mult,
                     reads=[self.psk(ob), ("rden", h % 2)], writes=[("attT", qw % 2, h)])
                if h == 7:
                    P.dma("act", s.attnT[:, qw * W:(qw + 1) * W].rearrange("(h p) t -> p h t", p=128), attT[qw % 2][:],
                          reads=[("attT", qw % 2, hh) for hh in range(8)])
        P.barrier()

    def _sin_layer(self, ps_ap, fcol, bcol, hv, tmp, tmp2, out_ap, psbank, okey):
        P = self.P
        MAGIC = 12582912.0
        P.ts("dve", tmp, ps_ap, hv[:, fcol:fcol + 1], hv[:, bcol:bcol + 1], ALU.mult, ALU.add, reads=[self.psk(psbank)], writes=["sl_tmp"])
        P.ts("dve", tmp2, tmp, 1.0 / (2 * math.pi), MAGIC, ALU.mult, ALU.add, reads=["sl_tmp"], writes=["sl_tmp2"])
        P.ts("dve", tmp2, tmp2, MAGIC, -2 * math.pi, ALU.subtract, ALU.mult, reads=["sl_tmp2"], writes=["sl_tmp2"])
        P.tt("dve", tmp, tmp, tmp2, ALU.add, reads=["sl_tmp", "sl_tmp2"], writes=["sl_tmp"])
        P.act(out_ap, tmp, AF.Sin, reads=["sl_tmp"], writes=[okey])

    def stage_k(self, l, tag):
        P = self.P
        P.sb_reset()
        I = self.inp
        hv = self.hfv[l]
        Kf = self.Kf[tag]
        w1s = P.sb([33, 64], F32, "w1s")
        w2s = P.sb([64, 64], F32, "w2s")
        w3f = P.sb([64, 2048], F32, "w3f")
        w3b = P.sb([64, 2048], BF16, "w3b")
        h2T = [P.sb([64, 8192], BF16, "h2T") for _ in range(2)]
        dl = P.sb([128, 1024], F32, "dl")
        drow = P.sb([128, 1024], F32, "drow")
        nrow = P.sb([128, 1024], F32, "nrow")
        negt = P.sb([128, 128], F32, "negt")
        mask = P.sb([128, 128], F32, "mask")
        F1 = P.sb([128, 2, 128], BF16, "F1")
        P.dma("sp", w1s[:], I["hf_w1"][l], writes=["w1s"])
        P.dma("sp", w2s[:], I["hf_w2"][l], writes=["w2s"])
        P.dma("sp", w3f[:], I["hf_w3"][l], writes=["w3f"])
        P.cp("pool", w3b[:], w3f[:], reads=["w3f"], writes=["w3b"])
        P.dma("act", dl[:], self.cst["deltas"].partition_broadcast(128).rearrange("p o c -> p (o c)"), writes=["dl"])
        P.dma("act", drow[:], I["hy_d"][l].rearrange("(o c) -> o c", o=1).partition_broadcast(128).rearrange("p o c -> p (o c)"), writes=["drow"])
        P.dma("act", negt[:], self.cst[f"negt_{tag}"], writes=["negt"])
        P.dma("act", mask[:], self.cst[f"mask_{tag}"], writes=["mask"])
        P.dma("act", F1[:], self.cst["F1"], writes=["F1"])
        mark = P.sb_off
        ft = [P.sb([33, 512], F32, "ft") for _ in range(2)]
        tmp = P.sb([64, 512], F32, "sl_tmp")
        tmp2 = P.sb([64, 512], F32, "sl_tmp2")
        h1 = P.sb([64, 512], F32, "h1")
        it = 0
        for d, nm in enumerate((f"featF_{tag}", f"featR_{tag}")):
            for w in range(16):
                b = it % 2
                it += 1
                P.dma("sp", ft[b][:], self.cst[nm][:, w * 512:(w + 1) * 512], writes=[("ft", b)])
                P.mm(self.ps[0][0:64, 0:512], w1s[:], ft[b][:], True, True, reads=["w1s", ("ft", b)], writes=[self.psk(0)])
                self._sin_layer(self.ps[0][0:64, 0:512], 1, 3, hv, tmp[:], tmp2[:], h1[:], 0, "h1")
                P.mm(self.ps[1][0:64, 0:512], w2s[:], h1[:], True, True, reads=["w2s", "h1"], writes=[self.psk(1)])
                self._sin_layer(self.ps[1][0:64, 0:512], 1, 4, hv, tmp[:], tmp2[:], h2T[d][:, w * 512:(w + 1) * 512], 1, ("h2T", d))
        P.sb_off = mark
        kt = [P.sb([128, 8, 1024], BF16, "kt") for _ in range(2)]
        wn = [P.sb([128, 1024], F32, "wn") for _ in range(2)]
        sqb = [P.sb([128, 1024], BF16, "sqk") for _ in range(2)]
        Bt = [[P.sb([128, 8, 1024], BF16, "Btk") for _ in range(2)] for _ in range(2)]
        for jg in range(16):
            kb = jg % 2
            for n2i in range(8):
                n2 = jg * 8 + n2i
                i = n2 % 2
                ba = 0 if i == 0 else 2
                for ch in range(2):
                    P.mm(self.ps[ba + ch][0:64, 0:512], h2T[0][:, n2:8192:128], w3b[:, ch * 512:(ch + 1) * 512], True, True,
                         reads=[("h2T", 0), "w3b"], writes=[self.psk(ba + ch)])
                    P.mm(self.ps[ba + ch][64:128, 0:512], h2T[1][:, n2:8192:128], w3b[:, 1024 + ch * 512:1024 + (ch + 1) * 512], True, True,
                         reads=[("h2T", 1), "w3b"], writes=[self.psk(ba + ch)])
                P.act(wn[i][:], dl[:], AF.Exp, scale=negt[:, n2:n2 + 1], reads=["dl", "negt"], writes=[("wn", i)])
                P.ts("pool", wn[i][:], wn[i][:], 0.05, None, ALU.add, None, reads=[("wn", i)], writes=[("wn", i)])
                for ch in range(2):
                    P.stt(kt[kb][:, n2i, ch * 512:(ch + 1) * 512], self.ps[ba + ch][:, 0:512], mask[:, n2:n2 + 1], wn[i][:, ch * 512:(ch + 1) * 512],
                          ALU.mult, ALU.mult, reads=[self.psk(ba + ch), "mask", ("wn", i)], writes=[("kt", kb, n2i)])
                P.act(sqb[i][:], kt[kb][:, n2i, :], AF.Square, reads=[("kt", kb, n2i)], writes=[("sqk", i)])
                for ch in range(2):
                    P.mm(self.ps[6 + ch][:, 0:512], self.ones_b[:], sqb[i][:, ch * 512:(ch + 1) * 512], n2 == 0, n2 == 127,
                         reads=[("sqk", i)], writes=[self.psk(6 + ch)])
            ktf = kt[kb][:].rearrange("p a c -> p (a c)")
            for ri in range(2):
                btf = Bt[ri][kb][:].rearrange("p a c -> p (a c)")
                for cw in range(16):
                    bank = 4 + (cw % 2)
                    P.mm(self.ps[bank][:, 0:512], F1[:, ri, :], ktf[:, cw * 512:(cw + 1) * 512], True, True,
                         reads=["F1"] + [("kt", kb, q) for q in range(8)], writes=[self.psk(bank)])
                    P.cp("act" if cw % 2 else "dve", btf[:, cw * 512:(cw + 1) * 512], self.ps[bank][:, 0:512],
                         reads=[self.psk(bank)], writes=[("Btk", ri, kb, cw)])
                P.dma("sp" if ri else "act", self.Bd[ri][:, jg * 8:(jg + 1) * 8, :], Bt[ri][kb][:],
                      reads=[("Btk", ri, kb, cw) for cw in range(16)], writes=[("Bd", ri, jg)])
        for ch in range(2):
            P.act(nrow[:, ch * 512:(ch + 1) * 512], self.ps[6 + ch][:, 0:512], AF.Ln, bias=self.epsT[:, 0:1], reads=[self.psk(6 + ch)], writes=[("nrow", ch)])
            P.act(nrow[:, ch * 512:(ch + 1) * 512], nrow[:, ch * 512:(ch + 1) * 512], AF.Exp, scale=-0.5, reads=[("nrow", ch)], writes=[("nrow", ch)])
        P.barrier()
        P.sb_off = mark
        Br = [[P.sb([128, 1024], BF16, "Brk") for _ in range(2)] for _ in range(2)]
        G = [P.sb([128, 3, 128], BF16, "Gk") for _ in range(2)]
        Kt = [[P.sb([128, 1024], BF16, "Kt") for _ in range(2)] for _ in range(2)]
        tz = [P.sb([128, 512], F32, "tz") for _ in range(2)]
        for k1 in range(128):
            b = k1 % 2
            for ri in range(2):
                P.dma("sp" if ri else "act", Br[ri][b][:], self.Bd[ri][k1], writes=[("Brk", ri, b)])
            P.dma("sp", G[b][:], self.cst["GT"][k1], writes=[("Gk", b)])
            for ch in range(2):
                bs = 0 if (2 * k1 + ch) % 2 == 0 else 2
                cs = slice(ch * 512, (ch + 1) * 512)
                rk = [("Brk", 0, b), ("Brk", 1, b), ("Gk", b)]
                P.mm(self.ps[bs][:, 0:512], G[b][:, 0, :], Br[0][b][:, cs], True, False, reads=rk, writes=[self.psk(bs)])
                P.mm(self.ps[bs][:, 0:512], G[b][:, 2, :], Br[1][b][:, cs], False, True, reads=rk, writes=[self.psk(bs)])
                P.mm(self.ps[bs + 1][:, 0:512], G[b][:, 1, :], Br[0][b][:, cs], True, False, reads=rk, writes=[self.psk(bs + 1)])
                P.mm(self.ps[bs + 1][:, 0:512], G[b][:, 0, :], Br[1][b][:, cs], False, True, reads=rk, writes=[self.psk(bs + 1)])
                P.tt("dve", tz[ch][:], self.ps[bs][:, 0:512], nrow[:, cs], ALU.mult, reads=[self.psk(bs), ("nrow", ch)], writes=[("tz", ch)])
                P.tt("pool", Kt[0][b][:, cs], tz[ch][:], drow[:, cs], ALU.add, reads=[("tz", ch), "drow"], writes=[("Kt", 0, b, ch)])
                P.tt("dve", Kt[1][b][:, cs], self.ps[bs + 1][:, 0:512], nrow[:, cs], ALU.mult, reads=[self.psk(bs + 1), ("nrow", ch)], writes=[("Kt", 1, b, ch)])
            for ri in range(2):
                P.dma("sp" if ri else "act", Kf[ri][k1], Kt[ri][b][:], reads=[("Kt", ri, b, 0), ("Kt", ri, b, 1)])
        P.barrier()

    def stage_h(self, l, s):
        P = self.P
        P.sb_reset()
        Kf = self.Kf[s.tag]
        nval = 64 if s.tag == "l" else s.T // 128
        F1 = P.sb([128, 2, 128], BF16, "F1")
        I2 = P.sb([128, 2, 64], BF16, "I2")
        P.dma("act", F1[:], self.cst["F1"], writes=["F1"])
        P.dma("act", I2[:], self.cst["I2"], writes=["I2"])
        mark = P.sb_off
        zt = [P.sb([64, 8, 1024], BF16, "zt") for _ in range(2)]
        Bt = [[P.sb([128, 8, 1024], BF16, "Bth") for _ in range(2)] for _ in range(2)]
        zv = s.z.rearrange("(a b) c -> a b c", b=128)
        if nval < 64:
            for b in range(2):
                P.memset("pool", zt[b][:], 0.0, writes=[("zt", b)])
        for jg in range(16):
            b = jg % 2
            P.dma("sp", zt[b][0:nval, :, :], zv[0:nval, jg * 8:(jg + 1) * 8, :], reads=[("zt", b)] if nval < 64 else [], writes=[("ztd", b)])
            ztf = zt[b][:].rearrange("p a c -> p (a c)")
            for ri in range(2):
                btf = Bt[ri][b][:].rearrange("p a c -> p (a c)")
                for cw in range(16):
                    bank = (cw % 4)
                    P.mm(self.ps[bank][:, 0:512], F1[0:64, ri, :], ztf[:, cw * 512:(cw + 1) * 512], True, True,
                         reads=["F1", ("ztd", b), ("zt", b)], writes=[self.psk(bank)])
                    P.cp("act" if cw % 2 else "dve", btf[:, cw * 512:(cw + 1) * 512], self.ps[bank][:, 0:512],
                         reads=[self.psk(bank)], writes=[("Bth", ri, b, cw)])
                P.dma("sp" if ri else "act", self.Bd[ri][:, jg * 8:(jg + 1) * 8, :], Bt[ri][b][:],
                      reads=[("Bth", ri, b, cw) for cw in range(16)], writes=[("Bd", ri, jg)])
        P.barrier()
        P.sb_off = mark
        Br = [[P.sb([128, 1024], BF16, "Brh") for _ in range(2)] for _ in range(2)]
        Kt = [[P.sb([128, 1024], BF16, "Kth") for _ in range(2)] for _ in range(2)]
        G = [P.sb([128, 3, 128], BF16, "Gh") for _ in range(2)]
        Hh = [P.sb([128, 3, 128], BF16, "Hh") for _ in range(2)]
        Y = [[P.sb([128, 512], BF16, "Yh") for _ in range(2)] for _ in range(2)]
        tq = [[P.sb([128, 512], F32, "tq") for _ in range(4)] for _ in range(2)]
        Dt = [[P.sb([128, 1024], BF16, "Dth") for _ in range(2)] for _ in range(2)]
        for k1 in range(128):
            b = k1 % 2
            for ri in range(2):
                P.dma("sp", Br[ri][b][:], self.Bd[ri][k1], writes=[("Brh", ri, b)])
                P.dma("act", Kt[ri][b][:], Kf[ri][k1], writes=[("Kth", ri, b)])
            P.dma("sp", G[b][:], self.cst["GT"][k1], writes=[("Gh", b)])
            P.dma("act", Hh[b][:], self.cst["HT"][k1], writes=[("Hh", b)])
            for ch in range(2):
                par = ch
                bs = 0 if par == 0 else 4
                cs = slice(ch * 512, (ch + 1) * 512)
                rk = [("Brh", 0, b), ("Brh", 1, b), ("Gh", b)]
                P.mm(self.ps[bs][:, 0:512], G[b][:, 0, :], Br[0][b][:, cs], True, False, reads=rk, writes=[self.psk(bs)])
                P.mm(self.ps[bs][:, 0:512], G[b][:, 2, :], Br[1][b][:, cs], False, True, reads=rk, writes=[self.psk(bs)])
                P.mm(self.ps[bs + 1][:, 0:512], G[b][:, 1, :], Br[0][b][:, cs], True, False, reads=rk, writes=[self.psk(bs + 1)])
                P.mm(self.ps[bs + 1][:, 0:512], G[b][:, 0, :], Br[1][b][:, cs], False, True, reads=rk, writes=[self.psk(bs + 1)])
                t = tq[par]
                kk = [("Kth", 0, b), ("Kth", 1, b)]
                P.tt("dve", t[0][:], self.ps[bs][:, 0:512], Kt[0][b][:, cs], ALU.mult, reads=[self.psk(bs)] + kk, writes=[("tq", par, 0)])
                P.tt("dve", t[1][:], self.ps[bs + 1][:, 0:512], Kt[1][b][:, cs], ALU.mult, reads=[self.psk(bs + 1)] + kk, writes=[("tq", par, 1)])
                P.tt("dve", t[2][:], self.ps[bs][:, 0:512], Kt[1][b][:, cs], ALU.mult, reads=[self.psk(bs)] + kk, writes=[("tq", par, 2)])
                P.tt("dve", t[3][:], self.ps[bs + 1][:, 0:512], Kt[0][b][:, cs], ALU.mult, reads=[self.psk(bs + 1)] + kk, writes=[("tq", par, 3)])
                P.tt("pool", Y[0][par][:], t[0][:], t[1][:], ALU.subtract, reads=[("tq", par, 0), ("tq", par, 1)], writes=[("Yh", 0, par)])
                P.tt("pool", Y[1][par][:], t[2][:], t[3][:], ALU.add, reads=[("tq", par, 2), ("tq", par, 3)], writes=[("Yh", 1, par)])
                yk = [("Yh", 0, par), ("Yh", 1, par), ("Hh", b)]
                P.mm(self.ps[bs + 2][:, 0:512], Hh[b][:, 0, :], Y[0][par][:], True, False, reads=yk, writes=[self.psk(bs + 2)])
                P.mm(self.ps[bs + 2][:, 0:512], Hh[b][:, 1, :], Y[1][par][:], False, True, reads=yk, writes=[self.psk(bs + 2)])
                P.mm(self.ps[bs + 3][:, 0:512], Hh[b][:, 2, :], Y[0][par][:], True, False, reads=yk, writes=[self.psk(bs + 3)])
                P.mm(self.ps[bs + 3][:, 0:512], Hh[b][:, 0, :], Y[1][par][:], False, True, reads=yk, writes=[self.psk(bs + 3)])
                P.cp("act", Dt[0][b][:, cs], self.ps[bs + 2][:, 0:512], reads=[self.psk(bs + 2)], writes=[("Dth", 0, b, ch)])
                P.cp("act", Dt[1][b][:, cs], self.ps[bs + 3][:, 0:512], reads=[self.psk(bs + 3)], writes=[("Dth", 1, b, ch)])
            for ri in range(2):
                P.dma("sp" if ri else "act", self.Dd[ri][k1], Dt[ri][b][:], reads=[("Dth", ri, b, 0), ("Dth", ri, b, 1)])
        P.barrier()
        P.sb_off = mark
        Dr = [[P.sb([128, 8, 1024], BF16, "Drh") for _ in range(2)] for _ in range(2)]
        yt = [P.sb([64, 8, 1024], F32, "yt") for _ in range(2)]
        yv = s.y.rearrange("(a b) c -> a b c", b=128)
        for jg in range(16):
            b = jg % 2
            for ri in range(2):
                P.dma("sp" if ri else "act", Dr[ri][b][:], self.Dd[ri][:, jg * 8:(jg + 1) * 8, :], writes=[("Drh", ri, b)])
            d0 = Dr[0][b][:].rearrange("p a c -> p (a c)")
            d1 = Dr[1][b][:].rearrange("p a c -> p (a c)")
            ytf = yt[b][:].rearrange("p a c -> p (a c)")
            for cw in range(16):
                bank = cw % 4
                cs = slice(cw * 512, (cw + 1) * 512)
                P.mm(self.ps[bank][0:64, 0:512], I2[:, 0, :], d0[:, cs], True, False, reads=["I2", ("Drh", 0, b), ("Drh", 1, b)], writes=[self.psk(bank)])
                P.mm(self.ps[bank][0:64, 0:512], I2[:, 1, :], d1[:, cs], False, True, reads=["I2", ("Drh", 0, b), ("Drh", 1, b)], writes=[self.psk(bank)])
                P.cp("act" if cw % 2 else "dve", ytf[:, cs], self.ps[bank][0:64, 0:512], reads=[self.psk(bank)], writes=[("yt", b, cw)])
            P.dma("sp", yv[0:nval, jg * 8:(jg + 1) * 8, :], yt[b][0:nval, :, :], reads=[("yt", b, cw) for cw in range(16)])
        P.barrier()

    def _load_w_resident(self, dst, src2d, nrow_chunks, ncols, key, stage, skey):
        P = self.P
        n = 0
        for c0 in range(0, ncols, 128):
            for k0 in range(0, nrow_chunks, 8):
                kn = min(8, nrow_chunks - k0)
                j = self._stg_i % len(stage)
                self._stg_i += 1
                P.dma("sp" if j % 2 else "act", stage[j][:, 0:kn, :],
                      src2d[k0 * 128:(k0 + kn) * 128, c0:c0 + 128].rearrange("(k p) n -> p k n", p=128), writes=[(skey, j)])
                P.cp("pool" if n % 2 else "dve", dst[:, k0:k0 + kn, c0:c0 + 128], stage[j][:, 0:kn, :], reads=[(skey, j)], writes=[(key, c0, k0)])
                n += 1
        return [[(key, c0, k0) for k0 in range(0, nrow_chunks, 8)] for c0 in range(0, ncols, 128)]

    def _layer_norm(self, rT, W, gcol, bcol, vec, l, outT, sq, stat, rkeys, okey):
        P = self.P
        mean, msq, var, rstd = stat
        for m in range(8):
            P.mm(self.ps[6][:, 0:W], self.ones_f[:], rT[:, m, :], m == 0, m == 7, reads=[rkeys[m]], writes=[self.psk(6)])
        for m in range(8):
            P.act(sq[:, m, :], rT[:, m, :], AF.Square, reads=[rkeys[m]], writes=[("lnsq", m)])
            P.mm(self.ps[7][:, 0:W], self.ones_f[:], sq[:, m, :], m == 0, m == 7, reads=[("lnsq", m)], writes=[self.psk(7)])
        P.act(mean[:, 0:W], self.ps[6][:, 0:W], AF.Copy, scale=1.0 / D, reads=[self.psk(6)], writes=["ln_mean"])
        P.tt("pool", msq[:, 0:W], mean[:, 0:W], mean[:, 0:W], ALU.mult, reads=["ln_mean"], writes=["ln_msq"])
        P.stt(var[:, 0:W], self.ps[7][:, 0:W], 1.0 / D, msq[:, 0:W], ALU.mult, ALU.subtract, reads=[self.psk(7), "ln_msq"], writes=["ln_var"])
        P.act(rstd[:, 0:W], var[:, 0:W], AF.Ln, bias=self.epsT[:, 0:1], reads=["ln_var"], writes=["ln_rstd"])
        P.act(rstd[:, 0:W], rstd[:, 0:W], AF.Exp, scale=-0.5, reads=["ln_rstd"], writes=["ln_rstd"])
        for m in range(8):
            P.tt("dve", sq[:, m, :], rT[:, m, :], mean[:, 0:W], ALU.subtract, reads=[rkeys[m], "ln_mean", ("lnsq", m)], writes=[("lnsq", m)])
            P.tt("pool", sq[:, m, :], sq[:, m, :], rstd[:, 0:W], ALU.mult, reads=[("lnsq", m), "ln_rstd"], writes=[("lnsq", m)])
            P.act(outT[:, m, :], sq[:, m, :], AF.Identity, scale=vec[:, gcol + m:gcol + m + 1], bias=vec[:, bcol + m:bcol + m + 1],
                  reads=[("lnsq", m), ("vecs", l)], writes=[(okey, m)])

    def stage_c(self, l, s):
        P = self.P
        P.sb_reset()
        T, W = s.T, 256
        si = s.sidx
        vec, modT = self.vecs[l], self.modT[l]
        nj = W // 128
        self._stg_i = 0
        stage = [P.sb([128, 8, 128], F32, "cst") for _ in range(3)]
        wb = [P.sb([128, 8, D], BF16, f"wb{i}") for i in range(3)]
        wo = P.sb([128, 8, D], BF16, "wo")
        wk = []
        for i in range(3):
            wk.append(self._load_w_resident(wb[i], self.inp["w_branch"][l, i], 8, D, f"wb{i}", stage, "cst"))
        wok = self._load_w_resident(wo, self.inp["w_out"][l], 8, D, "wo", stage, "cst")
        yt = P.sb([128, nj, D], F32, "yt")
        x0t = P.sb([128, 8, W], F32, "x0t")
        hyT = P.sb([128, 8, W], BF16, "hyT")
        atT = P.sb([128, 8, W], BF16, "atT")
        poT = P.sb([128, 8, W], BF16, "poT")
        gt = P.sb([128, 3, 8, W], BF16, "gt")
        xat = P.sb([128, 8, W], F32, "xat")
        mg = P.sb([128, 8, W], BF16, "mg")
        tA = [P.sb([128, W], F32, "tA") for _ in range(2)]
        tB = [P.sb([128, W], F32, "tB") for _ in range(2)]
        stat = [P.sb([128, 512], F32, "lnst") for _ in range(4)]
        sq = yt[:].rearrange("p j d -> p (j d)").rearrange("p (m w) -> p m w", m=8)
        srcs = (hyT, atT, poT)
        for tw in range(T // W):
            ts_ = slice(tw * W, (tw + 1) * W)
            P.dma("sp", yt[:], s.y[ts_, :].rearrange("(j p) d -> p j d", p=128), reads=[("lnsq", m) for m in range(8)], writes=["yt"])
            P.dma("act", x0t[:], s.x0T[:, ts_].rearrange("(m p) t -> p m t", p=128), writes=["x0t"])
            P.dma("sp", atT[:], s.attnT[:, ts_].rearrange("(m p) t -> p m t", p=128), writes=["atT"])
            P.dma("act", poT[:], s.poolT[:, ts_].rearrange("(m p) t -> p m t", p=128), writes=["poT"])
            for i in range(3):
                P.dma("sp" if i % 2 else "act", gt[:, i, :, :], s.gT[i, :, ts_].rearrange("(m p) t -> p m t", p=128), writes=[("gt", i)])
            P.dma("sp", xat[:], s.xa[:, ts_].rearrange("(m p) t -> p m t", p=128), writes=["xat"])
            for m in range(8):
                bank = m % 2
                for j in range(nj):
                    P.tr(self.ps[bank][:, j * 128:(j + 1) * 128], yt[:, j, m * 128:(m + 1) * 128], self.ident[:], reads=["yt"], writes=[self.psk(bank)])
                P.tt("dve", hyT[:, m, :], self.ps[bank][:, 0:W], x0t[:, m, :], ALU.mult, reads=[self.psk(bank), "x0t"], writes=[("hyT", m)])
            skeys = ([("hyT", m) for m in range(8)], ["atT"], ["poT"])
            for mo in range(8):
                q = mo % 2
                for i in range(3):
                    bank = 2 + i
                    for k in range(8):
                        P.mm(self.ps[bank][:, 0:W], wb[i][:, k, mo * 128:(mo + 1) * 128], srcs[i][:, k, :], k == 0, k == 7,
                             reads=wk[i][mo] + (skeys[i] if i else [("hyT", k)]), writes=[self.psk(bank)])
                P.tt("dve", tA[q][:], self.ps[2][:, 0:W], gt[:, 0, mo, :], ALU.mult, reads=[self.psk(2), ("gt", 0)], writes=[("tA", q)])
                P.tt("dve", tB[q][:], self.ps[3][:, 0:W], gt[:, 1, mo, :], ALU.mult, reads=[self.psk(3), ("gt", 1)], writes=[("tB", q)])
                P.tt("pool", tA[q][:], tA[q][:], tB[q][:], ALU.add, reads=[("tA", q), ("tB", q)], writes=[("tA", q)])
                P.tt("dve", tB[q][:], self.ps[4][:, 0:W], gt[:, 2, mo, :], ALU.mult, reads=[self.psk(4), ("gt", 2)], writes=[("tB", q)])
                P.tt("pool", mg[:, mo, :], tA[q][:], tB[q][:], ALU.add, reads=[("tA", q), ("tB", q)], writes=[("mg", mo)])
            for mo in range(8):
                bank = mo % 2
                for k in range(8):
                    P.mm(self.ps[bank][:, 0:W], wo[:, k, mo * 128:(mo + 1) * 128], mg[:, k, :], k == 0, k == 7,
                         reads=wok[mo] + [("mg", k)], writes=[self.psk(bank)])
                P.act(xat[:, mo, :], xat[:, mo, :], AF.Copy, scale=ALPHA, reads=["xat"], writes=[("xs", mo)])
                P.stt(xat[:, mo, :], self.ps[bank][:, 0:W], modT[:, si, 16 + mo:17 + mo], xat[:, mo, :], ALU.mult, ALU.add,
                      reads=[self.psk(bank), ("xs", mo), ("modT", l, si)], writes=[("rT", mo)])
            self._layer_norm(xat, W, self.V_LN, self.V_LN + 8, vec, l, xat, sq, stat, [("rT", m) for m in range(8)], "x1T")
            P.dma("sp", s.xb[:, ts_].rearrange("(m p) t -> p m t", p=128), xat[:], reads=[("x1T", m) for m in range(8)], writes=["xat"])
        P.barrier()

    def stage_d(self, l, s, final):
        P = self.P
        P.sb_reset()
        T = s.T
        W = 256
        si = s.sidx
        vec, modT, modP = self.vecs[l], self.modT[l], self.modP[l]
        NF = DFF // 128
        self._stg_i = 0
        stage = [P.sb([128, 8, 128], F32, "dst") for _ in range(3)]
        w1 = P.sb([128, 8, DFF], BF16, "w1")
        w3 = P.sb([128, 8, DFF], BF16, "w3")
        w2 = P.sb([128, NF, D], BF16, "w2")
        w1k = self._load_w_resident(w1, self.inp["ffn_w1"][l], 8, DFF, "w1", stage, "dst")
        w3k = self._load_w_resident(w3, self.inp["ffn_w3"][l], 8, DFF, "w3", stage, "dst")
        w2k = self._load_w_resident(w2, self.inp["ffn_w2"][l], NF, D, "w2", stage, "dst")
        x1t = P.sb([128, 8, W], F32, "x1t")
        h2T = P.sb([128, 8, W], BF16, "h2T")
        gT = P.sb([128, NF, W], BF16, "gT")
        sa = [P.sb([128, W], F32, "sa") for _ in range(2)]
        sqd = P.sb([128, 8, W], F32, "sqd")
        stat = [P.sb([128, W], F32, "lnst") for _ in range(4)]
        ot = P.sb([128, W // 128, D], F32, "ot") if final else None
        for tw in range(T // W):
            ts_ = slice(tw * W, (tw + 1) * W)
            P.dma("sp", x1t[:], s.xb[:, ts_].rearrange("(m p) t -> p m t", p=128), writes=["x1t"])
            for m in range(8):
                P.ts("dve" if m % 2 else "pool", h2T[:, m, :], x1t[:, m, :], modP[:, si, 32 + m:33 + m], modT[:, si, 24 + m:25 + m], ALU.mult, ALU.add,
                     reads=["x1t", ("modT", l, si), ("modP", l, si)], writes=[("h2T", m)])
            for f in range(NF):
                q = f % 2
                ba, bb = (0, 1) if q == 0 else (2, 3)
                for k in range(8):
                    P.mm(self.ps[ba][:, 0:W], w1[:, k, f * 128:(f + 1) * 128], h2T[:, k, :], k == 0, k == 7, reads=w1k[f] + [("h2T", k)], writes=[self.psk(ba)])
                for k in range(8):
                    P.mm(self.ps[bb][:, 0:W], w3[:, k, f * 128:(f + 1) * 128], h2T[:, k, :], k == 0, k == 7, reads=w3k[f] + [("h2T", k)], writes=[self.psk(bb)])
                P.act(sa[q][:], self.ps[ba][:, 0:W], AF.Silu, reads=[self.psk(ba)], writes=[("sa", q)])
                P.tt("dve", gT[:, f, :], self.ps[bb][:, 0:W], sa[q][:], ALU.mult, reads=[self.psk(bb), ("sa", q)], writes=[("gT", f)])
            for mo in range(8):
                bank = 4 + mo % 2
                for f in range(NF):
                    P.mm(self.ps[bank][:, 0:W], w2[:, f, mo * 128:(mo + 1) * 128], gT[:, f, :], f == 0, f == NF - 1,
                         reads=w2k[mo] + [("gT", f)], writes=[self.psk(bank)])
                P.act(x1t[:, mo, :], x1t[:, mo, :], AF.Copy, scale=ALPHA, reads=["x1t"], writes=[("xs", mo)])
                P.stt(x1t[:, mo, :], self.ps[bank][:, 0:W], modT[:, si, 40 + mo:41 + mo], x1t[:, mo, :], ALU.mult, ALU.add,
                      reads=[self.psk(bank), ("xs", mo), ("modT", l, si)], writes=[("rT", mo)])
            self._layer_norm(x1t, W, self.V_LN + 16, self.V_LN + 24, vec, l, x1t, sqd, stat, [("rT", m) for m in range(8)], "x2T")
            if not final:
                P.dma("sp", s.xa[:, ts_].rearrange("(m p) t -> p m t", p=128), x1t[:], reads=[("x2T", m) for m in range(8)], writes=["x1t"])
            else:
                for j in range(W // 128):
                    for half in range(2):
                        bank = half
                        for mm_ in range(4):
                            m = half * 4 + mm_
                            P.tr(self.ps[bank][:, mm_ * 128:(mm_ + 1) * 128], x1t[:, m, j * 128:(j + 1) * 128], self.ident[:],
                                 reads=[("x2T", m)], writes=[self.psk(bank)])
                        P.cp("act" if half else "dve", ot[:, j, half * 512:(half + 1) * 512], self.ps[bank][:, 0:512], reads=[self.psk(bank)], writes=[("ot", j, half)])
                P.dma("sp", self.out[ts_, :].rearrange("(j p) d -> p j d", p=128), ot[:],
                      reads=[("ot", j, h_) for j in range(W // 128) for h_ in range(2)] + [("x2T", m) for m in range(8)], writes=["x1t"], final=True)
        P.barrier()

    def build(self):
        st = self.stages
        def on(name):
            return st is None or name in st
        if on("p0"):
            self.stage_p0()
        if on("p1"):
            self.stage_p1()
        for l in range(DEPTH):
            if on(f"k{l}c") and l < DEPTH - 1:
                self.stage_k(l, "c")
            if on(f"k{l}l"):
                self.stage_k(l, "l")
            if on(f"a{l}c"):
                self.stage_a(l, self.st["c"])
            if on(f"a{l}l"):
                self.stage_a(l, self.st["l"])
            if on(f"h{l}c") and l < DEPTH - 1:
                self.stage_h(l, self.st["c"])
            if on(f"h{l}l"):
                self.stage_h(l, self.st["l"])
            if on(f"b{l}c") and l < DEPTH - 1:
                self.stage_b(l, self.st["c"], CTX)
            if on(f"b{l}l"):
                self.stage_b(l, self.st["l"], SEQ + CTX)
            if on(f"c{l}c") and l < DEPTH - 1:
                self.stage_c(l, self.st["c"])
            if on(f"d{l}c") and l < DEPTH - 1:
                self.stage_d(l, self.st["c"], False)
            if on(f"c{l}l"):
                self.stage_c(l, self.st["l"])
            if on(f"d{l}l"):
                self.stage_d(l, self.st["l"], l == DEPTH - 1)
        self.P.emit()
        return self.nc


def make_in_maps(inputs, n_cores=8):
    c = host_consts()
    shared = {k: np.ascontiguousarray(np.asarray(v, dtype=np.float32)) for k, v in inputs.items() if k not in ("x", "c", "ctx")}
    consts = {"k_" + k: v for k, v in c.items()}
    maps = []
    for b in range(n_cores):
        m = dict(shared)
        m.update(consts)
        m["x"] = np.ascontiguousarray(inputs["x"][b], dtype=np.float32)
        m["c"] = np.ascontiguousarray(inputs["c"][b], dtype=np.float32)
        m["ctx"] = np.ascontiguousarray(inputs["ctx"][b], dtype=np.float32)
        maps.append(m)
    return maps


def kernel(**inputs):
    bld = Builder()
    nc = bld.build()
    res = run_bass_kernel_spmd(nc, make_in_maps(inputs), core_ids=list(range(8)))
    return np.stack([np.asarray(r["out"], dtype=np.float32) for r in res.results], axis=0)
```

```python
import contextlib
import math
import numpy as np
import ml_dtypes
import concourse.bass as bass
import concourse.mybir as mybir
from concourse.bass_utils import run_bass_kernel_spmd

F32 = mybir.dt.float32
BF16 = mybir.dt.bfloat16
AF = mybir.ActivationFunctionType
ALU = mybir.AluOpType

D = 1024
SEQ = 8192
CTX = 256
DEPTH = 2
NH = 8
HD = 128
DFF = 2816
INW = 8704
C_Q, C_K, C_V, C_HY, C_POOL, C_GATE = 0, 1024, 1280, 1536, 4608, 5632
ALPHA = (2 * DEPTH) ** 0.25
EPS = 1e-6
NFFT = 16384
POOL_WINDOWS = (2, 4, 8, 16)


class _Op:
    __slots__ = ("eng", "fn", "deps", "dma", "marked", "cnt", "sem", "semval", "prev")


class Prog:
    COMPUTE = ("pe", "act", "dve", "pool")
    QUEUES = ("sp", "act", "pool")
    ALLENG = ("pe", "act", "dve", "pool", "sp")
    SB_BASE = 24576
    SB_LIMIT = 218 * 1024

    def __init__(self, ring=8, same_engine_sync=True):
        self.nc = bass.Bass("TRN2", target_bir_lowering=False)
        self.ops = []
        self.state = {}
        self.ring = ring
        self.same = same_engine_sync
        self.dma_count = {q: 0 for q in self.QUEUES}
        self.slot_last = {}
        self.slot_val = {}
        self.last_op = {e: None for e in self.ALLENG}
        self.sb_off = self.SB_BASE
        self.sb_mark = self.SB_BASE
        self.n_alloc = 0
        self.out_dmas = []
        self.psn = 0

    def sb(self, shape, dtype, name="t"):
        nbytes = int(np.prod(shape[1:])) * mybir.dt.size(dtype)
        nbytes_al = (nbytes + 63) // 64 * 64
        self.n_alloc += 1
        h = self.nc.alloc_sbuf_tensor_at(f"{name}_{self.n_alloc}", list(shape), dtype, offset=self.sb_off)
        self.sb_off += nbytes_al
        assert self.sb_off <= self.SB_LIMIT, f"SBUF overflow {self.sb_off} ({name})"
        return h

    def sb_persist_done(self):
        self.sb_mark = self.sb_off

    def sb_reset(self):
        self.sb_off = self.sb_mark

    def _collect(self, reads, writes):
        deps = set()
        for k in reads:
            st = self.state.get(k)
            if st is not None and st[0] is not None:
                deps.add(st[0])
        for k in writes:
            st = self.state.get(k)
            if st is not None:
                if st[0] is not None:
                    deps.add(st[0])
                deps.update(st[1].values())
                deps.update(st[2])
        return deps

    def add(self, eng, fn, reads=(), writes=(), dma=False, out=False):
        pr = [k for k in reads if isinstance(k, tuple) and k[0] == "ps"]
        if pr:
            reads = [k for k in reads if not (isinstance(k, tuple) and k[0] == "ps")]
            writes = list(writes) + pr
        i = len(self.ops)
        op = _Op()
        op.eng, op.fn, op.dma, op.marked, op.cnt, op.prev = eng, fn, dma, False, 0, None
        op.deps = self._collect(reads, writes)
        if dma:
            n = self.dma_count[eng]
            self.dma_count[eng] = n + 1
            slot = (eng, n % self.ring)
            op.prev = self.slot_last.get(slot)
            self.slot_last[slot] = i
            v = self.slot_val.get(slot, 0) + 16
            self.slot_val[slot] = v
            op.sem, op.semval = slot, v
            if out:
                self.out_dmas.append(i)
        self.ops.append(op)
        for k in reads:
            st = self.state.setdefault(k, [None, {}, []])
            if dma:
                st[2].append(i)
            else:
                st[1][eng] = i
        for k in writes:
            self.state[k] = [i, {}, []]
        self.last_op[eng] = i
        return i

    def barrier(self):
        snap = [v for v in self.last_op.values() if v is not None] + list(self.slot_last.values())
        for e in self.ALLENG:
            op = _Op()
            op.eng, op.fn, op.dma, op.marked, op.cnt, op.prev = e, None, False, False, 0, None
            op.deps = set(snap)
            self.ops.append(op)
        self.state = {}

    def dma(self, q, out, in_, reads=(), writes=(), final=False):
        return self.add(q, lambda e: e.dma_start(out=out, in_=in_), reads, writes, dma=True, out=final)

    def mm(self, out, lhsT, rhs, start, stop, reads=(), writes=()):
        return self.add("pe", lambda e: e.matmul(out, lhsT=lhsT, rhs=rhs, start=start, stop=stop), reads, writes)

    def tr(self, out, in_, ident, reads=(), writes=()):
        return self.add("pe", lambda e: e.transpose(out, in_, ident), reads, writes)

    def act(self, out, in_, func, reads=(), writes=(), bias=None, scale=None):
        kw = {}
        if bias is not None:
            kw["bias"] = bias
        if scale is not None:
            kw["scale"] = scale
        return self.add("act", lambda e: e.activation(out=out, in_=in_, func=func, **kw), reads, writes)

    def tt(self, eng, out, in0, in1, op, reads=(), writes=()):
        return self.add(eng, lambda e: e.tensor_tensor(out=out, in0=in0, in1=in1, op=op), reads, writes)

    def ts(self, eng, out, in0, s1, s2, op0, op1, reads=(), writes=()):
        if op1 is None:
            return self.add(eng, lambda e: e.tensor_scalar(out=out, in0=in0, scalar1=s1, scalar2=None, op0=op0), reads, writes)
        return self.add(eng, lambda e: e.tensor_scalar(out=out, in0=in0, scalar1=s1, scalar2=s2, op0=op0, op1=op1), reads, writes)

    def stt(self, out, in0, scalar, in1, op0, op1, reads=(), writes=()):
        return self.add("dve", lambda e: e.scalar_tensor_tensor(out=out, in0=in0, scalar=scalar, in1=in1, op0=op0, op1=op1), reads, writes)

    def cp(self, eng, out, in_, reads=(), writes=()):
        if eng == "act":
            return self.add("act", lambda e: e.copy(out=out, in_=in_), reads, writes)
        return self.add(eng, lambda e: e.tensor_copy(out=out, in_=in_), reads, writes)

    def memset(self, eng, ap, val, writes=()):
        return self.add(eng, lambda e: e.memset(ap, val), (), writes)

    def emit(self):
        nc = self.nc
        ops = self.ops
        fin = _Op()
        fin.eng, fin.fn, fin.dma, fin.marked, fin.cnt, fin.prev = "sp", None, False, False, 0, None
        fin.deps = set(self.out_dmas) | set(self.slot_last.values())
        ops.append(fin)
        for op in ops:
            for d in op.deps:
                dop = ops[d]
                if dop.dma or dop.fn is None:
                    continue
                dop.marked = True
        cnt = {e: 0 for e in self.COMPUTE}
        for op in ops:
            if op.fn is not None and not op.dma:
                if op.marked:
                    cnt[op.eng] += 1
                op.cnt = cnt[op.eng]
        per_eng = {e: [] for e in self.ALLENG}
        for op in ops:
            per_eng[op.eng].append(op)
        same = self.same

        with contextlib.ExitStack() as es:
            csem = {e: es.enter_context(nc.semaphore(f"c_{e}")) for e in self.COMPUTE}
            dsem = {}
            for q in self.QUEUES:
                for r in range(self.ring):
                    dsem[(q, r)] = es.enter_context(nc.semaphore(f"d_{q}_{r}"))
            block = es.enter_context(nc.Block())

            def run(ename, e):
                waited = {}
                for op in per_eng[ename]:
                    waits = {}
                    for d in op.deps:
                        dop = ops[d]
                        if dop.dma:
                            s = ("d", dop.sem)
                            waits[s] = max(waits.get(s, 0), dop.semval)
                        else:
                            if dop.fn is None:
                                continue
                            if dop.eng == ename:
                                if op.fn is None:
                                    continue
                                if not op.dma and (ename == "pe" or not same):
                                    continue
                            s = ("c", dop.eng)
                            waits[s] = max(waits.get(s, 0), dop.cnt)
                    if op.dma and op.prev is not None:
                        p = ops[op.prev]
                        s = ("d", p.sem)
                        waits[s] = max(waits.get(s, 0), p.semval)
                    for s, v in waits.items():
                        if v <= 0 or waited.get(s, 0) >= v:
                            continue
                        e.wait_ge(dsem[s[1]] if s[0] == "d" else csem[s[1]], v)
                        waited[s] = v
                    if op.fn is None:
                        continue
                    inst = op.fn(e)
                    if op.dma:
                        inst.then_inc(dsem[op.sem], 16)
                    elif op.marked:
                        inst.then_inc(csem[ename], 1)

            block.tensor(lambda e: run("pe", e))
            block.scalar(lambda e: run("act", e))
            block.vector(lambda e: run("dve", e))
            block.gpsimd(lambda e: run("pool", e))
            block.sync(lambda e: run("sp", e))
        return nc


def _bf(a):
    return np.asarray(a, dtype=np.float32).astype(ml_dtypes.bfloat16)


def _hy_tables(L):
    f32 = np.float32
    t_idx = np.arange(L, dtype=f32)
    t01 = (t_idx / f32(max(L - 1, 1))).astype(f32)
    bands = np.linspace(1e-4, 15.0, 16, dtype=f32)
    ang = (f32(2.0 * math.pi / L) * t_idx[:, None] * bands[None, :]).astype(f32)
    feats = np.concatenate([t01[:, None], np.cos(ang), -np.sin(ang)], axis=-1).astype(f32)
    fF = np.zeros((33, 8192), f32)
    fR = np.zeros((33, 8192), f32)
    negt = np.zeros((128, 128), f32)
    mask = np.zeros((128, 128), f32)
    fF[:, :L] = feats.T
    n = np.arange(8192)
    tF = np.where(n < L, n, 0)
    negt[:64] = -np.where(n < L, t01[tF], 0).reshape(64, 128)
    mask[:64] = (n < L).astype(f32).reshape(64, 128)
    m = 8192 - n
    valid = (m >= 1) & (m <= L - 1)
    mm = np.where(valid, m, 0)
    fR[:, valid] = feats[mm[valid]].T
    negt[64:] = -np.where(valid, t01[mm], 0).reshape(64, 128)
    mask[64:] = valid.astype(f32).reshape(64, 128)
    return fF, fR, negt, mask


def _rope_tables(T, grid_w=64, ctx=False):
    f32 = np.float32
    if ctx:
        return np.ones((128, T), f32), np.zeros((128, T), f32)
    t = np.arange(T)
    rows = (t // grid_w).astype(f32)
    cols = (t % grid_w).astype(f32)
    inv = np.power(f32(10000.0), -np.arange(32, dtype=f32) / f32(32)).astype(f32)
    C = np.zeros((128, T), f32)
    S = np.zeros((128, T), f32)
    for j in range(128):
        pos = rows if j < 64 else cols
        jj = j % 64
        ang = (pos * inv[jj % 32]).astype(f32)
        C[j] = np.cos(ang)
        S[j] = -np.sin(ang) if jj < 32 else np.sin(ang)
    return C, S


_CONST_CACHE = {}


def host_consts():
    if _CONST_CACHE:
        return _CONST_CACHE
    c = {}
    c["ident"] = np.eye(128, dtype=np.float32)
    c["identb"] = _bf(np.eye(128))
    c["ropeC_l"], c["ropeS_l"] = _rope_tables(SEQ)
    c["ropeC_c"], c["ropeS_c"] = _rope_tables(CTX, ctx=True)
    a = np.arange(128, dtype=np.float64)
    th = 2 * np.pi * np.outer(a, a) / 128.0
    c["F1"] = _bf(np.stack([np.cos(th), -np.sin(th)], axis=1))
    c["I2"] = _bf(np.stack([np.cos(th)[:, :64], -np.sin(th)[:, :64]], axis=1) / NFFT)
    k1 = a[:, None, None]
    n2 = a[None, :, None]
    k2 = a[None, None, :]
    th3 = 2 * np.pi * n2 * (k1 + 128.0 * k2) / NFFT
    G = np.stack([np.cos(th3), -np.sin(th3), np.sin(th3)], axis=2)
    c["GT"] = _bf(G)
    c["HT"] = _bf(np.transpose(G, (0, 3, 2, 1)))
    for tag, L in (("l", SEQ), ("c", CTX)):
        fF, fR, negt, mask = _hy_tables(L)
        c[f"featF_{tag}"], c[f"featR_{tag}"], c[f"negt_{tag}"], c[f"mask_{tag}"] = fF, fR, negt, mask
    lo, hi = math.log(1e-2) / 1.5, math.log(1e-2) / 0.3
    c["deltas"] = np.abs(np.linspace(lo, hi, 1024, dtype=np.float32)).reshape(1, 1024).astype(np.float32)
    edge = np.zeros((4, 2, 8), np.float32)
    Tt = 4096
    for g, w in enumerate(POOL_WINDOWS):
        before, after = w // 2, w - 1 - w // 2
        for side in range(2):
            for i in range(8):
                t = i if side == 0 else Tt - 8 + i
                cnt = min(t + after + 1, Tt) - max(t - before, 0)
                edge[g, side, i] = 1.0 / cnt
    c["pedge"] = np.broadcast_to(edge.reshape(1, 64), (128, 64)).copy()
    t01c = (np.arange(CTX, dtype=np.float32) / np.float32(CTX - 1)).astype(np.float32)
    c["t01row_c"] = np.broadcast_to(t01c.reshape(1, CTX), (128, CTX)).copy()
    c["ndeltasT"] = np.ascontiguousarray(-c["deltas"].reshape(8, 128).T)
    _CONST_CACHE.update(c)
    return c


CONST_SPECS = None


def const_specs():
    c = host_consts()
    return {k: (list(v.shape), BF16 if v.dtype == ml_dtypes.bfloat16 else F32) for k, v in c.items()}


INPUT_SHAPES = {
    "x": [SEQ, D], "c": [D], "ctx": [CTX, D], "c_ctx": [D],
    "w_ada": [DEPTH, D, 6 * D], "b_ada": [DEPTH, 6 * D], "w_in": [DEPTH, D, INW],
    "q_norm_g": [DEPTH, HD], "k_norm_g": [DEPTH, HD],
    "hy_conv_w": [DEPTH, 3, 3 * D], "hy_conv_b": [DEPTH, 3 * D],
    "hf_w1": [DEPTH, 33, 64], "hf_b1": [DEPTH, 64], "hf_freq": [DEPTH, 64],
    "hf_w2": [DEPTH, 64, 64], "hf_b2": [DEPTH, 64], "hf_w3": [DEPTH, 64, 2 * D],
    "hy_d": [DEPTH, D], "pool_w": [DEPTH, 4, 256, 256], "pool_scale": [DEPTH, D],
    "w_branch": [DEPTH, 3, D, D], "w_out": [DEPTH, D, D],
    "ln1_g": [DEPTH, D], "ln1_b": [DEPTH, D], "ln2_g": [DEPTH, D], "ln2_b": [DEPTH, D],
    "ffn_w1": [DEPTH, D, DFF], "ffn_w3": [DEPTH, D, DFF], "ffn_w2": [DEPTH, DFF, D],
}


class Stream:
    pass


class Builder:
    def __init__(self, debug_outs=(), stages=None):
        self.P = Prog()
        self.nc = self.P.nc
        self.debug_outs = set(debug_outs)
        self.stages = stages
        nc = self.nc
        self.inp = {k: nc.dram_tensor(k, shp, F32, kind="ExternalInput").ap() for k, shp in INPUT_SHAPES.items()}
        self.cst = {k: nc.dram_tensor("k_" + k, shp, dt, kind="ExternalInput").ap() for k, (shp, dt) in const_specs().items()}
        self.out = nc.dram_tensor("out", [SEQ, D], F32, kind="ExternalOutput").ap()
        self.ps = [nc.alloc_psum_tensor(f"psb{i}", [128, 512], F32) for i in range(8)]
        self.scr = {}
        self._persistent()
        self._streams()

    def dram(self, name, shape, dtype):
        kind = "ExternalOutput" if name in self.debug_outs else "Internal"
        t = self.nc.dram_tensor(name, list(shape), dtype, kind=kind).ap()
        self.scr[name] = t
        return t

    def psk(self, i):
        return ("ps", i)

    def _persistent(self):
        P = self.P
        self.ident = P.sb([128, 128], F32, "ident")
        self.identb = P.sb([128, 128], BF16, "identb")
        self.ones_f = P.sb([128, 128], F32, "ones_f")
        self.ones_b = P.sb([128, 128], BF16, "ones_b")
        self.epsT = P.sb([128, 1], F32, "epsT")
        self.vecs = [P.sb([128, 200], F32, f"vecs{l}") for l in range(DEPTH)]
        self.modT = [P.sb([128, 2, 48], F32, f"modT{l}") for l in range(DEPTH)]
        self.modP = [P.sb([128, 2, 48], F32, f"modP{l}") for l in range(DEPTH)]
        self.hfv = [P.sb([64, 8], F32, f"hfv{l}") for l in range(DEPTH)]
        self.pedge = P.sb([128, 64], F32, "pedge")
        P.sb_persist_done()

    V_QG, V_QGP, V_KG, V_KGP = 0, 1, 2, 3
    V_CW = 4
    V_CB = 76
    V_PS = 100
    V_LN = 108
    V_BA = 140
    V_HD = 188
    V_N = 196

    def _streams(self):
        self.XA = self.dram("XA", [D, SEQ], F32)
        self.XB = self.dram("XB", [D, SEQ], F32)
        self.XCA = self.dram("XCA", [D, CTX], F32)
        self.XCB = self.dram("XCB", [D, CTX], F32)
        self.kT_d = self.dram("kT_d", [256, SEQ + CTX], BF16)
        self.v_d = self.dram("v_d", [SEQ + CTX, 256], BF16)
        self.Bd = [self.dram(f"Bd{i}", [128, 128, 1024], BF16) for i in range(2)]
        self.Dd = [self.dram(f"Dd{i}", [128, 128, 1024], BF16) for i in range(2)]
        self.Kf = {t: [self.dram(f"Kf_{t}{i}", [128, 128, 1024], BF16) for i in range(2)] for t in ("l", "c")}
        self.st = {}
        for tag, T in (("l", SEQ), ("c", CTX)):
            s = Stream()
            s.tag, s.T = tag, T
            s.TS = 4096 if tag == "l" else 256
            s.W = 512 if tag == "l" else 256
            s.sidx = 0 if tag == "l" else 1
            s.xa = self.XA if tag == "l" else self.XCA
            s.xb = self.XB if tag == "l" else self.XCB
            s.ktok0 = CTX if tag == "l" else 0
            s.qT = self.dram(f"qT_{tag}", [D, T], BF16)
            s.z = self.dram(f"z_{tag}", [T, D], BF16)
            s.x0T = self.dram(f"x0T_{tag}", [D, T], F32)
            s.poolT = self.dram(f"poolT_{tag}", [D, T], BF16)
            s.gT = self.dram(f"gT_{tag}", [3, D, T], BF16)
            s.attnT = self.dram(f"attnT_{tag}", [D, T], BF16)
            s.y = self.dram(f"y_{tag}", [T, D], F32)
            s.zT = self.dram(f"zT_{tag}", [D, T], F32) if tag == "c" else None
            s.hyT = self.dram(f"hyT_{tag}", [D, T], BF16) if tag == "c" else None
            s.ropeC = self.cst[f"ropeC_{tag}"]
            s.ropeS = self.cst[f"ropeS_{tag}"]
            self.st[tag] = s

    def stage_p0(self):
        P = self.P
        P.sb_reset()
        P.dma("sp", self.ident[:], self.cst["ident"], writes=["ident"])
        P.dma("sp", self.identb[:], self.cst["identb"], writes=["identb"])
        P.memset("dve", self.ones_f[:], 1.0, writes=["ones_f"])
        P.memset("dve", self.ones_b[:], 1.0, writes=["ones_b"])
        P.memset("dve", self.epsT[:], EPS, writes=["epsT"])
        xt = [P.sb([128, 4, D], F32, "p0x") for _ in range(2)]
        xo = [P.sb([128, 8, 512], F32, "p0o") for _ in range(2)]
        it = 0
        for src, dst, T in ((self.inp["x"], self.XA, SEQ), (self.inp["ctx"], self.XCA, CTX)):
            W = min(512, T)
            nj = W // 128
            for w in range(T // W):
                b = it % 2
                it += 1
                P.dma("sp", xt[b][:, 0:nj, :], src[w * W:(w + 1) * W, :].rearrange("(j p) d -> p j d", p=128),
                      writes=[("p0x", b)])
                for m in range(8):
                    bank = m % 4
                    for j in range(nj):
                        P.tr(self.ps[bank][:, j * 128:(j + 1) * 128], xt[b][:, j, m * 128:(m + 1) * 128], self.ident[:],
                             reads=[("p0x", b), "ident"], writes=[self.psk(bank)])
                    P.cp("act" if m % 2 else "dve", xo[b][:, m, 0:W], self.ps[bank][:, 0:W],
                         reads=[self.psk(bank)], writes=[("p0o", b, m)])
                P.dma("act", dst[:, w * W:(w + 1) * W].rearrange("(m p) t -> p m t", p=128), xo[b][:, :, 0:W],
                      reads=[("p0o", b, m) for m in range(8)])
        P.barrier()

    def stage_p1(self):
        P = self.P
        P.sb_reset()
        I = self.inp
        stg = P.sb([128, 2, 128], F32, "stg")
        stgc = P.sb([16, 128], F32, "stgc")
        stg64 = P.sb([8, 64], F32, "stg64")
        scT = P.sb([128, 8, 2], F32, "scT")
        cT = P.sb([128, 16], F32, "cT")
        wa = [P.sb([128, 8, 512], F32, "wa") for _ in range(2)]
        P.dma("sp", stgc[0:8, :], I["c"].rearrange("(m p) -> m p", p=128), writes=["stgc"])
        P.dma("sp", stgc[8:16, :], I["c_ctx"].rearrange("(m p) -> m p", p=128), writes=["stgc2"])
        P.tr(self.ps[0][:, 0:16], stgc[0:16, :], self.ident[0:16, 0:16], reads=["stgc", "stgc2", "ident"], writes=[self.psk(0)])
        P.act(cT[:], self.ps[0][:, 0:16], AF.Silu, reads=[self.psk(0)], writes=["cT"])
        for s in range(2):
            P.cp("dve", scT[:, :, s], cT[:, s * 8:(s + 1) * 8], reads=["cT"], writes=[("scT", s)])
        for l in range(DEPTH):
            rows = []
            g = I["q_norm_g"][l]
            kg = I["k_norm_g"][l]
            rows.append(("full", g))
            rows.append(("perm", g))
            rows.append(("full", kg))
            rows.append(("perm", kg))
            for j in range(3):
                for m in range(24):
                    rows.append(("full", I["hy_conv_w"][l, j, m * 128:(m + 1) * 128]))
            for m in range(24):
                rows.append(("full", I["hy_conv_b"][l, m * 128:(m + 1) * 128]))
            for m in range(8):
                rows.append(("full", I["pool_scale"][l, m * 128:(m + 1) * 128]))
            for nm in ("ln1_g", "ln1_b", "ln2_g", "ln2_b"):
                for m in range(8):
                    rows.append(("full", I[nm][l, m * 128:(m + 1) * 128]))
            for m in range(48):
                rows.append(("full", I["b_ada"][l, m * 128:(m + 1) * 128]))
            for m in range(8):
                rows.append(("full", I["hy_d"][l, m * 128:(m + 1) * 128]))
            assert len(rows) == self.V_N
            def ld(r0, ap2d, n):
                grp, rr = divmod(r0, 128)
                assert rr + n <= 128
                P.dma("sp", stg[rr:rr + n, grp, :], ap2d, writes=[("stg", r0)])
                return ("stg", r0)
            keys = []
            for ri, (kind, ap) in enumerate(rows[:4]):
                grp, rr = divmod(ri, 128)
                if kind == "full":
                    P.dma("sp", stg[rr:rr + 1, grp, :], ap.rearrange("(o n) -> o n", o=1), writes=[("stg", ri)])
                else:
                    for q4, src0 in enumerate((32, 0, 96, 64)):
                        P.dma("sp", stg[rr:rr + 1, grp, q4 * 32:(q4 + 1) * 32],
                              ap[src0:src0 + 32].rearrange("(o n) -> o n", o=1), writes=[("stg", ri, q4)])
                        keys.append(("stg", ri, q4))
                keys.append(("stg", ri))
            keys.append(ld(4, I["hy_conv_w"][l].rearrange("j (m p) -> (j m) p", p=128), 72))
            keys.append(ld(76, I["hy_conv_b"][l].rearrange("(m p) -> m p", p=128), 24))
            keys.append(ld(100, I["pool_scale"][l].rearrange("(m p) -> m p", p=128), 8))
            for qi, nm in enumerate(("ln1_g", "ln1_b")):
                keys.append(ld(108 + qi * 8, I[nm][l].rearrange("(m p) -> m p", p=128), 8))
            keys.append(ld(124, I["ln2_g"][l, 0:512].rearrange("(m p) -> m p", p=128), 4))
            keys.append(ld(128, I["ln2_g"][l, 512:1024].rearrange("(m p) -> m p", p=128), 4))
            keys.append(ld(132, I["ln2_b"][l].rearrange("(m p) -> m p", p=128), 8))
            keys.append(ld(140, I["b_ada"][l].rearrange("(m p) -> m p", p=128), 48))
            keys.append(ld(188, I["hy_d"][l].rearrange("(m p) -> m p", p=128), 8))
            P.tr(self.ps[1][:, 0:128], stg[:, 0, :], self.ident[:], reads=keys + ["ident"], writes=[self.psk(1)])
            P.tr(self.ps[1][:, 128:128 + 68], stg[0:68, 1, :], self.ident[0:68, 0:68], reads=keys + ["ident"], writes=[self.psk(1)])
            P.cp("dve", self.vecs[l][:, 0:196], self.ps[1][:, 0:196], reads=[self.psk(1)], writes=[("vecs", l)])
            for ci, nm in enumerate(("hf_b1", "hf_freq", "hf_b2")):
                P.dma("sp", stg64[ci:ci + 1, :], I[nm][l].rearrange("(o n) -> o n", o=1), writes=[("stg64", ci)])
            P.tr(self.ps[2][0:64, 0:3], stg64[0:3, :], self.ident[0:3, 0:3],
                 reads=[("stg64", ci) for ci in range(3)] + ["ident"], writes=[self.psk(2)])
            P.cp("dve", self.hfv[l][:, 0:3], self.ps[2][0:64, 0:3], reads=[self.psk(2)], writes=[("hfv", l)])
            P.tt("dve", self.hfv[l][:, 3:4], self.hfv[l][:, 0:1], self.hfv[l][:, 1:2], ALU.mult, reads=[("hfv", l)], writes=[("hfv3", l)])
            P.tt("dve", self.hfv[l][:, 4:5], self.hfv[l][:, 2:3], self.hfv[l][:, 1:2], ALU.mult, reads=[("hfv", l)], writes=[("hfv4", l)])
            bank = 3
            for cg in range(12):
                b = cg % 2
                P.dma("sp" if cg % 2 else "act", wa[b][:], I["w_ada"][l][:, cg * 512:(cg + 1) * 512].rearrange("(k p) n -> p k n", p=128),
                      writes=[("wa", b)])
                for mm in range(4):
                    m = cg * 4 + mm
                    for k in range(8):
                        P.mm(self.ps[bank][:, m * 2:m * 2 + 2], wa[b][:, k, mm * 128:(mm + 1) * 128], scT[:, k, :],
                             start=(k == 0), stop=(k == 7), reads=[("wa", b), ("scT", 0), ("scT", 1)], writes=[self.psk(bank)])
            psv = self.ps[bank][:, 0:96].rearrange("p (m s) -> p m s", s=2)
            for s in range(2):
                P.tt("dve", self.modT[l][:, s, :], psv[:, :, s], self.vecs[l][:, self.V_BA:self.V_BA + 48], ALU.add,
                     reads=[self.psk(bank), ("vecs", l)], writes=[("modT", l, s)])
                P.ts("dve", self.modP[l][:, s, :], self.modT[l][:, s, :], 1.0, None, ALU.add, None,
                     reads=[("modT", l, s)], writes=[("modP", l, s)])
            if "dbg_mod" in self.debug_outs:
                if l == 0:
                    self.dbg_mod = self.dram("dbg_mod", [DEPTH, 128, 96], F32)
                    self.dbg_vec = self.dram("dbg_vec", [DEPTH, 128, 192], F32)
                P.dma("sp", self.dbg_mod[l], self.modT[l][:].rearrange("p s m -> p (s m)"), reads=[("modT", l, 0), ("modT", l, 1)])
                P.dma("sp", self.dbg_vec[l], self.vecs[l][:, 0:192], reads=[("vecs", l)])
        P.barrier()

    def _proj(self, ps_ap, wt, hT, w, W, keys_w, bank, col0=0, ncol=None):
        P = self.P
        for k in range(8):
            P.mm(ps_ap, wt[:, k, :], hT[:, k, col0 + w * W: col0 + w * W + (ncol or W)],
                 start=(k == 0), stop=(k == 7), reads=[keys_w, ("hT", k, w)], writes=[self.psk(bank)])

    def stage_a(self, l, s):
        P = self.P
        P.sb_reset()
        T, TS, W = s.T, s.TS, s.W
        NW = TS // W
        NB = TS // 128
        si = s.sidx
        vec, modT, modP = self.vecs[l], self.modT[l], self.modP[l]
        w_in = self.inp["w_in"][l]
        hT = P.sb([128, 8, TS + 16], BF16, "hT")
        wring = [P.sb([128, 8, 128], BF16, "wr") for _ in range(4)]
        wcount = [0]

        wstage = [P.sb([128, 8, 128], F32, "wst") for _ in range(3)]
        scount = [0]

        def load_w(col0, ncols=128, buf=None, key=None):
            if buf is None:
                i = wcount[0] % 4
                wcount[0] += 1
                buf, key = wring[i], ("wr", i)
            for c in range(0, ncols, 128):
                j = scount[0] % 3
                scount[0] += 1
                P.dma("sp" if j % 2 else "act", wstage[j][:], w_in[:, col0 + c:col0 + c + 128].rearrange("(k p) n -> p k n", p=128),
                      writes=[("wst", j)])
                P.cp("pool", buf[:, :, c:c + 128], wstage[j][:], reads=[("wst", j)], writes=[key if ncols == 128 else (key, c)])
            return buf, key

        mark = P.sb_off
        P.dma("sp", self.pedge[:], self.cst["pedge"], writes=["pedge"])
        for sti in range(T // TS):
            t0 = sti * TS
            P.sb_off = mark
            xs = [P.sb([128, 8, W], F32, "xs") for _ in range(2)]
            hal = P.sb([128, 8, 16], F32, "hal")
            for w in range(NW):
                b = w % 2
                P.dma("sp", xs[b][:], s.xa[:, t0 + w * W: t0 + (w + 1) * W].rearrange("(m p) t -> p m t", p=128), writes=[("xs", b)])
                for m in range(8):
                    eng = ("act", "dve", "pool")[m % 3]
                    o = hT[:, m, w * W:(w + 1) * W]
                    if eng == "act":
                        P.act(o, xs[b][:, m, :], AF.Identity, scale=modP[:, si, 8 + m:9 + m], bias=modT[:, si, m:m + 1],
                              reads=[("xs", b), ("modT", l, si), ("modP", l, si)], writes=[("hT", m, w)])
                    else:
                        P.ts(eng, o, xs[b][:, m, :], modP[:, si, 8 + m:9 + m], modT[:, si, m:m + 1], ALU.mult, ALU.add,
                             reads=[("xs", b), ("modT", l, si), ("modP", l, si)], writes=[("hT", m, w)])
            hk = []
            for side, (a, b_) in enumerate(((t0 - 8, t0), (t0 + TS, t0 + TS + 8))):
                if a >= 0 and b_ <= T:
                    P.dma("sp", hal[:, :, side * 8:(side + 1) * 8], s.xa[:, a:b_].rearrange("(m p) t -> p m t", p=128), writes=[("hal", side)])
                    for m in range(8):
                        P.ts("dve", hT[:, m, TS + side * 8: TS + side * 8 + 8], hal[:, m, side * 8:(side + 1) * 8],
                             modP[:, si, 8 + m:9 + m], modT[:, si, m:m + 1], ALU.mult, ALU.add,
                             reads=[("hal", side), ("modT", l, si), ("modP", l, si)], writes=[("hT", m, "h%d" % side)])
                else:
                    for m in range(8):
                        P.memset("dve", hT[:, m, TS + side * 8: TS + side * 8 + 8], 0.0, writes=[("hT", m, "h%d" % side)])
            P.barrier()
            if getattr(self, 'a_stop', None) == 'ph0':
                return

            def proj_halo(ps_ap, wt, wkey, bank):
                for k in range(8):
                    P.mm(ps_ap, wt[:, k, :], hT[:, k, TS:TS + 16], start=(k == 0), stop=(k == 7),
                         reads=[wkey, ("hT", k, "h0"), ("hT", k, "h1")], writes=[self.psk(bank)])

            P.sb_off = mark
            rC = P.sb([128, TS], F32, "rC")
            rS = P.sb([128, TS], F32, "rS")
            P.dma("sp", rC[:], s.ropeC[:, t0:t0 + TS], writes=["rC"])
            P.dma("act", rS[:], s.ropeS[:, t0:t0 + TS], writes=["rS"])
            wp = [P.sb([128, 8, 128], BF16, "wp") for _ in range(2)]
            sqb = [P.sb([128, W], F32, "sqb") for _ in range(2)]
            rs = [P.sb([128, W], F32, "rs") for _ in range(2)]
            t1 = [P.sb([128, W], F32, "t1") for _ in range(2)]
            t2 = [P.sb([128, W], F32, "t2") for _ in range(2)]
            qrow = [P.sb([128, TS], BF16, "qrow") for _ in range(2)]
            it = 0
            for hc in range(10):
                wq, wk = load_w(hc * 128)
                pb = hc % 2
                for q4, src0 in enumerate((32, 0, 96, 64)):
                    P.cp("pool", wp[pb][:, :, q4 * 32:(q4 + 1) * 32], wq[:, :, src0:src0 + 32], reads=[wk], writes=[("wp", pb, q4)])
                wpk = [("wp", pb, q4) for q4 in range(4)]
                gcol = self.V_QG if hc < 8 else self.V_KG
                r = hc % 2
                for w in range(NW):
                    i = it % 2
                    it += 1
                    bq, bp, bs = (0, 1, 4) if i == 0 else (2, 3, 5)
                    QL = 9
                    if QL < 2:
                        continue
                    self._proj(self.ps[bq][:, 0:W], wq, hT, w, W, wk, bq)
                    for k in range(8):
                        P.mm(self.ps[bp][:, 0:W], wp[pb][:, k, :], hT[:, k, w * W:(w + 1) * W], start=(k == 0), stop=(k == 7),
                             reads=wpk + [("hT", k, w)], writes=[self.psk(bp)])
                    if QL < 3:
                        continue
                    P.act(sqb[i][:], self.ps[bq][:, 0:W], AF.Square, reads=[self.psk(bq)], writes=[("sqb", i)])
                    P.mm(self.ps[bs][:, 0:W], self.ones_f[:], sqb[i][:], start=True, stop=True,
                         reads=["ones_f", ("sqb", i)], writes=[self.psk(bs)])
                    P.act(rs[i][:], self.ps[bs][:, 0:W], AF.Ln, scale=1.0 / 128.0, bias=self.epsT[:, 0:1],
                          reads=[self.psk(bs), "epsT"], writes=[("rs", i)])
                    P.act(rs[i][:], rs[i][:], AF.Exp, scale=-0.5, reads=[("rs", i)], writes=[("rs", i)])
                    if QL < 4:
                        continue
                    P.stt(t1[i][:], self.ps[bq][:, 0:W], vec[:, gcol:gcol + 1], rC[:, w * W:(w + 1) * W], ALU.mult, ALU.mult,
                          reads=[self.psk(bq), ("vecs", l), "rC"], writes=[("t1", i)])
                    P.stt(t2[i][:], self.ps[bp][:, 0:W], vec[:, gcol + 1:gcol + 2], rS[:, w * W:(w + 1) * W], ALU.mult, ALU.mult,
                          reads=[self.psk(bp), ("vecs", l), "rS"], writes=[("t2", i)])
                    if QL < 5:
                        continue
                    P.tt("pool", t1[i][:], t1[i][:], t2[i][:], ALU.add, reads=[("t1", i), ("t2", i)], writes=[("t1", i)])
                    P.tt("pool", qrow[r][:, w * W:(w + 1) * W], t1[i][:], rs[i][:], ALU.mult,
                         reads=[("t1", i), ("rs", i)], writes=[("qrow", r, w)])
                if hc < 8:
                    dst = s.qT[hc * 128:(hc + 1) * 128, t0:t0 + TS]
                else:
                    dst = self.kT_d[(hc - 8) * 128:(hc - 7) * 128, s.ktok0 + t0: s.ktok0 + t0 + TS]
                if QL >= 6:
                    P.dma("sp", dst, qrow[r][:], reads=[("qrow", r, w) for w in range(NW)])
            P.barrier()
            if getattr(self, 'a_stop', None) == 'qk':
                return

            P.sb_off = mark
            wv = P.sb([128, 8, 256], BF16, "wv")
            vrow = P.sb([128, NB, 256], BF16, "vrow")
            load_w(C_V, 256, wv, "wv")
            wvk = [("wv", 0), ("wv", 128)]
            for tb in range(NB):
                bank = tb % 4
                w = (tb * 128) // W
                for k in range(8):
                    P.mm(self.ps[bank][:, 0:256], hT[:, k, tb * 128:(tb + 1) * 128], wv[:, k, :], start=(k == 0), stop=(k == 7),
                         reads=wvk + [("hT", k, w)], writes=[self.psk(bank)])
                P.cp("act" if tb % 2 else "dve", vrow[:, tb, :], self.ps[bank][:, 0:256], reads=[self.psk(bank)], writes=[("vrow", tb)])
            P.dma("sp", self.v_d[s.ktok0 + t0: s.ktok0 + t0 + TS, :].rearrange("(b p) c -> p b c", p=128), vrow[:],
                  reads=[("vrow", tb) for tb in range(NB)])
            P.barrier()
            if getattr(self, 'a_stop', None) == 'v':
                return

            P.sb_off = mark
            ubuf = [P.sb([128, TS + 2], F32, "ubuf") for _ in range(2)]
            sA = P.sb([128, TS], F32, "sA")
            sB = P.sb([128, TS], F32, "sB")
            zrow = P.sb([128, TS], BF16, "zrow")
            ztile = P.sb([128, NB, 128], BF16, "ztile")
            uc = 0
            for j in range(8):
                for part, (cchunk, cm, dst, dk) in enumerate(((12 + j, j, sA, "sA"), (28 + j, 16 + j, sB, "sB"), (20 + j, 8 + j, sA, "sA"))):
                    ub = uc % 2
                    uc += 1
                    wt, wk = load_w(cchunk * 128)
                    ukeys = []
                    for w in range(NW):
                        bank = w % 4
                        self._proj(self.ps[bank][:, 0:W], wt, hT, w, W, wk, bank)
                        P.cp("act" if w % 2 else "dve", ubuf[ub][:, 1 + w * W: 1 + (w + 1) * W], self.ps[bank][:, 0:W],
                             reads=[self.psk(bank)], writes=[("ubuf", ub, w)])
                        ukeys.append(("ubuf", ub, w))
                    proj_halo(self.ps[4][:, 0:16], wt, wk, 4)
                    P.cp("dve", ubuf[ub][:, 0:TS + 2:TS + 1], self.ps[4][:, 7:9], reads=[self.psk(4)], writes=[("ubuf", ub, "h")])
                    ukeys.append(("ubuf", ub, "h"))
                    c0 = self.V_CW + cm
                    P.act(dst[:], ubuf[ub][:, 1:TS + 1], AF.Identity, scale=vec[:, c0 + 24:c0 + 25], bias=vec[:, self.V_CB + cm:self.V_CB + cm + 1],
                          reads=ukeys + [("vecs", l)], writes=[dk])
                    P.stt(dst[:], ubuf[ub][:, 0:TS], vec[:, c0:c0 + 1], dst[:], ALU.mult, ALU.add, reads=ukeys + [dk, ("vecs", l)], writes=[dk])
                    P.stt(dst[:], ubuf[ub][:, 2:TS + 2], vec[:, c0 + 48:c0 + 49], dst[:], ALU.mult, ALU.add, reads=ukeys + [dk, ("vecs", l)], writes=[dk])
                    if part == 1 and s.tag == "c":
                        P.tt("pool", sB[:], sA[:], sB[:], ALU.mult, reads=["sA", "sB"], writes=["sB"])
                        P.dma("sp", s.zT[j * 128:(j + 1) * 128, t0:t0 + TS], sB[:], reads=["sB"])
                    elif part == 1:
                        P.tt("pool", zrow[:], sA[:], sB[:], ALU.mult, reads=["sA", "sB"], writes=["zrow"])
                        for blk in range(NB):
                            bank = 6 + (blk // 8) % 2
                            pv = self.ps[bank][:].bitcast(BF16)
                            P.tr(pv[:, (blk % 8) * 128:(blk % 8 + 1) * 128], zrow[:, blk * 128:(blk + 1) * 128], self.identb[:],
                                 reads=["zrow", "identb"], writes=[self.psk(bank)])
                            if blk % 8 == 7 or blk == NB - 1:
                                b0 = blk - blk % 8
                                n = blk - b0 + 1
                                P.cp("act" if (blk // 8) % 2 else "dve", ztile[:, b0:b0 + n, :].rearrange("p b c -> p (b c)"), pv[:, 0:n * 128],
                                     reads=[self.psk(bank)], writes=[("ztile", b0)])
                        P.dma("sp", s.z[t0:t0 + TS, j * 128:(j + 1) * 128].rearrange("(b p) c -> p b c", p=128), ztile[:],
                              reads=[("ztile", b0) for b0 in range(0, NB, 8)])
                    if part == 2:
                        P.dma("act", s.x0T[j * 128:(j + 1) * 128, t0:t0 + TS], sA[:], reads=["sA"])
            P.barrier()
            if getattr(self, 'a_stop', None) == 'hy':
                return

            P.sb_off = mark
            n = TS + 16
            pbuf = [P.sb([128, n], F32, "pbuf") for _ in range(2)]
            A = P.sb([128, n], F32, "pA")
            Bb = P.sb([128, n], F32, "pB")
            mT = [P.sb([128, TS], BF16, "mT") for _ in range(2)]
            prow = [P.sb([128, TS], BF16, "prow") for _ in range(2)]
            pw = P.sb([128, 2, 256], BF16, "pw")
            pwf = P.sb([128, 2, 256], F32, "pwf")
            tmp8 = P.sb([128, 8], F32, "tmp8")
            pe4 = self.pedge[:].rearrange("p (g s e) -> p g s e", g=4, s=2)
            for g in range(4):
                wsz = POOL_WINDOWS[g]
                kk = g + 1
                o = 8 + wsz // 2 - 1
                P.dma("sp", pwf[:], self.inp["pool_w"][l, g].rearrange("(i p) o -> p i o", p=128), writes=["pwf"])
                P.cp("pool", pw[:], pwf[:], reads=["pwf"], writes=["pw"])
                for i in range(2):
                    wt, wk = load_w((36 + 2 * g + i) * 128)
                    pk = []
                    for w in range(NW):
                        bank = w % 4
                        self._proj(self.ps[bank][:, 0:W], wt, hT, w, W, wk, bank)
                        P.cp("act" if w % 2 else "dve", pbuf[i][:, 8 + w * W: 8 + (w + 1) * W], self.ps[bank][:, 0:W],
                             reads=[self.psk(bank)], writes=[("pbuf", i, w)])
                        pk.append(("pbuf", i, w))
                    proj_halo(self.ps[4][:, 0:16], wt, wk, 4)
                    P.cp("dve", pbuf[i][:, 0:8], self.ps[4][:, 0:8], reads=[self.psk(4)], writes=[("pbuf", i, "h0")])
                    P.cp("dve", pbuf[i][:, TS + 8:TS + 16], self.ps[4][:, 8:16], reads=[self.psk(4)], writes=[("pbuf", i, "h1")])
                    pk += [("pbuf", i, "h0"), ("pbuf", i, "h1")]
                    u = pbuf[i]
                    P.tt("pool", A[:, 1:n], u[:, 1:n], u[:, 0:n - 1], ALU.add, reads=pk, writes=["pA"])
                    R, rk = A, "pA"
                    if kk >= 2:
                        P.tt("pool", Bb[:, 3:n], A[:, 3:n], A[:, 1:n - 2], ALU.add, reads=["pA"], writes=["pB"])
                        R, rk = Bb, "pB"
                    if kk >= 3:
                        P.tt("pool", A[:, 7:n], Bb[:, 7:n], Bb[:, 3:n - 4], ALU.add, reads=["pB"], writes=["pA"])
                        R, rk = A, "pA"
                    if kk >= 4:
                        P.tt("pool", Bb[:, 15:n], A[:, 15:n], A[:, 7:n - 8], ALU.add, reads=["pA"], writes=["pB"])
                        R, rk = Bb, "pB"
                    P.stt(mT[i][:], R[:, o:o + TS], 1.0 / wsz, u[:, 8:8 + TS], ALU.mult, ALU.subtract, reads=[rk] + pk, writes=[("mT", i)])
                    if t0 == 0:
                        P.tt("dve", tmp8[:], R[:, o:o + 8], pe4[:, g, 0, :], ALU.mult, reads=[rk, "pedge"], writes=["tmp8"])
                        P.tt("dve", mT[i][:, 0:8], tmp8[:], u[:, 8:16], ALU.subtract, reads=["tmp8"] + pk, writes=[("mT", i)])
                    if t0 + TS == T:
                        P.tt("dve", tmp8[:], R[:, o + TS - 8:o + TS], pe4[:, g, 1, :], ALU.mult, reads=[rk, "pedge"], writes=["tmp8"])
                        P.tt("dve", mT[i][:, TS - 8:TS], tmp8[:], u[:, TS:TS + 8], ALU.subtract, reads=["tmp8"] + pk, writes=[("mT", i)])
                for oc in range(2):
                    for w in range(NW):
                        bank = w % 4
                        for i in range(2):
                            P.mm(self.ps[bank][:, 0:W], pw[:, i, oc * 128:(oc + 1) * 128], mT[i][:, w * W:(w + 1) * W],
                                 start=(i == 0), stop=(i == 1), reads=["pw", ("mT", i)], writes=[self.psk(bank)])
                        cidx = self.V_PS + 2 * g + oc
                        P.act(prow[oc][:, w * W:(w + 1) * W], self.ps[bank][:, 0:W], AF.Identity, scale=vec[:, cidx:cidx + 1],
                              reads=[self.psk(bank), ("vecs", l)], writes=[("prow", oc, w)])
                    P.dma("sp", s.poolT[(2 * g + oc) * 128:(2 * g + oc + 1) * 128, t0:t0 + TS], prow[oc][:],
                          reads=[("prow", oc, w) for w in range(NW)])
            P.barrier()
            if getattr(self, 'a_stop', None) == 'pool':
                return

            P.sb_off = mark
            grow = [P.sb([128, TS], BF16, "grow") for _ in range(2)]
            for gc in range(24):
                r = gc % 2
                wt, wk = load_w((44 + gc) * 128)
                for w in range(NW):
                    bank = w % 4
                    self._proj(self.ps[bank][:, 0:W], wt, hT, w, W, wk, bank)
                    P.act(grow[r][:, w * W:(w + 1) * W], self.ps[bank][:, 0:W], AF.Sigmoid, reads=[self.psk(bank)], writes=[("grow", r, w)])
                P.dma("sp", s.gT[gc // 8, (gc % 8) * 128:(gc % 8 + 1) * 128, t0:t0 + TS], grow[r][:],
                      reads=[("grow", r, w) for w in range(NW)])
            P.barrier()
            if getattr(self, 'a_stop', None) == 'gate':
                return

    def stage_b(self, l, s, NK):
        P = self.P
        P.sb_reset()
        NQ, W = s.T, s.W
        NB = NK // 128
        KT = P.sb([128, 2, NK], BF16, "KT")
        V = P.sb([128, NB, 256], BF16, "V")
        for kv in range(2):
            P.dma("sp" if kv else "act", KT[:, kv, :], self.kT_d[kv * 128:(kv + 1) * 128, 0:NK], writes=[("KT", kv)])
        vsrc = self.v_d[0:NK, :].rearrange("(b p) c -> p b c", p=128)
        vk = []
        for b0 in range(0, NB, 11):
            b1 = min(NB, b0 + 11)
            P.dma("sp", V[:, b0:b1, :], vsrc[:, b0:b1, :], writes=[("V", b0)])
            vk.append(("V", b0))
        QT = [P.sb([128, 8, W], BF16, "QT") for _ in range(2)]
        attT = [P.sb([128, 8, W], BF16, "attT") for _ in range(2)]
        NPT = 8
        pT = [P.sb([128, W], BF16, "pT") for _ in range(NPT)]
        pool_tbs = [tb for tb in range(NB) if tb % 8 in (1, 4, 6)]
        dve_tbs = [tb for tb in range(NB) if tb % 8 not in (1, 4, 6)]
        rden = [P.sb([128, W], F32, "rden") for _ in range(2)]
        accD = [P.sb([128, W], F32, "accD") for _ in range(2)]
        accP = [P.sb([128, W], F32, "accP") for _ in range(2)]
        scale = float(HD) ** -0.5
        steps = [(qw, h, tb) for qw in range(NQ // W) for h in range(8) for tb in range(NB)]
        n = len(steps)

        def issue_S(i):
            qw, h, tb = steps[i]
            r = i % 4
            if h == 0 and tb == 0:
                P.dma("sp", QT[qw % 2][:], s.qT[:, qw * W:(qw + 1) * W].rearrange("(h p) t -> p h t", p=128), writes=[("QT", qw % 2)])
            P.mm(self.ps[r][:, 0:W], KT[:, h // 4, tb * 128:(tb + 1) * 128], QT[qw % 2][:, h, :], True, True,
                 reads=[("KT", h // 4), ("QT", qw % 2)], writes=[self.psk(r)])

        LA = 3
        for i in range(min(LA, n)):
            issue_S(i)
        for i, (qw, h, tb) in enumerate(steps):
            r = i % 4
            r4 = i % NPT
            kv = h // 4
            hp = h % 2
            ob = 4 + hp
            P.act(pT[r4][:], self.ps[r][:, 0:W], AF.Exp, scale=scale, reads=[self.psk(r)], writes=[("pT", r4)])
            P.mm(self.ps[ob][:, 0:W], V[:, tb, kv * 128:(kv + 1) * 128], pT[r4][:], tb == 0, tb == NB - 1,
                 reads=vk + [("pT", r4)], writes=[self.psk(ob)])
            ab = 6 + hp
            if tb in dve_tbs:
                if tb == dve_tbs[0] and tb == dve_tbs[-1]:
                    P.cp("dve", accD[hp][:], pT[r4][:], reads=[("pT", r4)], writes=[("accD", hp)])
                elif tb == dve_tbs[0]:
                    P.cp("dve", self.ps[ab][:, 0:W], pT[r4][:], reads=[("pT", r4)], writes=[self.psk(ab)])
                elif tb != dve_tbs[-1]:
                    P.tt("dve", self.ps[ab][:, 0:W], self.ps[ab][:, 0:W], pT[r4][:], ALU.add, reads=[("pT", r4), self.psk(ab)], writes=[self.psk(ab)])
                else:
                    P.tt("dve", accD[hp][:], self.ps[ab][:, 0:W], pT[r4][:], ALU.add, reads=[("pT", r4), self.psk(ab)], writes=[("accD", hp)])
            else:
                if tb == pool_tbs[0]:
                    P.cp("pool", accP[hp][:], pT[r4][:], reads=[("pT", r4)], writes=[("accP", hp)])
                else:
                    P.tt("pool", accP[hp][:], accP[hp][:], pT[r4][:], ALU.add, reads=[("pT", r4), ("accP", hp)], writes=[("accP", hp)])
            if i + LA < n:
                issue_S(i + LA)
            if tb == NB - 1:
                rd = rden[hp]
                db = ab
                P.mm(self.ps[db][:, 0:W], self.ones_f[:], accD[hp][:], True, False, reads=[("accD", hp)], writes=[self.psk(db)])
                P.mm(self.ps[db][:, 0:W], self.ones_f[:], accP[hp][:], False, True, reads=[("accP", hp)], writes=[self.psk(db)])
                P.add("dve", lambda e, rd=rd, db=db: e.reciprocal(out=rd[:], in_=self.ps[db][:, 0:W]), reads=[self.psk(db)], writes=[("rden", hp)])
                P.tt("dve", attT[qw % 2][:, h, :], self.ps[ob][:, 0:W], rd[:], ALU.mult,
                     reads=[self.psk(ob), ("rden", hp)], writes=[("attT", qw % 2, h)])
                if h == 7:
                    P.dma("act", s.attnT[:, qw * W:(qw + 1) * W].rearrange("(h p) t -> p h t", p=128), attT[qw % 2][:],
                          reads=[("attT", qw % 2, hh) for hh in range(8)])
        P.barrier()

    def _sin_layer(self, ps_ap, fcol, bcol, hv, tmp, tmp2, out_ap, psbank, okey):
        P = self.P
        MAGIC = 12582912.0
        P.ts("dve", tmp, ps_ap, hv[:, fcol:fcol + 1], hv[:, bcol:bcol + 1], ALU.mult, ALU.add, reads=[self.psk(psbank)], writes=["sl_tmp"])
        P.ts("dve", tmp2, tmp, 1.0 / (2 * math.pi), MAGIC, ALU.mult, ALU.add, reads=["sl_tmp"], writes=["sl_tmp2"])
        P.ts("dve", tmp2, tmp2, MAGIC, -2 * math.pi, ALU.subtract, ALU.mult, reads=["sl_tmp2"], writes=["sl_tmp2"])
        P.tt("dve", tmp, tmp, tmp2, ALU.add, reads=["sl_tmp", "sl_tmp2"], writes=["sl_tmp"])
        P.act(out_ap, tmp, AF.Sin, reads=["sl_tmp"], writes=[okey])

    def stage_k(self, l, tag):
        P = self.P
        P.sb_reset()
        I = self.inp
        hv = self.hfv[l]
        Kf = self.Kf[tag]
        w1s = P.sb([33, 64], F32, "w1s")
        w2s = P.sb([64, 64], F32, "w2s")
        w3f = P.sb([64, 2048], F32, "w3f")
        w3b = P.sb([64, 2048], BF16, "w3b")
        h2T = [P.sb([64, 8192], BF16, "h2T") for _ in range(2)]
        dl = P.sb([128, 1024], F32, "dl")
        drow = P.sb([128, 1024], F32, "drow")
        nrow = P.sb([128, 1024], F32, "nrow")
        negt = P.sb([128, 128], F32, "negt")
        mask = P.sb([128, 128], F32, "mask")
        F1 = P.sb([128, 2, 128], BF16, "F1")
        P.dma("sp", w1s[:], I["hf_w1"][l], writes=["w1s"])
        P.dma("sp", w2s[:], I["hf_w2"][l], writes=["w2s"])
        P.dma("sp", w3f[:], I["hf_w3"][l], writes=["w3f"])
        P.cp("pool", w3b[:], w3f[:], reads=["w3f"], writes=["w3b"])
        P.dma("act", dl[:], self.cst["deltas"].partition_broadcast(128).rearrange("p o c -> p (o c)"), writes=["dl"])
        P.dma("act", drow[:], I["hy_d"][l].rearrange("(o c) -> o c", o=1).partition_broadcast(128).rearrange("p o c -> p (o c)"), writes=["drow"])
        P.dma("act", negt[:], self.cst[f"negt_{tag}"], writes=["negt"])
        P.dma("act", mask[:], self.cst[f"mask_{tag}"], writes=["mask"])
        P.dma("act", F1[:], self.cst["F1"], writes=["F1"])
        mark = P.sb_off
        ft = [P.sb([33, 512], F32, "ft") for _ in range(2)]
        tmp = P.sb([64, 512], F32, "sl_tmp")
        tmp2 = P.sb([64, 512], F32, "sl_tmp2")
        h1 = P.sb([64, 512], F32, "h1")
        it = 0
        for d, nm in enumerate((f"featF_{tag}", f"featR_{tag}")):
            for w in range(16):
                b = it % 2
                it += 1
                P.dma("sp", ft[b][:], self.cst[nm][:, w * 512:(w + 1) * 512], writes=[("ft", b)])
                P.mm(self.ps[0][0:64, 0:512], w1s[:], ft[b][:], True, True, reads=["w1s", ("ft", b)], writes=[self.psk(0)])
                self._sin_layer(self.ps[0][0:64, 0:512], 1, 3, hv, tmp[:], tmp2[:], h1[:], 0, "h1")
                P.mm(self.ps[1][0:64, 0:512], w2s[:], h1[:], True, True, reads=["w2s", "h1"], writes=[self.psk(1)])
                self._sin_layer(self.ps[1][0:64, 0:512], 1, 4, hv, tmp[:], tmp2[:], h2T[d][:, w * 512:(w + 1) * 512], 1, ("h2T", d))
        P.sb_off = mark
        kt = [P.sb([128, 8, 1024], BF16, "kt") for _ in range(2)]
        wn = [P.sb([128, 1024], F32, "wn") for _ in range(2)]
        sqb = [P.sb([128, 1024], BF16, "sqk") for _ in range(2)]
        Bt = [[P.sb([128, 8, 1024], BF16, "Btk") for _ in range(2)] for _ in range(2)]
        for jg in range(16):
            kb = jg % 2
            for n2i in range(8):
                n2 = jg * 8 + n2i
                i = n2 % 2
                ba = 0 if i == 0 else 2
                for ch in range(2):
                    P.mm(self.ps[ba + ch][0:64, 0:512], h2T[0][:, n2:8192:128], w3b[:, ch * 512:(ch + 1) * 512], True, True,
                         reads=[("h2T", 0), "w3b"], writes=[self.psk(ba + ch)])
                    P.mm(self.ps[ba + ch][64:128, 0:512], h2T[1][:, n2:8192:128], w3b[:, 1024 + ch * 512:1024 + (ch + 1) * 512], True, True,
                         reads=[("h2T", 1), "w3b"], writes=[self.psk(ba + ch)])
                P.act(wn[i][:], dl[:], AF.Exp, scale=negt[:, n2:n2 + 1], reads=["dl", "negt"], writes=[("wn", i)])
                P.ts("pool", wn[i][:], wn[i][:], 0.05, None, ALU.add, None, reads=[("wn", i)], writes=[("wn", i)])
                for ch in range(2):
                    P.stt(kt[kb][:, n2i, ch * 512:(ch + 1) * 512], self.ps[ba + ch][:, 0:512], mask[:, n2:n2 + 1], wn[i][:, ch * 512:(ch + 1) * 512],
                          ALU.mult, ALU.mult, reads=[self.psk(ba + ch), "mask", ("wn", i)], writes=[("kt", kb, n2i)])
                P.act(sqb[i][:], kt[kb][:, n2i, :], AF.Square, reads=[("kt", kb, n2i)], writes=[("sqk", i)])
                for ch in range(2):
                    P.mm(self.ps[6 + ch][:, 0:512], self.ones_b[:], sqb[i][:, ch * 512:(ch + 1) * 512], n2 == 0, n2 == 127,
                         reads=[("sqk", i)], writes=[self.psk(6 + ch)])
            ktf = kt[kb][:].rearrange("p a c -> p (a c)")
            for ri in range(2):
                btf = Bt[ri][kb][:].rearrange("p a c -> p (a c)")
                for cw in range(16):
                    bank = 4 + (cw % 2)
                    P.mm(self.ps[bank][:, 0:512], F1[:, ri, :], ktf[:, cw * 512:(cw + 1) * 512], True, True,
                         reads=["F1"] + [("kt", kb, q) for q in range(8)], writes=[self.psk(bank)])
                    P.cp("act" if cw % 2 else "dve", btf[:, cw * 512:(cw + 1) * 512], self.ps[bank][:, 0:512],
                         reads=[self.psk(bank)], writes=[("Btk", ri, kb, cw)])
                P.dma("sp" if ri else "act", self.Bd[ri][:, jg * 8:(jg + 1) * 8, :], Bt[ri][kb][:],
                      reads=[("Btk", ri, kb, cw) for cw in range(16)], writes=[("Bd", ri, jg)])
        for ch in range(2):
            P.act(nrow[:, ch * 512:(ch + 1) * 512], self.ps[6 + ch][:, 0:512], AF.Ln, bias=self.epsT[:, 0:1], reads=[self.psk(6 + ch)], writes=[("nrow", ch)])
            P.act(nrow[:, ch * 512:(ch + 1) * 512], nrow[:, ch * 512:(ch + 1) * 512], AF.Exp, scale=-0.5, reads=[("nrow", ch)], writes=[("nrow", ch)])
        P.barrier()
        P.sb_off = mark
        Br = [[P.sb([128, 1024], BF16, "Brk") for _ in range(2)] for _ in range(2)]
        G = [P.sb([128, 3, 128], BF16, "Gk") for _ in range(2)]
        Kt = [[P.sb([128, 1024], BF16, "Kt") for _ in range(2)] for _ in range(2)]
        tz = [P.sb([128, 512], F32, "tz") for _ in range(2)]
        for k1 in range(128):
            b = k1 % 2
            for ri in range(2):
                P.dma("sp" if ri else "act", Br[ri][b][:], self.Bd[ri][k1], writes=[("Brk", ri, b)])
            P.dma("sp", G[b][:], self.cst["GT"][k1], writes=[("Gk", b)])
            for ch in range(2):
                bs = 0 if (2 * k1 + ch) % 2 == 0 else 2
                cs = slice(ch * 512, (ch + 1) * 512)
                rk = [("Brk", 0, b), ("Brk", 1, b), ("Gk", b)]
                P.mm(self.ps[bs][:, 0:512], G[b][:, 0, :], Br[0][b][:, cs], True, False, reads=rk, writes=[self.psk(bs)])
                P.mm(self.ps[bs][:, 0:512], G[b][:, 2, :], Br[1][b][:, cs], False, True, reads=rk, writes=[self.psk(bs)])
                P.mm(self.ps[bs + 1][:, 0:512], G[b][:, 1, :], Br[0][b][:, cs], True, False, reads=rk, writes=[self.psk(bs + 1)])
                P.mm(self.ps[bs + 1][:, 0:512], G[b][:, 0, :], Br[1][b][:, cs], False, True, reads=rk, writes=[self.psk(bs + 1)])
                P.tt("dve", tz[ch][:], self.ps[bs][:, 0:512], nrow[:, cs], ALU.mult, reads=[self.psk(bs), ("nrow", ch)], writes=[("tz", ch)])
                P.tt("pool", Kt[0][b][:, cs], tz[ch][:], drow[:, cs], ALU.add, reads=[("tz", ch), "drow"], writes=[("Kt", 0, b, ch)])
                P.tt("dve", Kt[1][b][:, cs], self.ps[bs + 1][:, 0:512], nrow[:, cs], ALU.mult, reads=[self.psk(bs + 1), ("nrow", ch)], writes=[("Kt", 1, b, ch)])
            for ri in range(2):
                P.dma("sp" if ri else "act", Kf[ri][k1], Kt[ri][b][:], reads=[("Kt", ri, b, 0), ("Kt", ri, b, 1)])
        P.barrier()

    def stage_h(self, l, s):
        P = self.P
        P.sb_reset()
        Kf = self.Kf[s.tag]
        nval = 64 if s.tag == "l" else s.T // 128
        F1 = P.sb([128, 2, 128], BF16, "F1")
        I2 = P.sb([128, 2, 64], BF16, "I2")
        P.dma("act", F1[:], self.cst["F1"], writes=["F1"])
        P.dma("act", I2[:], self.cst["I2"], writes=["I2"])
        mark = P.sb_off
        zt = [P.sb([64, 8, 1024], BF16, "zt") for _ in range(2)]
        Bt = [[P.sb([128, 8, 1024], BF16, "Bth") for _ in range(2)] for _ in range(2)]
        zv = s.z.rearrange("(a b) c -> a b c", b=128)
        if nval < 64:
            for b in range(2):
                P.memset("pool", zt[b][:], 0.0, writes=[("zt", b)])
        for jg in range(16):
            b = jg % 2
            P.dma("sp", zt[b][0:nval, :, :], zv[0:nval, jg * 8:(jg + 1) * 8, :], reads=[("zt", b)] if nval < 64 else [], writes=[("ztd", b)])
            ztf = zt[b][:].rearrange("p a c -> p (a c)")
            for ri in range(2):
                btf = Bt[ri][b][:].rearrange("p a c -> p (a c)")
                for cw in range(16):
                    bank = (cw % 4)
                    P.mm(self.ps[bank][:, 0:512], F1[0:64, ri, :], ztf[:, cw * 512:(cw + 1) * 512], True, True,
                         reads=["F1", ("ztd", b), ("zt", b)], writes=[self.psk(bank)])
                    P.cp("act" if cw % 2 else "dve", btf[:, cw * 512:(cw + 1) * 512], self.ps[bank][:, 0:512],
                         reads=[self.psk(bank)], writes=[("Bth", ri, b, cw)])
                P.dma("sp" if ri else "act", self.Bd[ri][:, jg * 8:(jg + 1) * 8, :], Bt[ri][b][:],
                      reads=[("Bth", ri, b, cw) for cw in range(16)], writes=[("Bd", ri, jg)])
        P.barrier()
        P.sb_off = mark
        Br = [[P.sb([128, 1024], BF16, "Brh") for _ in range(2)] for _ in range(2)]
        Kt = [[P.sb([128, 1024], BF16, "Kth") for _ in range(2)] for _ in range(2)]
        G = [P.sb([128, 3, 128], BF16, "Gh") for _ in range(2)]
        Hh = [P.sb([128, 3, 128], BF16, "Hh") for _ in range(2)]
        Y = [[P.sb([128, 512], BF16, "Yh") for _ in range(2)] for _ in range(2)]
        tq = [[P.sb([128, 512], F32, "tq") for _ in range(4)] for _ in range(2)]
        Dt = [[P.sb([128, 1024], BF16, "Dth") for _ in range(2)] for _ in range(2)]
        for k1 in range(128):
            b = k1 % 2
            for ri in range(2):
                P.dma("sp", Br[ri][b][:], self.Bd[ri][k1], writes=[("Brh", ri, b)])
                P.dma("act", Kt[ri][b][:], Kf[ri][k1], writes=[("Kth", ri, b)])
            P.dma("sp", G[b][:], self.cst["GT"][k1], writes=[("Gh", b)])
            P.dma("act", Hh[b][:], self.cst["HT"][k1], writes=[("Hh", b)])
            for ch in range(2):
                par = ch
                bs = 0 if par == 0 else 4
                cs = slice(ch * 512, (ch + 1) * 512)
                rk = [("Brh", 0, b), ("Brh", 1, b), ("Gh", b)]
                P.mm(self.ps[bs][:, 0:512], G[b][:, 0, :], Br[0][b][:, cs], True, False, reads=rk, writes=[self.psk(bs)])
                P.mm(self.ps[bs][:, 0:512], G[b][:, 2, :], Br[1][b][:, cs], False, True, reads=rk, writes=[self.psk(bs)])
                P.mm(self.ps[bs + 1][:, 0:512], G[b][:, 1, :], Br[0][b][:, cs], True, False, reads=rk, writes=[self.psk(bs + 1)])
                P.mm(self.ps[bs + 1][:, 0:512], G[b][:, 0, :], Br[1][b][:, cs], False, True, reads=rk, writes=[self.psk(bs + 1)])
                t = tq[par]
                kk = [("Kth", 0, b), ("Kth", 1, b)]
                P.tt("dve", t[0][:], self.ps[bs][:, 0:512], Kt[0][b][:, cs], ALU.mult, reads=[self.psk(bs)] + kk, writes=[("tq", par, 0)])
                P.tt("dve", t[1][:], self.ps[bs + 1][:, 0:512], Kt[1][b][:, cs], ALU.mult, reads=[self.psk(bs + 1)] + kk, writes=[("tq", par, 1)])
                P.tt("dve", t[2][:], self.ps[bs][:, 0:512], Kt[1][b][:, cs], ALU.mult, reads=[self.psk(bs)] + kk, writes=[("tq", par, 2)])
                P.tt("dve", t[3][:], self.ps[bs + 1][:, 0:512], Kt[0][b][:, cs], ALU.mult, reads=[self.psk(bs + 1)] + kk, writes=[("tq", par, 3)])
                P.tt("pool", Y[0][par][:], t[0][:], t[1][:], ALU.subtract, reads=[("tq", par, 0), ("tq", par, 1)], writes=[("Yh", 0, par)])
                P.tt("pool", Y[1][par][:], t[2][:], t[3][:], ALU.add, reads=[("tq", par, 2), ("tq", par, 3)], writes=[("Yh", 1, par)])
                yk = [("Yh", 0, par), ("Yh", 1, par), ("Hh", b)]
                P.mm(self.ps[bs + 2][:, 0:512], Hh[b][:, 0, :], Y[0][par][:], True, False, reads=yk, writes=[self.psk(bs + 2)])
                P.mm(self.ps[bs + 2][:, 0:512], Hh[b][:, 1, :], Y[1][par][:], False, True, reads=yk, writes=[self.psk(bs + 2)])
                P.mm(self.ps[bs + 3][:, 0:512], Hh[b][:, 2, :], Y[0][par][:], True, False, reads=yk, writes=[self.psk(bs + 3)])
                P.mm(self.ps[bs + 3][:, 0:512], Hh[b][:, 0, :], Y[1][par][:], False, True, reads=yk, writes=[self.psk(bs + 3)])
                P.cp("act", Dt[0][b][:, cs], self.ps[bs + 2][:, 0:512], reads=[self.psk(bs + 2)], writes=[("Dth", 0, b, ch)])
                P.cp("act", Dt[1][b][:, cs], self.ps[bs + 3][:, 0:512], reads=[self.psk(bs + 3)], writes=[("Dth", 1, b, ch)])
            for ri in range(2):
                P.dma("sp" if ri else "act", self.Dd[ri][k1], Dt[ri][b][:], reads=[("Dth", ri, b, 0), ("Dth", ri, b, 1)])
        P.barrier()
        P.sb_off = mark
        Dr = [[P.sb([128, 8, 1024], BF16, "Drh") for _ in range(2)] for _ in range(2)]
        yt = [P.sb([64, 8, 1024], F32, "yt") for _ in range(2)]
        yv = s.y.rearrange("(a b) c -> a b c", b=128)
        for jg in range(16):
            b = jg % 2
            for ri in range(2):
                P.dma("sp" if ri else "act", Dr[ri][b][:], self.Dd[ri][:, jg * 8:(jg + 1) * 8, :], writes=[("Drh", ri, b)])
            d0 = Dr[0][b][:].rearrange("p a c -> p (a c)")
            d1 = Dr[1][b][:].rearrange("p a c -> p (a c)")
            ytf = yt[b][:].rearrange("p a c -> p (a c)")
            for cw in range(16):
                bank = cw % 4
                cs = slice(cw * 512, (cw + 1) * 512)
                P.mm(self.ps[bank][0:64, 0:512], I2[:, 0, :], d0[:, cs], True, False, reads=["I2", ("Drh", 0, b), ("Drh", 1, b)], writes=[self.psk(bank)])
                P.mm(self.ps[bank][0:64, 0:512], I2[:, 1, :], d1[:, cs], False, True, reads=["I2", ("Drh", 0, b), ("Drh", 1, b)], writes=[self.psk(bank)])
                P.cp("act" if cw % 2 else "dve", ytf[:, cs], self.ps[bank][0:64, 0:512], reads=[self.psk(bank)], writes=[("yt", b, cw)])
            P.dma("sp", yv[0:nval, jg * 8:(jg + 1) * 8, :], yt[b][0:nval, :, :], reads=[("yt", b, cw) for cw in range(16)])
        P.barrier()

    def stage_hc(self, l, s):
        P = self.P
        P.sb_reset()
        I = self.inp
        hv, vec = self.hfv[l], self.vecs[l]
        L = s.T
        w1s = P.sb([33, 64], F32, "w1s")
        w2s = P.sb([64, 64], F32, "w2s")
        w3f = P.sb([64, 2048], F32, "w3f")
        ft = P.sb([33, L], F32, "ft")
        tmp = P.sb([64, L], F32, "sl_tmp")
        tmp2 = P.sb([64, L], F32, "sl_tmp2")
        h1 = P.sb([64, L], F32, "h1")
        h2 = P.sb([64, L], F32, "h2")
        t01 = P.sb([128, L], F32, "t01")
        ndl = P.sb([128, 8], F32, "ndl")
        P.dma("sp", w1s[:], I["hf_w1"][l], writes=["w1s"])
        P.dma("sp", w2s[:], I["hf_w2"][l], writes=["w2s"])
        P.dma("sp", w3f[:], I["hf_w3"][l], writes=["w3f"])
        P.dma("act", ft[:], self.cst["featF_c"][:, 0:L], writes=["ft"])
        P.dma("act", t01[:], self.cst["t01row_c"], writes=["t01"])
        P.dma("act", ndl[:], self.cst["ndeltasT"], writes=["ndl"])
        P.mm(self.ps[0][0:64, 0:L], w1s[:], ft[:], True, True, reads=["w1s", "ft"], writes=[self.psk(0)])
        self._sin_layer(self.ps[0][0:64, 0:L], 1, 3, hv, tmp[:], tmp2[:], h1[:], 0, "h1")
        P.mm(self.ps[1][0:64, 0:L], w2s[:], h1[:], True, True, reads=["w2s", "h1"], writes=[self.psk(1)])
        self._sin_layer(self.ps[1][0:64, 0:L], 1, 4, hv, tmp[:], tmp2[:], h2[:], 1, "h2")
        NP = 3 * L - 2
        zp = [P.sb([128, NP], F32, "zp") for _ in range(2)]
        win = [P.sb([128, L], F32, "win") for _ in range(2)]
        kF = [P.sb([128, L], F32, "kF") for _ in range(2)]
        kB = [P.sb([128, L], F32, "kB") for _ in range(2)]
        junk = P.sb([128, L], F32, "junk")
        ss = [P.sb([128, 4], F32, "ss") for _ in range(2)]
        accF = [P.sb([128, L], F32, "accF") for _ in range(2)]
        accB = [P.sb([128, L], F32, "accB") for _ in range(2)]
        x0c = [P.sb([128, L], F32, "x0c") for _ in range(2)]
        hyo = [P.sb([128, L], BF16, "hyo") for _ in range(2)]
        for b in range(2):
            P.memset("pool", zp[b][:], 0.0, writes=[("zp", b)])
        for j in range(8):
            b = j % 2
            bf, bb = (2, 3) if b == 0 else (4, 5)
            P.dma("sp", zp[b][:, L - 1:2 * L - 1], s.zT[j * 128:(j + 1) * 128, :], reads=[("zp", b)], writes=[("zpd", b)])
            P.dma("act", x0c[b][:], s.x0T[j * 128:(j + 1) * 128, :], writes=[("x0c", b)])
            P.mm(self.ps[bf][:, 0:L], w3f[:, j * 128:(j + 1) * 128], h2[:], True, True, reads=["w3f", "h2"], writes=[self.psk(bf)])
            P.mm(self.ps[bb][:, 0:L], w3f[:, 1024 + j * 128:1024 + (j + 1) * 128], h2[:], True, True, reads=["w3f", "h2"], writes=[self.psk(bb)])
            P.act(win[b][:], t01[:], AF.Exp, scale=ndl[:, j:j + 1], reads=["t01", "ndl"], writes=[("win", b)])
            P.ts("pool", win[b][:], win[b][:], 0.05, None, ALU.add, None, reads=[("win", b)], writes=[("win", b)])
            P.tt("dve", kF[b][:], self.ps[bf][:, 0:L], win[b][:], ALU.mult, reads=[self.psk(bf), ("win", b)], writes=[("kF", b)])
            P.tt("dve", kB[b][:], self.ps[bb][:, 0:L], win[b][:], ALU.mult, reads=[self.psk(bb), ("win", b)], writes=[("kB", b)])
            P.memset("dve", kB[b][:, 0:1], 0.0, writes=[("kB", b)])
            P.add("act", lambda e, b=b: e.activation(out=junk[:], in_=kF[b][:], func=AF.Square, accum_out=ss[b][:, 0:1]),
                  reads=[("kF", b)], writes=[("ss", b, 0), "junk"])
            P.add("act", lambda e, b=b: e.activation(out=junk[:], in_=kB[b][:], func=AF.Square, accum_out=ss[b][:, 1:2]),
                  reads=[("kB", b)], writes=[("ss", b, 1), "junk"])
            P.tt("pool", ss[b][:, 2:3], ss[b][:, 0:1], ss[b][:, 1:2], ALU.add, reads=[("ss", b, 0), ("ss", b, 1)], writes=[("ss", b, 2)])
            P.act(ss[b][:, 2:3], ss[b][:, 2:3], AF.Ln, bias=self.epsT[:, 0:1], reads=[("ss", b, 2)], writes=[("ss", b, 2)])
            P.act(ss[b][:, 3:4], ss[b][:, 2:3], AF.Exp, scale=-0.5, reads=[("ss", b, 2)], writes=[("ss", b, 3)])
            zk = [("zp", b), ("zpd", b)]
            P.ts("dve", accF[b][:], zp[b][:, L - 1:2 * L - 1], kF[b][:, 0:1], None, ALU.mult, None, reads=zk + [("kF", b)], writes=[("accF", b)])
            P.ts("dve", accB[b][:], zp[b][:, L:2 * L], kB[b][:, 1:2], None, ALU.mult, None, reads=zk + [("kB", b)], writes=[("accB", b)])
            for m in range(1, L):
                P.stt(accF[b][:], zp[b][:, L - 1 - m:2 * L - 1 - m], kF[b][:, m:m + 1], accF[b][:], ALU.mult, ALU.add,
                      reads=[("accF", b)], writes=[("accF", b)])
                if m >= 2:
                    P.stt(accB[b][:], zp[b][:, L - 1 + m:2 * L - 1 + m], kB[b][:, m:m + 1], accB[b][:], ALU.mult, ALU.add,
                          reads=[("accB", b)], writes=[("accB", b)])
            P.tt("pool", accF[b][:], accF[b][:], accB[b][:], ALU.add, reads=[("accF", b), ("accB", b)], writes=[("accF", b)])
            P.ts("pool", accB[b][:], zp[b][:, L - 1:2 * L - 1], vec[:, self.V_HD + j:self.V_HD + j + 1], None, ALU.mult, None,
                 reads=zk + [("accB", b), ("vecs", l)], writes=[("accB", b)])
            P.stt(accF[b][:], accF[b][:], ss[b][:, 3:4], accB[b][:], ALU.mult, ALU.add, reads=[("accF", b), ("accB", b), ("ss", b, 3)], writes=[("accF", b)])
            P.tt("pool", hyo[b][:], accF[b][:], x0c[b][:], ALU.mult, reads=[("accF", b), ("x0c", b)], writes=[("hyo", b)])
            P.dma("sp", s.hyT[j * 128:(j + 1) * 128, :], hyo[b][:], reads=[("hyo", b)])
        P.barrier()

    def _load_w_resident(self, dst, src2d, nrow_chunks, ncols, key, stage, skey):
        P = self.P
        n = 0
        for c0 in range(0, ncols, 128):
            for k0 in range(0, nrow_chunks, 8):
                kn = min(8, nrow_chunks - k0)
                j = self._stg_i % len(stage)
                self._stg_i += 1
                P.dma("sp" if j % 2 else "act", stage[j][:, 0:kn, :],
                      src2d[k0 * 128:(k0 + kn) * 128, c0:c0 + 128].rearrange("(k p) n -> p k n", p=128), writes=[(skey, j)])
                P.cp("pool" if n % 2 else "dve", dst[:, k0:k0 + kn, c0:c0 + 128], stage[j][:, 0:kn, :], reads=[(skey, j)], writes=[(key, c0, k0)])
                n += 1
        return [[(key, c0, k0) for k0 in range(0, nrow_chunks, 8)] for c0 in range(0, ncols, 128)]

    def _layer_norm(self, rT, W, gcol, bcol, vec, l, outT, sq, stat, rkeys, okey):
        P = self.P
        mean, msq, var, rstd = stat
        for m in range(8):
            P.mm(self.ps[6][:, 0:W], self.ones_f[:], rT[:, m, :], m == 0, m == 7, reads=[rkeys[m]], writes=[self.psk(6)])
        for m in range(8):
            P.act(sq[:, m, :], rT[:, m, :], AF.Square, reads=[rkeys[m]], writes=[("lnsq", m)])
            P.mm(self.ps[7][:, 0:W], self.ones_f[:], sq[:, m, :], m == 0, m == 7, reads=[("lnsq", m)], writes=[self.psk(7)])
        P.act(mean[:, 0:W], self.ps[6][:, 0:W], AF.Copy, scale=1.0 / D, reads=[self.psk(6)], writes=["ln_mean"])
        P.tt("pool", msq[:, 0:W], mean[:, 0:W], mean[:, 0:W], ALU.mult, reads=["ln_mean"], writes=["ln_msq"])
        P.stt(var[:, 0:W], self.ps[7][:, 0:W], 1.0 / D, msq[:, 0:W], ALU.mult, ALU.subtract, reads=[self.psk(7), "ln_msq"], writes=["ln_var"])
        P.act(rstd[:, 0:W], var[:, 0:W], AF.Ln, bias=self.epsT[:, 0:1], reads=["ln_var"], writes=["ln_rstd"])
        P.act(rstd[:, 0:W], rstd[:, 0:W], AF.Exp, scale=-0.5, reads=["ln_rstd"], writes=["ln_rstd"])
        for m in range(8):
            P.tt("dve", sq[:, m, :], rT[:, m, :], mean[:, 0:W], ALU.subtract, reads=[rkeys[m], "ln_mean", ("lnsq", m)], writes=[("lnsq", m)])
            P.tt("pool", sq[:, m, :], sq[:, m, :], rstd[:, 0:W], ALU.mult, reads=[("lnsq", m), "ln_rstd"], writes=[("lnsq", m)])
            P.act(outT[:, m, :], sq[:, m, :], AF.Identity, scale=vec[:, gcol + m:gcol + m + 1], bias=vec[:, bcol + m:bcol + m + 1],
                  reads=[("lnsq", m), ("vecs", l)], writes=[(okey, m)])

    def stage_c(self, l, s):
        P = self.P
        P.sb_reset()
        T, W = s.T, 256
        si = s.sidx
        vec, modT = self.vecs[l], self.modT[l]
        nj = W // 128
        self._stg_i = 0
        stage = [P.sb([128, 8, 128], F32, "cst") for _ in range(3)]
        wb = [P.sb([128, 8, D], BF16, f"wb{i}") for i in range(3)]
        wo = P.sb([128, 8, D], BF16, "wo")
        wk = []
        for i in range(3):
            wk.append(self._load_w_resident(wb[i], self.inp["w_branch"][l, i], 8, D, f"wb{i}", stage, "cst"))
        wok = self._load_w_resident(wo, self.inp["w_out"][l], 8, D, "wo", stage, "cst")
        yt = P.sb([128, nj, D], F32, "yt")
        x0t = P.sb([128, 8, W], F32, "x0t")
        hyT = P.sb([128, 8, W], BF16, "hyT")
        atT = P.sb([128, 8, W], BF16, "atT")
        poT = P.sb([128, 8, W], BF16, "poT")
        gt = P.sb([128, 3, 8, W], BF16, "gt")
        xat = P.sb([128, 8, W], F32, "xat")
        mg = P.sb([128, 8, W], BF16, "mg")
        tA = [P.sb([128, W], F32, "tA") for _ in range(2)]
        tB = [P.sb([128, W], F32, "tB") for _ in range(2)]
        stat = [P.sb([128, 512], F32, "lnst") for _ in range(4)]
        sq = yt[:].rearrange("p j d -> p (j d)").rearrange("p (m w) -> p m w", m=8)
        srcs = (hyT, atT, poT)
        for tw in range(T // W):
            ts_ = slice(tw * W, (tw + 1) * W)
            direct = s.hyT is not None
            if direct:
                P.dma("sp", hyT[:], s.hyT[:, ts_].rearrange("(m p) t -> p m t", p=128), writes=[("hyT", m) for m in range(8)])
            else:
                P.dma("sp", yt[:], s.y[ts_, :].rearrange("(j p) d -> p j d", p=128), reads=[("lnsq", m) for m in range(8)], writes=["yt"])
                P.dma("act", x0t[:], s.x0T[:, ts_].rearrange("(m p) t -> p m t", p=128), writes=["x0t"])
            P.dma("sp", atT[:], s.attnT[:, ts_].rearrange("(m p) t -> p m t", p=128), writes=["atT"])
            P.dma("act", poT[:], s.poolT[:, ts_].rearrange("(m p) t -> p m t", p=128), writes=["poT"])
            for i in range(3):
                P.dma("sp" if i % 2 else "act", gt[:, i, :, :], s.gT[i, :, ts_].rearrange("(m p) t -> p m t", p=128), writes=[("gt", i)])
            P.dma("sp", xat[:], s.xa[:, ts_].rearrange("(m p) t -> p m t", p=128), writes=["xat"])
            for m in range(8):
                if direct:
                    break
                bank = m % 2
                for j in range(nj):
                    P.tr(self.ps[bank][:, j * 128:(j + 1) * 128], yt[:, j, m * 128:(m + 1) * 128], self.ident[:], reads=["yt"], writes=[self.psk(bank)])
                P.tt("dve", hyT[:, m, :], self.ps[bank][:, 0:W], x0t[:, m, :], ALU.mult, reads=[self.psk(bank), "x0t"], writes=[("hyT", m)])
            skeys = ([("hyT", m) for m in range(8)], ["atT"], ["poT"])
            for mo in range(8):
                q = mo % 2
                for i in range(3):
                    bank = 2 + i
                    for k in range(8):
                        P.mm(self.ps[bank][:, 0:W], wb[i][:, k, mo * 128:(mo + 1) * 128], srcs[i][:, k, :], k == 0, k == 7,
                             reads=wk[i][mo] + (skeys[i] if i else [("hyT", k)]), writes=[self.psk(bank)])
                P.tt("dve", tA[q][:], self.ps[2][:, 0:W], gt[:, 0, mo, :], ALU.mult, reads=[self.psk(2), ("gt", 0)], writes=[("tA", q)])
                P.tt("dve", tB[q][:], self.ps[3][:, 0:W], gt[:, 1, mo, :], ALU.mult, reads=[self.psk(3), ("gt", 1)], writes=[("tB", q)])
                P.tt("pool", tA[q][:], tA[q][:], tB[q][:], ALU.add, reads=[("tA", q), ("tB", q)], writes=[("tA", q)])
                P.tt("dve", tB[q][:], self.ps[4][:, 0:W], gt[:, 2, mo, :], ALU.mult, reads=[self.psk(4), ("gt", 2)], writes=[("tB", q)])
                P.tt("pool", mg[:, mo, :], tA[q][:], tB[q][:], ALU.add, reads=[("tA", q), ("tB", q)], writes=[("mg", mo)])
            for mo in range(8):
                bank = mo % 2
                for k in range(8):
                    P.mm(self.ps[bank][:, 0:W], wo[:, k, mo * 128:(mo + 1) * 128], mg[:, k, :], k == 0, k == 7,
                         reads=wok[mo] + [("mg", k)], writes=[self.psk(bank)])
                P.act(xat[:, mo, :], xat[:, mo, :], AF.Copy, scale=ALPHA, reads=["xat"], writes=[("xs", mo)])
                P.stt(xat[:, mo, :], self.ps[bank][:, 0:W], modT[:, si, 16 + mo:17 + mo], xat[:, mo, :], ALU.mult, ALU.add,
                      reads=[self.psk(bank), ("xs", mo), ("modT", l, si)], writes=[("rT", mo)])
            self._layer_norm(xat, W, self.V_LN, self.V_LN + 8, vec, l, xat, sq, stat, [("rT", m) for m in range(8)], "x1T")
            P.dma("sp", s.xb[:, ts_].rearrange("(m p) t -> p m t", p=128), xat[:], reads=[("x1T", m) for m in range(8)], writes=["xat"])
        P.barrier()

    def stage_d(self, l, s, final):
        P = self.P
        P.sb_reset()
        T = s.T
        W = 256
        si = s.sidx
        vec, modT, modP = self.vecs[l], self.modT[l], self.modP[l]
        NF = DFF // 128
        self._stg_i = 0
        stage = [P.sb([128, 8, 128], F32, "dst") for _ in range(3)]
        w1 = P.sb([128, 8, DFF], BF16, "w1")
        w3 = P.sb([128, 8, DFF], BF16, "w3")
        w2 = P.sb([128, NF, D], BF16, "w2")
        w1k = self._load_w_resident(w1, self.inp["ffn_w1"][l], 8, DFF, "w1", stage, "dst")
        w3k = self._load_w_resident(w3, self.inp["ffn_w3"][l], 8, DFF, "w3", stage, "dst")
        w2k = self._load_w_resident(w2, self.inp["ffn_w2"][l], NF, D, "w2", stage, "dst")
        x1t = P.sb([128, 8, W], F32, "x1t")
        h2T = P.sb([128, 8, W], BF16, "h2T")
        gT = P.sb([128, NF, W], BF16, "gT")
        sa = [P.sb([128, W], F32, "sa") for _ in range(2)]
        sqd = P.sb([128, 8, W], F32, "sqd")
        stat = [P.sb([128, W], F32, "lnst") for _ in range(4)]
        ot = sqd[:].rearrange("p m w -> p (m w)").rearrange("p (j d) -> p j d", j=W // 128) if final else None
        for tw in range(T // W):
            ts_ = slice(tw * W, (tw + 1) * W)
            P.dma("sp", x1t[:], s.xb[:, ts_].rearrange("(m p) t -> p m t", p=128), writes=["x1t"])
            for m in range(8):
                P.ts("dve" if m % 2 else "pool", h2T[:, m, :], x1t[:, m, :], modP[:, si, 32 + m:33 + m], modT[:, si, 24 + m:25 + m], ALU.mult, ALU.add,
                     reads=["x1t", ("modT", l, si), ("modP", l, si)], writes=[("h2T", m)])
            for f in range(NF):
                q = f % 2
                ba, bb = (0, 1) if q == 0 else (2, 3)
                for k in range(8):
                    P.mm(self.ps[ba][:, 0:W], w1[:, k, f * 128:(f + 1) * 128], h2T[:, k, :], k == 0, k == 7, reads=w1k[f] + [("h2T", k)], writes=[self.psk(ba)])
                for k in range(8):
                    P.mm(self.ps[bb][:, 0:W], w3[:, k, f * 128:(f + 1) * 128], h2T[:, k, :], k == 0, k == 7, reads=w3k[f] + [("h2T", k)], writes=[self.psk(bb)])
                P.act(sa[q][:], self.ps[ba][:, 0:W], AF.Silu, reads=[self.psk(ba)], writes=[("sa", q)])
                P.tt("dve", gT[:, f, :], self.ps[bb][:, 0:W], sa[q][:], ALU.mult, reads=[self.psk(bb), ("sa", q)], writes=[("gT", f)])
            for mo in range(8):
                bank = 4 + mo % 2
                for f in range(NF):
                    P.mm(self.ps[bank][:, 0:W], w2[:, f, mo * 128:(mo + 1) * 128], gT[:, f, :], f == 0, f == NF - 1,
                         reads=w2k[mo] + [("gT", f)], writes=[self.psk(bank)])
                P.act(x1t[:, mo, :], x1t[:, mo, :], AF.Copy, scale=ALPHA, reads=["x1t"], writes=[("xs", mo)])
                P.stt(x1t[:, mo, :], self.ps[bank][:, 0:W], modT[:, si, 40 + mo:41 + mo], x1t[:, mo, :], ALU.mult, ALU.add,
                      reads=[self.psk(bank), ("xs", mo), ("modT", l, si)], writes=[("rT", mo)])
            self._layer_norm(x1t, W, self.V_LN + 16, self.V_LN + 24, vec, l, x1t, sqd, stat, [("rT", m) for m in range(8)], "x2T")
            if not final:
                P.dma("sp", s.xa[:, ts_].rearrange("(m p) t -> p m t", p=128), x1t[:], reads=[("x2T", m) for m in range(8)], writes=["x1t"])
            else:
                for j in range(W // 128):
                    for half in range(2):
                        bank = half
                        for mm_ in range(4):
                            m = half * 4 + mm_
                            P.tr(self.ps[bank][:, mm_ * 128:(mm_ + 1) * 128], x1t[:, m, j * 128:(j + 1) * 128], self.ident[:],
                                 reads=[("x2T", m)], writes=[self.psk(bank)])
                        P.cp("act" if half else "dve", ot[:, j, half * 512:(half + 1) * 512], self.ps[bank][:, 0:512], reads=[self.psk(bank)],
                             writes=[("ot", j, half)] + ([("lnsq", m) for m in range(8)] if (j == 0 and half == 0) else []))
                P.dma("sp", self.out[ts_, :].rearrange("(j p) d -> p j d", p=128), ot,
                      reads=[("ot", j, h_) for j in range(W // 128) for h_ in range(2)] + [("x2T", m) for m in range(8)] + [("lnsq", m) for m in range(8)],
                      writes=["x1t"], final=True)
        P.barrier()

    def build(self):
        st = self.stages
        def on(name):
            return st is None or name in st
        if on("p0"):
            self.stage_p0()
        if on("p1"):
            self.stage_p1()
        for l in range(DEPTH):
            if on(f"k{l}l"):
                self.stage_k(l, "l")
            if on(f"a{l}c"):
                self.stage_a(l, self.st["c"])
            if on(f"a{l}l"):
                self.stage_a(l, self.st["l"])
            if on(f"h{l}c") and l < DEPTH - 1:
                self.stage_hc(l, self.st["c"])
            if on(f"h{l}l"):
                self.stage_h(l, self.st["l"])
            if on(f"b{l}c") and l < DEPTH - 1:
                self.stage_b(l, self.st["c"], CTX)
            if on(f"b{l}l"):
                self.stage_b(l, self.st["l"], SEQ + CTX)
            if on(f"c{l}c") and l < DEPTH - 1:
                self.stage_c(l, self.st["c"])
            if on(f"d{l}c") and l < DEPTH - 1:
                self.stage_d(l, self.st["c"], False)
            if on(f"c{l}l"):
                self.stage_c(l, self.st["l"])
            if on(f"d{l}l"):
                self.stage_d(l, self.st["l"], l == DEPTH - 1)
        self.P.emit()
        return self.nc


def make_in_maps(inputs, n_cores=8):
    c = host_consts()
    shared = {k: np.ascontiguousarray(np.asarray(v, dtype=np.float32)) for k, v in inputs.items() if k not in ("x", "c", "ctx")}
    consts = {"k_" + k: v for k, v in c.items()}
    maps = []
    for b in range(n_cores):
        m = dict(shared)
        m.update(consts)
        m["x"] = np.ascontiguousarray(inputs["x"][b], dtype=np.float32)
        m["c"] = np.ascontiguousarray(inputs["c"][b], dtype=np.float32)
        m["ctx"] = np.ascontiguousarray(inputs["ctx"][b], dtype=np.float32)
        maps.append(m)
    return maps


def kernel(**inputs):
    bld = Builder()
    nc = bld.build()
    res = run_bass_kernel_spmd(nc, make_in_maps(inputs), core_ids=list(range(8)))
    return np.stack([np.asarray(r["out"], dtype=np.float32) for r in res.results], axis=0)
```

```python
import contextlib
import math
import numpy as np
import ml_dtypes
import concourse.bass as bass
import concourse.mybir as mybir
from concourse.bass_utils import run_bass_kernel_spmd

F32 = mybir.dt.float32
BF16 = mybir.dt.bfloat16
AF = mybir.ActivationFunctionType
ALU = mybir.AluOpType

D = 1024
SEQ = 8192
CTX = 256
DEPTH = 2
NH = 8
HD = 128
DFF = 2816
INW = 8704
C_Q, C_K, C_V, C_HY, C_POOL, C_GATE = 0, 1024, 1280, 1536, 4608, 5632
ALPHA = (2 * DEPTH) ** 0.25
EPS = 1e-6
NFFT = 16384
POOL_WINDOWS = (2, 4, 8, 16)


class _Op:
    __slots__ = ("eng", "fn", "deps", "dma", "marked", "cnt", "sem", "semval", "prev")


class Prog:
    COMPUTE = ("pe", "act", "dve", "pool")
    QUEUES = ("sp", "act", "pool")
    ALLENG = ("pe", "act", "dve", "pool", "sp")
    SB_BASE = 24576
    SB_LIMIT = 218 * 1024

    def __init__(self, ring=8, same_engine_sync=True):
        self.nc = bass.Bass("TRN2", target_bir_lowering=False)
        self.ops = []
        self.state = {}
        self.ring = ring
        self.same = same_engine_sync
        self.dma_count = {q: 0 for q in self.QUEUES}
        self.slot_last = {}
        self.slot_val = {}
        self.last_op = {e: None for e in self.ALLENG}
        self.sb_off = self.SB_BASE
        self.sb_mark = self.SB_BASE
        self.n_alloc = 0
        self.out_dmas = []
        self.psn = 0

    def sb(self, shape, dtype, name="t"):
        nbytes = int(np.prod(shape[1:])) * mybir.dt.size(dtype)
        nbytes_al = (nbytes + 63) // 64 * 64
        self.n_alloc += 1
        h = self.nc.alloc_sbuf_tensor_at(f"{name}_{self.n_alloc}", list(shape), dtype, offset=self.sb_off)
        self.sb_off += nbytes_al
        assert self.sb_off <= self.SB_LIMIT, f"SBUF overflow {self.sb_off} ({name})"
        return h

    def sb_persist_done(self):
        self.sb_mark = self.sb_off

    def sb_reset(self):
        self.sb_off = self.sb_mark

    def _collect(self, reads, writes):
        deps = set()
        for k in reads:
            st = self.state.get(k)
            if st is not None and st[0] is not None:
                deps.add(st[0])
        for k in writes:
            st = self.state.get(k)
            if st is not None:
                if st[0] is not None:
                    deps.add(st[0])
                deps.update(st[1].values())
                deps.update(st[2])
        return deps

    def add(self, eng, fn, reads=(), writes=(), dma=False, out=False):
        pr = [k for k in reads if isinstance(k, tuple) and k[0] == "ps"]
        if pr:
            reads = [k for k in reads if not (isinstance(k, tuple) and k[0] == "ps")]
            writes = list(writes) + pr
        i = len(self.ops)
        op = _Op()
        op.eng, op.fn, op.dma, op.marked, op.cnt, op.prev = eng, fn, dma, False, 0, None
        op.deps = self._collect(reads, writes)
        if dma:
            n = self.dma_count[eng]
            self.dma_count[eng] = n + 1
            slot = (eng, n % self.ring)
            op.prev = self.slot_last.get(slot)
            self.slot_last[slot] = i
            v = self.slot_val.get(slot, 0) + 16
            self.slot_val[slot] = v
            op.sem, op.semval = slot, v
            if out:
                self.out_dmas.append(i)
        self.ops.append(op)
        for k in reads:
            st = self.state.setdefault(k, [None, {}, []])
            if dma:
                st[2].append(i)
            else:
                st[1][eng] = i
        for k in writes:
            self.state[k] = [i, {}, []]
        self.last_op[eng] = i
        return i

    def barrier(self):
        snap = [v for v in self.last_op.values() if v is not None] + list(self.slot_last.values())
        for e in self.ALLENG:
            op = _Op()
            op.eng, op.fn, op.dma, op.marked, op.cnt, op.prev = e, None, False, False, 0, None
            op.deps = set(snap)
            self.ops.append(op)
        self.state = {}

    def dma(self, q, out, in_, reads=(), writes=(), final=False):
        return self.add(q, lambda e: e.dma_start(out=out, in_=in_), reads, writes, dma=True, out=final)

    def mm(self, out, lhsT, rhs, start, stop, reads=(), writes=()):
        return self.add("pe", lambda e: e.matmul(out, lhsT=lhsT, rhs=rhs, start=start, stop=stop), reads, writes)

    def tr(self, out, in_, ident, reads=(), writes=()):
        return self.add("pe", lambda e: e.transpose(out, in_, ident), reads, writes)

    def act(self, out, in_, func, reads=(), writes=(), bias=None, scale=None):
        kw = {}
        if bias is not None:
            kw["bias"] = bias
        if scale is not None:
            kw["scale"] = scale
        return self.add("act", lambda e: e.activation(out=out, in_=in_, func=func, **kw), reads, writes)

    def tt(self, eng, out, in0, in1, op, reads=(), writes=()):
        return self.add(eng, lambda e: e.tensor_tensor(out=out, in0=in0, in1=in1, op=op), reads, writes)

    def ts(self, eng, out, in0, s1, s2, op0, op1, reads=(), writes=()):
        if op1 is None:
            return self.add(eng, lambda e: e.tensor_scalar(out=out, in0=in0, scalar1=s1, scalar2=None, op0=op0), reads, writes)
        return self.add(eng, lambda e: e.tensor_scalar(out=out, in0=in0, scalar1=s1, scalar2=s2, op0=op0, op1=op1), reads, writes)

    def stt(self, out, in0, scalar, in1, op0, op1, reads=(), writes=()):
        return self.add("dve", lambda e: e.scalar_tensor_tensor(out=out, in0=in0, scalar=scalar, in1=in1, op0=op0, op1=op1), reads, writes)

    def cp(self, eng, out, in_, reads=(), writes=()):
        if eng == "act":
            return self.add("act", lambda e: e.copy(out=out, in_=in_), reads, writes)
        return self.add(eng, lambda e: e.tensor_copy(out=out, in_=in_), reads, writes)

    def memset(self, eng, ap, val, writes=()):
        return self.add(eng, lambda e: e.memset(ap, val), (), writes)

    def emit(self):
        nc = self.nc
        ops = self.ops
        fin = _Op()
        fin.eng, fin.fn, fin.dma, fin.marked, fin.cnt, fin.prev = "sp", None, False, False, 0, None
        fin.deps = set(self.out_dmas) | set(self.slot_last.values())
        ops.append(fin)
        for op in ops:
            for d in op.deps:
                dop = ops[d]
                if dop.dma or dop.fn is None:
                    continue
                dop.marked = True
        cnt = {e: 0 for e in self.COMPUTE}
        for op in ops:
            if op.fn is not None and not op.dma:
                if op.marked:
                    cnt[op.eng] += 1
                op.cnt = cnt[op.eng]
        per_eng = {e: [] for e in self.ALLENG}
        for op in ops:
            per_eng[op.eng].append(op)
        same = self.same

        with contextlib.ExitStack() as es:
            csem = {e: es.enter_context(nc.semaphore(f"c_{e}")) for e in self.COMPUTE}
            dsem = {}
            for q in self.QUEUES:
                for r in range(self.ring):
                    dsem[(q, r)] = es.enter_context(nc.semaphore(f"d_{q}_{r}"))
            block = es.enter_context(nc.Block())

            def run(ename, e):
                waited = {}
                for op in per_eng[ename]:
                    waits = {}
                    for d in op.deps:
                        dop = ops[d]
                        if dop.dma:
                            s = ("d", dop.sem)
                            waits[s] = max(waits.get(s, 0), dop.semval)
                        else:
                            if dop.fn is None:
                                continue
                            if dop.eng == ename:
                                if op.fn is None:
                                    continue
                                if not op.dma and (ename == "pe" or not same):
                                    continue
                            s = ("c", dop.eng)
                            waits[s] = max(waits.get(s, 0), dop.cnt)
                    if op.dma and op.prev is not None:
                        p = ops[op.prev]
                        s = ("d", p.sem)
                        waits[s] = max(waits.get(s, 0), p.semval)
                    for s, v in waits.items():
                        if v <= 0 or waited.get(s, 0) >= v:
                            continue
                        e.wait_ge(dsem[s[1]] if s[0] == "d" else csem[s[1]], v)
                        waited[s] = v
                    if op.fn is None:
                        continue
                    inst = op.fn(e)
                    if op.dma:
                        inst.then_inc(dsem[op.sem], 16)
                    elif op.marked:
                        inst.then_inc(csem[ename], 1)

            block.tensor(lambda e: run("pe", e))
            block.scalar(lambda e: run("act", e))
            block.vector(lambda e: run("dve", e))
            block.gpsimd(lambda e: run("pool", e))
            block.sync(lambda e: run("sp", e))
        return nc


def _bf(a):
    return np.asarray(a, dtype=np.float32).astype(ml_dtypes.bfloat16)


def _hy_tables(L):
    f32 = np.float32
    t_idx = np.arange(L, dtype=f32)
    t01 = (t_idx / f32(max(L - 1, 1))).astype(f32)
    bands = np.linspace(1e-4, 15.0, 16, dtype=f32)
    ang = (f32(2.0 * math.pi / L) * t_idx[:, None] * bands[None, :]).astype(f32)
    feats = np.concatenate([t01[:, None], np.cos(ang), -np.sin(ang)], axis=-1).astype(f32)
    fF = np.zeros((33, 8192), f32)
    fR = np.zeros((33, 8192), f32)
    negt = np.zeros((128, 128), f32)
    mask = np.zeros((128, 128), f32)
    fF[:, :L] = feats.T
    n = np.arange(8192)
    tF = np.where(n < L, n, 0)
    negt[:64] = -np.where(n < L, t01[tF], 0).reshape(64, 128)
    mask[:64] = (n < L).astype(f32).reshape(64, 128)
    m = 8192 - n
    valid = (m >= 1) & (m <= L - 1)
    mm = np.where(valid, m, 0)
    fR[:, valid] = feats[mm[valid]].T
    negt[64:] = -np.where(valid, t01[mm], 0).reshape(64, 128)
    mask[64:] = valid.astype(f32).reshape(64, 128)
    return fF, fR, negt, mask


def _rope_tables(T, grid_w=64, ctx=False):
    f32 = np.float32
    if ctx:
        return np.ones((128, T), f32), np.zeros((128, T), f32)
    t = np.arange(T)
    rows = (t // grid_w).astype(f32)
    cols = (t % grid_w).astype(f32)
    inv = np.power(f32(10000.0), -np.arange(32, dtype=f32) / f32(32)).astype(f32)
    C = np.zeros((128, T), f32)
    S = np.zeros((128, T), f32)
    for j in range(128):
        pos = rows if j < 64 else cols
        jj = j % 64
        ang = (pos * inv[jj % 32]).astype(f32)
        C[j] = np.cos(ang)
        S[j] = -np.sin(ang) if jj < 32 else np.sin(ang)
    return C, S


_CONST_CACHE = {}


def host_consts():
    if _CONST_CACHE:
        return _CONST_CACHE
    c = {}
    c["ident"] = np.eye(128, dtype=np.float32)
    c["identb"] = _bf(np.eye(128))
    c["ropeC_l"], c["ropeS_l"] = _rope_tables(SEQ)
    c["ropeC_c"], c["ropeS_c"] = _rope_tables(CTX, ctx=True)
    a = np.arange(128, dtype=np.float64)
    th = 2 * np.pi * np.outer(a, a) / 128.0
    c["F1"] = _bf(np.stack([np.cos(th), -np.sin(th)], axis=1))
    c["I2"] = _bf(np.stack([np.cos(th)[:, :64], -np.sin(th)[:, :64]], axis=1) / NFFT)
    k1 = a[:, None, None]
    n2 = a[None, :, None]
    k2 = a[None, None, :]
    th3 = 2 * np.pi * n2 * (k1 + 128.0 * k2) / NFFT
    G = np.stack([np.cos(th3), -np.sin(th3), np.sin(th3)], axis=2)
    c["GT"] = _bf(G)
    c["HT"] = _bf(np.transpose(G, (0, 3, 2, 1)))
    for tag, L in (("l", SEQ), ("c", CTX)):
        fF, fR, negt, mask = _hy_tables(L)
        c[f"featF_{tag}"], c[f"featR_{tag}"], c[f"negt_{tag}"], c[f"mask_{tag}"] = fF, fR, negt, mask
    lo, hi = math.log(1e-2) / 1.5, math.log(1e-2) / 0.3
    c["deltas"] = np.abs(np.linspace(lo, hi, 1024, dtype=np.float32)).reshape(1, 1024).astype(np.float32)
    edge = np.zeros((4, 2, 8), np.float32)
    Tt = 4096
    for g, w in enumerate(POOL_WINDOWS):
        before, after = w // 2, w - 1 - w // 2
        for side in range(2):
            for i in range(8):
                t = i if side == 0 else Tt - 8 + i
                cnt = min(t + after + 1, Tt) - max(t - before, 0)
                edge[g, side, i] = 1.0 / cnt
    c["pedge"] = np.broadcast_to(edge.reshape(1, 64), (128, 64)).copy()
    t01c = (np.arange(CTX, dtype=np.float32) / np.float32(CTX - 1)).astype(np.float32)
    c["t01row_c"] = np.broadcast_to(t01c.reshape(1, CTX), (128, CTX)).copy()
    c["ndeltasT"] = np.ascontiguousarray(-c["deltas"].reshape(8, 128).T)
    _CONST_CACHE.update(c)
    return c


CONST_SPECS = None


def const_specs():
    c = host_consts()
    return {k: (list(v.shape), BF16 if v.dtype == ml_dtypes.bfloat16 else F32) for k, v in c.items()}


INPUT_SHAPES = {
    "x": [SEQ, D], "c": [D], "ctx": [CTX, D], "c_ctx": [D],
    "w_ada": [DEPTH, D, 6 * D], "b_ada": [DEPTH, 6 * D], "w_in": [DEPTH, D, INW],
    "q_norm_g": [DEPTH, HD], "k_norm_g": [DEPTH, HD],
    "hy_conv_w": [DEPTH, 3, 3 * D], "hy_conv_b": [DEPTH, 3 * D],
    "hf_w1": [DEPTH, 33, 64], "hf_b1": [DEPTH, 64], "hf_freq": [DEPTH, 64],
    "hf_w2": [DEPTH, 64, 64], "hf_b2": [DEPTH, 64], "hf_w3": [DEPTH, 64, 2 * D],
    "hy_d": [DEPTH, D], "pool_w": [DEPTH, 4, 256, 256], "pool_scale": [DEPTH, D],
    "w_branch": [DEPTH, 3, D, D], "w_out": [DEPTH, D, D],
    "ln1_g": [DEPTH, D], "ln1_b": [DEPTH, D], "ln2_g": [DEPTH, D], "ln2_b": [DEPTH, D],
    "ffn_w1": [DEPTH, D, DFF], "ffn_w3": [DEPTH, D, DFF], "ffn_w2": [DEPTH, DFF, D],
}


class Stream:
    pass


class Builder:
    def __init__(self, debug_outs=(), stages=None):
        self.P = Prog()
        self.nc = self.P.nc
        self.debug_outs = set(debug_outs)
        self.stages = stages
        nc = self.nc
        self.inp = {k: nc.dram_tensor(k, shp, F32, kind="ExternalInput").ap() for k, shp in INPUT_SHAPES.items()}
        self.cst = {k: nc.dram_tensor("k_" + k, shp, dt, kind="ExternalInput").ap() for k, (shp, dt) in const_specs().items()}
        self.out = nc.dram_tensor("out", [SEQ, D], F32, kind="ExternalOutput").ap()
        self.ps = [nc.alloc_psum_tensor(f"psb{i}", [128, 512], F32) for i in range(8)]
        self.scr = {}
        self._persistent()
        self._streams()

    def dram(self, name, shape, dtype):
        kind = "ExternalOutput" if name in self.debug_outs else "Internal"
        t = self.nc.dram_tensor(name, list(shape), dtype, kind=kind).ap()
        self.scr[name] = t
        return t

    def psk(self, i):
        return ("ps", i)

    def _persistent(self):
        P = self.P
        self.ident = P.sb([128, 128], F32, "ident")
        self.identb = P.sb([128, 128], BF16, "identb")
        self.ones_f = P.sb([128, 128], F32, "ones_f")
        self.ones_b = P.sb([128, 128], BF16, "ones_b")
        self.epsT = P.sb([128, 1], F32, "epsT")
        self.vecs = [P.sb([128, 200], F32, f"vecs{l}") for l in range(DEPTH)]
        self.modT = [P.sb([128, 2, 48], F32, f"modT{l}") for l in range(DEPTH)]
        self.modP = [P.sb([128, 2, 48], F32, f"modP{l}") for l in range(DEPTH)]
        self.hfv = [P.sb([64, 8], F32, f"hfv{l}") for l in range(DEPTH)]
        self.pedge = P.sb([128, 64], F32, "pedge")
        P.sb_persist_done()

    V_QG, V_QGP, V_KG, V_KGP = 0, 1, 2, 3
    V_CW = 4
    V_CB = 76
    V_PS = 100
    V_LN = 108
    V_BA = 140
    V_HD = 188
    V_N = 196

    def _streams(self):
        self.XA = self.dram("XA", [D, SEQ], F32)
        self.XB = self.dram("XB", [D, SEQ], F32)
        self.XCA = self.dram("XCA", [D, CTX], F32)
        self.XCB = self.dram("XCB", [D, CTX], F32)
        self.kT_d = self.dram("kT_d", [256, SEQ + CTX], BF16)
        self.v_d = self.dram("v_d", [SEQ + CTX, 256], BF16)
        self.Bd = [self.dram(f"Bd{i}", [128, 128, 1024], BF16) for i in range(2)]
        self.Dd = [self.dram(f"Dd{i}", [128, 128, 1024], BF16) for i in range(2)]
        self.Kf = {t: [self.dram(f"Kf_{t}{i}", [128, 128, 1024], BF16) for i in range(2)] for t in ("l", "c")}
        self.st = {}
        for tag, T in (("l", SEQ), ("c", CTX)):
            s = Stream()
            s.tag, s.T = tag, T
            s.TS = 4096 if tag == "l" else 256
            s.W = 512 if tag == "l" else 256
            s.sidx = 0 if tag == "l" else 1
            s.xa = self.XA if tag == "l" else self.XCA
            s.xb = self.XB if tag == "l" else self.XCB
            s.ktok0 = CTX if tag == "l" else 0
            s.qT = self.dram(f"qT_{tag}", [D, T], BF16)
            s.z = self.dram(f"z_{tag}", [T, D], BF16)
            s.x0T = self.dram(f"x0T_{tag}", [D, T], F32)
            s.poolT = self.dram(f"poolT_{tag}", [D, T], BF16)
            s.gT = self.dram(f"gT_{tag}", [3, D, T], BF16)
            s.attnT = self.dram(f"attnT_{tag}", [D, T], BF16)
            s.y = self.dram(f"y_{tag}", [T, D], F32)
            s.zT = self.dram(f"zT_{tag}", [D, T], F32) if tag == "c" else None
            s.hyT = self.dram(f"hyT_{tag}", [D, T], BF16) if tag == "c" else None
            s.ropeC = self.cst[f"ropeC_{tag}"]
            s.ropeS = self.cst[f"ropeS_{tag}"]
            self.st[tag] = s

    def stage_p0(self):
        P = self.P
        P.sb_reset()
        P.dma("sp", self.ident[:], self.cst["ident"], writes=["ident"])
        P.dma("sp", self.identb[:], self.cst["identb"], writes=["identb"])
        P.memset("dve", self.ones_f[:], 1.0, writes=["ones_f"])
        P.memset("dve", self.ones_b[:], 1.0, writes=["ones_b"])
        P.memset("dve", self.epsT[:], EPS, writes=["epsT"])
        xt = [P.sb([128, 4, D], F32, "p0x") for _ in range(2)]
        xo = [P.sb([128, 8, 512], F32, "p0o") for _ in range(2)]
        it = 0
        for src, dst, T in ((self.inp["x"], self.XA, SEQ), (self.inp["ctx"], self.XCA, CTX)):
            W = min(512, T)
            nj = W // 128
            for w in range(T // W):
                b = it % 2
                it += 1
                P.dma("sp", xt[b][:, 0:nj, :], src[w * W:(w + 1) * W, :].rearrange("(j p) d -> p j d", p=128),
                      writes=[("p0x", b)])
                for m in range(8):
                    bank = m % 4
                    for j in range(nj):
                        P.tr(self.ps[bank][:, j * 128:(j + 1) * 128], xt[b][:, j, m * 128:(m + 1) * 128], self.ident[:],
                             reads=[("p0x", b), "ident"], writes=[self.psk(bank)])
                    P.cp("act" if m % 2 else "dve", xo[b][:, m, 0:W], self.ps[bank][:, 0:W],
                         reads=[self.psk(bank)], writes=[("p0o", b, m)])
                P.dma("act", dst[:, w * W:(w + 1) * W].rearrange("(m p) t -> p m t", p=128), xo[b][:, :, 0:W],
                      reads=[("p0o", b, m) for m in range(8)])
        P.barrier()

    def stage_p1(self):
        P = self.P
        P.sb_reset()
        I = self.inp
        stg = P.sb([128, 2, 128], F32, "stg")
        stgc = P.sb([16, 128], F32, "stgc")
        stg64 = P.sb([8, 64], F32, "stg64")
        scT = P.sb([128, 8, 2], F32, "scT")
        cT = P.sb([128, 16], F32, "cT")
        wa = [P.sb([128, 8, 512], F32, "wa") for _ in range(2)]
        P.dma("sp", stgc[0:8, :], I["c"].rearrange("(m p) -> m p", p=128), writes=["stgc"])
        P.dma("sp", stgc[8:16, :], I["c_ctx"].rearrange("(m p) -> m p", p=128), writes=["stgc2"])
        P.tr(self.ps[0][:, 0:16], stgc[0:16, :], self.ident[0:16, 0:16], reads=["stgc", "stgc2", "ident"], writes=[self.psk(0)])
        P.act(cT[:], self.ps[0][:, 0:16], AF.Silu, reads=[self.psk(0)], writes=["cT"])
        for s in range(2):
            P.cp("dve", scT[:, :, s], cT[:, s * 8:(s + 1) * 8], reads=["cT"], writes=[("scT", s)])
        for l in range(DEPTH):
            rows = []
            g = I["q_norm_g"][l]
            kg = I["k_norm_g"][l]
            rows.append(("full", g))
            rows.append(("perm", g))
            rows.append(("full", kg))
            rows.append(("perm", kg))
            for j in range(3):
                for m in range(24):
                    rows.append(("full", I["hy_conv_w"][l, j, m * 128:(m + 1) * 128]))
            for m in range(24):
                rows.append(("full", I["hy_conv_b"][l, m * 128:(m + 1) * 128]))
            for m in range(8):
                rows.append(("full", I["pool_scale"][l, m * 128:(m + 1) * 128]))
            for nm in ("ln1_g", "ln1_b", "ln2_g", "ln2_b"):
                for m in range(8):
                    rows.append(("full", I[nm][l, m * 128:(m + 1) * 128]))
            for m in range(48):
                rows.append(("full", I["b_ada"][l, m * 128:(m + 1) * 128]))
            for m in range(8):
                rows.append(("full", I["hy_d"][l, m * 128:(m + 1) * 128]))
            assert len(rows) == self.V_N
            def ld(r0, ap2d, n):
                grp, rr = divmod(r0, 128)
                assert rr + n <= 128
                P.dma("sp", stg[rr:rr + n, grp, :], ap2d, writes=[("stg", r0)])
                return ("stg", r0)
            keys = []
            for ri, (kind, ap) in enumerate(rows[:4]):
                grp, rr = divmod(ri, 128)
                if kind == "full":
                    P.dma("sp", stg[rr:rr + 1, grp, :], ap.rearrange("(o n) -> o n", o=1), writes=[("stg", ri)])
                else:
                    for q4, src0 in enumerate((32, 0, 96, 64)):
                        P.dma("sp", stg[rr:rr + 1, grp, q4 * 32:(q4 + 1) * 32],
                              ap[src0:src0 + 32].rearrange("(o n) -> o n", o=1), writes=[("stg", ri, q4)])
                        keys.append(("stg", ri, q4))
                keys.append(("stg", ri))
            keys.append(ld(4, I["hy_conv_w"][l].rearrange("j (m p) -> (j m) p", p=128), 72))
            keys.append(ld(76, I["hy_conv_b"][l].rearrange("(m p) -> m p", p=128), 24))
            keys.append(ld(100, I["pool_scale"][l].rearrange("(m p) -> m p", p=128), 8))
            for qi, nm in enumerate(("ln1_g", "ln1_b")):
                keys.append(ld(108 + qi * 8, I[nm][l].rearrange("(m p) -> m p", p=128), 8))
            keys.append(ld(124, I["ln2_g"][l, 0:512].rearrange("(m p) -> m p", p=128), 4))
            keys.append(ld(128, I["ln2_g"][l, 512:1024].rearrange("(m p) -> m p", p=128), 4))
            keys.append(ld(132, I["ln2_b"][l].rearrange("(m p) -> m p", p=128), 8))
            keys.append(ld(140, I["b_ada"][l].rearrange("(m p) -> m p", p=128), 48))
            keys.append(ld(188, I["hy_d"][l].rearrange("(m p) -> m p", p=128), 8))
            P.tr(self.ps[1][:, 0:128], stg[:, 0, :], self.ident[:], reads=keys + ["ident"], writes=[self.psk(1)])
            P.tr(self.ps[1][:, 128:128 + 68], stg[0:68, 1, :], self.ident[0:68, 0:68], reads=keys + ["ident"], writes=[self.psk(1)])
            P.cp("dve", self.vecs[l][:, 0:196], self.ps[1][:, 0:196], reads=[self.psk(1)], writes=[("vecs", l)])
            for ci, nm in enumerate(("hf_b1", "hf_freq", "hf_b2")):
                P.dma("sp", stg64[ci:ci + 1, :], I[nm][l].rearrange("(o n) -> o n", o=1), writes=[("stg64", ci)])
            P.tr(self.ps[2][0:64, 0:3], stg64[0:3, :], self.ident[0:3, 0:3],
                 reads=[("stg64", ci) for ci in range(3)] + ["ident"], writes=[self.psk(2)])
            P.cp("dve", self.hfv[l][:, 0:3], self.ps[2][0:64, 0:3], reads=[self.psk(2)], writes=[("hfv", l)])
            P.tt("dve", self.hfv[l][:, 3:4], self.hfv[l][:, 0:1], self.hfv[l][:, 1:2], ALU.mult, reads=[("hfv", l)], writes=[("hfv3", l)])
            P.tt("dve", self.hfv[l][:, 4:5], self.hfv[l][:, 2:3], self.hfv[l][:, 1:2], ALU.mult, reads=[("hfv", l)], writes=[("hfv4", l)])
            bank = 3
            for cg in range(12):
                b = cg % 2
                P.dma("sp" if cg % 2 else "act", wa[b][:], I["w_ada"][l][:, cg * 512:(cg + 1) * 512].rearrange("(k p) n -> p k n", p=128),
                      writes=[("wa", b)])
                for mm in range(4):
                    m = cg * 4 + mm
                    for k in range(8):
                        P.mm(self.ps[bank][:, m * 2:m * 2 + 2], wa[b][:, k, mm * 128:(mm + 1) * 128], scT[:, k, :],
                             start=(k == 0), stop=(k == 7), reads=[("wa", b), ("scT", 0), ("scT", 1)], writes=[self.psk(bank)])
            psv = self.ps[bank][:, 0:96].rearrange("p (m s) -> p m s", s=2)
            for s in range(2):
                P.tt("dve", self.modT[l][:, s, :], psv[:, :, s], self.vecs[l][:, self.V_BA:self.V_BA + 48], ALU.add,
                     reads=[self.psk(bank), ("vecs", l)], writes=[("modT", l, s)])
                P.ts("dve", self.modP[l][:, s, :], self.modT[l][:, s, :], 1.0, None, ALU.add, None,
                     reads=[("modT", l, s)], writes=[("modP", l, s)])
            if "dbg_mod" in self.debug_outs:
                if l == 0:
                    self.dbg_mod = self.dram("dbg_mod", [DEPTH, 128, 96], F32)
                    self.dbg_vec = self.dram("dbg_vec", [DEPTH, 128, 192], F32)
                P.dma("sp", self.dbg_mod[l], self.modT[l][:].rearrange("p s m -> p (s m)"), reads=[("modT", l, 0), ("modT", l, 1)])
                P.dma("sp", self.dbg_vec[l], self.vecs[l][:, 0:192], reads=[("vecs", l)])
        P.barrier()

    def _proj(self, ps_ap, wt, hT, w, W, keys_w, bank, col0=0, ncol=None):
        P = self.P
        for k in range(8):
            P.mm(ps_ap, wt[:, k, :], hT[:, k, col0 + w * W: col0 + w * W + (ncol or W)],
                 start=(k == 0), stop=(k == 7), reads=[keys_w, ("hT", k, w)], writes=[self.psk(bank)])

    def stage_a(self, l, s):
        P = self.P
        P.sb_reset()
        T, TS, W = s.T, s.TS, s.W
        NW = TS // W
        NB = TS // 128
        si = s.sidx
        vec, modT, modP = self.vecs[l], self.modT[l], self.modP[l]
        w_in = self.inp["w_in"][l]
        hT = P.sb([128, 8, TS + 16], BF16, "hT")
        wring = [P.sb([128, 8, 128], BF16, "wr") for _ in range(4)]
        wcount = [0]

        wstage = [P.sb([128, 8, 128], F32, "wst") for _ in range(3)]
        scount = [0]

        def load_w(col0, ncols=128, buf=None, key=None):
            if buf is None:
                i = wcount[0] % 4
                wcount[0] += 1
                buf, key = wring[i], ("wr", i)
            for c in range(0, ncols, 128):
                j = scount[0] % 3
                scount[0] += 1
                P.dma("sp" if j % 2 else "act", wstage[j][:], w_in[:, col0 + c:col0 + c + 128].rearrange("(k p) n -> p k n", p=128),
                      writes=[("wst", j)])
                P.cp("pool", buf[:, :, c:c + 128], wstage[j][:], reads=[("wst", j)], writes=[key if ncols == 128 else (key, c)])
            return buf, key

        mark = P.sb_off
        P.dma("sp", self.pedge[:], self.cst["pedge"], writes=["pedge"])
        for sti in range(T // TS):
            t0 = sti * TS
            P.sb_off = mark
            xs = [P.sb([128, 8, W], F32, "xs") for _ in range(2)]
            hal = P.sb([128, 8, 16], F32, "hal")
            for w in range(NW):
                b = w % 2
                P.dma("sp", xs[b][:], s.xa[:, t0 + w * W: t0 + (w + 1) * W].rearrange("(m p) t -> p m t", p=128), writes=[("xs", b)])
                for m in range(8):
                    eng = ("act", "dve", "pool")[m % 3]
                    o = hT[:, m, w * W:(w + 1) * W]
                    if eng == "act":
                        P.act(o, xs[b][:, m, :], AF.Identity, scale=modP[:, si, 8 + m:9 + m], bias=modT[:, si, m:m + 1],
                              reads=[("xs", b), ("modT", l, si), ("modP", l, si)], writes=[("hT", m, w)])
                    else:
                        P.ts(eng, o, xs[b][:, m, :], modP[:, si, 8 + m:9 + m], modT[:, si, m:m + 1], ALU.mult, ALU.add,
                             reads=[("xs", b), ("modT", l, si), ("modP", l, si)], writes=[("hT", m, w)])
            hk = []
            for side, (a, b_) in enumerate(((t0 - 8, t0), (t0 + TS, t0 + TS + 8))):
                if a >= 0 and b_ <= T:
                    P.dma("sp", hal[:, :, side * 8:(side + 1) * 8], s.xa[:, a:b_].rearrange("(m p) t -> p m t", p=128), writes=[("hal", side)])
                    for m in range(8):
                        P.ts("dve", hT[:, m, TS + side * 8: TS + side * 8 + 8], hal[:, m, side * 8:(side + 1) * 8],
                             modP[:, si, 8 + m:9 + m], modT[:, si, m:m + 1], ALU.mult, ALU.add,
                             reads=[("hal", side), ("modT", l, si), ("modP", l, si)], writes=[("hT", m, "h%d" % side)])
                else:
                    for m in range(8):
                        P.memset("dve", hT[:, m, TS + side * 8: TS + side * 8 + 8], 0.0, writes=[("hT", m, "h%d" % side)])
            P.barrier()
            if getattr(self, 'a_stop', None) == 'ph0':
                return

            def proj_halo(ps_ap, wt, wkey, bank):
                for k in range(8):
                    P.mm(ps_ap, wt[:, k, :], hT[:, k, TS:TS + 16], start=(k == 0), stop=(k == 7),
                         reads=[wkey, ("hT", k, "h0"), ("hT", k, "h1")], writes=[self.psk(bank)])

            P.sb_off = mark
            rC = P.sb([128, TS], F32, "rC")
            rS = P.sb([128, TS], F32, "rS")
            P.dma("sp", rC[:], s.ropeC[:, t0:t0 + TS], writes=["rC"])
            P.dma("act", rS[:], s.ropeS[:, t0:t0 + TS], writes=["rS"])
            wp = [P.sb([128, 8, 128], BF16, "wp") for _ in range(2)]
            sqb = [P.sb([128, W], F32, "sqb") for _ in range(2)]
            rs = [P.sb([128, W], F32, "rs") for _ in range(2)]
            t1 = [P.sb([128, W], F32, "t1") for _ in range(2)]
            t2 = [P.sb([128, W], F32, "t2") for _ in range(2)]
            qrow = [P.sb([128, TS], BF16, "qrow") for _ in range(2)]
            it = 0
            for hc in range(10):
                wq, wk = load_w(hc * 128)
                pb = hc % 2
                for q4, src0 in enumerate((32, 0, 96, 64)):
                    P.cp("pool", wp[pb][:, :, q4 * 32:(q4 + 1) * 32], wq[:, :, src0:src0 + 32], reads=[wk], writes=[("wp", pb, q4)])
                wpk = [("wp", pb, q4) for q4 in range(4)]
                gcol = self.V_QG if hc < 8 else self.V_KG
                r = hc % 2
                for w in range(NW):
                    i = it % 2
                    it += 1
                    bq, bp, bs = (0, 1, 4) if i == 0 else (2, 3, 5)
                    QL = 9
                    if QL < 2:
                        continue
                    self._proj(self.ps[bq][:, 0:W], wq, hT, w, W, wk, bq)
                    for k in range(8):
                        P.mm(self.ps[bp][:, 0:W], wp[pb][:, k, :], hT[:, k, w * W:(w + 1) * W], start=(k == 0), stop=(k == 7),
                             reads=wpk + [("hT", k, w)], writes=[self.psk(bp)])
                    if QL < 3:
                        continue
                    P.act(sqb[i][:], self.ps[bq][:, 0:W], AF.Square, reads=[self.psk(bq)], writes=[("sqb", i)])
                    P.mm(self.ps[bs][:, 0:W], self.ones_f[:], sqb[i][:], start=True, stop=True,
                         reads=["ones_f", ("sqb", i)], writes=[self.psk(bs)])
                    P.act(rs[i][:], self.ps[bs][:, 0:W], AF.Ln, scale=1.0 / 128.0, bias=self.epsT[:, 0:1],
                          reads=[self.psk(bs), "epsT"], writes=[("rs", i)])
                    P.act(rs[i][:], rs[i][:], AF.Exp, scale=-0.5, reads=[("rs", i)], writes=[("rs", i)])
                    if QL < 4:
                        continue
                    P.stt(t1[i][:], self.ps[bq][:, 0:W], vec[:, gcol:gcol + 1], rC[:, w * W:(w + 1) * W], ALU.mult, ALU.mult,
                          reads=[self.psk(bq), ("vecs", l), "rC"], writes=[("t1", i)])
                    P.stt(t2[i][:], self.ps[bp][:, 0:W], vec[:, gcol + 1:gcol + 2], rS[:, w * W:(w + 1) * W], ALU.mult, ALU.mult,
                          reads=[self.psk(bp), ("vecs", l), "rS"], writes=[("t2", i)])
                    if QL < 5:
                        continue
                    P.tt("pool", t1[i][:], t1[i][:], t2[i][:], ALU.add, reads=[("t1", i), ("t2", i)], writes=[("t1", i)])
                    P.tt("pool", qrow[r][:, w * W:(w + 1) * W], t1[i][:], rs[i][:], ALU.mult,
                         reads=[("t1", i), ("rs", i)], writes=[("qrow", r, w)])
                if hc < 8:
                    dst = s.qT[hc * 128:(hc + 1) * 128, t0:t0 + TS]
                else:
                    dst = self.kT_d[(hc - 8) * 128:(hc - 7) * 128, s.ktok0 + t0: s.ktok0 + t0 + TS]
                if QL >= 6:
                    P.dma("sp", dst, qrow[r][:], reads=[("qrow", r, w) for w in range(NW)])
            P.barrier()
            if getattr(self, 'a_stop', None) == 'qk':
                return

            P.sb_off = mark
            wv = P.sb([128, 8, 256], BF16, "wv")
            vrow = P.sb([128, NB, 256], BF16, "vrow")
            load_w(C_V, 256, wv, "wv")
            wvk = [("wv", 0), ("wv", 128)]
            for tb in range(NB):
                bank = tb % 4
                w = (tb * 128) // W
                for k in range(8):
                    P.mm(self.ps[bank][:, 0:256], hT[:, k, tb * 128:(tb + 1) * 128], wv[:, k, :], start=(k == 0), stop=(k == 7),
                         reads=wvk + [("hT", k, w)], writes=[self.psk(bank)])
                P.cp("act" if tb % 2 else "dve", vrow[:, tb, :], self.ps[bank][:, 0:256], reads=[self.psk(bank)], writes=[("vrow", tb)])
            P.dma("sp", self.v_d[s.ktok0 + t0: s.ktok0 + t0 + TS, :].rearrange("(b p) c -> p b c", p=128), vrow[:],
                  reads=[("vrow", tb) for tb in range(NB)])
            P.barrier()
            if getattr(self, 'a_stop', None) == 'v':
                return

            P.sb_off = mark
            ubuf = [P.sb([128, TS + 2], F32, "ubuf") for _ in range(2)]
            sA = P.sb([128, TS], F32, "sA")
            sB = P.sb([128, TS], F32, "sB")
            zrow = P.sb([128, TS], BF16, "zrow")
            ztile = P.sb([128, NB, 128], BF16, "ztile")
            uc = 0
            for j in range(8):
                for part, (cchunk, cm, dst, dk) in enumerate(((12 + j, j, sA, "sA"), (28 + j, 16 + j, sB, "sB"), (20 + j, 8 + j, sA, "sA"))):
                    ub = uc % 2
                    uc += 1
                    wt, wk = load_w(cchunk * 128)
                    ukeys = []
                    for w in range(NW):
                        bank = w % 4
                        self._proj(self.ps[bank][:, 0:W], wt, hT, w, W, wk, bank)
                        P.cp("act" if w % 2 else "dve", ubuf[ub][:, 1 + w * W: 1 + (w + 1) * W], self.ps[bank][:, 0:W],
                             reads=[self.psk(bank)], writes=[("ubuf", ub, w)])
                        ukeys.append(("ubuf", ub, w))
                    proj_halo(self.ps[4][:, 0:16], wt, wk, 4)
                    P.cp("dve", ubuf[ub][:, 0:TS + 2:TS + 1], self.ps[4][:, 7:9], reads=[self.psk(4)], writes=[("ubuf", ub, "h")])
                    ukeys.append(("ubuf", ub, "h"))
                    c0 = self.V_CW + cm
                    P.act(dst[:], ubuf[ub][:, 1:TS + 1], AF.Identity, scale=vec[:, c0 + 24:c0 + 25], bias=vec[:, self.V_CB + cm:self.V_CB + cm + 1],
                          reads=ukeys + [("vecs", l)], writes=[dk])
                    P.stt(dst[:], ubuf[ub][:, 0:TS], vec[:, c0:c0 + 1], dst[:], ALU.mult, ALU.add, reads=ukeys + [dk, ("vecs", l)], writes=[dk])
                    P.stt(dst[:], ubuf[ub][:, 2:TS + 2], vec[:, c0 + 48:c0 + 49], dst[:], ALU.mult, ALU.add, reads=ukeys + [dk, ("vecs", l)], writes=[dk])
                    if part == 1 and s.tag == "c":
                        P.tt("pool", sB[:], sA[:], sB[:], ALU.mult, reads=["sA", "sB"], writes=["sB"])
                        P.dma("sp", s.zT[j * 128:(j + 1) * 128, t0:t0 + TS], sB[:], reads=["sB"])
                    elif part == 1:
                        P.tt("pool", zrow[:], sA[:], sB[:], ALU.mult, reads=["sA", "sB"], writes=["zrow"])
                        for blk in range(NB):
                            bank = 6 + (blk // 8) % 2
                            pv = self.ps[bank][:].bitcast(BF16)
                            P.tr(pv[:, (blk % 8) * 128:(blk % 8 + 1) * 128], zrow[:, blk * 128:(blk + 1) * 128], self.identb[:],
                                 reads=["zrow", "identb"], writes=[self.psk(bank)])
                            if blk % 8 == 7 or blk == NB - 1:
                                b0 = blk - blk % 8
                                n = blk - b0 + 1
                                P.cp("act" if (blk // 8) % 2 else "dve", ztile[:, b0:b0 + n, :].rearrange("p b c -> p (b c)"), pv[:, 0:n * 128],
                                     reads=[self.psk(bank)], writes=[("ztile", b0)])
                        P.dma("sp", s.z[t0:t0 + TS, j * 128:(j + 1) * 128].rearrange("(b p) c -> p b c", p=128), ztile[:],
                              reads=[("ztile", b0) for b0 in range(0, NB, 8)])
                    if part == 2:
                        P.dma("act", s.x0T[j * 128:(j + 1) * 128, t0:t0 + TS], sA[:], reads=["sA"])
            P.barrier()
            if getattr(self, 'a_stop', None) == 'hy':
                return

            P.sb_off = mark
            n = TS + 16
            pbuf = [P.sb([128, n], F32, "pbuf") for _ in range(2)]
            A = P.sb([128, n], F32, "pA")
            Bb = P.sb([128, n], F32, "pB")
            mT = [P.sb([128, TS], BF16, "mT") for _ in range(2)]
            prow = [P.sb([128, TS], BF16, "prow") for _ in range(2)]
            pw = P.sb([128, 2, 256], BF16, "pw")
            pwf = P.sb([128, 2, 256], F32, "pwf")
            tmp8 = P.sb([128, 8], F32, "tmp8")
            pe4 = self.pedge[:].rearrange("p (g s e) -> p g s e", g=4, s=2)
            for g in range(4):
                wsz = POOL_WINDOWS[g]
                kk = g + 1
                o = 8 + wsz // 2 - 1
                P.dma("sp", pwf[:], self.inp["pool_w"][l, g].rearrange("(i p) o -> p i o", p=128), writes=["pwf"])
                P.cp("pool", pw[:], pwf[:], reads=["pwf"], writes=["pw"])
                for i in range(2):
                    wt, wk = load_w((36 + 2 * g + i) * 128)
                    pk = []
                    for w in range(NW):
                        bank = w % 4
                        self._proj(self.ps[bank][:, 0:W], wt, hT, w, W, wk, bank)
                        P.cp("act" if w % 2 else "dve", pbuf[i][:, 8 + w * W: 8 + (w + 1) * W], self.ps[bank][:, 0:W],
                             reads=[self.psk(bank)], writes=[("pbuf", i, w)])
                        pk.append(("pbuf", i, w))
                    proj_halo(self.ps[4][:, 0:16], wt, wk, 4)
                    P.cp("dve", pbuf[i][:, 0:8], self.ps[4][:, 0:8], reads=[self.psk(4)], writes=[("pbuf", i, "h0")])
                    P.cp("dve", pbuf[i][:, TS + 8:TS + 16], self.ps[4][:, 8:16], reads=[self.psk(4)], writes=[("pbuf", i, "h1")])
                    pk += [("pbuf", i, "h0"), ("pbuf", i, "h1")]
                    u = pbuf[i]
                    P.tt("pool", A[:, 1:n], u[:, 1:n], u[:, 0:n - 1], ALU.add, reads=pk, writes=["pA"])
                    R, rk = A, "pA"
                    if kk >= 2:
                        P.tt("pool", Bb[:, 3:n], A[:, 3:n], A[:, 1:n - 2], ALU.add, reads=["pA"], writes=["pB"])
                        R, rk = Bb, "pB"
                    if kk >= 3:
                        P.tt("pool", A[:, 7:n], Bb[:, 7:n], Bb[:, 3:n - 4], ALU.add, reads=["pB"], writes=["pA"])
                        R, rk = A, "pA"
                    if kk >= 4:
                        P.tt("pool", Bb[:, 15:n], A[:, 15:n], A[:, 7:n - 8], ALU.add, reads=["pA"], writes=["pB"])
                        R, rk = Bb, "pB"
                    P.stt(mT[i][:], R[:, o:o + TS], 1.0 / wsz, u[:, 8:8 + TS], ALU.mult, ALU.subtract, reads=[rk] + pk, writes=[("mT", i)])
                    if t0 == 0:
                        P.tt("dve", tmp8[:], R[:, o:o + 8], pe4[:, g, 0, :], ALU.mult, reads=[rk, "pedge"], writes=["tmp8"])
                        P.tt("dve", mT[i][:, 0:8], tmp8[:], u[:, 8:16], ALU.subtract, reads=["tmp8"] + pk, writes=[("mT", i)])
                    if t0 + TS == T:
                        P.tt("dve", tmp8[:], R[:, o + TS - 8:o + TS], pe4[:, g, 1, :], ALU.mult, reads=[rk, "pedge"], writes=["tmp8"])
                        P.tt("dve", mT[i][:, TS - 8:TS], tmp8[:], u[:, TS:TS + 8], ALU.subtract, reads=["tmp8"] + pk, writes=[("mT", i)])
                for oc in range(2):
                    for w in range(NW):
                        bank = w % 4
                        for i in range(2):
                            P.mm(self.ps[bank][:, 0:W], pw[:, i, oc * 128:(oc + 1) * 128], mT[i][:, w * W:(w + 1) * W],
                                 start=(i == 0), stop=(i == 1), reads=["pw", ("mT", i)], writes=[self.psk(bank)])
                        cidx = self.V_PS + 2 * g + oc
                        P.act(prow[oc][:, w * W:(w + 1) * W], self.ps[bank][:, 0:W], AF.Identity, scale=vec[:, cidx:cidx + 1],
                              reads=[self.psk(bank), ("vecs", l)], writes=[("prow", oc, w)])
                    P.dma("sp", s.poolT[(2 * g + oc) * 128:(2 * g + oc + 1) * 128, t0:t0 + TS], prow[oc][:],
                          reads=[("prow", oc, w) for w in range(NW)])
            P.barrier()
            if getattr(self, 'a_stop', None) == 'pool':
                return

            P.sb_off = mark
            grow = [P.sb([128, TS], BF16, "grow") for _ in range(2)]
            for gc in range(24):
                r = gc % 2
                wt, wk = load_w((44 + gc) * 128)
                for w in range(NW):
                    bank = w % 4
                    self._proj(self.ps[bank][:, 0:W], wt, hT, w, W, wk, bank)
                    P.act(grow[r][:, w * W:(w + 1) * W], self.ps[bank][:, 0:W], AF.Sigmoid, reads=[self.psk(bank)], writes=[("grow", r, w)])
                P.dma("sp", s.gT[gc // 8, (gc % 8) * 128:(gc % 8 + 1) * 128, t0:t0 + TS], grow[r][:],
                      reads=[("grow", r, w) for w in range(NW)])
            P.barrier()
            if getattr(self, 'a_stop', None) == 'gate':
                return

    def stage_b(self, l, s, NK):
        P = self.P
        P.sb_reset()
        NQ, W = s.T, s.W
        NB = NK // 128
        KT = P.sb([128, 2, NK], BF16, "KT")
        V = P.sb([128, NB, 256], BF16, "V")
        for kv in range(2):
            P.dma("sp" if kv else "act", KT[:, kv, :], self.kT_d[kv * 128:(kv + 1) * 128, 0:NK], writes=[("KT", kv)])
        vsrc = self.v_d[0:NK, :].rearrange("(b p) c -> p b c", p=128)
        vk = []
        for b0 in range(0, NB, 11):
            b1 = min(NB, b0 + 11)
            P.dma("sp", V[:, b0:b1, :], vsrc[:, b0:b1, :], writes=[("V", b0)])
            vk.append(("V", b0))
        QT = [P.sb([128, 8, W], BF16, "QT") for _ in range(2)]
        attT = [P.sb([128, 8, W], BF16, "attT") for _ in range(2)]
        NPT = 8
        pT = [P.sb([128, W], BF16, "pT") for _ in range(NPT)]
        pool_tbs = [tb for tb in range(NB) if tb % 8 in (1, 4, 6)]
        dve_tbs = [tb for tb in range(NB) if tb % 8 not in (1, 4, 6)]
        rden = [P.sb([128, W], F32, "rden") for _ in range(2)]
        accD = [P.sb([128, W], F32, "accD") for _ in range(2)]
        accP = [P.sb([128, W], F32, "accP") for _ in range(2)]
        scale = float(HD) ** -0.5
        steps = [(qw, h, tb) for qw in range(NQ // W) for h in range(8) for tb in range(NB)]
        n = len(steps)

        def issue_S(i):
            qw, h, tb = steps[i]
            r = i % 4
            if h == 0 and tb == 0:
                P.dma("sp", QT[qw % 2][:], s.qT[:, qw * W:(qw + 1) * W].rearrange("(h p) t -> p h t", p=128), writes=[("QT", qw % 2)])
            P.mm(self.ps[r][:, 0:W], KT[:, h // 4, tb * 128:(tb + 1) * 128], QT[qw % 2][:, h, :], True, True,
                 reads=[("KT", h // 4), ("QT", qw % 2)], writes=[self.psk(r)])

        LA = 3
        for i in range(min(LA, n)):
            issue_S(i)
        for i, (qw, h, tb) in enumerate(steps):
            r = i % 4
            r4 = i % NPT
            kv = h // 4
            hp = h % 2
            ob = 4 + hp
            P.act(pT[r4][:], self.ps[r][:, 0:W], AF.Exp, scale=scale, reads=[self.psk(r)], writes=[("pT", r4)])
            P.mm(self.ps[ob][:, 0:W], V[:, tb, kv * 128:(kv + 1) * 128], pT[r4][:], tb == 0, tb == NB - 1,
                 reads=vk + [("pT", r4)], writes=[self.psk(ob)])
            ab = 6 + hp
            if tb in dve_tbs:
                if tb == dve_tbs[0] and tb == dve_tbs[-1]:
                    P.cp("dve", accD[hp][:], pT[r4][:], reads=[("pT", r4)], writes=[("accD", hp)])
                elif tb == dve_tbs[0]:
                    P.cp("dve", self.ps[ab][:, 0:W], pT[r4][:], reads=[("pT", r4)], writes=[self.psk(ab)])
                elif tb != dve_tbs[-1]:
                    P.tt("dve", self.ps[ab][:, 0:W], self.ps[ab][:, 0:W], pT[r4][:], ALU.add, reads=[("pT", r4), self.psk(ab)], writes=[self.psk(ab)])
                else:
                    P.tt("dve", accD[hp][:], self.ps[ab][:, 0:W], pT[r4][:], ALU.add, reads=[("pT", r4), self.psk(ab)], writes=[("accD", hp)])
            else:
                if tb == pool_tbs[0]:
                    P.cp("pool", accP[hp][:], pT[r4][:], reads=[("pT", r4)], writes=[("accP", hp)])
                else:
                    P.tt("pool", accP[hp][:], accP[hp][:], pT[r4][:], ALU.add, reads=[("pT", r4), ("accP", hp)], writes=[("accP", hp)])
            if i + LA < n:
                issue_S(i + LA)
            if tb == NB - 1:
                rd = rden[hp]
                db = ab
                P.mm(self.ps[db][:, 0:W], self.ones_f[:], accD[hp][:], True, False, reads=[("accD", hp)], writes=[self.psk(db)])
                P.mm(self.ps[db][:, 0:W], self.ones_f[:], accP[hp][:], False, True, reads=[("accP", hp)], writes=[self.psk(db)])
                P.add("dve", lambda e, rd=rd, db=db: e.reciprocal(out=rd[:], in_=self.ps[db][:, 0:W]), reads=[self.psk(db)], writes=[("rden", hp)])
                P.tt("dve", attT[qw % 2][:, h, :], self.ps[ob][:, 0:W], rd[:], ALU.mult,
                     reads=[self.psk(ob), ("rden", hp)], writes=[("attT", qw % 2, h)])
                if h == 7:
                    P.dma("act", s.attnT[:, qw * W:(qw + 1) * W].rearrange("(h p) t -> p h t", p=128), attT[qw % 2][:],
                          reads=[("attT", qw % 2, hh) for hh in range(8)])
        P.barrier()

    def _sin_layer(self, ps_ap, fcol, bcol, hv, tmp, tmp2, out_ap, psbank, okey):
        P = self.P
        MAGIC = 12582912.0
        P.ts("dve", tmp, ps_ap, hv[:, fcol:fcol + 1], hv[:, bcol:bcol + 1], ALU.mult, ALU.add, reads=[self.psk(psbank)], writes=["sl_tmp"])
        P.ts("dve", tmp2, tmp, 1.0 / (2 * math.pi), MAGIC, ALU.mult, ALU.add, reads=["sl_tmp"], writes=["sl_tmp2"])
        P.ts("dve", tmp2, tmp2, MAGIC, -2 * math.pi, ALU.subtract, ALU.mult, reads=["sl_tmp2"], writes=["sl_tmp2"])
        P.tt("dve", tmp, tmp, tmp2, ALU.add, reads=["sl_tmp", "sl_tmp2"], writes=["sl_tmp"])
        P.act(out_ap, tmp, AF.Sin, reads=["sl_tmp"], writes=[okey])

    def stage_k(self, l, tag):
        P = self.P
        P.sb_reset()
        I = self.inp
        hv = self.hfv[l]
        Kf = self.Kf[tag]
        w1s = P.sb([33, 64], F32, "w1s")
        w2s = P.sb([64, 64], F32, "w2s")
        w3f = P.sb([64, 2048], F32, "w3f")
        w3b = P.sb([64, 2048], BF16, "w3b")
        h2T = [P.sb([64, 8192], BF16, "h2T") for _ in range(2)]
        dl = P.sb([128, 1024], F32, "dl")
        drow = P.sb([128, 1024], F32, "drow")
        nrow = P.sb([128, 1024], F32, "nrow")
        negt = P.sb([128, 128], F32, "negt")
        mask = P.sb([128, 128], F32, "mask")
        F1 = P.sb([128, 2, 128], BF16, "F1")
        P.dma("sp", w1s[:], I["hf_w1"][l], writes=["w1s"])
        P.dma("sp", w2s[:], I["hf_w2"][l], writes=["w2s"])
        P.dma("sp", w3f[:], I["hf_w3"][l], writes=["w3f"])
        P.cp("pool", w3b[:], w3f[:], reads=["w3f"], writes=["w3b"])
        P.dma("act", dl[:], self.cst["deltas"].partition_broadcast(128).rearrange("p o c -> p (o c)"), writes=["dl"])
        P.dma("act", drow[:], I["hy_d"][l].rearrange("(o c) -> o c", o=1).partition_broadcast(128).rearrange("p o c -> p (o c)"), writes=["drow"])
        P.dma("act", negt[:], self.cst[f"negt_{tag}"], writes=["negt"])
        P.dma("act", mask[:], self.cst[f"mask_{tag}"], writes=["mask"])
        P.dma("act", F1[:], self.cst["F1"], writes=["F1"])
        mark = P.sb_off
        ft = [P.sb([33, 512], F32, "ft") for _ in range(2)]
        tmp = P.sb([64, 512], F32, "sl_tmp")
        tmp2 = P.sb([64, 512], F32, "sl_tmp2")
        h1 = P.sb([64, 512], F32, "h1")
        it = 0
        for d, nm in enumerate((f"featF_{tag}", f"featR_{tag}")):
            for w in range(16):
                b = it % 2
                it += 1
                P.dma("sp", ft[b][:], self.cst[nm][:, w * 512:(w + 1) * 512], writes=[("ft", b)])
                P.mm(self.ps[0][0:64, 0:512], w1s[:], ft[b][:], True, True, reads=["w1s", ("ft", b)], writes=[self.psk(0)])
                self._sin_layer(self.ps[0][0:64, 0:512], 1, 3, hv, tmp[:], tmp2[:], h1[:], 0, "h1")
                P.mm(self.ps[1][0:64, 0:512], w2s[:], h1[:], True, True, reads=["w2s", "h1"], writes=[self.psk(1)])
                self._sin_layer(self.ps[1][0:64, 0:512], 1, 4, hv, tmp[:], tmp2[:], h2T[d][:, w * 512:(w + 1) * 512], 1, ("h2T", d))
        P.sb_off = mark
        kt = [P.sb([128, 8, 1024], BF16, "kt") for _ in range(2)]
        wn = [P.sb([128, 1024], F32, "wn") for _ in range(2)]
        sqb = [P.sb([128, 1024], BF16, "sqk") for _ in range(2)]
        Bt = [[P.sb([128, 8, 1024], BF16, "Btk") for _ in range(2)] for _ in range(2)]
        for jg in range(16):
            kb = jg % 2
            for n2i in range(8):
                n2 = jg * 8 + n2i
                i = n2 % 2
                ba = 0 if i == 0 else 2
                for ch in range(2):
                    P.mm(self.ps[ba + ch][0:64, 0:512], h2T[0][:, n2:8192:128], w3b[:, ch * 512:(ch + 1) * 512], True, True,
                         reads=[("h2T", 0), "w3b"], writes=[self.psk(ba + ch)])
                    P.mm(self.ps[ba + ch][64:128, 0:512], h2T[1][:, n2:8192:128], w3b[:, 1024 + ch * 512:1024 + (ch + 1) * 512], True, True,
                         reads=[("h2T", 1), "w3b"], writes=[self.psk(ba + ch)])
                P.act(wn[i][:], dl[:], AF.Exp, scale=negt[:, n2:n2 + 1], reads=["dl", "negt"], writes=[("wn", i)])
                P.ts("dve", wn[i][:], wn[i][:], 0.05, None, ALU.add, None, reads=[("wn", i)], writes=[("wn", i)])
                for ch in range(2):
                    P.stt(kt[kb][:, n2i, ch * 512:(ch + 1) * 512], self.ps[ba + ch][:, 0:512], mask[:, n2:n2 + 1], wn[i][:, ch * 512:(ch + 1) * 512],
                          ALU.mult, ALU.mult, reads=[self.psk(ba + ch), "mask", ("wn", i)], writes=[("kt", kb, n2i)])
                P.act(sqb[i][:], kt[kb][:, n2i, :], AF.Square, reads=[("kt", kb, n2i)], writes=[("sqk", i)])
                for ch in range(2):
                    P.mm(self.ps[6 + ch][:, 0:512], self.ones_b[:], sqb[i][:, ch * 512:(ch + 1) * 512], n2 == 0, n2 == 127,
                         reads=[("sqk", i)], writes=[self.psk(6 + ch)])
            ktf = kt[kb][:].rearrange("p a c -> p (a c)")
            for ri in range(2):
                btf = Bt[ri][kb][:].rearrange("p a c -> p (a c)")
                for cw in range(16):
                    bank = 4 + (cw % 2)
                    P.mm(self.ps[bank][:, 0:512], F1[:, ri, :], ktf[:, cw * 512:(cw + 1) * 512], True, True,
                         reads=["F1"] + [("kt", kb, q) for q in range(8)], writes=[self.psk(bank)])
                    P.cp("act" if cw % 2 else "dve", btf[:, cw * 512:(cw + 1) * 512], self.ps[bank][:, 0:512],
                         reads=[self.psk(bank)], writes=[("Btk", ri, kb, cw)])
                P.dma("sp" if ri else "act", self.Bd[ri][:, jg * 8:(jg + 1) * 8, :], Bt[ri][kb][:],
                      reads=[("Btk", ri, kb, cw) for cw in range(16)], writes=[("Bd", ri, jg)])
        for ch in range(2):
            P.act(nrow[:, ch * 512:(ch + 1) * 512], self.ps[6 + ch][:, 0:512], AF.Ln, bias=self.epsT[:, 0:1], reads=[self.psk(6 + ch)], writes=[("nrow", ch)])
            P.act(nrow[:, ch * 512:(ch + 1) * 512], nrow[:, ch * 512:(ch + 1) * 512], AF.Exp, scale=-0.5, reads=[("nrow", ch)], writes=[("nrow", ch)])
        P.barrier()
        P.sb_off = mark
        Br = [[P.sb([128, 1024], BF16, "Brk") for _ in range(2)] for _ in range(2)]
        G = [P.sb([128, 3, 128], BF16, "Gk") for _ in range(2)]
        Kt = [[P.sb([128, 1024], BF16, "Kt") for _ in range(2)] for _ in range(2)]
        tz = [P.sb([128, 512], F32, "tz") for _ in range(2)]
        for k1 in range(128):
            b = k1 % 2
            for ri in range(2):
                P.dma("sp" if ri else "act", Br[ri][b][:], self.Bd[ri][k1], writes=[("Brk", ri, b)])
            P.dma("sp", G[b][:], self.cst["GT"][k1], writes=[("Gk", b)])
            for ch in range(2):
                bs = 0 if (2 * k1 + ch) % 2 == 0 else 2
                cs = slice(ch * 512, (ch + 1) * 512)
                rk = [("Brk", 0, b), ("Brk", 1, b), ("Gk", b)]
                P.mm(self.ps[bs][:, 0:512], G[b][:, 0, :], Br[0][b][:, cs], True, False, reads=rk, writes=[self.psk(bs)])
                P.mm(self.ps[bs][:, 0:512], G[b][:, 2, :], Br[1][b][:, cs], False, True, reads=rk, writes=[self.psk(bs)])
                P.mm(self.ps[bs + 1][:, 0:512], G[b][:, 1, :], Br[0][b][:, cs], True, False, reads=rk, writes=[self.psk(bs + 1)])
                P.mm(self.ps[bs + 1][:, 0:512], G[b][:, 0, :], Br[1][b][:, cs], False, True, reads=rk, writes=[self.psk(bs + 1)])
                P.tt("dve", tz[ch][:], self.ps[bs][:, 0:512], nrow[:, cs], ALU.mult, reads=[self.psk(bs), ("nrow", ch)], writes=[("tz", ch)])
                P.tt("pool", Kt[0][b][:, cs], tz[ch][:], drow[:, cs], ALU.add, reads=[("tz", ch), "drow"], writes=[("Kt", 0, b, ch)])
                P.tt("dve", Kt[1][b][:, cs], self.ps[bs + 1][:, 0:512], nrow[:, cs], ALU.mult, reads=[self.psk(bs + 1), ("nrow", ch)], writes=[("Kt", 1, b, ch)])
            for ri in range(2):
                P.dma("sp" if ri else "act", Kf[ri][k1], Kt[ri][b][:], reads=[("Kt", ri, b, 0), ("Kt", ri, b, 1)])
        P.barrier()

    def stage_h(self, l, s):
        P = self.P
        P.sb_reset()
        Kf = self.Kf[s.tag]
        nval = 64 if s.tag == "l" else s.T // 128
        F1 = P.sb([128, 2, 128], BF16, "F1")
        I2 = P.sb([128, 2, 64], BF16, "I2")
        P.dma("act", F1[:], self.cst["F1"], writes=["F1"])
        P.dma("act", I2[:], self.cst["I2"], writes=["I2"])
        mark = P.sb_off
        zt = [P.sb([64, 8, 1024], BF16, "zt") for _ in range(2)]
        Bt = [[P.sb([128, 8, 1024], BF16, "Bth") for _ in range(2)] for _ in range(2)]
        zv = s.z.rearrange("(a b) c -> a b c", b=128)
        if nval < 64:
            for b in range(2):
                P.memset("pool", zt[b][:], 0.0, writes=[("zt", b)])
        for jg in range(16):
            b = jg % 2
            P.dma("sp", zt[b][0:nval, :, :], zv[0:nval, jg * 8:(jg + 1) * 8, :], reads=[("zt", b)] if nval < 64 else [], writes=[("ztd", b)])
            ztf = zt[b][:].rearrange("p a c -> p (a c)")
            for ri in range(2):
                btf = Bt[ri][b][:].rearrange("p a c -> p (a c)")
                for cw in range(16):
                    bank = (cw % 4)
                    P.mm(self.ps[bank][:, 0:512], F1[0:64, ri, :], ztf[:, cw * 512:(cw + 1) * 512], True, True,
                         reads=["F1", ("ztd", b), ("zt", b)], writes=[self.psk(bank)])
                    P.cp("act" if cw % 2 else "dve", btf[:, cw * 512:(cw + 1) * 512], self.ps[bank][:, 0:512],
                         reads=[self.psk(bank)], writes=[("Bth", ri, b, cw)])
                P.dma("sp" if ri else "act", self.Bd[ri][:, jg * 8:(jg + 1) * 8, :], Bt[ri][b][:],
                      reads=[("Bth", ri, b, cw) for cw in range(16)], writes=[("Bd", ri, jg)])
        P.barrier()
        P.sb_off = mark
        Br = [[P.sb([128, 1024], BF16, "Brh") for _ in range(2)] for _ in range(2)]
        Kt = [[P.sb([128, 1024], BF16, "Kth") for _ in range(2)] for _ in range(2)]
        G = [P.sb([128, 3, 128], BF16, "Gh") for _ in range(2)]
        Hh = [P.sb([128, 3, 128], BF16, "Hh") for _ in range(2)]
        Y = [[P.sb([128, 512], BF16, "Yh") for _ in range(2)] for _ in range(2)]
        tq = [[P.sb([128, 512], F32, "tq") for _ in range(4)] for _ in range(2)]
        Dt = [[P.sb([128, 1024], BF16, "Dth") for _ in range(2)] for _ in range(2)]
        def second_half(k1, ch):
            b = k1 % 2
            par = ch
            bs = 0 if par == 0 else 4
            cs = slice(ch * 512, (ch + 1) * 512)
            yk = [("Yh", 0, par), ("Yh", 1, par), ("Hh", b)]
            P.mm(self.ps[bs + 2][:, 0:512], Hh[b][:, 0, :], Y[0][par][:], True, False, reads=yk, writes=[self.psk(bs + 2)])
            P.mm(self.ps[bs + 2][:, 0:512], Hh[b][:, 1, :], Y[1][par][:], False, True, reads=yk, writes=[self.psk(bs + 2)])
            P.mm(self.ps[bs + 3][:, 0:512], Hh[b][:, 2, :], Y[0][par][:], True, False, reads=yk, writes=[self.psk(bs + 3)])
            P.mm(self.ps[bs + 3][:, 0:512], Hh[b][:, 0, :], Y[1][par][:], False, True, reads=yk, writes=[self.psk(bs + 3)])
            P.cp("act", Dt[0][b][:, cs], self.ps[bs + 2][:, 0:512], reads=[self.psk(bs + 2)], writes=[("Dth", 0, b, ch)])
            P.cp("act", Dt[1][b][:, cs], self.ps[bs + 3][:, 0:512], reads=[self.psk(bs + 3)], writes=[("Dth", 1, b, ch)])
            if ch == 1:
                for ri in range(2):
                    P.dma("sp" if ri else "act", self.Dd[ri][k1], Dt[ri][b][:], reads=[("Dth", ri, b, 0), ("Dth", ri, b, 1)])

        prev = None
        for k1 in range(128):
            b = k1 % 2
            for ri in range(2):
                P.dma("sp", Br[ri][b][:], self.Bd[ri][k1], writes=[("Brh", ri, b)])
                P.dma("act", Kt[ri][b][:], Kf[ri][k1], writes=[("Kth", ri, b)])
            P.dma("sp", G[b][:], self.cst["GT"][k1], writes=[("Gh", b)])
            P.dma("act", Hh[b][:], self.cst["HT"][k1], writes=[("Hh", b)])
            for ch in range(2):
                par = ch
                bs = 0 if par == 0 else 4
                cs = slice(ch * 512, (ch + 1) * 512)
                rk = [("Brh", 0, b), ("Brh", 1, b), ("Gh", b)]
                P.mm(self.ps[bs][:, 0:512], G[b][:, 0, :], Br[0][b][:, cs], True, False, reads=rk, writes=[self.psk(bs)])
                P.mm(self.ps[bs][:, 0:512], G[b][:, 2, :], Br[1][b][:, cs], False, True, reads=rk, writes=[self.psk(bs)])
                P.mm(self.ps[bs + 1][:, 0:512], G[b][:, 1, :], Br[0][b][:, cs], True, False, reads=rk, writes=[self.psk(bs + 1)])
                P.mm(self.ps[bs + 1][:, 0:512], G[b][:, 0, :], Br[1][b][:, cs], False, True, reads=rk, writes=[self.psk(bs + 1)])
                if prev is not None:
                    second_half(*prev)
                t = tq[par]
                kk = [("Kth", 0, b), ("Kth", 1, b)]
                P.tt("dve", t[0][:], self.ps[bs][:, 0:512], Kt[0][b][:, cs], ALU.mult, reads=[self.psk(bs)] + kk, writes=[("tq", par, 0)])
                P.tt("dve", t[1][:], self.ps[bs + 1][:, 0:512], Kt[1][b][:, cs], ALU.mult, reads=[self.psk(bs + 1)] + kk, writes=[("tq", par, 1)])
                P.tt("dve", t[2][:], self.ps[bs][:, 0:512], Kt[1][b][:, cs], ALU.mult, reads=[self.psk(bs)] + kk, writes=[("tq", par, 2)])
                P.tt("dve", t[3][:], self.ps[bs + 1][:, 0:512], Kt[0][b][:, cs], ALU.mult, reads=[self.psk(bs + 1)] + kk, writes=[("tq", par, 3)])
                P.tt("pool", Y[0][par][:], t[0][:], t[1][:], ALU.subtract, reads=[("tq", par, 0), ("tq", par, 1)], writes=[("Yh", 0, par)])
                P.tt("pool", Y[1][par][:], t[2][:], t[3][:], ALU.add, reads=[("tq", par, 2), ("tq", par, 3)], writes=[("Yh", 1, par)])
                prev = (k1, ch)
        second_half(*prev)
        P.barrier()
        P.sb_off = mark
        Dr = [[P.sb([128, 8, 1024], BF16, "Drh") for _ in range(2)] for _ in range(2)]
        yt = [P.sb([64, 8, 1024], F32, "yt") for _ in range(2)]
        yv = s.y.rearrange("(a b) c -> a b c", b=128)
        for jg in range(16):
            b = jg % 2
            for ri in range(2):
                P.dma("sp" if ri else "act", Dr[ri][b][:], self.Dd[ri][:, jg * 8:(jg + 1) * 8, :], writes=[("Drh", ri, b)])
            d0 = Dr[0][b][:].rearrange("p a c -> p (a c)")
            d1 = Dr[1][b][:].rearrange("p a c -> p (a c)")
            ytf = yt[b][:].rearrange("p a c -> p (a c)")
            for cw in range(16):
                bank = cw % 4
                cs = slice(cw * 512, (cw + 1) * 512)
                P.mm(self.ps[bank][0:64, 0:512], I2[:, 0, :], d0[:, cs], True, False, reads=["I2", ("Drh", 0, b), ("Drh", 1, b)], writes=[self.psk(bank)])
                P.mm(self.ps[bank][0:64, 0:512], I2[:, 1, :], d1[:, cs], False, True, reads=["I2", ("Drh", 0, b), ("Drh", 1, b)], writes=[self.psk(bank)])
                P.cp("act" if cw % 2 else "dve", ytf[:, cs], self.ps[bank][0:64, 0:512], reads=[self.psk(bank)], writes=[("yt", b, cw)])
            P.dma("sp", yv[0:nval, jg * 8:(jg + 1) * 8, :], yt[b][0:nval, :, :], reads=[("yt", b, cw) for cw in range(16)])
        P.barrier()

    def stage_hc(self, l, s):
        P = self.P
        P.sb_reset()
        I = self.inp
        hv, vec = self.hfv[l], self.vecs[l]
        L = s.T
        w1s = P.sb([33, 64], F32, "w1s")
        w2s = P.sb([64, 64], F32, "w2s")
        w3f = P.sb([64, 2048], F32, "w3f")
        ft = P.sb([33, L], F32, "ft")
        tmp = P.sb([64, L], F32, "sl_tmp")
        tmp2 = P.sb([64, L], F32, "sl_tmp2")
        h1 = P.sb([64, L], F32, "h1")
        h2 = P.sb([64, L], F32, "h2")
        t01 = P.sb([128, L], F32, "t01")
        ndl = P.sb([128, 8], F32, "ndl")
        P.dma("sp", w1s[:], I["hf_w1"][l], writes=["w1s"])
        P.dma("sp", w2s[:], I["hf_w2"][l], writes=["w2s"])
        P.dma("sp", w3f[:], I["hf_w3"][l], writes=["w3f"])
        P.dma("act", ft[:], self.cst["featF_c"][:, 0:L], writes=["ft"])
        P.dma("act", t01[:], self.cst["t01row_c"], writes=["t01"])
        P.dma("act", ndl[:], self.cst["ndeltasT"], writes=["ndl"])
        P.mm(self.ps[0][0:64, 0:L], w1s[:], ft[:], True, True, reads=["w1s", "ft"], writes=[self.psk(0)])
        self._sin_layer(self.ps[0][0:64, 0:L], 1, 3, hv, tmp[:], tmp2[:], h1[:], 0, "h1")
        P.mm(self.ps[1][0:64, 0:L], w2s[:], h1[:], True, True, reads=["w2s", "h1"], writes=[self.psk(1)])
        self._sin_layer(self.ps[1][0:64, 0:L], 1, 4, hv, tmp[:], tmp2[:], h2[:], 1, "h2")
        NP = 3 * L - 2
        zp = [P.sb([128, NP], F32, "zp") for _ in range(2)]
        win = [P.sb([128, L], F32, "win") for _ in range(2)]
        kF = [P.sb([128, L], F32, "kF") for _ in range(2)]
        kB = [P.sb([128, L], F32, "kB") for _ in range(2)]
        junk = P.sb([128, L], F32, "junk")
        ss = [P.sb([128, 4], F32, "ss") for _ in range(2)]
        accF = [P.sb([128, L], F32, "accF") for _ in range(2)]
        accB = [P.sb([128, L], F32, "accB") for _ in range(2)]
        x0c = [P.sb([128, L], F32, "x0c") for _ in range(2)]
        hyo = [P.sb([128, L], BF16, "hyo") for _ in range(2)]
        for b in range(2):
            P.memset("pool", zp[b][:], 0.0, writes=[("zp", b)])
        for j in range(8):
            b = j % 2
            bf, bb = (2, 3) if b == 0 else (4, 5)
            P.dma("sp", zp[b][:, L - 1:2 * L - 1], s.zT[j * 128:(j + 1) * 128, :], reads=[("zp", b)], writes=[("zpd", b)])
            P.dma("act", x0c[b][:], s.x0T[j * 128:(j + 1) * 128, :], writes=[("x0c", b)])
            P.mm(self.ps[bf][:, 0:L], w3f[:, j * 128:(j + 1) * 128], h2[:], True, True, reads=["w3f", "h2"], writes=[self.psk(bf)])
            P.mm(self.ps[bb][:, 0:L], w3f[:, 1024 + j * 128:1024 + (j + 1) * 128], h2[:], True, True, reads=["w3f", "h2"], writes=[self.psk(bb)])
            P.act(win[b][:], t01[:], AF.Exp, scale=ndl[:, j:j + 1], reads=["t01", "ndl"], writes=[("win", b)])
            P.ts("pool", win[b][:], win[b][:], 1.0, 0.05, ALU.mult, ALU.add, reads=[("win", b)], writes=[("win", b)])
            P.tt("dve", kF[b][:], self.ps[bf][:, 0:L], win[b][:], ALU.mult, reads=[self.psk(bf), ("win", b)], writes=[("kF", b)])
            P.tt("dve", kB[b][:], self.ps[bb][:, 0:L], win[b][:], ALU.mult, reads=[self.psk(bb), ("win", b)], writes=[("kB", b)])
            P.memset("dve", kB[b][:, 0:1], 0.0, writes=[("kB", b)])
            P.add("act", lambda e, b=b: e.activation(out=junk[:], in_=kF[b][:], func=AF.Square, accum_out=ss[b][:, 0:1]),
                  reads=[("kF", b)], writes=[("ss", b, 0), "junk"])
            P.add("act", lambda e, b=b: e.activation(out=junk[:], in_=kB[b][:], func=AF.Square, accum_out=ss[b][:, 1:2]),
                  reads=[("kB", b)], writes=[("ss", b, 1), "junk"])
            P.tt("pool", ss[b][:, 2:3], ss[b][:, 0:1], ss[b][:, 1:2], ALU.add, reads=[("ss", b, 0), ("ss", b, 1)], writes=[("ss", b, 2)])
            P.act(ss[b][:, 2:3], ss[b][:, 2:3], AF.Ln, bias=self.epsT[:, 0:1], reads=[("ss", b, 2)], writes=[("ss", b, 2)])
            P.act(ss[b][:, 3:4], ss[b][:, 2:3], AF.Exp, scale=-0.5, reads=[("ss", b, 2)], writes=[("ss", b, 3)])
            zk = [("zp", b), ("zpd", b)]
            P.ts("dve", accF[b][:], zp[b][:, L - 1:2 * L - 1], kF[b][:, 0:1], None, ALU.mult, None, reads=zk + [("kF", b)], writes=[("accF", b)])
            P.ts("dve", accB[b][:], zp[b][:, L:2 * L], kB[b][:, 1:2], None, ALU.mult, None, reads=zk + [("kB", b)], writes=[("accB", b)])
            for m in range(1, L):
                P.stt(accF[b][:], zp[b][:, L - 1 - m:2 * L - 1 - m], kF[b][:, m:m + 1], accF[b][:], ALU.mult, ALU.add,
                      reads=[("accF", b)], writes=[("accF", b)])
                if m >= 2:
                    P.stt(accB[b][:], zp[b][:, L - 1 + m:2 * L - 1 + m], kB[b][:, m:m + 1], accB[b][:], ALU.mult, ALU.add,
                          reads=[("accB", b)], writes=[("accB", b)])
            P.tt("pool", accF[b][:], accF[b][:], accB[b][:], ALU.add, reads=[("accF", b), ("accB", b)], writes=[("accF", b)])
            P.ts("pool", accB[b][:], zp[b][:, L - 1:2 * L - 1], vec[:, self.V_HD + j:self.V_HD + j + 1], 1.0, ALU.mult, ALU.mult,
                 reads=zk + [("accB", b), ("vecs", l)], writes=[("accB", b)])
            P.stt(accF[b][:], accF[b][:], ss[b][:, 3:4], accB[b][:], ALU.mult, ALU.add, reads=[("accF", b), ("accB", b), ("ss", b, 3)], writes=[("accF", b)])
            P.tt("pool", hyo[b][:], accF[b][:], x0c[b][:], ALU.mult, reads=[("accF", b), ("x0c", b)], writes=[("hyo", b)])
            P.dma("sp", s.hyT[j * 128:(j + 1) * 128, :], hyo[b][:], reads=[("hyo", b)])
        P.barrier()

    def _load_w_resident(self, dst, src2d, nrow_chunks, ncols, key, stage, skey):
        P = self.P
        n = 0
        for c0 in range(0, ncols, 128):
            for k0 in range(0, nrow_chunks, 8):
                kn = min(8, nrow_chunks - k0)
                j = self._stg_i % len(stage)
                self._stg_i += 1
                P.dma("sp" if j % 2 else "act", stage[j][:, 0:kn, :],
                      src2d[k0 * 128:(k0 + kn) * 128, c0:c0 + 128].rearrange("(k p) n -> p k n", p=128), writes=[(skey, j)])
                P.cp("pool" if n % 2 else "dve", dst[:, k0:k0 + kn, c0:c0 + 128], stage[j][:, 0:kn, :], reads=[(skey, j)], writes=[(key, c0, k0)])
                n += 1
        return [[(key, c0, k0) for k0 in range(0, nrow_chunks, 8)] for c0 in range(0, ncols, 128)]

    def _layer_norm(self, rT, W, gcol, bcol, vec, l, outT, sq, stat, rkeys, okey, sqk="lnsq"):
        P = self.P
        mean, msq, var, rstd = stat
        for m in range(8):
            P.mm(self.ps[6][:, 0:W], self.ones_f[:], rT[:, m, :], m == 0, m == 7, reads=[rkeys[m]], writes=[self.psk(6)])
        for m in range(8):
            P.act(sq[:, m, :], rT[:, m, :], AF.Square, reads=[rkeys[m]], writes=[(sqk, m)])
            P.mm(self.ps[7][:, 0:W], self.ones_f[:], sq[:, m, :], m == 0, m == 7, reads=[(sqk, m)], writes=[self.psk(7)])
        P.act(mean[:, 0:W], self.ps[6][:, 0:W], AF.Copy, scale=1.0 / D, reads=[self.psk(6)], writes=["ln_mean"])
        P.tt("pool", msq[:, 0:W], mean[:, 0:W], mean[:, 0:W], ALU.mult, reads=["ln_mean"], writes=["ln_msq"])
        P.stt(var[:, 0:W], self.ps[7][:, 0:W], 1.0 / D, msq[:, 0:W], ALU.mult, ALU.subtract, reads=[self.psk(7), "ln_msq"], writes=["ln_var"])
        P.act(rstd[:, 0:W], var[:, 0:W], AF.Ln, bias=self.epsT[:, 0:1], reads=["ln_var"], writes=["ln_rstd"])
        P.act(rstd[:, 0:W], rstd[:, 0:W], AF.Exp, scale=-0.5, reads=["ln_rstd"], writes=["ln_rstd"])
        for m in range(8):
            P.tt("dve", sq[:, m, :], rT[:, m, :], mean[:, 0:W], ALU.subtract, reads=[rkeys[m], "ln_mean", (sqk, m)], writes=[(sqk, m)])
            P.tt("pool", sq[:, m, :], sq[:, m, :], rstd[:, 0:W], ALU.mult, reads=[(sqk, m), "ln_rstd"], writes=[(sqk, m)])
            P.act(outT[:, m, :], sq[:, m, :], AF.Identity, scale=vec[:, gcol + m:gcol + m + 1], bias=vec[:, bcol + m:bcol + m + 1],
                  reads=[(sqk, m), ("vecs", l)], writes=[(okey, m)])

    def stage_c(self, l, s):
        P = self.P
        P.sb_reset()
        T, W = s.T, 256
        si = s.sidx
        vec, modT = self.vecs[l], self.modT[l]
        nj = W // 128
        self._stg_i = 0
        wb = [P.sb([128, 8, D], BF16, f"wb{i}") for i in range(3)]
        wo = P.sb([128, 8, D], BF16, "wo")
        mark0 = P.sb_off
        stage = [P.sb([128, 8, 128], F32, "cst") for _ in range(3)]
        wk = []
        for i in range(3):
            wk.append(self._load_w_resident(wb[i], self.inp["w_branch"][l, i], 8, D, f"wb{i}", stage, "cst"))
        wok = self._load_w_resident(wo, self.inp["w_out"][l], 8, D, "wo", stage, "cst")
        P.barrier()
        P.sb_off = mark0
        NBUF = 2
        yt = [P.sb([128, nj, D], F32, "yt") for _ in range(NBUF)]
        x0t = [P.sb([128, 8, W], F32, "x0t") for _ in range(NBUF)]
        hyT = [P.sb([128, 8, W], BF16, "hyT") for _ in range(NBUF)]
        atT = [P.sb([128, 8, W], BF16, "atT") for _ in range(NBUF)]
        poT = [P.sb([128, 8, W], BF16, "poT") for _ in range(NBUF)]
        gt = [P.sb([128, 3, 8, W], BF16, "gt") for _ in range(NBUF)]
        xat = [P.sb([128, 8, W], F32, "xat") for _ in range(NBUF)]
        mg = [P.sb([128, 8, W], BF16, "mg") for _ in range(NBUF)]
        tA = [P.sb([128, W], F32, "tA") for _ in range(2)]
        tB = [P.sb([128, W], F32, "tB") for _ in range(2)]
        stat = [P.sb([128, W], F32, "lnst") for _ in range(4)]
        direct = s.hyT is not None

        def finish_c(p, ts_, sq):
            self._layer_norm(xat[p], W, self.V_LN, self.V_LN + 8, vec, l, xat[p], sq, stat, [("rT", p, m) for m in range(8)], ("x1T", p), ("lnsq", p))
            P.dma("sp", s.xb[:, ts_].rearrange("(m p) t -> p m t", p=128), xat[p][:], reads=[(("x1T", p), m) for m in range(8)], writes=[("xat", p)])

        pend = None
        for tw in range(T // W):
            p = tw % NBUF
            ts_ = slice(tw * W, (tw + 1) * W)
            sq = yt[p][:].rearrange("p j d -> p (j d)").rearrange("p (m w) -> p m w", m=8)
            if direct:
                P.dma("sp", hyT[p][:], s.hyT[:, ts_].rearrange("(m p) t -> p m t", p=128), writes=[("hyT", p, m) for m in range(8)])
            else:
                P.dma("sp", yt[p][:], s.y[ts_, :].rearrange("(j p) d -> p j d", p=128), reads=[(("lnsq", p), m) for m in range(8)], writes=[("yt", p)])
                P.dma("act", x0t[p][:], s.x0T[:, ts_].rearrange("(m p) t -> p m t", p=128), writes=[("x0t", p)])
            P.dma("sp", atT[p][:], s.attnT[:, ts_].rearrange("(m p) t -> p m t", p=128), writes=[("atT", p)])
            P.dma("act", poT[p][:], s.poolT[:, ts_].rearrange("(m p) t -> p m t", p=128), writes=[("poT", p)])
            for i in range(3):
                P.dma("sp" if i % 2 else "act", gt[p][:, i, :, :], s.gT[i, :, ts_].rearrange("(m p) t -> p m t", p=128), writes=[("gt", p, i)])
            P.dma("sp", xat[p][:], s.xa[:, ts_].rearrange("(m p) t -> p m t", p=128), writes=[("xat", p)])
            for m in range(8):
                if direct:
                    break
                bank = m % 2
                for j in range(nj):
                    P.tr(self.ps[bank][:, j * 128:(j + 1) * 128], yt[p][:, j, m * 128:(m + 1) * 128], self.ident[:], reads=[("yt", p)], writes=[self.psk(bank)])
                P.tt("dve", hyT[p][:, m, :], self.ps[bank][:, 0:W], x0t[p][:, m, :], ALU.mult, reads=[self.psk(bank), ("x0t", p)], writes=[("hyT", p, m)])
            srcs = (hyT[p], atT[p], poT[p])
            skeys = ([("hyT", p, m) for m in range(8)], [("atT", p)], [("poT", p)])
            for mo in range(8):
                q = mo % 2
                for i in range(3):
                    bank = 2 + i
                    for k in range(8):
                        P.mm(self.ps[bank][:, 0:W], wb[i][:, k, mo * 128:(mo + 1) * 128], srcs[i][:, k, :], k == 0, k == 7,
                             reads=(skeys[i] if i else [("hyT", p, k)]), writes=[self.psk(bank)])
                P.tt("dve", tA[q][:], self.ps[2][:, 0:W], gt[p][:, 0, mo, :], ALU.mult, reads=[self.psk(2), ("gt", p, 0)], writes=[("tA", q)])
                P.tt("dve", tB[q][:], self.ps[3][:, 0:W], gt[p][:, 1, mo, :], ALU.mult, reads=[self.psk(3), ("gt", p, 1)], writes=[("tB", q)])
                P.tt("pool", tA[q][:], tA[q][:], tB[q][:], ALU.add, reads=[("tA", q), ("tB", q)], writes=[("tA", q)])
                P.tt("dve", tB[q][:], self.ps[4][:, 0:W], gt[p][:, 2, mo, :], ALU.mult, reads=[self.psk(4), ("gt", p, 2)], writes=[("tB", q)])
                P.tt("pool", mg[p][:, mo, :], tA[q][:], tB[q][:], ALU.add, reads=[("tA", q), ("tB", q)], writes=[("mg", p, mo)])
            if pend is not None:
                finish_c(*pend)
            for mo in range(8):
                bank = mo % 2
                for k in range(8):
                    P.mm(self.ps[bank][:, 0:W], wo[:, k, mo * 128:(mo + 1) * 128], mg[p][:, k, :], k == 0, k == 7,
                         reads=[("mg", p, k)], writes=[self.psk(bank)])
                P.act(xat[p][:, mo, :], xat[p][:, mo, :], AF.Copy, scale=ALPHA, reads=[("xat", p)], writes=[("xs", p, mo)])
                P.stt(xat[p][:, mo, :], self.ps[bank][:, 0:W], modT[:, si, 16 + mo:17 + mo], xat[p][:, mo, :], ALU.mult, ALU.add,
                      reads=[self.psk(bank), ("xs", p, mo), ("modT", l, si)], writes=[("rT", p, mo)])
            pend = (p, ts_, sq)
        finish_c(*pend)
        P.barrier()

    def stage_d(self, l, s, final):
        P = self.P
        P.sb_reset()
        T = s.T
        W = 256
        si = s.sidx
        vec, modT, modP = self.vecs[l], self.modT[l], self.modP[l]
        NF = DFF // 128
        self._stg_i = 0
        w1 = P.sb([128, 8, DFF], BF16, "w1")
        w3 = P.sb([128, 8, DFF], BF16, "w3")
        w2 = P.sb([128, NF, D], BF16, "w2")
        mark0 = P.sb_off
        stage = [P.sb([128, 8, 128], F32, "dst") for _ in range(3)]
        w1k = self._load_w_resident(w1, self.inp["ffn_w1"][l], 8, DFF, "w1", stage, "dst")
        w3k = self._load_w_resident(w3, self.inp["ffn_w3"][l], 8, DFF, "w3", stage, "dst")
        w2k = self._load_w_resident(w2, self.inp["ffn_w2"][l], NF, D, "w2", stage, "dst")
        P.barrier()
        P.sb_off = mark0
        x1t = [P.sb([128, 8, W], F32, "x1t") for _ in range(2)]
        h2T = [P.sb([128, 8, W], BF16, "h2T") for _ in range(2)]
        gT = P.sb([128, NF, W], BF16, "gT")
        sa = [P.sb([128, W], F32, "sa") for _ in range(2)]
        sqd = P.sb([128, 8, W], F32, "sqd")
        stat = [P.sb([128, W], F32, "lnst") for _ in range(4)]
        ot = sqd[:].rearrange("p m w -> p (m w)").rearrange("p (j d) -> p j d", j=W // 128) if final else None

        def finish_d(p, ts_):
            xt = x1t[p]
            self._layer_norm(xt, W, self.V_LN + 16, self.V_LN + 24, vec, l, xt, sqd, stat, [("rT", p, m) for m in range(8)], ("x2T", p))
            xk = [(("x2T", p), m) for m in range(8)]
            if not final:
                P.dma("sp", s.xa[:, ts_].rearrange("(m p) t -> p m t", p=128), xt[:], reads=xk, writes=[("x1t", p)])
            else:
                for j in range(W // 128):
                    for half in range(2):
                        bank = half
                        for mm_ in range(4):
                            m = half * 4 + mm_
                            P.tr(self.ps[bank][:, mm_ * 128:(mm_ + 1) * 128], xt[:, m, j * 128:(j + 1) * 128], self.ident[:],
                                 reads=[(("x2T", p), m)], writes=[self.psk(bank)])
                        P.cp("act" if half else "dve", ot[:, j, half * 512:(half + 1) * 512], self.ps[bank][:, 0:512], reads=[self.psk(bank)],
                             writes=[("ot", j, half)] + ([("lnsq", m) for m in range(8)] if (j == 0 and half == 0) else []))
                P.dma("sp", self.out[ts_, :].rearrange("(j p) d -> p j d", p=128), ot,
                      reads=[("ot", j, h_) for j in range(W // 128) for h_ in range(2)] + xk + [("lnsq", m) for m in range(8)],
                      writes=[("x1t", p)], final=True)

        pend = None
        for tw in range(T // W):
            p = tw % 2
            ts_ = slice(tw * W, (tw + 1) * W)
            P.dma("sp", x1t[p][:], s.xb[:, ts_].rearrange("(m p) t -> p m t", p=128), writes=[("x1t", p)])
            for m in range(8):
                P.ts("dve" if m % 2 else "pool", h2T[p][:, m, :], x1t[p][:, m, :], modP[:, si, 32 + m:33 + m], modT[:, si, 24 + m:25 + m], ALU.mult, ALU.add,
                     reads=[("x1t", p), ("modT", l, si), ("modP", l, si)], writes=[("h2T", p, m)])
            for f in range(NF):
                q = f % 2
                ba, bb = (0, 1) if q == 0 else (2, 3)
                for k in range(8):
                    P.mm(self.ps[ba][:, 0:W], w1[:, k, f * 128:(f + 1) * 128], h2T[p][:, k, :], k == 0, k == 7, reads=[("h2T", p, k)], writes=[self.psk(ba)])
                for k in range(8):
                    P.mm(self.ps[bb][:, 0:W], w3[:, k, f * 128:(f + 1) * 128], h2T[p][:, k, :], k == 0, k == 7, reads=[("h2T", p, k)], writes=[self.psk(bb)])
                P.act(sa[q][:], self.ps[ba][:, 0:W], AF.Silu, reads=[self.psk(ba)], writes=[("sa", q)])
                P.tt("dve", gT[:, f, :], self.ps[bb][:, 0:W], sa[q][:], ALU.mult, reads=[self.psk(bb), ("sa", q)], writes=[("gT", f)])
                if f == 2 and pend is not None:
                    finish_d(*pend)
                    pend = None
            for mo in range(8):
                bank = 4 + mo % 2
                for f in range(NF):
                    P.mm(self.ps[bank][:, 0:W], w2[:, f, mo * 128:(mo + 1) * 128], gT[:, f, :], f == 0, f == NF - 1,
                         reads=[("gT", f)], writes=[self.psk(bank)])
                P.act(x1t[p][:, mo, :], x1t[p][:, mo, :], AF.Copy, scale=ALPHA, reads=[("x1t", p)], writes=[("xs", p, mo)])
                P.stt(x1t[p][:, mo, :], self.ps[bank][:, 0:W], modT[:, si, 40 + mo:41 + mo], x1t[p][:, mo, :], ALU.mult, ALU.add,
                      reads=[self.psk(bank), ("xs", p, mo), ("modT", l, si)], writes=[("rT", p, mo)])
            pend = (p, ts_)
        finish_d(*pend)
        P.barrier()

    def build(self):
        st = self.stages
        def on(name):
            return st is None or name in st
        if on("p0"):
            self.stage_p0()
        if on("p1"):
            self.stage_p1()
        for l in range(DEPTH):
            if on(f"k{l}l"):
                self.stage_k(l, "l")
            if on(f"a{l}c"):
                self.stage_a(l, self.st["c"])
            if on(f"a{l}l"):
                self.stage_a(l, self.st["l"])
            if on(f"h{l}c") and l < DEPTH - 1:
                self.stage_hc(l, self.st["c"])
            if on(f"h{l}l"):
                self.stage_h(l, self.st["l"])
            if on(f"b{l}c") and l < DEPTH - 1:
                self.stage_b(l, self.st["c"], CTX)
            if on(f"b{l}l"):
                self.stage_b(l, self.st["l"], SEQ + CTX)
            if on(f"c{l}c") and l < DEPTH - 1:
                self.stage_c(l, self.st["c"])
            if on(f"d{l}c") and l < DEPTH - 1:
                self.stage_d(l, self.st["c"], False)
            if on(f"c{l}l"):
                self.stage_c(l, self.st["l"])
            if on(f"d{l}l"):
                self.stage_d(l, self.st["l"], l == DEPTH - 1)
        self.P.emit()
        return self.nc


def make_in_maps(inputs, n_cores=8):
    c = host_consts()
    shared = {k: np.ascontiguousarray(np.asarray(v, dtype=np.float32)) for k, v in inputs.items() if k not in ("x", "c", "ctx")}
    consts = {"k_" + k: v for k, v in c.items()}
    maps = []
    for b in range(n_cores):
        m = dict(shared)
        m.update(consts)
        m["x"] = np.ascontiguousarray(inputs["x"][b], dtype=np.float32)
        m["c"] = np.ascontiguousarray(inputs["c"][b], dtype=np.float32)
        m["ctx"] = np.ascontiguousarray(inputs["ctx"][b], dtype=np.float32)
        maps.append(m)
    return maps


def kernel(**inputs):
    bld = Builder()
    nc = bld.build()
    res = run_bass_kernel_spmd(nc, make_in_maps(inputs), core_ids=list(range(8)))
    return np.stack([np.asarray(r["out"], dtype=np.float32) for r in res.results], axis=0)
```

```python
import contextlib
import math
import numpy as np
import ml_dtypes
import concourse.bass as bass
import concourse.mybir as mybir
from concourse.bass_utils import run_bass_kernel_spmd

F32 = mybir.dt.float32
BF16 = mybir.dt.bfloat16
AF = mybir.ActivationFunctionType
ALU = mybir.AluOpType

D = 1024
SEQ = 8192
CTX = 256
DEPTH = 2
NH = 8
HD = 128
DFF = 2816
INW = 8704
C_Q, C_K, C_V, C_HY, C_POOL, C_GATE = 0, 1024, 1280, 1536, 4608, 5632
ALPHA = (2 * DEPTH) ** 0.25
EPS = 1e-6
NFFT = 16384
POOL_WINDOWS = (2, 4, 8, 16)


class _Op:
    __slots__ = ("eng", "fn", "deps", "dma", "marked", "cnt", "sem", "semval", "prev")


class Prog:
    COMPUTE = ("pe", "act", "dve", "pool")
    QUEUES = ("sp", "act", "pool")
    ALLENG = ("pe", "act", "dve", "pool", "sp")
    SB_BASE = 24576
    SB_LIMIT = 218 * 1024

    def __init__(self, ring=8, same_engine_sync=True):
        self.nc = bass.Bass("TRN2", target_bir_lowering=False)
        self.ops = []
        self.state = {}
        self.ring = ring
        self.same = same_engine_sync
        self.dma_count = {q: 0 for q in self.QUEUES}
        self.slot_last = {}
        self.slot_val = {}
        self.last_op = {e: None for e in self.ALLENG}
        self.sb_off = self.SB_BASE
        self.sb_mark = self.SB_BASE
        self.n_alloc = 0
        self.out_dmas = []
        self.psn = 0

    def sb(self, shape, dtype, name="t"):
        nbytes = int(np.prod(shape[1:])) * mybir.dt.size(dtype)
        nbytes_al = (nbytes + 63) // 64 * 64
        self.n_alloc += 1
        h = self.nc.alloc_sbuf_tensor_at(f"{name}_{self.n_alloc}", list(shape), dtype, offset=self.sb_off)
        self.sb_off += nbytes_al
        assert self.sb_off <= self.SB_LIMIT, f"SBUF overflow {self.sb_off} ({name})"
        return h

    def sb_persist_done(self):
        self.sb_mark = self.sb_off

    def sb_reset(self):
        self.sb_off = self.sb_mark

    def _collect(self, reads, writes):
        deps = set()
        for k in reads:
            st = self.state.get(k)
            if st is not None and st[0] is not None:
                deps.add(st[0])
        for k in writes:
            st = self.state.get(k)
            if st is not None:
                if st[0] is not None:
                    deps.add(st[0])
                deps.update(st[1].values())
                deps.update(st[2])
        return deps

    def add(self, eng, fn, reads=(), writes=(), dma=False, out=False):
        pr = [k for k in reads if isinstance(k, tuple) and k[0] == "ps"]
        if pr:
            reads = [k for k in reads if not (isinstance(k, tuple) and k[0] == "ps")]
            writes = list(writes) + pr
        i = len(self.ops)
        op = _Op()
        op.eng, op.fn, op.dma, op.marked, op.cnt, op.prev = eng, fn, dma, False, 0, None
        op.deps = self._collect(reads, writes)
        if dma:
            n = self.dma_count[eng]
            self.dma_count[eng] = n + 1
            slot = (eng, n % self.ring)
            op.prev = self.slot_last.get(slot)
            self.slot_last[slot] = i
            v = self.slot_val.get(slot, 0) + 16
            self.slot_val[slot] = v
            op.sem, op.semval = slot, v
            if out:
                self.out_dmas.append(i)
        self.ops.append(op)
        for k in reads:
            st = self.state.setdefault(k, [None, {}, []])
            if dma:
                st[2].append(i)
            else:
                st[1][eng] = i
        for k in writes:
            self.state[k] = [i, {}, []]
        self.last_op[eng] = i
        return i

    def barrier(self):
        snap = [v for v in self.last_op.values() if v is not None] + list(self.slot_last.values())
        for e in self.ALLENG:
            op = _Op()
            op.eng, op.fn, op.dma, op.marked, op.cnt, op.prev = e, None, False, False, 0, None
            op.deps = set(snap)
            self.ops.append(op)
        self.state = {}

    def dma(self, q, out, in_, reads=(), writes=(), final=False):
        return self.add(q, lambda e: e.dma_start(out=out, in_=in_), reads, writes, dma=True, out=final)

    def mm(self, out, lhsT, rhs, start, stop, reads=(), writes=()):
        return self.add("pe", lambda e: e.matmul(out, lhsT=lhsT, rhs=rhs, start=start, stop=stop), reads, writes)

    def tr(self, out, in_, ident, reads=(), writes=()):
        return self.add("pe", lambda e: e.transpose(out, in_, ident), reads, writes)

    def act(self, out, in_, func, reads=(), writes=(), bias=None, scale=None):
        kw = {}
        if bias is not None:
            kw["bias"] = bias
        if scale is not None:
            kw["scale"] = scale
        return self.add("act", lambda e: e.activation(out=out, in_=in_, func=func, **kw), reads, writes)

    def tt(self, eng, out, in0, in1, op, reads=(), writes=()):
        return self.add(eng, lambda e: e.tensor_tensor(out=out, in0=in0, in1=in1, op=op), reads, writes)

    def ts(self, eng, out, in0, s1, s2, op0, op1, reads=(), writes=()):
        if op1 is None:
            return self.add(eng, lambda e: e.tensor_scalar(out=out, in0=in0, scalar1=s1, scalar2=None, op0=op0), reads, writes)
        return self.add(eng, lambda e: e.tensor_scalar(out=out, in0=in0, scalar1=s1, scalar2=s2, op0=op0, op1=op1), reads, writes)

    def stt(self, out, in0, scalar, in1, op0, op1, reads=(), writes=()):
        return self.add("dve", lambda e: e.scalar_tensor_tensor(out=out, in0=in0, scalar=scalar, in1=in1, op0=op0, op1=op1), reads, writes)

    def cp(self, eng, out, in_, reads=(), writes=()):
        if eng == "act":
            return self.add("act", lambda e: e.copy(out=out, in_=in_), reads, writes)
        return self.add(eng, lambda e: e.tensor_copy(out=out, in_=in_), reads, writes)

    def memset(self, eng, ap, val, writes=()):
        return self.add(eng, lambda e: e.memset(ap, val), (), writes)

    def emit(self):
        nc = self.nc
        ops = self.ops
        fin = _Op()
        fin.eng, fin.fn, fin.dma, fin.marked, fin.cnt, fin.prev = "sp", None, False, False, 0, None
        fin.deps = set(self.out_dmas) | set(self.slot_last.values())
        ops.append(fin)
        for op in ops:
            for d in op.deps:
                dop = ops[d]
                if dop.dma or dop.fn is None:
                    continue
                dop.marked = True
        cnt = {e: 0 for e in self.COMPUTE}
        for op in ops:
            if op.fn is not None and not op.dma:
                if op.marked:
                    cnt[op.eng] += 1
                op.cnt = cnt[op.eng]
        per_eng = {e: [] for e in self.ALLENG}
        for op in ops:
            per_eng[op.eng].append(op)
        same = self.same

        with contextlib.ExitStack() as es:
            csem = {e: es.enter_context(nc.semaphore(f"c_{e}")) for e in self.COMPUTE}
            dsem = {}
            for q in self.QUEUES:
                for r in range(self.ring):
                    dsem[(q, r)] = es.enter_context(nc.semaphore(f"d_{q}_{r}"))
            block = es.enter_context(nc.Block())

            def run(ename, e):
                waited = {}
                for op in per_eng[ename]:
                    waits = {}
                    for d in op.deps:
                        dop = ops[d]
                        if dop.dma:
                            s = ("d", dop.sem)
                            waits[s] = max(waits.get(s, 0), dop.semval)
                        else:
                            if dop.fn is None:
                                continue
                            if dop.eng == ename:
                                if op.fn is None:
                                    continue
                                if not op.dma and (ename == "pe" or not same):
                                    continue
                            s = ("c", dop.eng)
                            waits[s] = max(waits.get(s, 0), dop.cnt)
                    if op.dma and op.prev is not None:
                        p = ops[op.prev]
                        s = ("d", p.sem)
                        waits[s] = max(waits.get(s, 0), p.semval)
                    for s, v in waits.items():
                        if v <= 0 or waited.get(s, 0) >= v:
                            continue
                        e.wait_ge(dsem[s[1]] if s[0] == "d" else csem[s[1]], v)
                        waited[s] = v
                    if op.fn is None:
                        continue
                    inst = op.fn(e)
                    if op.dma:
                        inst.then_inc(dsem[op.sem], 16)
                    elif op.marked:
                        inst.then_inc(csem[ename], 1)

            block.tensor(lambda e: run("pe", e))
            block.scalar(lambda e: run("act", e))
            block.vector(lambda e: run("dve", e))
            block.gpsimd(lambda e: run("pool", e))
            block.sync(lambda e: run("sp", e))
        return nc


def _bf(a):
    return np.asarray(a, dtype=np.float32).astype(ml_dtypes.bfloat16)


def _hy_tables(L):
    f32 = np.float32
    t_idx = np.arange(L, dtype=f32)
    t01 = (t_idx / f32(max(L - 1, 1))).astype(f32)
    bands = np.linspace(1e-4, 15.0, 16, dtype=f32)
    ang = (f32(2.0 * math.pi / L) * t_idx[:, None] * bands[None, :]).astype(f32)
    feats = np.concatenate([t01[:, None], np.cos(ang), -np.sin(ang)], axis=-1).astype(f32)
    fF = np.zeros((33, 8192), f32)
    fR = np.zeros((33, 8192), f32)
    negt = np.zeros((128, 128), f32)
    mask = np.zeros((128, 128), f32)
    fF[:, :L] = feats.T
    n = np.arange(8192)
    tF = np.where(n < L, n, 0)
    negt[:64] = -np.where(n < L, t01[tF], 0).reshape(64, 128)
    mask[:64] = (n < L).astype(f32).reshape(64, 128)
    m = 8192 - n
    valid = (m >= 1) & (m <= L - 1)
    mm = np.where(valid, m, 0)
    fR[:, valid] = feats[mm[valid]].T
    negt[64:] = -np.where(valid, t01[mm], 0).reshape(64, 128)
    mask[64:] = valid.astype(f32).reshape(64, 128)
    return fF, fR, negt, mask


def _rope_tables(T, grid_w=64, ctx=False):
    f32 = np.float32
    if ctx:
        return np.ones((128, T), f32), np.zeros((128, T), f32)
    t = np.arange(T)
    rows = (t // grid_w).astype(f32)
    cols = (t % grid_w).astype(f32)
    inv = np.power(f32(10000.0), -np.arange(32, dtype=f32) / f32(32)).astype(f32)
    C = np.zeros((128, T), f32)
    S = np.zeros((128, T), f32)
    for j in range(128):
        pos = rows if j < 64 else cols
        jj = j % 64
        ang = (pos * inv[jj % 32]).astype(f32)
        C[j] = np.cos(ang)
        S[j] = -np.sin(ang) if jj < 32 else np.sin(ang)
    return C, S


_CONST_CACHE = {}


def host_consts():
    if _CONST_CACHE:
        return _CONST_CACHE
    c = {}
    c["ident"] = np.eye(128, dtype=np.float32)
    c["identb"] = _bf(np.eye(128))
    c["ropeC_l"], c["ropeS_l"] = _rope_tables(SEQ)
    c["ropeC_c"], c["ropeS_c"] = _rope_tables(CTX, ctx=True)
    a = np.arange(128, dtype=np.float64)
    th = 2 * np.pi * np.outer(a, a) / 128.0
    c["F1"] = _bf(np.stack([np.cos(th), -np.sin(th)], axis=1))
    c["I2"] = _bf(np.stack([np.cos(th)[:, :64], -np.sin(th)[:, :64]], axis=1) / NFFT)
    k1 = a[:, None, None]
    n2 = a[None, :, None]
    k2 = a[None, None, :]
    th3 = 2 * np.pi * n2 * (k1 + 128.0 * k2) / NFFT
    G = np.stack([np.cos(th3), -np.sin(th3), np.sin(th3)], axis=2)
    c["GT"] = _bf(G)
    c["HT"] = _bf(np.transpose(G, (0, 3, 2, 1)))
    for tag, L in (("l", SEQ), ("c", CTX)):
        fF, fR, negt, mask = _hy_tables(L)
        c[f"featF_{tag}"], c[f"featR_{tag}"], c[f"negt_{tag}"], c[f"mask_{tag}"] = fF, fR, negt, mask
    lo, hi = math.log(1e-2) / 1.5, math.log(1e-2) / 0.3
    c["deltas"] = np.abs(np.linspace(lo, hi, 1024, dtype=np.float32)).reshape(1, 1024).astype(np.float32)
    edge = np.zeros((4, 2, 8), np.float32)
    Tt = 4096
    for g, w in enumerate(POOL_WINDOWS):
        before, after = w // 2, w - 1 - w // 2
        for side in range(2):
            for i in range(8):
                t = i if side == 0 else Tt - 8 + i
                cnt = min(t + after + 1, Tt) - max(t - before, 0)
                edge[g, side, i] = 1.0 / cnt
    c["pedge"] = np.broadcast_to(edge.reshape(1, 64), (128, 64)).copy()
    t01c = (np.arange(CTX, dtype=np.float32) / np.float32(CTX - 1)).astype(np.float32)
    c["t01row_c"] = np.broadcast_to(t01c.reshape(1, CTX), (128, CTX)).copy()
    c["ndeltasT"] = np.ascontiguousarray(-c["deltas"].reshape(8, 128).T)
    _CONST_CACHE.update(c)
    return c


CONST_SPECS = None


def const_specs():
    c = host_consts()
    return {k: (list(v.shape), BF16 if v.dtype == ml_dtypes.bfloat16 else F32) for k, v in c.items()}


INPUT_SHAPES = {
    "x": [SEQ, D], "c": [D], "ctx": [CTX, D], "c_ctx": [D],
    "w_ada": [DEPTH, D, 6 * D], "b_ada": [DEPTH, 6 * D], "w_in": [DEPTH, D, INW],
    "q_norm_g": [DEPTH, HD], "k_norm_g": [DEPTH, HD],
    "hy_conv_w": [DEPTH, 3, 3 * D], "hy_conv_b": [DEPTH, 3 * D],
    "hf_w1": [DEPTH, 33, 64], "hf_b1": [DEPTH, 64], "hf_freq": [DEPTH, 64],
    "hf_w2": [DEPTH, 64, 64], "hf_b2": [DEPTH, 64], "hf_w3": [DEPTH, 64, 2 * D],
    "hy_d": [DEPTH, D], "pool_w": [DEPTH, 4, 256, 256], "pool_scale": [DEPTH, D],
    "w_branch": [DEPTH, 3, D, D], "w_out": [DEPTH, D, D],
    "ln1_g": [DEPTH, D], "ln1_b": [DEPTH, D], "ln2_g": [DEPTH, D], "ln2_b": [DEPTH, D],
    "ffn_w1": [DEPTH, D, DFF], "ffn_w3": [DEPTH, D, DFF], "ffn_w2": [DEPTH, DFF, D],
}


class Stream:
    pass


class Builder:
    def __init__(self, debug_outs=(), stages=None):
        self.P = Prog()
        self.nc = self.P.nc
        self.debug_outs = set(debug_outs)
        self.stages = stages
        nc = self.nc
        self.inp = {k: nc.dram_tensor(k, shp, F32, kind="ExternalInput").ap() for k, shp in INPUT_SHAPES.items()}
        self.cst = {k: nc.dram_tensor("k_" + k, shp, dt, kind="ExternalInput").ap() for k, (shp, dt) in const_specs().items()}
        self.out = nc.dram_tensor("out", [SEQ, D], F32, kind="ExternalOutput").ap()
        self.ps = [nc.alloc_psum_tensor(f"psb{i}", [128, 512], F32) for i in range(8)]
        self.scr = {}
        self._persistent()
        self._streams()

    def dram(self, name, shape, dtype):
        kind = "ExternalOutput" if name in self.debug_outs else "Internal"
        t = self.nc.dram_tensor(name, list(shape), dtype, kind=kind).ap()
        self.scr[name] = t
        return t

    def psk(self, i):
        return ("ps", i)

    def _persistent(self):
        P = self.P
        self.ident = P.sb([128, 128], F32, "ident")
        self.identb = P.sb([128, 128], BF16, "identb")
        self.ones_f = P.sb([128, 128], F32, "ones_f")
        self.ones_b = P.sb([128, 128], BF16, "ones_b")
        self.epsT = P.sb([128, 1], F32, "epsT")
        self.vecs = [P.sb([128, 200], F32, f"vecs{l}") for l in range(DEPTH)]
        self.modT = [P.sb([128, 2, 48], F32, f"modT{l}") for l in range(DEPTH)]
        self.modP = [P.sb([128, 2, 48], F32, f"modP{l}") for l in range(DEPTH)]
        self.hfv = [P.sb([64, 8], F32, f"hfv{l}") for l in range(DEPTH)]
        self.pedge = P.sb([128, 64], F32, "pedge")
        P.sb_persist_done()

    V_QG, V_QGP, V_KG, V_KGP = 0, 1, 2, 3
    V_CW = 4
    V_CB = 76
    V_PS = 100
    V_LN = 108
    V_BA = 140
    V_HD = 188
    V_N = 196

    def _streams(self):
        self.XA = self.dram("XA", [D, SEQ], F32)
        self.XB = self.dram("XB", [D, SEQ], F32)
        self.XCA = self.dram("XCA", [D, CTX], F32)
        self.XCB = self.dram("XCB", [D, CTX], F32)
        self.kT_d = self.dram("kT_d", [256, SEQ + CTX], BF16)
        self.v_d = self.dram("v_d", [SEQ + CTX, 256], BF16)
        self.Bd = [self.dram(f"Bd{i}", [128, 128, 1024], BF16) for i in range(2)]
        self.Dd = [self.dram(f"Dd{i}", [128, 128, 1024], BF16) for i in range(2)]
        self.Kf = {t: [self.dram(f"Kf_{t}{i}", [128, 128, 1024], BF16) for i in range(2)] for t in ("l", "c")}
        self.st = {}
        for tag, T in (("l", SEQ), ("c", CTX)):
            s = Stream()
            s.tag, s.T = tag, T
            s.TS = 4096 if tag == "l" else 256
            s.W = 512 if tag == "l" else 256
            s.sidx = 0 if tag == "l" else 1
            s.xa = self.XA if tag == "l" else self.XCA
            s.xb = self.XB if tag == "l" else self.XCB
            s.ktok0 = CTX if tag == "l" else 0
            s.qT = self.dram(f"qT_{tag}", [D, T], BF16)
            s.z = self.dram(f"z_{tag}", [T, D], BF16)
            s.x0T = self.dram(f"x0T_{tag}", [D, T], F32)
            s.poolT = self.dram(f"poolT_{tag}", [D, T], BF16)
            s.gT = self.dram(f"gT_{tag}", [3, D, T], BF16)
            s.attnT = self.dram(f"attnT_{tag}", [D, T], BF16)
            s.y = self.dram(f"y_{tag}", [T, D], F32)
            s.zT = self.dram(f"zT_{tag}", [D, T], F32) if tag == "c" else None
            s.hyT = self.dram(f"hyT_{tag}", [D, T], BF16) if tag == "c" else None
            s.ropeC = self.cst[f"ropeC_{tag}"]
            s.ropeS = self.cst[f"ropeS_{tag}"]
            self.st[tag] = s

    def stage_p0(self):
        P = self.P
        P.sb_reset()
        P.dma("sp", self.ident[:], self.cst["ident"], writes=["ident"])
        P.dma("sp", self.identb[:], self.cst["identb"], writes=["identb"])
        P.memset("dve", self.ones_f[:], 1.0, writes=["ones_f"])
        P.memset("dve", self.ones_b[:], 1.0, writes=["ones_b"])
        P.memset("dve", self.epsT[:], EPS, writes=["epsT"])
        xt = [P.sb([128, 4, D], F32, "p0x") for _ in range(2)]
        xo = [P.sb([128, 8, 512], F32, "p0o") for _ in range(2)]
        it = 0
        for src, dst, T in ((self.inp["x"], self.XA, SEQ), (self.inp["ctx"], self.XCA, CTX)):
            W = min(512, T)
            nj = W // 128
            for w in range(T // W):
                b = it % 2
                it += 1
                P.dma("sp", xt[b][:, 0:nj, :], src[w * W:(w + 1) * W, :].rearrange("(j p) d -> p j d", p=128),
                      writes=[("p0x", b)])
                for m in range(8):
                    bank = m % 4
                    for j in range(nj):
                        P.tr(self.ps[bank][:, j * 128:(j + 1) * 128], xt[b][:, j, m * 128:(m + 1) * 128], self.ident[:],
                             reads=[("p0x", b), "ident"], writes=[self.psk(bank)])
                    P.cp("act" if m % 2 else "dve", xo[b][:, m, 0:W], self.ps[bank][:, 0:W],
                         reads=[self.psk(bank)], writes=[("p0o", b, m)])
                P.dma("act", dst[:, w * W:(w + 1) * W].rearrange("(m p) t -> p m t", p=128), xo[b][:, :, 0:W],
                      reads=[("p0o", b, m) for m in range(8)])
        P.barrier()

    def stage_p1(self):
        P = self.P
        P.sb_reset()
        I = self.inp
        stg = P.sb([128, 2, 128], F32, "stg")
        stgc = P.sb([16, 128], F32, "stgc")
        stg64 = P.sb([8, 64], F32, "stg64")
        scT = P.sb([128, 8, 2], F32, "scT")
        cT = P.sb([128, 16], F32, "cT")
        wa = [P.sb([128, 8, 512], F32, "wa") for _ in range(2)]
        P.dma("sp", stgc[0:8, :], I["c"].rearrange("(m p) -> m p", p=128), writes=["stgc"])
        P.dma("sp", stgc[8:16, :], I["c_ctx"].rearrange("(m p) -> m p", p=128), writes=["stgc2"])
        P.tr(self.ps[0][:, 0:16], stgc[0:16, :], self.ident[0:16, 0:16], reads=["stgc", "stgc2", "ident"], writes=[self.psk(0)])
        P.act(cT[:], self.ps[0][:, 0:16], AF.Silu, reads=[self.psk(0)], writes=["cT"])
        for s in range(2):
            P.cp("dve", scT[:, :, s], cT[:, s * 8:(s + 1) * 8], reads=["cT"], writes=[("scT", s)])
        for l in range(DEPTH):
            rows = []
            g = I["q_norm_g"][l]
            kg = I["k_norm_g"][l]
            rows.append(("full", g))
            rows.append(("perm", g))
            rows.append(("full", kg))
            rows.append(("perm", kg))
            for j in range(3):
                for m in range(24):
                    rows.append(("full", I["hy_conv_w"][l, j, m * 128:(m + 1) * 128]))
            for m in range(24):
                rows.append(("full", I["hy_conv_b"][l, m * 128:(m + 1) * 128]))
            for m in range(8):
                rows.append(("full", I["pool_scale"][l, m * 128:(m + 1) * 128]))
            for nm in ("ln1_g", "ln1_b", "ln2_g", "ln2_b"):
                for m in range(8):
                    rows.append(("full", I[nm][l, m * 128:(m + 1) * 128]))
            for m in range(48):
                rows.append(("full", I["b_ada"][l, m * 128:(m + 1) * 128]))
            for m in range(8):
                rows.append(("full", I["hy_d"][l, m * 128:(m + 1) * 128]))
            assert len(rows) == self.V_N
            def ld(r0, ap2d, n):
                grp, rr = divmod(r0, 128)
                assert rr + n <= 128
                P.dma("sp", stg[rr:rr + n, grp, :], ap2d, writes=[("stg", r0)])
                return ("stg", r0)
            keys = []
            for ri, (kind, ap) in enumerate(rows[:4]):
                grp, rr = divmod(ri, 128)
                if kind == "full":
                    P.dma("sp", stg[rr:rr + 1, grp, :], ap.rearrange("(o n) -> o n", o=1), writes=[("stg", ri)])
                else:
                    for q4, src0 in enumerate((32, 0, 96, 64)):
                        P.dma("sp", stg[rr:rr + 1, grp, q4 * 32:(q4 + 1) * 32],
                              ap[src0:src0 + 32].rearrange("(o n) -> o n", o=1), writes=[("stg", ri, q4)])
                        keys.append(("stg", ri, q4))
                keys.append(("stg", ri))
            keys.append(ld(4, I["hy_conv_w"][l].rearrange("j (m p) -> (j m) p", p=128), 72))
            keys.append(ld(76, I["hy_conv_b"][l].rearrange("(m p) -> m p", p=128), 24))
            keys.append(ld(100, I["pool_scale"][l].rearrange("(m p) -> m p", p=128), 8))
            for qi, nm in enumerate(("ln1_g", "ln1_b")):
                keys.append(ld(108 + qi * 8, I[nm][l].rearrange("(m p) -> m p", p=128), 8))
            keys.append(ld(124, I["ln2_g"][l, 0:512].rearrange("(m p) -> m p", p=128), 4))
            keys.append(ld(128, I["ln2_g"][l, 512:1024].rearrange("(m p) -> m p", p=128), 4))
            keys.append(ld(132, I["ln2_b"][l].rearrange("(m p) -> m p", p=128), 8))
            keys.append(ld(140, I["b_ada"][l].rearrange("(m p) -> m p", p=128), 48))
            keys.append(ld(188, I["hy_d"][l].rearrange("(m p) -> m p", p=128), 8))
            P.tr(self.ps[1][:, 0:128], stg[:, 0, :], self.ident[:], reads=keys + ["ident"], writes=[self.psk(1)])
            P.tr(self.ps[1][:, 128:128 + 68], stg[0:68, 1, :], self.ident[0:68, 0:68], reads=keys + ["ident"], writes=[self.psk(1)])
            P.cp("dve", self.vecs[l][:, 0:196], self.ps[1][:, 0:196], reads=[self.psk(1)], writes=[("vecs", l)])
            for ci, nm in enumerate(("hf_b1", "hf_freq", "hf_b2")):
                P.dma("sp", stg64[ci:ci + 1, :], I[nm][l].rearrange("(o n) -> o n", o=1), writes=[("stg64", ci)])
            P.tr(self.ps[2][0:64, 0:3], stg64[0:3, :], self.ident[0:3, 0:3],
                 reads=[("stg64", ci) for ci in range(3)] + ["ident"], writes=[self.psk(2)])
            P.cp("dve", self.hfv[l][:, 0:3], self.ps[2][0:64, 0:3], reads=[self.psk(2)], writes=[("hfv", l)])
            P.tt("dve", self.hfv[l][:, 3:4], self.hfv[l][:, 0:1], self.hfv[l][:, 1:2], ALU.mult, reads=[("hfv", l)], writes=[("hfv3", l)])
            P.tt("dve", self.hfv[l][:, 4:5], self.hfv[l][:, 2:3], self.hfv[l][:, 1:2], ALU.mult, reads=[("hfv", l)], writes=[("hfv4", l)])
            bank = 3
            for cg in range(12):
                b = cg % 2
                P.dma("sp" if cg % 2 else "act", wa[b][:], I["w_ada"][l][:, cg * 512:(cg + 1) * 512].rearrange("(k p) n -> p k n", p=128),
                      writes=[("wa", b)])
                for mm in range(4):
                    m = cg * 4 + mm
                    for k in range(8):
                        P.mm(self.ps[bank][:, m * 2:m * 2 + 2], wa[b][:, k, mm * 128:(mm + 1) * 128], scT[:, k, :],
                             start=(k == 0), stop=(k == 7), reads=[("wa", b), ("scT", 0), ("scT", 1)], writes=[self.psk(bank)])
            psv = self.ps[bank][:, 0:96].rearrange("p (m s) -> p m s", s=2)
            for s in range(2):
                P.tt("dve", self.modT[l][:, s, :], psv[:, :, s], self.vecs[l][:, self.V_BA:self.V_BA + 48], ALU.add,
                     reads=[self.psk(bank), ("vecs", l)], writes=[("modT", l, s)])
                P.ts("dve", self.modP[l][:, s, :], self.modT[l][:, s, :], 1.0, None, ALU.add, None,
                     reads=[("modT", l, s)], writes=[("modP", l, s)])
            if "dbg_mod" in self.debug_outs:
                if l == 0:
                    self.dbg_mod = self.dram("dbg_mod", [DEPTH, 128, 96], F32)
                    self.dbg_vec = self.dram("dbg_vec", [DEPTH, 128, 192], F32)
                P.dma("sp", self.dbg_mod[l], self.modT[l][:].rearrange("p s m -> p (s m)"), reads=[("modT", l, 0), ("modT", l, 1)])
                P.dma("sp", self.dbg_vec[l], self.vecs[l][:, 0:192], reads=[("vecs", l)])
        P.barrier()

    def _proj(self, ps_ap, wt, hT, w, W, keys_w, bank, col0=0, ncol=None):
        P = self.P
        for k in range(8):
            P.mm(ps_ap, wt[:, k, :], hT[:, k, col0 + w * W: col0 + w * W + (ncol or W)],
                 start=(k == 0), stop=(k == 7), reads=[keys_w, ("hT", k, w)], writes=[self.psk(bank)])

    def stage_a(self, l, s):
        P = self.P
        P.sb_reset()
        T, TS, W = s.T, s.TS, s.W
        NW = TS // W
        NB = TS // 128
        si = s.sidx
        vec, modT, modP = self.vecs[l], self.modT[l], self.modP[l]
        w_in = self.inp["w_in"][l]
        hT = P.sb([128, 8, TS + 16], BF16, "hT")
        wring = [P.sb([128, 8, 128], BF16, "wr") for _ in range(4)]
        wcount = [0]

        wstage = [P.sb([128, 8, 128], F32, "wst") for _ in range(3)]
        scount = [0]

        def load_w(col0, ncols=128, buf=None, key=None):
            if buf is None:
                i = wcount[0] % 4
                wcount[0] += 1
                buf, key = wring[i], ("wr", i)
            for c in range(0, ncols, 128):
                j = scount[0] % 3
                scount[0] += 1
                P.dma("sp", wstage[j][:], w_in[:, col0 + c:col0 + c + 128].rearrange("(k p) n -> p k n", p=128),
                      writes=[("wst", j)])
                P.cp("pool", buf[:, :, c:c + 128], wstage[j][:], reads=[("wst", j)], writes=[key if ncols == 128 else (key, c)])
            return buf, key

        mark = P.sb_off
        P.dma("sp", self.pedge[:], self.cst["pedge"], writes=["pedge"])
        for sti in range(T // TS):
            t0 = sti * TS
            P.sb_off = mark
            xs = [P.sb([128, 8, W], F32, "xs") for _ in range(2)]
            hal = P.sb([128, 8, 16], F32, "hal")
            for w in range(NW):
                b = w % 2
                P.dma("sp", xs[b][:], s.xa[:, t0 + w * W: t0 + (w + 1) * W].rearrange("(m p) t -> p m t", p=128), writes=[("xs", b)])
                for m in range(8):
                    eng = ("act", "dve", "pool")[m % 3]
                    o = hT[:, m, w * W:(w + 1) * W]
                    if eng == "act":
                        P.act(o, xs[b][:, m, :], AF.Identity, scale=modP[:, si, 8 + m:9 + m], bias=modT[:, si, m:m + 1],
                              reads=[("xs", b), ("modT", l, si), ("modP", l, si)], writes=[("hT", m, w)])
                    else:
                        P.ts(eng, o, xs[b][:, m, :], modP[:, si, 8 + m:9 + m], modT[:, si, m:m + 1], ALU.mult, ALU.add,
                             reads=[("xs", b), ("modT", l, si), ("modP", l, si)], writes=[("hT", m, w)])
            hk = []
            for side, (a, b_) in enumerate(((t0 - 8, t0), (t0 + TS, t0 + TS + 8))):
                if a >= 0 and b_ <= T:
                    P.dma("sp", hal[:, :, side * 8:(side + 1) * 8], s.xa[:, a:b_].rearrange("(m p) t -> p m t", p=128), writes=[("hal", side)])
                    for m in range(8):
                        P.ts("dve", hT[:, m, TS + side * 8: TS + side * 8 + 8], hal[:, m, side * 8:(side + 1) * 8],
                             modP[:, si, 8 + m:9 + m], modT[:, si, m:m + 1], ALU.mult, ALU.add,
                             reads=[("hal", side), ("modT", l, si), ("modP", l, si)], writes=[("hT", m, "h%d" % side)])
                else:
                    for m in range(8):
                        P.memset("dve", hT[:, m, TS + side * 8: TS + side * 8 + 8], 0.0, writes=[("hT", m, "h%d" % side)])
            P.barrier()
            if getattr(self, 'a_stop', None) == 'ph0':
                return

            def proj_halo(ps_ap, wt, wkey, bank):
                for k in range(8):
                    P.mm(ps_ap, wt[:, k, :], hT[:, k, TS:TS + 16], start=(k == 0), stop=(k == 7),
                         reads=[wkey, ("hT", k, "h0"), ("hT", k, "h1")], writes=[self.psk(bank)])

            P.sb_off = mark
            rC = P.sb([128, TS], F32, "rC")
            rS = P.sb([128, TS], F32, "rS")
            P.dma("sp", rC[:], s.ropeC[:, t0:t0 + TS], writes=["rC"])
            P.dma("sp", rS[:], s.ropeS[:, t0:t0 + TS], writes=["rS"])
            wp = [P.sb([128, 8, 128], BF16, "wp") for _ in range(2)]
            sqb = [P.sb([128, W], F32, "sqb") for _ in range(2)]
            rs = [P.sb([128, W], F32, "rs") for _ in range(2)]
            t1 = [P.sb([128, W], F32, "t1") for _ in range(2)]
            t2 = [P.sb([128, W], F32, "t2") for _ in range(2)]
            qrow = [P.sb([128, TS], BF16, "qrow") for _ in range(2)]
            it = 0
            for hc in range(10):
                wq, wk = load_w(hc * 128)
                pb = hc % 2
                for q4, src0 in enumerate((32, 0, 96, 64)):
                    P.cp("pool", wp[pb][:, :, q4 * 32:(q4 + 1) * 32], wq[:, :, src0:src0 + 32], reads=[wk], writes=[("wp", pb, q4)])
                wpk = [("wp", pb, q4) for q4 in range(4)]
                gcol = self.V_QG if hc < 8 else self.V_KG
                r = hc % 2
                for w in range(NW):
                    i = it % 2
                    it += 1
                    bq, bp, bs = (0, 1, 4) if i == 0 else (2, 3, 5)
                    QL = 9
                    if QL < 2:
                        continue
                    self._proj(self.ps[bq][:, 0:W], wq, hT, w, W, wk, bq)
                    for k in range(8):
                        P.mm(self.ps[bp][:, 0:W], wp[pb][:, k, :], hT[:, k, w * W:(w + 1) * W], start=(k == 0), stop=(k == 7),
                             reads=wpk + [("hT", k, w)], writes=[self.psk(bp)])
                    if QL < 3:
                        continue
                    P.act(sqb[i][:], self.ps[bq][:, 0:W], AF.Square, reads=[self.psk(bq)], writes=[("sqb", i)])
                    P.mm(self.ps[bs][:, 0:W], self.ones_f[:], sqb[i][:], start=True, stop=True,
                         reads=["ones_f", ("sqb", i)], writes=[self.psk(bs)])
                    P.act(rs[i][:], self.ps[bs][:, 0:W], AF.Ln, scale=1.0 / 128.0, bias=self.epsT[:, 0:1],
                          reads=[self.psk(bs), "epsT"], writes=[("rs", i)])
                    P.act(rs[i][:], rs[i][:], AF.Exp, scale=-0.5, reads=[("rs", i)], writes=[("rs", i)])
                    if QL < 4:
                        continue
                    P.stt(t1[i][:], self.ps[bq][:, 0:W], vec[:, gcol:gcol + 1], rC[:, w * W:(w + 1) * W], ALU.mult, ALU.mult,
                          reads=[self.psk(bq), ("vecs", l), "rC"], writes=[("t1", i)])
                    P.stt(t2[i][:], self.ps[bp][:, 0:W], vec[:, gcol + 1:gcol + 2], rS[:, w * W:(w + 1) * W], ALU.mult, ALU.mult,
                          reads=[self.psk(bp), ("vecs", l), "rS"], writes=[("t2", i)])
                    if QL < 5:
                        continue
                    P.tt("pool", t1[i][:], t1[i][:], t2[i][:], ALU.add, reads=[("t1", i), ("t2", i)], writes=[("t1", i)])
                    P.tt("pool", qrow[r][:, w * W:(w + 1) * W], t1[i][:], rs[i][:], ALU.mult,
                         reads=[("t1", i), ("rs", i)], writes=[("qrow", r, w)])
                if hc < 8:
                    dst = s.qT[hc * 128:(hc + 1) * 128, t0:t0 + TS]
                else:
                    dst = self.kT_d[(hc - 8) * 128:(hc - 7) * 128, s.ktok0 + t0: s.ktok0 + t0 + TS]
                if QL >= 6:
                    P.dma("pool", dst, qrow[r][:], reads=[("qrow", r, w) for w in range(NW)])
            P.barrier()
            if getattr(self, 'a_stop', None) == 'qk':
                return

            P.sb_off = mark
            wv = P.sb([128, 8, 256], BF16, "wv")
            vrow = P.sb([128, NB, 256], BF16, "vrow")
            load_w(C_V, 256, wv, "wv")
            wvk = [("wv", 0), ("wv", 128)]
            for tb in range(NB):
                bank = tb % 4
                w = (tb * 128) // W
                for k in range(8):
                    P.mm(self.ps[bank][:, 0:256], hT[:, k, tb * 128:(tb + 1) * 128], wv[:, k, :], start=(k == 0), stop=(k == 7),
                         reads=wvk + [("hT", k, w)], writes=[self.psk(bank)])
                P.cp("act" if tb % 2 else "dve", vrow[:, tb, :], self.ps[bank][:, 0:256], reads=[self.psk(bank)], writes=[("vrow", tb)])
            P.dma("act", self.v_d[s.ktok0 + t0: s.ktok0 + t0 + TS, :].rearrange("(b p) c -> p b c", p=128), vrow[:],
                  reads=[("vrow", tb) for tb in range(NB)])
            P.barrier()
            if getattr(self, 'a_stop', None) == 'v':
                return

            P.sb_off = mark
            ubuf = [P.sb([128, TS + 2], F32, "ubuf") for _ in range(2)]
            sA = P.sb([128, TS], F32, "sA")
            sB = P.sb([128, TS], F32, "sB")
            zrow = P.sb([128, TS], BF16, "zrow")
            ztile = P.sb([128, NB, 128], BF16, "ztile")
            uc = 0
            for j in range(8):
                for part, (cchunk, cm, dst, dk) in enumerate(((12 + j, j, sA, "sA"), (28 + j, 16 + j, sB, "sB"), (20 + j, 8 + j, sA, "sA"))):
                    ub = uc % 2
                    uc += 1
                    wt, wk = load_w(cchunk * 128)
                    ukeys = []
                    for w in range(NW):
                        bank = w % 4
                        self._proj(self.ps[bank][:, 0:W], wt, hT, w, W, wk, bank)
                        P.cp("act" if w % 2 else "dve", ubuf[ub][:, 1 + w * W: 1 + (w + 1) * W], self.ps[bank][:, 0:W],
                             reads=[self.psk(bank)], writes=[("ubuf", ub, w)])
                        ukeys.append(("ubuf", ub, w))
                    proj_halo(self.ps[4][:, 0:16], wt, wk, 4)
                    P.cp("dve", ubuf[ub][:, 0:TS + 2:TS + 1], self.ps[4][:, 7:9], reads=[self.psk(4)], writes=[("ubuf", ub, "h")])
                    ukeys.append(("ubuf", ub, "h"))
                    c0 = self.V_CW + cm
                    P.act(dst[:], ubuf[ub][:, 1:TS + 1], AF.Identity, scale=vec[:, c0 + 24:c0 + 25], bias=vec[:, self.V_CB + cm:self.V_CB + cm + 1],
                          reads=ukeys + [("vecs", l)], writes=[dk])
                    P.stt(dst[:], ubuf[ub][:, 0:TS], vec[:, c0:c0 + 1], dst[:], ALU.mult, ALU.add, reads=ukeys + [dk, ("vecs", l)], writes=[dk])
                    P.stt(dst[:], ubuf[ub][:, 2:TS + 2], vec[:, c0 + 48:c0 + 49], dst[:], ALU.mult, ALU.add, reads=ukeys + [dk, ("vecs", l)], writes=[dk])
                    if part == 1 and s.tag == "c":
                        P.tt("pool", sB[:], sA[:], sB[:], ALU.mult, reads=["sA", "sB"], writes=["sB"])
                        P.dma("pool", s.zT[j * 128:(j + 1) * 128, t0:t0 + TS], sB[:], reads=["sB"])
                    elif part == 1:
                        P.tt("pool", zrow[:], sA[:], sB[:], ALU.mult, reads=["sA", "sB"], writes=["zrow"])
                        for blk in range(NB):
                            bank = 6 + (blk // 8) % 2
                            pv = self.ps[bank][:].bitcast(BF16)
                            P.tr(pv[:, (blk % 8) * 128:(blk % 8 + 1) * 128], zrow[:, blk * 128:(blk + 1) * 128], self.identb[:],
                                 reads=["zrow", "identb"], writes=[self.psk(bank)])
                            if blk % 8 == 7 or blk == NB - 1:
                                b0 = blk - blk % 8
                                n = blk - b0 + 1
                                P.cp("act" if (blk // 8) % 2 else "dve", ztile[:, b0:b0 + n, :].rearrange("p b c -> p (b c)"), pv[:, 0:n * 128],
                                     reads=[self.psk(bank)], writes=[("ztile", b0)])
                        P.dma("act", s.z[t0:t0 + TS, j * 128:(j + 1) * 128].rearrange("(b p) c -> p b c", p=128), ztile[:],
                              reads=[("ztile", b0) for b0 in range(0, NB, 8)])
                    if part == 2:
                        P.dma("pool", s.x0T[j * 128:(j + 1) * 128, t0:t0 + TS], sA[:], reads=["sA"])
            P.barrier()
            if getattr(self, 'a_stop', None) == 'hy':
                return

            P.sb_off = mark
            n = TS + 16
            pbuf = [P.sb([128, n], F32, "pbuf") for _ in range(2)]
            A = P.sb([128, n], F32, "pA")
            Bb = P.sb([128, n], F32, "pB")
            mT = [P.sb([128, TS], BF16, "mT") for _ in range(2)]
            prow = [P.sb([128, TS], BF16, "prow") for _ in range(2)]
            pw = P.sb([128, 2, 256], BF16, "pw")
            pwf = P.sb([128, 2, 256], F32, "pwf")
            tmp8 = P.sb([128, 8], F32, "tmp8")
            pe4 = self.pedge[:].rearrange("p (g s e) -> p g s e", g=4, s=2)
            for g in range(4):
                wsz = POOL_WINDOWS[g]
                kk = g + 1
                o = 8 + wsz // 2 - 1
                P.dma("sp", pwf[:], self.inp["pool_w"][l, g].rearrange("(i p) o -> p i o", p=128), writes=["pwf"])
                P.cp("pool", pw[:], pwf[:], reads=["pwf"], writes=["pw"])
                for i in range(2):
                    wt, wk = load_w((36 + 2 * g + i) * 128)
                    pk = []
                    for w in range(NW):
                        bank = w % 4
                        self._proj(self.ps[bank][:, 0:W], wt, hT, w, W, wk, bank)
                        P.cp("act" if w % 2 else "dve", pbuf[i][:, 8 + w * W: 8 + (w + 1) * W], self.ps[bank][:, 0:W],
                             reads=[self.psk(bank)], writes=[("pbuf", i, w)])
                        pk.append(("pbuf", i, w))
                    proj_halo(self.ps[4][:, 0:16], wt, wk, 4)
                    P.cp("dve", pbuf[i][:, 0:8], self.ps[4][:, 0:8], reads=[self.psk(4)], writes=[("pbuf", i, "h0")])
                    P.cp("dve", pbuf[i][:, TS + 8:TS + 16], self.ps[4][:, 8:16], reads=[self.psk(4)], writes=[("pbuf", i, "h1")])
                    pk += [("pbuf", i, "h0"), ("pbuf", i, "h1")]
                    u = pbuf[i]
                    P.tt("pool", A[:, 1:n], u[:, 1:n], u[:, 0:n - 1], ALU.add, reads=pk, writes=["pA"])
                    R, rk = A, "pA"
                    if kk >= 2:
                        P.tt("pool", Bb[:, 3:n], A[:, 3:n], A[:, 1:n - 2], ALU.add, reads=["pA"], writes=["pB"])
                        R, rk = Bb, "pB"
                    if kk >= 3:
                        P.tt("pool", A[:, 7:n], Bb[:, 7:n], Bb[:, 3:n - 4], ALU.add, reads=["pB"], writes=["pA"])
                        R, rk = A, "pA"
                    if kk >= 4:
                        P.tt("pool", Bb[:, 15:n], A[:, 15:n], A[:, 7:n - 8], ALU.add, reads=["pA"], writes=["pB"])
                        R, rk = Bb, "pB"
                    P.stt(mT[i][:], R[:, o:o + TS], 1.0 / wsz, u[:, 8:8 + TS], ALU.mult, ALU.subtract, reads=[rk] + pk, writes=[("mT", i)])
                    if t0 == 0:
                        P.tt("dve", tmp8[:], R[:, o:o + 8], pe4[:, g, 0, :], ALU.mult, reads=[rk, "pedge"], writes=["tmp8"])
                        P.tt("dve", mT[i][:, 0:8], tmp8[:], u[:, 8:16], ALU.subtract, reads=["tmp8"] + pk, writes=[("mT", i)])
                    if t0 + TS == T:
                        P.tt("dve", tmp8[:], R[:, o + TS - 8:o + TS], pe4[:, g, 1, :], ALU.mult, reads=[rk, "pedge"], writes=["tmp8"])
                        P.tt("dve", mT[i][:, TS - 8:TS], tmp8[:], u[:, TS:TS + 8], ALU.subtract, reads=["tmp8"] + pk, writes=[("mT", i)])
                for oc in range(2):
                    for w in range(NW):
                        bank = w % 4
                        for i in range(2):
                            P.mm(self.ps[bank][:, 0:W], pw[:, i, oc * 128:(oc + 1) * 128], mT[i][:, w * W:(w + 1) * W],
                                 start=(i == 0), stop=(i == 1), reads=["pw", ("mT", i)], writes=[self.psk(bank)])
                        cidx = self.V_PS + 2 * g + oc
                        P.act(prow[oc][:, w * W:(w + 1) * W], self.ps[bank][:, 0:W], AF.Identity, scale=vec[:, cidx:cidx + 1],
                              reads=[self.psk(bank), ("vecs", l)], writes=[("prow", oc, w)])
                    P.dma("act", s.poolT[(2 * g + oc) * 128:(2 * g + oc + 1) * 128, t0:t0 + TS], prow[oc][:],
                          reads=[("prow", oc, w) for w in range(NW)])
            P.barrier()
            if getattr(self, 'a_stop', None) == 'pool':
                return

            P.sb_off = mark
            grow = [P.sb([128, TS], BF16, "grow") for _ in range(2)]
            for gc in range(24):
                r = gc % 2
                wt, wk = load_w((44 + gc) * 128)
                for w in range(NW):
                    bank = w % 4
                    self._proj(self.ps[bank][:, 0:W], wt, hT, w, W, wk, bank)
                    P.act(grow[r][:, w * W:(w + 1) * W], self.ps[bank][:, 0:W], AF.Sigmoid, reads=[self.psk(bank)], writes=[("grow", r, w)])
                P.dma("act", s.gT[gc // 8, (gc % 8) * 128:(gc % 8 + 1) * 128, t0:t0 + TS], grow[r][:],
                      reads=[("grow", r, w) for w in range(NW)])
            P.barrier()
            if getattr(self, 'a_stop', None) == 'gate':
                return

    def stage_b(self, l, s, NK):
        P = self.P
        P.sb_reset()
        NQ, W = s.T, s.W
        NB = NK // 128
        KT = P.sb([128, 2, NK], BF16, "KT")
        V = P.sb([128, NB, 256], BF16, "V")
        for kv in range(2):
            P.dma("sp" if kv else "act", KT[:, kv, :], self.kT_d[kv * 128:(kv + 1) * 128, 0:NK], writes=[("KT", kv)])
        vsrc = self.v_d[0:NK, :].rearrange("(b p) c -> p b c", p=128)
        vk = []
        for b0 in range(0, NB, 11):
            b1 = min(NB, b0 + 11)
            P.dma("sp", V[:, b0:b1, :], vsrc[:, b0:b1, :], writes=[("V", b0)])
            vk.append(("V", b0))
        QT = [P.sb([128, 8, W], BF16, "QT") for _ in range(2)]
        attT = [P.sb([128, 8, W], BF16, "attT") for _ in range(2)]
        NPT = 8
        pT = [P.sb([128, W], BF16, "pT") for _ in range(NPT)]
        pool_tbs = [tb for tb in range(NB) if tb % 8 in (1, 4, 6)]
        dve_tbs = [tb for tb in range(NB) if tb % 8 not in (1, 4, 6)]
        rden = [P.sb([128, W], F32, "rden") for _ in range(2)]
        accD = [P.sb([128, W], F32, "accD") for _ in range(2)]
        accP = [P.sb([128, W], F32, "accP") for _ in range(2)]
        scale = float(HD) ** -0.5
        steps = [(qw, h, tb) for qw in range(NQ // W) for h in range(8) for tb in range(NB)]
        n = len(steps)

        def issue_S(i):
            qw, h, tb = steps[i]
            r = i % 4
            if h == 0 and tb == 0:
                P.dma("sp", QT[qw % 2][:], s.qT[:, qw * W:(qw + 1) * W].rearrange("(h p) t -> p h t", p=128), writes=[("QT", qw % 2)])
            P.mm(self.ps[r][:, 0:W], KT[:, h // 4, tb * 128:(tb + 1) * 128], QT[qw % 2][:, h, :], True, True,
                 reads=[("KT", h // 4), ("QT", qw % 2)], writes=[self.psk(r)])

        LA = 3
        for i in range(min(LA, n)):
            issue_S(i)
        for i, (qw, h, tb) in enumerate(steps):
            r = i % 4
            r4 = i % NPT
            kv = h // 4
            hp = h % 2
            ob = 4 + hp
            P.act(pT[r4][:], self.ps[r][:, 0:W], AF.Exp, scale=scale, reads=[self.psk(r)], writes=[("pT", r4)])
            P.mm(self.ps[ob][:, 0:W], V[:, tb, kv * 128:(kv + 1) * 128], pT[r4][:], tb == 0, tb == NB - 1,
                 reads=vk + [("pT", r4)], writes=[self.psk(ob)])
            ab = 6 + hp
            if tb in dve_tbs:
                if tb == dve_tbs[0] and tb == dve_tbs[-1]:
                    P.cp("dve", accD[hp][:], pT[r4][:], reads=[("pT", r4)], writes=[("accD", hp)])
                elif tb == dve_tbs[0]:
                    P.cp("dve", self.ps[ab][:, 0:W], pT[r4][:], reads=[("pT", r4)], writes=[self.psk(ab)])
                elif tb != dve_tbs[-1]:
                    P.tt("dve", self.ps[ab][:, 0:W], self.ps[ab][:, 0:W], pT[r4][:], ALU.add, reads=[("pT", r4), self.psk(ab)], writes=[self.psk(ab)])
                else:
                    P.tt("dve", accD[hp][:], self.ps[ab][:, 0:W], pT[r4][:], ALU.add, reads=[("pT", r4), self.psk(ab)], writes=[("accD", hp)])
            else:
                if tb == pool_tbs[0]:
                    P.cp("pool", accP[hp][:], pT[r4][:], reads=[("pT", r4)], writes=[("accP", hp)])
                else:
                    P.tt("pool", accP[hp][:], accP[hp][:], pT[r4][:], ALU.add, reads=[("pT", r4), ("accP", hp)], writes=[("accP", hp)])
            if i + LA < n:
                issue_S(i + LA)
            if tb == NB - 1:
                rd = rden[hp]
                db = ab
                P.mm(self.ps[db][:, 0:W], self.ones_f[:], accD[hp][:], True, False, reads=[("accD", hp)], writes=[self.psk(db)])
                P.mm(self.ps[db][:, 0:W], self.ones_f[:], accP[hp][:], False, True, reads=[("accP", hp)], writes=[self.psk(db)])
                P.add("dve", lambda e, rd=rd, db=db: e.reciprocal(out=rd[:], in_=self.ps[db][:, 0:W]), reads=[self.psk(db)], writes=[("rden", hp)])
                P.tt("dve", attT[qw % 2][:, h, :], self.ps[ob][:, 0:W], rd[:], ALU.mult,
                     reads=[self.psk(ob), ("rden", hp)], writes=[("attT", qw % 2, h)])
                if h == 7:
                    P.dma("pool", s.attnT[:, qw * W:(qw + 1) * W].rearrange("(h p) t -> p h t", p=128), attT[qw % 2][:],
                          reads=[("attT", qw % 2, hh) for hh in range(8)])
        P.barrier()

    def _sin_layer(self, ps_ap, fcol, bcol, hv, tmp, tmp2, out_ap, psbank, okey):
        P = self.P
        MAGIC = 12582912.0
        P.ts("dve", tmp, ps_ap, hv[:, fcol:fcol + 1], hv[:, bcol:bcol + 1], ALU.mult, ALU.add, reads=[self.psk(psbank)], writes=["sl_tmp"])
        P.ts("dve", tmp2, tmp, 1.0 / (2 * math.pi), MAGIC, ALU.mult, ALU.add, reads=["sl_tmp"], writes=["sl_tmp2"])
        P.ts("dve", tmp2, tmp2, MAGIC, -2 * math.pi, ALU.subtract, ALU.mult, reads=["sl_tmp2"], writes=["sl_tmp2"])
        P.tt("dve", tmp, tmp, tmp2, ALU.add, reads=["sl_tmp", "sl_tmp2"], writes=["sl_tmp"])
        P.act(out_ap, tmp, AF.Sin, reads=["sl_tmp"], writes=[okey])

    def stage_k(self, l, tag):
        P = self.P
        P.sb_reset()
        I = self.inp
        hv = self.hfv[l]
        Kf = self.Kf[tag]
        w1s = P.sb([33, 64], F32, "w1s")
        w2s = P.sb([64, 64], F32, "w2s")
        w3f = P.sb([64, 2048], F32, "w3f")
        w3b = P.sb([64, 2048], BF16, "w3b")
        h2T = [P.sb([64, 8192], BF16, "h2T") for _ in range(2)]
        dl = P.sb([128, 1024], F32, "dl")
        drow = P.sb([128, 1024], F32, "drow")
        nrow = P.sb([128, 1024], F32, "nrow")
        negt = P.sb([128, 128], F32, "negt")
        mask = P.sb([128, 128], F32, "mask")
        F1 = P.sb([128, 2, 128], BF16, "F1")
        P.dma("sp", w1s[:], I["hf_w1"][l], writes=["w1s"])
        P.dma("sp", w2s[:], I["hf_w2"][l], writes=["w2s"])
        P.dma("sp", w3f[:], I["hf_w3"][l], writes=["w3f"])
        P.cp("pool", w3b[:], w3f[:], reads=["w3f"], writes=["w3b"])
        P.dma("act", dl[:], self.cst["deltas"].partition_broadcast(128).rearrange("p o c -> p (o c)"), writes=["dl"])
        P.dma("act", drow[:], I["hy_d"][l].rearrange("(o c) -> o c", o=1).partition_broadcast(128).rearrange("p o c -> p (o c)"), writes=["drow"])
        P.dma("act", negt[:], self.cst[f"negt_{tag}"], writes=["negt"])
        P.dma("act", mask[:], self.cst[f"mask_{tag}"], writes=["mask"])
        P.dma("act", F1[:], self.cst["F1"], writes=["F1"])
        mark = P.sb_off
        ft = [P.sb([33, 512], F32, "ft") for _ in range(2)]
        tmp = P.sb([64, 512], F32, "sl_tmp")
        tmp2 = P.sb([64, 512], F32, "sl_tmp2")
        h1 = P.sb([64, 512], F32, "h1")
        it = 0
        for d, nm in enumerate((f"featF_{tag}", f"featR_{tag}")):
            for w in range(16):
                b = it % 2
                it += 1
                P.dma("sp", ft[b][:], self.cst[nm][:, w * 512:(w + 1) * 512], writes=[("ft", b)])
                P.mm(self.ps[0][0:64, 0:512], w1s[:], ft[b][:], True, True, reads=["w1s", ("ft", b)], writes=[self.psk(0)])
                self._sin_layer(self.ps[0][0:64, 0:512], 1, 3, hv, tmp[:], tmp2[:], h1[:], 0, "h1")
                P.mm(self.ps[1][0:64, 0:512], w2s[:], h1[:], True, True, reads=["w2s", "h1"], writes=[self.psk(1)])
                self._sin_layer(self.ps[1][0:64, 0:512], 1, 4, hv, tmp[:], tmp2[:], h2T[d][:, w * 512:(w + 1) * 512], 1, ("h2T", d))
        P.sb_off = mark
        kt = [P.sb([128, 8, 1024], BF16, "kt") for _ in range(2)]
        wn = [P.sb([128, 1024], F32, "wn") for _ in range(2)]
        sqb = [P.sb([128, 1024], BF16, "sqk") for _ in range(2)]
        Bt = [[P.sb([128, 8, 1024], BF16, "Btk") for _ in range(2)] for _ in range(2)]
        for jg in range(16):
            kb = jg % 2
            for n2i in range(8):
                n2 = jg * 8 + n2i
                i = n2 % 2
                ba = 0 if i == 0 else 2
                for ch in range(2):
                    P.mm(self.ps[ba + ch][0:64, 0:512], h2T[0][:, n2:8192:128], w3b[:, ch * 512:(ch + 1) * 512], True, True,
                         reads=[("h2T", 0), "w3b"], writes=[self.psk(ba + ch)])
                    P.mm(self.ps[ba + ch][64:128, 0:512], h2T[1][:, n2:8192:128], w3b[:, 1024 + ch * 512:1024 + (ch + 1) * 512], True, True,
                         reads=[("h2T", 1), "w3b"], writes=[self.psk(ba + ch)])
                P.act(wn[i][:], dl[:], AF.Exp, scale=negt[:, n2:n2 + 1], reads=["dl", "negt"], writes=[("wn", i)])
                P.ts("dve", wn[i][:], wn[i][:], 0.05, None, ALU.add, None, reads=[("wn", i)], writes=[("wn", i)])
                for ch in range(2):
                    P.stt(kt[kb][:, n2i, ch * 512:(ch + 1) * 512], self.ps[ba + ch][:, 0:512], mask[:, n2:n2 + 1], wn[i][:, ch * 512:(ch + 1) * 512],
                          ALU.mult, ALU.mult, reads=[self.psk(ba + ch), "mask", ("wn", i)], writes=[("kt", kb, n2i)])
                P.act(sqb[i][:], kt[kb][:, n2i, :], AF.Square, reads=[("kt", kb, n2i)], writes=[("sqk", i)])
                for ch in range(2):
                    P.mm(self.ps[6 + ch][:, 0:512], self.ones_b[:], sqb[i][:, ch * 512:(ch + 1) * 512], n2 == 0, n2 == 127,
                         reads=[("sqk", i)], writes=[self.psk(6 + ch)])
            ktf = kt[kb][:].rearrange("p a c -> p (a c)")
            for ri in range(2):
                btf = Bt[ri][kb][:].rearrange("p a c -> p (a c)")
                for cw in range(16):
                    bank = 4 + (cw % 2)
                    P.mm(self.ps[bank][:, 0:512], F1[:, ri, :], ktf[:, cw * 512:(cw + 1) * 512], True, True,
                         reads=["F1"] + [("kt", kb, q) for q in range(8)], writes=[self.psk(bank)])
                    P.cp("act" if cw % 2 else "dve", btf[:, cw * 512:(cw + 1) * 512], self.ps[bank][:, 0:512],
                         reads=[self.psk(bank)], writes=[("Btk", ri, kb, cw)])
                P.dma("pool", self.Bd[ri][:, jg * 8:(jg + 1) * 8, :], Bt[ri][kb][:],
                      reads=[("Btk", ri, kb, cw) for cw in range(16)], writes=[("Bd", ri, jg)])
        for ch in range(2):
            P.act(nrow[:, ch * 512:(ch + 1) * 512], self.ps[6 + ch][:, 0:512], AF.Ln, bias=self.epsT[:, 0:1], reads=[self.psk(6 + ch)], writes=[("nrow", ch)])
            P.act(nrow[:, ch * 512:(ch + 1) * 512], nrow[:, ch * 512:(ch + 1) * 512], AF.Exp, scale=-0.5, reads=[("nrow", ch)], writes=[("nrow", ch)])
        P.barrier()
        P.sb_off = mark
        Br = [[P.sb([128, 1024], BF16, "Brk") for _ in range(2)] for _ in range(2)]
        G = [P.sb([128, 3, 128], BF16, "Gk") for _ in range(2)]
        Kt = [[P.sb([128, 1024], BF16, "Kt") for _ in range(2)] for _ in range(2)]
        tz = [P.sb([128, 512], F32, "tz") for _ in range(2)]
        for k1 in range(128):
            b = k1 % 2
            for ri in range(2):
                P.dma("sp", Br[ri][b][:], self.Bd[ri][k1], writes=[("Brk", ri, b)])
            P.dma("sp", G[b][:], self.cst["GT"][k1], writes=[("Gk", b)])
            for ch in range(2):
                bs = 0 if (2 * k1 + ch) % 2 == 0 else 2
                cs = slice(ch * 512, (ch + 1) * 512)
                rk = [("Brk", 0, b), ("Brk", 1, b), ("Gk", b)]
                P.mm(self.ps[bs][:, 0:512], G[b][:, 0, :], Br[0][b][:, cs], True, False, reads=rk, writes=[self.psk(bs)])
                P.mm(self.ps[bs][:, 0:512], G[b][:, 2, :], Br[1][b][:, cs], False, True, reads=rk, writes=[self.psk(bs)])
                P.mm(self.ps[bs + 1][:, 0:512], G[b][:, 1, :], Br[0][b][:, cs], True, False, reads=rk, writes=[self.psk(bs + 1)])
                P.mm(self.ps[bs + 1][:, 0:512], G[b][:, 0, :], Br[1][b][:, cs], False, True, reads=rk, writes=[self.psk(bs + 1)])
                P.tt("dve", tz[ch][:], self.ps[bs][:, 0:512], nrow[:, cs], ALU.mult, reads=[self.psk(bs), ("nrow", ch)], writes=[("tz", ch)])
                P.tt("pool", Kt[0][b][:, cs], tz[ch][:], drow[:, cs], ALU.add, reads=[("tz", ch), "drow"], writes=[("Kt", 0, b, ch)])
                P.tt("dve", Kt[1][b][:, cs], self.ps[bs + 1][:, 0:512], nrow[:, cs], ALU.mult, reads=[self.psk(bs + 1), ("nrow", ch)], writes=[("Kt", 1, b, ch)])
            for ri in range(2):
                P.dma("pool", Kf[ri][k1], Kt[ri][b][:], reads=[("Kt", ri, b, 0), ("Kt", ri, b, 1)])
        P.barrier()

    def stage_h(self, l, s):
        P = self.P
        P.sb_reset()
        Kf = self.Kf[s.tag]
        nval = 64 if s.tag == "l" else s.T // 128
        F1 = P.sb([128, 2, 128], BF16, "F1")
        I2 = P.sb([128, 2, 64], BF16, "I2")
        P.dma("act", F1[:], self.cst["F1"], writes=["F1"])
        P.dma("act", I2[:], self.cst["I2"], writes=["I2"])
        mark = P.sb_off
        zt = [P.sb([64, 8, 1024], BF16, "zt") for _ in range(2)]
        Bt = [[P.sb([128, 8, 1024], BF16, "Bth") for _ in range(2)] for _ in range(2)]
        zv = s.z.rearrange("(a b) c -> a b c", b=128)
        if nval < 64:
            for b in range(2):
                P.memset("pool", zt[b][:], 0.0, writes=[("zt", b)])
        for jg in range(16):
            b = jg % 2
            P.dma("sp", zt[b][0:nval, :, :], zv[0:nval, jg * 8:(jg + 1) * 8, :], reads=[("zt", b)] if nval < 64 else [], writes=[("ztd", b)])
            ztf = zt[b][:].rearrange("p a c -> p (a c)")
            for ri in range(2):
                btf = Bt[ri][b][:].rearrange("p a c -> p (a c)")
                for cw in range(16):
                    bank = (cw % 4)
                    P.mm(self.ps[bank][:, 0:512], F1[0:64, ri, :], ztf[:, cw * 512:(cw + 1) * 512], True, True,
                         reads=["F1", ("ztd", b), ("zt", b)], writes=[self.psk(bank)])
                    P.cp("act" if cw % 2 else "dve", btf[:, cw * 512:(cw + 1) * 512], self.ps[bank][:, 0:512],
                         reads=[self.psk(bank)], writes=[("Bth", ri, b, cw)])
                P.dma("pool", self.Bd[ri][:, jg * 8:(jg + 1) * 8, :], Bt[ri][b][:],
                      reads=[("Bth", ri, b, cw) for cw in range(16)], writes=[("Bd", ri, jg)])
        P.barrier()
        P.sb_off = mark
        Br = [[P.sb([128, 1024], BF16, "Brh") for _ in range(2)] for _ in range(2)]
        Kt = [[P.sb([128, 1024], BF16, "Kth") for _ in range(2)] for _ in range(2)]
        G = [P.sb([128, 3, 128], BF16, "Gh") for _ in range(2)]
        Hh = [P.sb([128, 3, 128], BF16, "Hh") for _ in range(2)]
        Y = [[P.sb([128, 512], BF16, "Yh") for _ in range(2)] for _ in range(2)]
        tq = [[P.sb([128, 512], F32, "tq") for _ in range(4)] for _ in range(2)]
        Dt = [[P.sb([128, 1024], BF16, "Dth") for _ in range(2)] for _ in range(2)]
        def second_half(k1, ch):
            b = k1 % 2
            par = ch
            bs = 0 if par == 0 else 4
            cs = slice(ch * 512, (ch + 1) * 512)
            yk = [("Yh", 0, par), ("Yh", 1, par), ("Hh", b)]
            P.mm(self.ps[bs + 2][:, 0:512], Hh[b][:, 0, :], Y[0][par][:], True, False, reads=yk, writes=[self.psk(bs + 2)])
            P.mm(self.ps[bs + 2][:, 0:512], Hh[b][:, 1, :], Y[1][par][:], False, True, reads=yk, writes=[self.psk(bs + 2)])
            P.mm(self.ps[bs + 3][:, 0:512], Hh[b][:, 2, :], Y[0][par][:], True, False, reads=yk, writes=[self.psk(bs + 3)])
            P.mm(self.ps[bs + 3][:, 0:512], Hh[b][:, 0, :], Y[1][par][:], False, True, reads=yk, writes=[self.psk(bs + 3)])
            P.cp("act", Dt[0][b][:, cs], self.ps[bs + 2][:, 0:512], reads=[self.psk(bs + 2)], writes=[("Dth", 0, b, ch)])
            P.cp("act", Dt[1][b][:, cs], self.ps[bs + 3][:, 0:512], reads=[self.psk(bs + 3)], writes=[("Dth", 1, b, ch)])
            if ch == 1:
                for ri in range(2):
                    P.dma("act", self.Dd[ri][k1], Dt[ri][b][:], reads=[("Dth", ri, b, 0), ("Dth", ri, b, 1)])

        prev = None
        for k1 in range(128):
            b = k1 % 2
            for ri in range(2):
                P.dma("sp", Br[ri][b][:], self.Bd[ri][k1], writes=[("Brh", ri, b)])
                P.dma("sp", Kt[ri][b][:], Kf[ri][k1], writes=[("Kth", ri, b)])
            P.dma("sp", G[b][:], self.cst["GT"][k1], writes=[("Gh", b)])
            P.dma("sp", Hh[b][:], self.cst["HT"][k1], writes=[("Hh", b)])
            for ch in range(2):
                par = ch
                bs = 0 if par == 0 else 4
                cs = slice(ch * 512, (ch + 1) * 512)
                rk = [("Brh", 0, b), ("Brh", 1, b), ("Gh", b)]
                P.mm(self.ps[bs][:, 0:512], G[b][:, 0, :], Br[0][b][:, cs], True, False, reads=rk, writes=[self.psk(bs)])
                P.mm(self.ps[bs][:, 0:512], G[b][:, 2, :], Br[1][b][:, cs], False, True, reads=rk, writes=[self.psk(bs)])
                P.mm(self.ps[bs + 1][:, 0:512], G[b][:, 1, :], Br[0][b][:, cs], True, False, reads=rk, writes=[self.psk(bs + 1)])
                P.mm(self.ps[bs + 1][:, 0:512], G[b][:, 0, :], Br[1][b][:, cs], False, True, reads=rk, writes=[self.psk(bs + 1)])
                if prev is not None:
                    second_half(*prev)
                t = tq[par]
                kk = [("Kth", 0, b), ("Kth", 1, b)]
                P.tt("dve", t[0][:], self.ps[bs][:, 0:512], Kt[0][b][:, cs], ALU.mult, reads=[self.psk(bs)] + kk, writes=[("tq", par, 0)])
                P.tt("dve", t[1][:], self.ps[bs + 1][:, 0:512], Kt[1][b][:, cs], ALU.mult, reads=[self.psk(bs + 1)] + kk, writes=[("tq", par, 1)])
                P.tt("dve", t[2][:], self.ps[bs][:, 0:512], Kt[1][b][:, cs], ALU.mult, reads=[self.psk(bs)] + kk, writes=[("tq", par, 2)])
                P.tt("dve", t[3][:], self.ps[bs + 1][:, 0:512], Kt[0][b][:, cs], ALU.mult, reads=[self.psk(bs + 1)] + kk, writes=[("tq", par, 3)])
                P.tt("pool", Y[0][par][:], t[0][:], t[1][:], ALU.subtract, reads=[("tq", par, 0), ("tq", par, 1)], writes=[("Yh", 0, par)])
                P.tt("pool", Y[1][par][:], t[2][:], t[3][:], ALU.add, reads=[("tq", par, 2), ("tq", par, 3)], writes=[("Yh", 1, par)])
                prev = (k1, ch)
        second_half(*prev)
        P.barrier()
        P.sb_off = mark
        Dr = [[P.sb([128, 8, 1024], BF16, "Drh") for _ in range(2)] for _ in range(2)]
        yt = [P.sb([64, 8, 1024], F32, "yt") for _ in range(2)]
        yv = s.y.rearrange("(a b) c -> a b c", b=128)
        for jg in range(16):
            b = jg % 2
            for ri in range(2):
                P.dma("sp", Dr[ri][b][:], self.Dd[ri][:, jg * 8:(jg + 1) * 8, :], writes=[("Drh", ri, b)])
            d0 = Dr[0][b][:].rearrange("p a c -> p (a c)")
            d1 = Dr[1][b][:].rearrange("p a c -> p (a c)")
            ytf = yt[b][:].rearrange("p a c -> p (a c)")
            for cw in range(16):
                bank = cw % 4
                cs = slice(cw * 512, (cw + 1) * 512)
                P.mm(self.ps[bank][0:64, 0:512], I2[:, 0, :], d0[:, cs], True, False, reads=["I2", ("Drh", 0, b), ("Drh", 1, b)], writes=[self.psk(bank)])
                P.mm(self.ps[bank][0:64, 0:512], I2[:, 1, :], d1[:, cs], False, True, reads=["I2", ("Drh", 0, b), ("Drh", 1, b)], writes=[self.psk(bank)])
                P.cp("act" if cw % 2 else "dve", ytf[:, cs], self.ps[bank][0:64, 0:512], reads=[self.psk(bank)], writes=[("yt", b, cw)])
            P.dma("pool", yv[0:nval, jg * 8:(jg + 1) * 8, :], yt[b][0:nval, :, :], reads=[("yt", b, cw) for cw in range(16)])
        P.barrier()

    def stage_hc(self, l, s):
        P = self.P
        P.sb_reset()
        I = self.inp
        hv, vec = self.hfv[l], self.vecs[l]
        L = s.T
        w1s = P.sb([33, 64], F32, "w1s")
        w2s = P.sb([64, 64], F32, "w2s")
        w3f = P.sb([64, 2048], F32, "w3f")
        ft = P.sb([33, L], F32, "ft")
        tmp = P.sb([64, L], F32, "sl_tmp")
        tmp2 = P.sb([64, L], F32, "sl_tmp2")
        h1 = P.sb([64, L], F32, "h1")
        h2 = P.sb([64, L], F32, "h2")
        t01 = P.sb([128, L], F32, "t01")
        ndl = P.sb([128, 8], F32, "ndl")
        P.dma("sp", w1s[:], I["hf_w1"][l], writes=["w1s"])
        P.dma("sp", w2s[:], I["hf_w2"][l], writes=["w2s"])
        P.dma("sp", w3f[:], I["hf_w3"][l], writes=["w3f"])
        P.dma("act", ft[:], self.cst["featF_c"][:, 0:L], writes=["ft"])
        P.dma("act", t01[:], self.cst["t01row_c"], writes=["t01"])
        P.dma("act", ndl[:], self.cst["ndeltasT"], writes=["ndl"])
        P.mm(self.ps[0][0:64, 0:L], w1s[:], ft[:], True, True, reads=["w1s", "ft"], writes=[self.psk(0)])
        self._sin_layer(self.ps[0][0:64, 0:L], 1, 3, hv, tmp[:], tmp2[:], h1[:], 0, "h1")
        P.mm(self.ps[1][0:64, 0:L], w2s[:], h1[:], True, True, reads=["w2s", "h1"], writes=[self.psk(1)])
        self._sin_layer(self.ps[1][0:64, 0:L], 1, 4, hv, tmp[:], tmp2[:], h2[:], 1, "h2")
        NP = 3 * L - 2
        zp = [P.sb([128, NP], F32, "zp") for _ in range(2)]
        win = [P.sb([128, L], F32, "win") for _ in range(2)]
        kF = [P.sb([128, L], F32, "kF") for _ in range(2)]
        kB = [P.sb([128, L], F32, "kB") for _ in range(2)]
        junk = P.sb([128, L], F32, "junk")
        ss = [P.sb([128, 4], F32, "ss") for _ in range(2)]
        accF = [P.sb([128, L], F32, "accF") for _ in range(2)]
        accB = [P.sb([128, L], F32, "accB") for _ in range(2)]
        x0c = [P.sb([128, L], F32, "x0c") for _ in range(2)]
        hyo = [P.sb([128, L], BF16, "hyo") for _ in range(2)]
        for b in range(2):
            P.memset("pool", zp[b][:], 0.0, writes=[("zp", b)])
        for j in range(8):
            b = j % 2
            bf, bb = (2, 3) if b == 0 else (4, 5)
            P.dma("sp", zp[b][:, L - 1:2 * L - 1], s.zT[j * 128:(j + 1) * 128, :], reads=[("zp", b)], writes=[("zpd", b)])
            P.dma("act", x0c[b][:], s.x0T[j * 128:(j + 1) * 128, :], writes=[("x0c", b)])
            P.mm(self.ps[bf][:, 0:L], w3f[:, j * 128:(j + 1) * 128], h2[:], True, True, reads=["w3f", "h2"], writes=[self.psk(bf)])
            P.mm(self.ps[bb][:, 0:L], w3f[:, 1024 + j * 128:1024 + (j + 1) * 128], h2[:], True, True, reads=["w3f", "h2"], writes=[self.psk(bb)])
            P.act(win[b][:], t01[:], AF.Exp, scale=ndl[:, j:j + 1], reads=["t01", "ndl"], writes=[("win", b)])
            P.ts("pool", win[b][:], win[b][:], 1.0, 0.05, ALU.mult, ALU.add, reads=[("win", b)], writes=[("win", b)])
            P.tt("dve", kF[b][:], self.ps[bf][:, 0:L], win[b][:], ALU.mult, reads=[self.psk(bf), ("win", b)], writes=[("kF", b)])
            P.tt("dve", kB[b][:], self.ps[bb][:, 0:L], win[b][:], ALU.mult, reads=[self.psk(bb), ("win", b)], writes=[("kB", b)])
            P.memset("dve", kB[b][:, 0:1], 0.0, writes=[("kB", b)])
            P.add("act", lambda e, b=b: e.activation(out=junk[:], in_=kF[b][:], func=AF.Square, accum_out=ss[b][:, 0:1]),
                  reads=[("kF", b)], writes=[("ss", b, 0), "junk"])
            P.add("act", lambda e, b=b: e.activation(out=junk[:], in_=kB[b][:], func=AF.Square, accum_out=ss[b][:, 1:2]),
                  reads=[("kB", b)], writes=[("ss", b, 1), "junk"])
            P.tt("pool", ss[b][:, 2:3], ss[b][:, 0:1], ss[b][:, 1:2], ALU.add, reads=[("ss", b, 0), ("ss", b, 1)], writes=[("ss", b, 2)])
            P.act(ss[b][:, 2:3], ss[b][:, 2:3], AF.Ln, bias=self.epsT[:, 0:1], reads=[("ss", b, 2)], writes=[("ss", b, 2)])
            P.act(ss[b][:, 3:4], ss[b][:, 2:3], AF.Exp, scale=-0.5, reads=[("ss", b, 2)], writes=[("ss", b, 3)])
            zk = [("zp", b), ("zpd", b)]
            P.ts("dve", accF[b][:], zp[b][:, L - 1:2 * L - 1], kF[b][:, 0:1], None, ALU.mult, None, reads=zk + [("kF", b)], writes=[("accF", b)])
            P.ts("dve", accB[b][:], zp[b][:, L:2 * L], kB[b][:, 1:2], None, ALU.mult, None, reads=zk + [("kB", b)], writes=[("accB", b)])
            for m in range(1, L):
                P.stt(accF[b][:], zp[b][:, L - 1 - m:2 * L - 1 - m], kF[b][:, m:m + 1], accF[b][:], ALU.mult, ALU.add,
                      reads=[("accF", b)], writes=[("accF", b)])
                if m >= 2:
                    P.stt(accB[b][:], zp[b][:, L - 1 + m:2 * L - 1 + m], kB[b][:, m:m + 1], accB[b][:], ALU.mult, ALU.add,
                          reads=[("accB", b)], writes=[("accB", b)])
            P.tt("pool", accF[b][:], accF[b][:], accB[b][:], ALU.add, reads=[("accF", b), ("accB", b)], writes=[("accF", b)])
            P.ts("pool", accB[b][:], zp[b][:, L - 1:2 * L - 1], vec[:, self.V_HD + j:self.V_HD + j + 1], 1.0, ALU.mult, ALU.mult,
                 reads=zk + [("accB", b), ("vecs", l)], writes=[("accB", b)])
            P.stt(accF[b][:], accF[b][:], ss[b][:, 3:4], accB[b][:], ALU.mult, ALU.add, reads=[("accF", b), ("accB", b), ("ss", b, 3)], writes=[("accF", b)])
            P.tt("pool", hyo[b][:], accF[b][:], x0c[b][:], ALU.mult, reads=[("accF", b), ("x0c", b)], writes=[("hyo", b)])
            P.dma("sp", s.hyT[j * 128:(j + 1) * 128, :], hyo[b][:], reads=[("hyo", b)])
        P.barrier()

    def _load_w_resident(self, dst, src2d, nrow_chunks, ncols, key, stage, skey):
        P = self.P
        n = 0
        for c0 in range(0, ncols, 128):
            for k0 in range(0, nrow_chunks, 8):
                kn = min(8, nrow_chunks - k0)
                j = self._stg_i % len(stage)
                self._stg_i += 1
                P.dma("sp", stage[j][:, 0:kn, :],
                      src2d[k0 * 128:(k0 + kn) * 128, c0:c0 + 128].rearrange("(k p) n -> p k n", p=128), writes=[(skey, j)])
                P.cp("pool" if n % 2 else "dve", dst[:, k0:k0 + kn, c0:c0 + 128], stage[j][:, 0:kn, :], reads=[(skey, j)], writes=[(key, c0, k0)])
                n += 1
        return [[(key, c0, k0) for k0 in range(0, nrow_chunks, 8)] for c0 in range(0, ncols, 128)]

    def _layer_norm(self, rT, W, gcol, bcol, vec, l, outT, sq, stat, rkeys, okey, sqk="lnsq"):
        P = self.P
        mean, msq, var, rstd = stat
        for m in range(8):
            P.mm(self.ps[6][:, 0:W], self.ones_f[:], rT[:, m, :], m == 0, m == 7, reads=[rkeys[m]], writes=[self.psk(6)])
        for m in range(8):
            P.act(sq[:, m, :], rT[:, m, :], AF.Square, reads=[rkeys[m]], writes=[(sqk, m)])
            P.mm(self.ps[7][:, 0:W], self.ones_f[:], sq[:, m, :], m == 0, m == 7, reads=[(sqk, m)], writes=[self.psk(7)])
        P.act(mean[:, 0:W], self.ps[6][:, 0:W], AF.Copy, scale=1.0 / D, reads=[self.psk(6)], writes=["ln_mean"])
        P.tt("pool", msq[:, 0:W], mean[:, 0:W], mean[:, 0:W], ALU.mult, reads=["ln_mean"], writes=["ln_msq"])
        P.stt(var[:, 0:W], self.ps[7][:, 0:W], 1.0 / D, msq[:, 0:W], ALU.mult, ALU.subtract, reads=[self.psk(7), "ln_msq"], writes=["ln_var"])
        P.act(rstd[:, 0:W], var[:, 0:W], AF.Ln, bias=self.epsT[:, 0:1], reads=["ln_var"], writes=["ln_rstd"])
        P.act(rstd[:, 0:W], rstd[:, 0:W], AF.Exp, scale=-0.5, reads=["ln_rstd"], writes=["ln_rstd"])
        for m in range(8):
            P.tt("dve", sq[:, m, :], rT[:, m, :], mean[:, 0:W], ALU.subtract, reads=[rkeys[m], "ln_mean", (sqk, m)], writes=[(sqk, m)])
            P.tt("pool", sq[:, m, :], sq[:, m, :], rstd[:, 0:W], ALU.mult, reads=[(sqk, m), "ln_rstd"], writes=[(sqk, m)])
            P.act(outT[:, m, :], sq[:, m, :], AF.Identity, scale=vec[:, gcol + m:gcol + m + 1], bias=vec[:, bcol + m:bcol + m + 1],
                  reads=[(sqk, m), ("vecs", l)], writes=[(okey, m)])

    def stage_c(self, l, s):
        P = self.P
        P.sb_reset()
        T, W = s.T, 256
        si = s.sidx
        vec, modT = self.vecs[l], self.modT[l]
        nj = W // 128
        self._stg_i = 0
        wb = [P.sb([128, 8, D], BF16, f"wb{i}") for i in range(3)]
        wo = P.sb([128, 8, D], BF16, "wo")
        mark0 = P.sb_off
        stage = [P.sb([128, 8, 128], F32, "cst") for _ in range(3)]
        wk = []
        for i in range(3):
            wk.append(self._load_w_resident(wb[i], self.inp["w_branch"][l, i], 8, D, f"wb{i}", stage, "cst"))
        wok = self._load_w_resident(wo, self.inp["w_out"][l], 8, D, "wo", stage, "cst")
        P.barrier()
        P.sb_off = mark0
        NBUF = 2
        yt = [P.sb([128, nj, D], F32, "yt") for _ in range(NBUF)]
        x0t = [P.sb([128, 8, W], F32, "x0t") for _ in range(NBUF)]
        hyT = [P.sb([128, 8, W], BF16, "hyT") for _ in range(NBUF)]
        atT = [P.sb([128, 8, W], BF16, "atT") for _ in range(NBUF)]
        poT = [P.sb([128, 8, W], BF16, "poT") for _ in range(NBUF)]
        gt = [P.sb([128, 3, 8, W], BF16, "gt") for _ in range(NBUF)]
        xat = [P.sb([128, 8, W], F32, "xat") for _ in range(NBUF)]
        mg = [P.sb([128, 8, W], BF16, "mg") for _ in range(NBUF)]
        tA = [P.sb([128, W], F32, "tA") for _ in range(2)]
        tB = [P.sb([128, W], F32, "tB") for _ in range(2)]
        stat = [P.sb([128, W], F32, "lnst") for _ in range(4)]
        direct = s.hyT is not None

        def finish_c(p, ts_, sq):
            self._layer_norm(xat[p], W, self.V_LN, self.V_LN + 8, vec, l, xat[p], sq, stat, [("rT", p, m) for m in range(8)], ("x1T", p), ("lnsq", p))
            P.dma("act", s.xb[:, ts_].rearrange("(m p) t -> p m t", p=128), xat[p][:], reads=[(("x1T", p), m) for m in range(8)], writes=[("xat", p)])

        pend = None
        for tw in range(T // W):
            p = tw % NBUF
            ts_ = slice(tw * W, (tw + 1) * W)
            sq = yt[p][:].rearrange("p j d -> p (j d)").rearrange("p (m w) -> p m w", m=8)
            if direct:
                P.dma("sp", hyT[p][:], s.hyT[:, ts_].rearrange("(m p) t -> p m t", p=128), writes=[("hyT", p, m) for m in range(8)])
            else:
                P.dma("sp", yt[p][:], s.y[ts_, :].rearrange("(j p) d -> p j d", p=128), reads=[(("lnsq", p), m) for m in range(8)], writes=[("yt", p)])
                P.dma("sp", x0t[p][:], s.x0T[:, ts_].rearrange("(m p) t -> p m t", p=128), writes=[("x0t", p)])
            P.dma("sp", atT[p][:], s.attnT[:, ts_].rearrange("(m p) t -> p m t", p=128), writes=[("atT", p)])
            P.dma("sp", poT[p][:], s.poolT[:, ts_].rearrange("(m p) t -> p m t", p=128), writes=[("poT", p)])
            for i in range(3):
                P.dma("sp", gt[p][:, i, :, :], s.gT[i, :, ts_].rearrange("(m p) t -> p m t", p=128), writes=[("gt", p, i)])
            P.dma("sp", xat[p][:], s.xa[:, ts_].rearrange("(m p) t -> p m t", p=128), writes=[("xat", p)])
            for m in range(8):
                if direct:
                    break
                bank = m % 2
                for j in range(nj):
                    P.tr(self.ps[bank][:, j * 128:(j + 1) * 128], yt[p][:, j, m * 128:(m + 1) * 128], self.ident[:], reads=[("yt", p)], writes=[self.psk(bank)])
                P.tt("dve", hyT[p][:, m, :], self.ps[bank][:, 0:W], x0t[p][:, m, :], ALU.mult, reads=[self.psk(bank), ("x0t", p)], writes=[("hyT", p, m)])
            srcs = (hyT[p], atT[p], poT[p])
            skeys = ([("hyT", p, m) for m in range(8)], [("atT", p)], [("poT", p)])
            for mo in range(8):
                q = mo % 2
                for i in range(3):
                    bank = 2 + i
                    for k in range(8):
                        P.mm(self.ps[bank][:, 0:W], wb[i][:, k, mo * 128:(mo + 1) * 128], srcs[i][:, k, :], k == 0, k == 7,
                             reads=(skeys[i] if i else [("hyT", p, k)]), writes=[self.psk(bank)])
                P.tt("dve", tA[q][:], self.ps[2][:, 0:W], gt[p][:, 0, mo, :], ALU.mult, reads=[self.psk(2), ("gt", p, 0)], writes=[("tA", q)])
                P.tt("dve", tB[q][:], self.ps[3][:, 0:W], gt[p][:, 1, mo, :], ALU.mult, reads=[self.psk(3), ("gt", p, 1)], writes=[("tB", q)])
                P.tt("pool", tA[q][:], tA[q][:], tB[q][:], ALU.add, reads=[("tA", q), ("tB", q)], writes=[("tA", q)])
                P.tt("dve", tB[q][:], self.ps[4][:, 0:W], gt[p][:, 2, mo, :], ALU.mult, reads=[self.psk(4), ("gt", p, 2)], writes=[("tB", q)])
                P.tt("pool", mg[p][:, mo, :], tA[q][:], tB[q][:], ALU.add, reads=[("tA", q), ("tB", q)], writes=[("mg", p, mo)])
            if pend is not None:
                finish_c(*pend)
            for mo in range(8):
                bank = mo % 2
                for k in range(8):
                    P.mm(self.ps[bank][:, 0:W], wo[:, k, mo * 128:(mo + 1) * 128], mg[p][:, k, :], k == 0, k == 7,
                         reads=[("mg", p, k)], writes=[self.psk(bank)])
                P.act(xat[p][:, mo, :], xat[p][:, mo, :], AF.Copy, scale=ALPHA, reads=[("xat", p)], writes=[("xs", p, mo)])
                P.stt(xat[p][:, mo, :], self.ps[bank][:, 0:W], modT[:, si, 16 + mo:17 + mo], xat[p][:, mo, :], ALU.mult, ALU.add,
                      reads=[self.psk(bank), ("xs", p, mo), ("modT", l, si)], writes=[("rT", p, mo)])
            pend = (p, ts_, sq)
        finish_c(*pend)
        P.barrier()

    def stage_d(self, l, s, final):
        P = self.P
        P.sb_reset()
        T = s.T
        W = 256
        si = s.sidx
        vec, modT, modP = self.vecs[l], self.modT[l], self.modP[l]
        NF = DFF // 128
        self._stg_i = 0
        w1 = P.sb([128, 8, DFF], BF16, "w1")
        w3 = P.sb([128, 8, DFF], BF16, "w3")
        w2 = P.sb([128, NF, D], BF16, "w2")
        mark0 = P.sb_off
        stage = [P.sb([128, 8, 128], F32, "dst") for _ in range(3)]
        w1k = self._load_w_resident(w1, self.inp["ffn_w1"][l], 8, DFF, "w1", stage, "dst")
        w3k = self._load_w_resident(w3, self.inp["ffn_w3"][l], 8, DFF, "w3", stage, "dst")
        w2k = self._load_w_resident(w2, self.inp["ffn_w2"][l], NF, D, "w2", stage, "dst")
        P.barrier()
        P.sb_off = mark0
        x1t = [P.sb([128, 8, W], F32, "x1t") for _ in range(2)]
        h2T = [P.sb([128, 8, W], BF16, "h2T") for _ in range(2)]
        gT = P.sb([128, NF, W], BF16, "gT")
        sa = [P.sb([128, W], F32, "sa") for _ in range(2)]
        sqd = P.sb([128, 8, W], F32, "sqd")
        stat = [P.sb([128, W], F32, "lnst") for _ in range(4)]
        ot = sqd[:].rearrange("p m w -> p (m w)").rearrange("p (j d) -> p j d", j=W // 128) if final else None

        def finish_d(p, ts_):
            xt = x1t[p]
            self._layer_norm(xt, W, self.V_LN + 16, self.V_LN + 24, vec, l, xt, sqd, stat, [("rT", p, m) for m in range(8)], ("x2T", p))
            xk = [(("x2T", p), m) for m in range(8)]
            if not final:
                P.dma("act", s.xa[:, ts_].rearrange("(m p) t -> p m t", p=128), xt[:], reads=xk, writes=[("x1t", p)])
            else:
                for j in range(W // 128):
                    for half in range(2):
                        bank = half
                        for mm_ in range(4):
                            m = half * 4 + mm_
                            P.tr(self.ps[bank][:, mm_ * 128:(mm_ + 1) * 128], xt[:, m, j * 128:(j + 1) * 128], self.ident[:],
                                 reads=[(("x2T", p), m)], writes=[self.psk(bank)])
                        P.cp("act" if half else "dve", ot[:, j, half * 512:(half + 1) * 512], self.ps[bank][:, 0:512], reads=[self.psk(bank)],
                             writes=[("ot", j, half)] + ([("lnsq", m) for m in range(8)] if (j == 0 and half == 0) else []))
                P.dma("act", self.out[ts_, :].rearrange("(j p) d -> p j d", p=128), ot,
                      reads=[("ot", j, h_) for j in range(W // 128) for h_ in range(2)] + xk + [("lnsq", m) for m in range(8)],
                      writes=[("x1t", p)], final=True)

        pend = None
        for tw in range(T // W):
            p = tw % 2
            ts_ = slice(tw * W, (tw + 1) * W)
            P.dma("sp", x1t[p][:], s.xb[:, ts_].rearrange("(m p) t -> p m t", p=128), writes=[("x1t", p)])
            for m in range(8):
                P.ts("dve" if m % 2 else "pool", h2T[p][:, m, :], x1t[p][:, m, :], modP[:, si, 32 + m:33 + m], modT[:, si, 24 + m:25 + m], ALU.mult, ALU.add,
                     reads=[("x1t", p), ("modT", l, si), ("modP", l, si)], writes=[("h2T", p, m)])
            for f in range(NF):
                q = f % 2
                ba, bb = (0, 1) if q == 0 else (2, 3)
                for k in range(8):
                    P.mm(self.ps[ba][:, 0:W], w1[:, k, f * 128:(f + 1) * 128], h2T[p][:, k, :], k == 0, k == 7, reads=[("h2T", p, k)], writes=[self.psk(ba)])
                for k in range(8):
                    P.mm(self.ps[bb][:, 0:W], w3[:, k, f * 128:(f + 1) * 128], h2T[p][:, k, :], k == 0, k == 7, reads=[("h2T", p, k)], writes=[self.psk(bb)])
                P.act(sa[q][:], self.ps[ba][:, 0:W], AF.Silu, reads=[self.psk(ba)], writes=[("sa", q)])
                P.tt("dve", gT[:, f, :], self.ps[bb][:, 0:W], sa[q][:], ALU.mult, reads=[self.psk(bb), ("sa", q)], writes=[("gT", f)])
                if f == 2 and pend is not None:
                    finish_d(*pend)
                    pend = None
            for mo in range(8):
                bank = 4 + mo % 2
                for f in range(NF):
                    P.mm(self.ps[bank][:, 0:W], w2[:, f, mo * 128:(mo + 1) * 128], gT[:, f, :], f == 0, f == NF - 1,
                         reads=[("gT", f)], writes=[self.psk(bank)])
                P.act(x1t[p][:, mo, :], x1t[p][:, mo, :], AF.Copy, scale=ALPHA, reads=[("x1t", p)], writes=[("xs", p, mo)])
                P.stt(x1t[p][:, mo, :], self.ps[bank][:, 0:W], modT[:, si, 40 + mo:41 + mo], x1t[p][:, mo, :], ALU.mult, ALU.add,
                      reads=[self.psk(bank), ("xs", p, mo), ("modT", l, si)], writes=[("rT", p, mo)])
            pend = (p, ts_)
        finish_d(*pend)
        P.barrier()

    def build(self):
        st = self.stages
        def on(name):
            return st is None or name in st
        if on("p0"):
            self.stage_p0()
        if on("p1"):
            self.stage_p1()
        for l in range(DEPTH):
            if on(f"k{l}l"):
                self.stage_k(l, "l")
            if on(f"a{l}c"):
                self.stage_a(l, self.st["c"])
            if on(f"a{l}l"):
                self.stage_a(l, self.st["l"])
            if on(f"h{l}c") and l < DEPTH - 1:
                self.stage_hc(l, self.st["c"])
            if on(f"h{l}l"):
                self.stage_h(l, self.st["l"])
            if on(f"b{l}c") and l < DEPTH - 1:
                self.stage_b(l, self.st["c"], CTX)
            if on(f"b{l}l"):
                self.stage_b(l, self.st["l"], SEQ + CTX)
            if on(f"c{l}c") and l < DEPTH - 1:
                self.stage_c(l, self.st["c"])
            if on(f"d{l}c") and l < DEPTH - 1:
                self.stage_d(l, self.st["c"], False)
            if on(f"c{l}l"):
                self.stage_c(l, self.st["l"])
            if on(f"d{l}l"):
                self.stage_d(l, self.st["l"], l == DEPTH - 1)
        self.P.emit()
        return self.nc


def make_in_maps(inputs, n_cores=8):
    c = host_consts()
    shared = {k: np.ascontiguousarray(np.asarray(v, dtype=np.float32)) for k, v in inputs.items() if k not in ("x", "c", "ctx")}
    consts = {"k_" + k: v for k, v in c.items()}
    maps = []
    for b in range(n_cores):
        m = dict(shared)
        m.update(consts)
        m["x"] = np.ascontiguousarray(inputs["x"][b], dtype=np.float32)
        m["c"] = np.ascontiguousarray(inputs["c"][b], dtype=np.float32)
        m["ctx"] = np.ascontiguousarray(inputs["ctx"][b], dtype=np.float32)
        maps.append(m)
    return maps


def kernel(**inputs):
    bld = Builder()
    nc = bld.build()
    res = run_bass_kernel_spmd(nc, make_in_maps(inputs), core_ids=list(range(8)))
    return np.stack([np.asarray(r["out"], dtype=np.float32) for r in res.results], axis=0)
```

```python
import contextlib
import math
import numpy as np
import ml_dtypes
import concourse.bass as bass
import concourse.mybir as mybir
from concourse.bass_utils import run_bass_kernel_spmd

F32 = mybir.dt.float32
BF16 = mybir.dt.bfloat16
AF = mybir.ActivationFunctionType
ALU = mybir.AluOpType

D = 1024
SEQ = 8192
CTX = 256
DEPTH = 2
NH = 8
HD = 128
DFF = 2816
INW = 8704
C_Q, C_K, C_V, C_HY, C_POOL, C_GATE = 0, 1024, 1280, 1536, 4608, 5632
ALPHA = (2 * DEPTH) ** 0.25
EPS = 1e-6
NFFT = 16384
POOL_WINDOWS = (2, 4, 8, 16)


class _Op:
    __slots__ = ("eng", "fn", "deps", "dma", "marked", "cnt", "sem", "semval", "prev")


class Prog:
    COMPUTE = ("pe", "act", "dve", "pool")
    QUEUES = ("sp", "act", "pool")
    ALLENG = ("pe", "act", "dve", "pool", "sp")
    SB_BASE = 24576
    SB_LIMIT = 218 * 1024

    def __init__(self, ring=16, same_engine_sync=True):
        self.nc = bass.Bass("TRN2", target_bir_lowering=False)
        self.ops = []
        self.state = {}
        self.ring = ring
        self.same = same_engine_sync
        self.dma_count = {q: 0 for q in self.QUEUES}
        self.slot_last = {}
        self.slot_val = {}
        self.last_op = {e: None for e in self.ALLENG}
        self.sb_off = self.SB_BASE
        self.sb_mark = self.SB_BASE
        self.n_alloc = 0
        self.out_dmas = []
        self.psn = 0

    def sb(self, shape, dtype, name="t"):
        nbytes = int(np.prod(shape[1:])) * mybir.dt.size(dtype)
        nbytes_al = (nbytes + 63) // 64 * 64
        self.n_alloc += 1
        h = self.nc.alloc_sbuf_tensor_at(f"{name}_{self.n_alloc}", list(shape), dtype, offset=self.sb_off)
        self.sb_off += nbytes_al
        assert self.sb_off <= self.SB_LIMIT, f"SBUF overflow {self.sb_off} ({name})"
        return h

    def sb_persist_done(self):
        self.sb_mark = self.sb_off

    def sb_reset(self):
        self.sb_off = self.sb_mark

    def _collect(self, reads, writes):
        deps = set()
        for k in reads:
            st = self.state.get(k)
            if st is not None and st[0] is not None:
                deps.add(st[0])
        for k in writes:
            st = self.state.get(k)
            if st is not None:
                if st[0] is not None:
                    deps.add(st[0])
                deps.update(st[1].values())
                deps.update(st[2])
        return deps

    def add(self, eng, fn, reads=(), writes=(), dma=False, out=False):
        pr = [k for k in reads if isinstance(k, tuple) and k[0] == "ps"]
        if pr:
            reads = [k for k in reads if not (isinstance(k, tuple) and k[0] == "ps")]
            writes = list(writes) + pr
        i = len(self.ops)
        op = _Op()
        op.eng, op.fn, op.dma, op.marked, op.cnt, op.prev = eng, fn, dma, False, 0, None
        op.deps = self._collect(reads, writes)
        if dma:
            n = self.dma_count[eng]
            self.dma_count[eng] = n + 1
            slot = (eng, n % self.ring)
            op.prev = self.slot_last.get(slot)
            self.slot_last[slot] = i
            v = self.slot_val.get(slot, 0) + 16
            self.slot_val[slot] = v
            op.sem, op.semval = slot, v
            if out:
                self.out_dmas.append(i)
        self.ops.append(op)
        for k in reads:
            st = self.state.setdefault(k, [None, {}, []])
            if dma:
                st[2].append(i)
            else:
                st[1][eng] = i
        for k in writes:
            self.state[k] = [i, {}, []]
        self.last_op[eng] = i
        return i

    def barrier(self):
        snap = [v for v in self.last_op.values() if v is not None] + list(self.slot_last.values())
        for e in self.ALLENG:
            op = _Op()
            op.eng, op.fn, op.dma, op.marked, op.cnt, op.prev = e, None, False, False, 0, None
            op.deps = set(snap)
            self.ops.append(op)
        self.state = {}

    def dma(self, q, out, in_, reads=(), writes=(), final=False):
        return self.add(q, lambda e: e.dma_start(out=out, in_=in_), reads, writes, dma=True, out=final)

    def mm(self, out, lhsT, rhs, start, stop, reads=(), writes=()):
        return self.add("pe", lambda e: e.matmul(out, lhsT=lhsT, rhs=rhs, start=start, stop=stop), reads, writes)

    def tr(self, out, in_, ident, reads=(), writes=()):
        return self.add("pe", lambda e: e.transpose(out, in_, ident), reads, writes)

    def act(self, out, in_, func, reads=(), writes=(), bias=None, scale=None):
        kw = {}
        if bias is not None:
            kw["bias"] = bias
        if scale is not None:
            kw["scale"] = scale
        return self.add("act", lambda e: e.activation(out=out, in_=in_, func=func, **kw), reads, writes)

    def tt(self, eng, out, in0, in1, op, reads=(), writes=()):
        return self.add(eng, lambda e: e.tensor_tensor(out=out, in0=in0, in1=in1, op=op), reads, writes)

    def ts(self, eng, out, in0, s1, s2, op0, op1, reads=(), writes=()):
        if op1 is None:
            return self.add(eng, lambda e: e.tensor_scalar(out=out, in0=in0, scalar1=s1, scalar2=None, op0=op0), reads, writes)
        return self.add(eng, lambda e: e.tensor_scalar(out=out, in0=in0, scalar1=s1, scalar2=s2, op0=op0, op1=op1), reads, writes)

    def stt(self, out, in0, scalar, in1, op0, op1, reads=(), writes=()):
        return self.add("dve", lambda e: e.scalar_tensor_tensor(out=out, in0=in0, scalar=scalar, in1=in1, op0=op0, op1=op1), reads, writes)

    def cp(self, eng, out, in_, reads=(), writes=()):
        if eng == "act":
            return self.add("act", lambda e: e.copy(out=out, in_=in_), reads, writes)
        return self.add(eng, lambda e: e.tensor_copy(out=out, in_=in_), reads, writes)

    def memset(self, eng, ap, val, writes=()):
        return self.add(eng, lambda e: e.memset(ap, val), (), writes)

    def emit(self):
        nc = self.nc
        ops = self.ops
        fin = _Op()
        fin.eng, fin.fn, fin.dma, fin.marked, fin.cnt, fin.prev = "sp", None, False, False, 0, None
        fin.deps = set(self.out_dmas) | set(self.slot_last.values())
        ops.append(fin)
        for op in ops:
            for d in op.deps:
                dop = ops[d]
                if dop.dma or dop.fn is None:
                    continue
                dop.marked = True
        cnt = {e: 0 for e in self.COMPUTE}
        for op in ops:
            if op.fn is not None and not op.dma:
                if op.marked:
                    cnt[op.eng] += 1
                op.cnt = cnt[op.eng]
        per_eng = {e: [] for e in self.ALLENG}
        for op in ops:
            per_eng[op.eng].append(op)
        same = self.same

        with contextlib.ExitStack() as es:
            csem = {e: es.enter_context(nc.semaphore(f"c_{e}")) for e in self.COMPUTE}
            dsem = {}
            for q in self.QUEUES:
                for r in range(self.ring):
                    dsem[(q, r)] = es.enter_context(nc.semaphore(f"d_{q}_{r}"))
            block = es.enter_context(nc.Block())

            def run(ename, e):
                waited = {}
                for op in per_eng[ename]:
                    waits = {}
                    for d in op.deps:
                        dop = ops[d]
                        if dop.dma:
                            s = ("d", dop.sem)
                            waits[s] = max(waits.get(s, 0), dop.semval)
                        else:
                            if dop.fn is None:
                                continue
                            if dop.eng == ename:
                                if op.fn is None:
                                    continue
                                if not op.dma and (ename == "pe" or not same):
                                    continue
                            s = ("c", dop.eng)
                            waits[s] = max(waits.get(s, 0), dop.cnt)
                    if op.dma and op.prev is not None:
                        p = ops[op.prev]
                        s = ("d", p.sem)
                        waits[s] = max(waits.get(s, 0), p.semval)
                    for s, v in waits.items():
                        if v <= 0 or waited.get(s, 0) >= v:
                            continue
                        e.wait_ge(dsem[s[1]] if s[0] == "d" else csem[s[1]], v)
                        waited[s] = v
                    if op.fn is None:
                        continue
                    inst = op.fn(e)
                    if op.dma:
                        inst.then_inc(dsem[op.sem], 16)
                    elif op.marked:
                        inst.then_inc(csem[ename], 1)

            block.tensor(lambda e: run("pe", e))
            block.scalar(lambda e: run("act", e))
            block.vector(lambda e: run("dve", e))
            block.gpsimd(lambda e: run("pool", e))
            block.sync(lambda e: run("sp", e))
        return nc


def _bf(a):
    return np.asarray(a, dtype=np.float32).astype(ml_dtypes.bfloat16)


def _hy_tables(L):
    f32 = np.float32
    t_idx = np.arange(L, dtype=f32)
    t01 = (t_idx / f32(max(L - 1, 1))).astype(f32)
    bands = np.linspace(1e-4, 15.0, 16, dtype=f32)
    ang = (f32(2.0 * math.pi / L) * t_idx[:, None] * bands[None, :]).astype(f32)
    feats = np.concatenate([t01[:, None], np.cos(ang), -np.sin(ang)], axis=-1).astype(f32)
    fF = np.zeros((33, 8192), f32)
    fR = np.zeros((33, 8192), f32)
    negt = np.zeros((128, 128), f32)
    mask = np.zeros((128, 128), f32)
    fF[:, :L] = feats.T
    n = np.arange(8192)
    tF = np.where(n < L, n, 0)
    negt[:64] = -np.where(n < L, t01[tF], 0).reshape(64, 128)
    mask[:64] = (n < L).astype(f32).reshape(64, 128)
    m = 8192 - n
    valid = (m >= 1) & (m <= L - 1)
    mm = np.where(valid, m, 0)
    fR[:, valid] = feats[mm[valid]].T
    negt[64:] = -np.where(valid, t01[mm], 0).reshape(64, 128)
    mask[64:] = valid.astype(f32).reshape(64, 128)
    return fF, fR, negt, mask


def _rope_tables(T, grid_w=64, ctx=False):
    f32 = np.float32
    if ctx:
        return np.ones((128, T), f32), np.zeros((128, T), f32)
    t = np.arange(T)
    rows = (t // grid_w).astype(f32)
    cols = (t % grid_w).astype(f32)
    inv = np.power(f32(10000.0), -np.arange(32, dtype=f32) / f32(32)).astype(f32)
    C = np.zeros((128, T), f32)
    S = np.zeros((128, T), f32)
    for j in range(128):
        pos = rows if j < 64 else cols
        jj = j % 64
        ang = (pos * inv[jj % 32]).astype(f32)
        C[j] = np.cos(ang)
        S[j] = -np.sin(ang) if jj < 32 else np.sin(ang)
    return C, S


_CONST_CACHE = {}


def host_consts():
    if _CONST_CACHE:
        return _CONST_CACHE
    c = {}
    c["ident"] = np.eye(128, dtype=np.float32)
    c["identb"] = _bf(np.eye(128))
    c["ropeC_l"], c["ropeS_l"] = _rope_tables(SEQ)
    c["ropeC_c"], c["ropeS_c"] = _rope_tables(CTX, ctx=True)
    a = np.arange(128, dtype=np.float64)
    th = 2 * np.pi * np.outer(a, a) / 128.0
    c["F1"] = _bf(np.stack([np.cos(th), -np.sin(th)], axis=1))
    c["I2"] = _bf(np.stack([np.cos(th)[:, :64], -np.sin(th)[:, :64]], axis=1) / NFFT)
    k1 = a[:, None, None]
    n2 = a[None, :, None]
    k2 = a[None, None, :]
    th3 = 2 * np.pi * n2 * (k1 + 128.0 * k2) / NFFT
    G = np.stack([np.cos(th3), -np.sin(th3), np.sin(th3)], axis=2)
    c["GT"] = _bf(G)
    c["HT"] = _bf(np.transpose(G, (0, 3, 2, 1)))
    for tag, L in (("l", SEQ), ("c", CTX)):
        fF, fR, negt, mask = _hy_tables(L)
        c[f"featF_{tag}"], c[f"featR_{tag}"], c[f"negt_{tag}"], c[f"mask_{tag}"] = fF, fR, negt, mask
    lo, hi = math.log(1e-2) / 1.5, math.log(1e-2) / 0.3
    c["deltas"] = np.abs(np.linspace(lo, hi, 1024, dtype=np.float32)).reshape(1, 1024).astype(np.float32)
    edge = np.zeros((4, 2, 8), np.float32)
    Tt = 4096
    for g, w in enumerate(POOL_WINDOWS):
        before, after = w // 2, w - 1 - w // 2
        for side in range(2):
            for i in range(8):
                t = i if side == 0 else Tt - 8 + i
                cnt = min(t + after + 1, Tt) - max(t - before, 0)
                edge[g, side, i] = 1.0 / cnt
    c["pedge"] = np.broadcast_to(edge.reshape(1, 64), (128, 64)).copy()
    t01c = (np.arange(CTX, dtype=np.float32) / np.float32(CTX - 1)).astype(np.float32)
    c["t01row_c"] = np.broadcast_to(t01c.reshape(1, CTX), (128, CTX)).copy()
    c["ndeltasT"] = np.ascontiguousarray(-c["deltas"].reshape(8, 128).T)
    _CONST_CACHE.update(c)
    return c


CONST_SPECS = None


def const_specs():
    c = host_consts()
    return {k: (list(v.shape), BF16 if v.dtype == ml_dtypes.bfloat16 else F32) for k, v in c.items()}


INPUT_SHAPES = {
    "x": [SEQ, D], "c": [D], "ctx": [CTX, D], "c_ctx": [D],
    "w_ada": [DEPTH, D, 6 * D], "b_ada": [DEPTH, 6 * D], "w_in": [DEPTH, D, INW],
    "q_norm_g": [DEPTH, HD], "k_norm_g": [DEPTH, HD],
    "hy_conv_w": [DEPTH, 3, 3 * D], "hy_conv_b": [DEPTH, 3 * D],
    "hf_w1": [DEPTH, 33, 64], "hf_b1": [DEPTH, 64], "hf_freq": [DEPTH, 64],
    "hf_w2": [DEPTH, 64, 64], "hf_b2": [DEPTH, 64], "hf_w3": [DEPTH, 64, 2 * D],
    "hy_d": [DEPTH, D], "pool_w": [DEPTH, 4, 256, 256], "pool_scale": [DEPTH, D],
    "w_branch": [DEPTH, 3, D, D], "w_out": [DEPTH, D, D],
    "ln1_g": [DEPTH, D], "ln1_b": [DEPTH, D], "ln2_g": [DEPTH, D], "ln2_b": [DEPTH, D],
    "ffn_w1": [DEPTH, D, DFF], "ffn_w3": [DEPTH, D, DFF], "ffn_w2": [DEPTH, DFF, D],
}


class Stream:
    pass


class Builder:
    def __init__(self, debug_outs=(), stages=None):
        self.P = Prog()
        self.nc = self.P.nc
        self.debug_outs = set(debug_outs)
        self.stages = stages
        nc = self.nc
        self.inp = {k: nc.dram_tensor(k, shp, F32, kind="ExternalInput").ap() for k, shp in INPUT_SHAPES.items()}
        self.cst = {k: nc.dram_tensor("k_" + k, shp, dt, kind="ExternalInput").ap() for k, (shp, dt) in const_specs().items()}
        self.out = nc.dram_tensor("out", [SEQ, D], F32, kind="ExternalOutput").ap()
        self.ps = [nc.alloc_psum_tensor(f"psb{i}", [128, 512], F32) for i in range(8)]
        self.scr = {}
        self._persistent()
        self._streams()

    def dram(self, name, shape, dtype):
        kind = "ExternalOutput" if name in self.debug_outs else "Internal"
        t = self.nc.dram_tensor(name, list(shape), dtype, kind=kind).ap()
        self.scr[name] = t
        return t

    def psk(self, i):
        return ("ps", i)

    def _persistent(self):
        P = self.P
        self.ident = P.sb([128, 128], F32, "ident")
        self.identb = P.sb([128, 128], BF16, "identb")
        self.ones_f = P.sb([128, 128], F32, "ones_f")
        self.ones_b = P.sb([128, 128], BF16, "ones_b")
        self.epsT = P.sb([128, 1], F32, "epsT")
        self.vecs = [P.sb([128, 200], F32, f"vecs{l}") for l in range(DEPTH)]
        self.modT = [P.sb([128, 2, 48], F32, f"modT{l}") for l in range(DEPTH)]
        self.modP = [P.sb([128, 2, 48], F32, f"modP{l}") for l in range(DEPTH)]
        self.hfv = [P.sb([64, 8], F32, f"hfv{l}") for l in range(DEPTH)]
        self.pedge = P.sb([128, 64], F32, "pedge")
        P.sb_persist_done()

    V_QG, V_QGP, V_KG, V_KGP = 0, 1, 2, 3
    V_CW = 4
    V_CB = 76
    V_PS = 100
    V_LN = 108
    V_BA = 140
    V_HD = 188
    V_N = 196

    def _streams(self):
        self.XA = self.dram("XA", [D, SEQ], F32)
        self.XB = self.dram("XB", [D, SEQ], F32)
        self.XCA = self.dram("XCA", [D, CTX], F32)
        self.XCB = self.dram("XCB", [D, CTX], F32)
        self.kT_d = self.dram("kT_d", [256, SEQ + CTX], BF16)
        self.v_d = self.dram("v_d", [SEQ + CTX, 256], BF16)
        self.Bd = [self.dram(f"Bd{i}", [128, 128, 1024], BF16) for i in range(2)]
        self.Dd = [self.dram(f"Dd{i}", [128, 128, 1024], BF16) for i in range(2)]
        self.Kf = {t: [self.dram(f"Kf_{t}{i}", [128, 128, 1024], BF16) for i in range(2)] for t in ("l", "c")}
        self.st = {}
        for tag, T in (("l", SEQ), ("c", CTX)):
            s = Stream()
            s.tag, s.T = tag, T
            s.TS = 4096 if tag == "l" else 256
            s.W = 512 if tag == "l" else 256
            s.sidx = 0 if tag == "l" else 1
            s.xa = self.XA if tag == "l" else self.XCA
            s.xb = self.XB if tag == "l" else self.XCB
            s.ktok0 = CTX if tag == "l" else 0
            s.qT = self.dram(f"qT_{tag}", [D, T], BF16)
            s.z = self.dram(f"z_{tag}", [T, D], BF16)
            s.x0T = self.dram(f"x0T_{tag}", [D, T], F32)
            s.poolT = self.dram(f"poolT_{tag}", [D, T], BF16)
            s.gT = self.dram(f"gT_{tag}", [3, D, T], BF16)
            s.attnT = self.dram(f"attnT_{tag}", [D, T], BF16)
            s.y = self.dram(f"y_{tag}", [T, D], F32)
            s.zT = self.dram(f"zT_{tag}", [D, T], F32) if tag == "c" else None
            s.hyT = self.dram(f"hyT_{tag}", [D, T], BF16) if tag == "c" else None
            s.ropeC = self.cst[f"ropeC_{tag}"]
            s.ropeS = self.cst[f"ropeS_{tag}"]
            self.st[tag] = s

    def stage_p0(self):
        P = self.P
        P.sb_reset()
        P.dma("sp", self.ident[:], self.cst["ident"], writes=["ident"])
        P.dma("sp", self.identb[:], self.cst["identb"], writes=["identb"])
        P.memset("dve", self.ones_f[:], 1.0, writes=["ones_f"])
        P.memset("dve", self.ones_b[:], 1.0, writes=["ones_b"])
        P.memset("dve", self.epsT[:], EPS, writes=["epsT"])
        xt = [P.sb([128, 4, D], F32, "p0x") for _ in range(2)]
        xo = [P.sb([128, 8, 512], F32, "p0o") for _ in range(2)]
        it = 0
        for src, dst, T in ((self.inp["x"], self.XA, SEQ), (self.inp["ctx"], self.XCA, CTX)):
            W = min(512, T)
            nj = W // 128
            for w in range(T // W):
                b = it % 2
                it += 1
                P.dma("sp", xt[b][:, 0:nj, :], src[w * W:(w + 1) * W, :].rearrange("(j p) d -> p j d", p=128),
                      writes=[("p0x", b)])
                for m in range(8):
                    bank = m % 4
                    for j in range(nj):
                        P.tr(self.ps[bank][:, j * 128:(j + 1) * 128], xt[b][:, j, m * 128:(m + 1) * 128], self.ident[:],
                             reads=[("p0x", b), "ident"], writes=[self.psk(bank)])
                    P.cp("act" if m % 2 else "dve", xo[b][:, m, 0:W], self.ps[bank][:, 0:W],
                         reads=[self.psk(bank)], writes=[("p0o", b, m)])
                P.dma("act", dst[:, w * W:(w + 1) * W].rearrange("(m p) t -> p m t", p=128), xo[b][:, :, 0:W],
                      reads=[("p0o", b, m) for m in range(8)])
        P.barrier()

    def stage_p1(self):
        P = self.P
        P.sb_reset()
        I = self.inp
        stg = P.sb([128, 2, 128], F32, "stg")
        stgc = P.sb([16, 128], F32, "stgc")
        stg64 = P.sb([8, 64], F32, "stg64")
        scT = P.sb([128, 8, 2], F32, "scT")
        cT = P.sb([128, 16], F32, "cT")
        wa = [P.sb([128, 8, 512], F32, "wa") for _ in range(2)]
        P.dma("sp", stgc[0:8, :], I["c"].rearrange("(m p) -> m p", p=128), writes=["stgc"])
        P.dma("sp", stgc[8:16, :], I["c_ctx"].rearrange("(m p) -> m p", p=128), writes=["stgc2"])
        P.tr(self.ps[0][:, 0:16], stgc[0:16, :], self.ident[0:16, 0:16], reads=["stgc", "stgc2", "ident"], writes=[self.psk(0)])
        P.act(cT[:], self.ps[0][:, 0:16], AF.Silu, reads=[self.psk(0)], writes=["cT"])
        for s in range(2):
            P.cp("dve", scT[:, :, s], cT[:, s * 8:(s + 1) * 8], reads=["cT"], writes=[("scT", s)])
        for l in range(DEPTH):
            rows = []
            g = I["q_norm_g"][l]
            kg = I["k_norm_g"][l]
            rows.append(("full", g))
            rows.append(("perm", g))
            rows.append(("full", kg))
            rows.append(("perm", kg))
            for j in range(3):
                for m in range(24):
                    rows.append(("full", I["hy_conv_w"][l, j, m * 128:(m + 1) * 128]))
            for m in range(24):
                rows.append(("full", I["hy_conv_b"][l, m * 128:(m + 1) * 128]))
            for m in range(8):
                rows.append(("full", I["pool_scale"][l, m * 128:(m + 1) * 128]))
            for nm in ("ln1_g", "ln1_b", "ln2_g", "ln2_b"):
                for m in range(8):
                    rows.append(("full", I[nm][l, m * 128:(m + 1) * 128]))
            for m in range(48):
                rows.append(("full", I["b_ada"][l, m * 128:(m + 1) * 128]))
            for m in range(8):
                rows.append(("full", I["hy_d"][l, m * 128:(m + 1) * 128]))
            assert len(rows) == self.V_N
            def ld(r0, ap2d, n):
                grp, rr = divmod(r0, 128)
                assert rr + n <= 128
                P.dma("sp", stg[rr:rr + n, grp, :], ap2d, writes=[("stg", r0)])
                return ("stg", r0)
            keys = []
            for ri, (kind, ap) in enumerate(rows[:4]):
                grp, rr = divmod(ri, 128)
                if kind == "full":
                    P.dma("sp", stg[rr:rr + 1, grp, :], ap.rearrange("(o n) -> o n", o=1), writes=[("stg", ri)])
                else:
                    for q4, src0 in enumerate((32, 0, 96, 64)):
                        P.dma("sp", stg[rr:rr + 1, grp, q4 * 32:(q4 + 1) * 32],
                              ap[src0:src0 + 32].rearrange("(o n) -> o n", o=1), writes=[("stg", ri, q4)])
                        keys.append(("stg", ri, q4))
                keys.append(("stg", ri))
            keys.append(ld(4, I["hy_conv_w"][l].rearrange("j (m p) -> (j m) p", p=128), 72))
            keys.append(ld(76, I["hy_conv_b"][l].rearrange("(m p) -> m p", p=128), 24))
            keys.append(ld(100, I["pool_scale"][l].rearrange("(m p) -> m p", p=128), 8))
            for qi, nm in enumerate(("ln1_g", "ln1_b")):
                keys.append(ld(108 + qi * 8, I[nm][l].rearrange("(m p) -> m p", p=128), 8))
            keys.append(ld(124, I["ln2_g"][l, 0:512].rearrange("(m p) -> m p", p=128), 4))
            keys.append(ld(128, I["ln2_g"][l, 512:1024].rearrange("(m p) -> m p", p=128), 4))
            keys.append(ld(132, I["ln2_b"][l].rearrange("(m p) -> m p", p=128), 8))
            keys.append(ld(140, I["b_ada"][l].rearrange("(m p) -> m p", p=128), 48))
            keys.append(ld(188, I["hy_d"][l].rearrange("(m p) -> m p", p=128), 8))
            P.tr(self.ps[1][:, 0:128], stg[:, 0, :], self.ident[:], reads=keys + ["ident"], writes=[self.psk(1)])
            P.tr(self.ps[1][:, 128:128 + 68], stg[0:68, 1, :], self.ident[0:68, 0:68], reads=keys + ["ident"], writes=[self.psk(1)])
            P.cp("dve", self.vecs[l][:, 0:196], self.ps[1][:, 0:196], reads=[self.psk(1)], writes=[("vecs", l)])
            for ci, nm in enumerate(("hf_b1", "hf_freq", "hf_b2")):
                P.dma("sp", stg64[ci:ci + 1, :], I[nm][l].rearrange("(o n) -> o n", o=1), writes=[("stg64", ci)])
            P.tr(self.ps[2][0:64, 0:3], stg64[0:3, :], self.ident[0:3, 0:3],
                 reads=[("stg64", ci) for ci in range(3)] + ["ident"], writes=[self.psk(2)])
            P.cp("dve", self.hfv[l][:, 0:3], self.ps[2][0:64, 0:3], reads=[self.psk(2)], writes=[("hfv", l)])
            P.tt("dve", self.hfv[l][:, 3:4], self.hfv[l][:, 0:1], self.hfv[l][:, 1:2], ALU.mult, reads=[("hfv", l)], writes=[("hfv3", l)])
            P.tt("dve", self.hfv[l][:, 4:5], self.hfv[l][:, 2:3], self.hfv[l][:, 1:2], ALU.mult, reads=[("hfv", l)], writes=[("hfv4", l)])
            bank = 3
            for cg in range(12):
                b = cg % 2
                P.dma("sp" if cg % 2 else "act", wa[b][:], I["w_ada"][l][:, cg * 512:(cg + 1) * 512].rearrange("(k p) n -> p k n", p=128),
                      writes=[("wa", b)])
                for mm in range(4):
                    m = cg * 4 + mm
                    for k in range(8):
                        P.mm(self.ps[bank][:, m * 2:m * 2 + 2], wa[b][:, k, mm * 128:(mm + 1) * 128], scT[:, k, :],
                             start=(k == 0), stop=(k == 7), reads=[("wa", b), ("scT", 0), ("scT", 1)], writes=[self.psk(bank)])
            psv = self.ps[bank][:, 0:96].rearrange("p (m s) -> p m s", s=2)
            for s in range(2):
                P.tt("dve", self.modT[l][:, s, :], psv[:, :, s], self.vecs[l][:, self.V_BA:self.V_BA + 48], ALU.add,
                     reads=[self.psk(bank), ("vecs", l)], writes=[("modT", l, s)])
                P.ts("dve", self.modP[l][:, s, :], self.modT[l][:, s, :], 1.0, None, ALU.add, None,
                     reads=[("modT", l, s)], writes=[("modP", l, s)])
            if "dbg_mod" in self.debug_outs:
                if l == 0:
                    self.dbg_mod = self.dram("dbg_mod", [DEPTH, 128, 96], F32)
                    self.dbg_vec = self.dram("dbg_vec", [DEPTH, 128, 192], F32)
                P.dma("sp", self.dbg_mod[l], self.modT[l][:].rearrange("p s m -> p (s m)"), reads=[("modT", l, 0), ("modT", l, 1)])
                P.dma("sp", self.dbg_vec[l], self.vecs[l][:, 0:192], reads=[("vecs", l)])
        P.barrier()

    def _proj(self, ps_ap, wt, hT, w, W, keys_w, bank, col0=0, ncol=None):
        P = self.P
        for k in range(8):
            P.mm(ps_ap, wt[:, k, :], hT[:, k, col0 + w * W: col0 + w * W + (ncol or W)],
                 start=(k == 0), stop=(k == 7), reads=[keys_w, ("hT", k, w)], writes=[self.psk(bank)])

    def stage_a(self, l, s):
        P = self.P
        P.sb_reset()
        T, TS, W = s.T, s.TS, s.W
        NW = TS // W
        NB = TS // 128
        si = s.sidx
        vec, modT, modP = self.vecs[l], self.modT[l], self.modP[l]
        w_in = self.inp["w_in"][l]
        hT = P.sb([128, 8, TS + 16], BF16, "hT")
        wring = [P.sb([128, 8, 128], BF16, "wr") for _ in range(4)]
        wcount = [0]

        wstage = [P.sb([128, 8, 128], F32, "wst") for _ in range(3)]
        scount = [0]

        def load_w(col0, ncols=128, buf=None, key=None):
            if buf is None:
                i = wcount[0] % 4
                wcount[0] += 1
                buf, key = wring[i], ("wr", i)
            for c in range(0, ncols, 128):
                j = scount[0] % 3
                scount[0] += 1
                P.dma("sp", wstage[j][:], w_in[:, col0 + c:col0 + c + 128].rearrange("(k p) n -> p k n", p=128),
                      writes=[("wst", j)])
                P.cp("dve", buf[:, :, c:c + 128], wstage[j][:], reads=[("wst", j)], writes=[key if ncols == 128 else (key, c)])
            return buf, key

        mark = P.sb_off
        P.dma("sp", self.pedge[:], self.cst["pedge"], writes=["pedge"])
        for sti in range(T // TS):
            t0 = sti * TS
            P.sb_off = mark
            xs = [P.sb([128, 8, W], F32, "xs") for _ in range(2)]
            hal = P.sb([128, 8, 16], F32, "hal")
            for w in range(NW):
                b = w % 2
                P.dma("sp", xs[b][:], s.xa[:, t0 + w * W: t0 + (w + 1) * W].rearrange("(m p) t -> p m t", p=128), writes=[("xs", b)])
                for m in range(8):
                    eng = ("act", "dve", "pool")[m % 3]
                    o = hT[:, m, w * W:(w + 1) * W]
                    if eng == "act":
                        P.act(o, xs[b][:, m, :], AF.Identity, scale=modP[:, si, 8 + m:9 + m], bias=modT[:, si, m:m + 1],
                              reads=[("xs", b), ("modT", l, si), ("modP", l, si)], writes=[("hT", m, w)])
                    else:
                        P.ts(eng, o, xs[b][:, m, :], modP[:, si, 8 + m:9 + m], modT[:, si, m:m + 1], ALU.mult, ALU.add,
                             reads=[("xs", b), ("modT", l, si), ("modP", l, si)], writes=[("hT", m, w)])
            hk = []
            for side, (a, b_) in enumerate(((t0 - 8, t0), (t0 + TS, t0 + TS + 8))):
                if a >= 0 and b_ <= T:
                    P.dma("sp", hal[:, :, side * 8:(side + 1) * 8], s.xa[:, a:b_].rearrange("(m p) t -> p m t", p=128), writes=[("hal", side)])
                    for m in range(8):
                        P.ts("dve", hT[:, m, TS + side * 8: TS + side * 8 + 8], hal[:, m, side * 8:(side + 1) * 8],
                             modP[:, si, 8 + m:9 + m], modT[:, si, m:m + 1], ALU.mult, ALU.add,
                             reads=[("hal", side), ("modT", l, si), ("modP", l, si)], writes=[("hT", m, "h%d" % side)])
                else:
                    for m in range(8):
                        P.memset("dve", hT[:, m, TS + side * 8: TS + side * 8 + 8], 0.0, writes=[("hT", m, "h%d" % side)])
            P.barrier()
            if getattr(self, 'a_stop', None) == 'ph0':
                return

            def proj_halo(ps_ap, wt, wkey, bank):
                for k in range(8):
                    P.mm(ps_ap, wt[:, k, :], hT[:, k, TS:TS + 16], start=(k == 0), stop=(k == 7),
                         reads=[wkey, ("hT", k, "h0"), ("hT", k, "h1")], writes=[self.psk(bank)])

            P.sb_off = mark
            rC = P.sb([128, TS], F32, "rC")
            rS = P.sb([128, TS], F32, "rS")
            P.dma("sp", rC[:], s.ropeC[:, t0:t0 + TS], writes=["rC"])
            P.dma("sp", rS[:], s.ropeS[:, t0:t0 + TS], writes=["rS"])
            wp = [P.sb([128, 8, 128], BF16, "wp") for _ in range(2)]
            sqb = [P.sb([128, W], F32, "sqb") for _ in range(2)]
            rs = [P.sb([128, W], F32, "rs") for _ in range(2)]
            t1 = [P.sb([128, W], F32, "t1") for _ in range(2)]
            t2 = [P.sb([128, W], F32, "t2") for _ in range(2)]
            qrow = [P.sb([128, TS], BF16, "qrow") for _ in range(2)]
            it = 0
            for hc in range(10):
                wq, wk = load_w(hc * 128)
                pb = hc % 2
                for q4, src0 in enumerate((32, 0, 96, 64)):
                    P.cp("pool", wp[pb][:, :, q4 * 32:(q4 + 1) * 32], wq[:, :, src0:src0 + 32], reads=[wk], writes=[("wp", pb, q4)])
                wpk = [("wp", pb, q4) for q4 in range(4)]
                gcol = self.V_QG if hc < 8 else self.V_KG
                r = hc % 2
                for w in range(NW):
                    i = it % 2
                    it += 1
                    bq, bp, bs = (0, 1, 4) if i == 0 else (2, 3, 5)
                    QL = 9
                    if QL < 2:
                        continue
                    self._proj(self.ps[bq][:, 0:W], wq, hT, w, W, wk, bq)
                    for k in range(8):
                        P.mm(self.ps[bp][:, 0:W], wp[pb][:, k, :], hT[:, k, w * W:(w + 1) * W], start=(k == 0), stop=(k == 7),
                             reads=wpk + [("hT", k, w)], writes=[self.psk(bp)])
                    if QL < 3:
                        continue
                    P.act(sqb[i][:], self.ps[bq][:, 0:W], AF.Square, reads=[self.psk(bq)], writes=[("sqb", i)])
                    P.mm(self.ps[bs][:, 0:W], self.ones_f[:], sqb[i][:], start=True, stop=True,
                         reads=["ones_f", ("sqb", i)], writes=[self.psk(bs)])
                    P.act(rs[i][:], self.ps[bs][:, 0:W], AF.Ln, scale=1.0 / 128.0, bias=self.epsT[:, 0:1],
                          reads=[self.psk(bs), "epsT"], writes=[("rs", i)])
                    P.act(rs[i][:], rs[i][:], AF.Exp, scale=-0.5, reads=[("rs", i)], writes=[("rs", i)])
                    if QL < 4:
                        continue
                    P.stt(t1[i][:], self.ps[bq][:, 0:W], vec[:, gcol:gcol + 1], rC[:, w * W:(w + 1) * W], ALU.mult, ALU.mult,
                          reads=[self.psk(bq), ("vecs", l), "rC"], writes=[("t1", i)])
                    P.stt(t2[i][:], self.ps[bp][:, 0:W], vec[:, gcol + 1:gcol + 2], rS[:, w * W:(w + 1) * W], ALU.mult, ALU.mult,
                          reads=[self.psk(bp), ("vecs", l), "rS"], writes=[("t2", i)])
                    if QL < 5:
                        continue
                    P.tt("pool", t1[i][:], t1[i][:], t2[i][:], ALU.add, reads=[("t1", i), ("t2", i)], writes=[("t1", i)])
                    P.tt("pool", qrow[r][:, w * W:(w + 1) * W], t1[i][:], rs[i][:], ALU.mult,
                         reads=[("t1", i), ("rs", i)], writes=[("qrow", r, w)])
                if hc < 8:
                    dst = s.qT[hc * 128:(hc + 1) * 128, t0:t0 + TS]
                else:
                    dst = self.kT_d[(hc - 8) * 128:(hc - 7) * 128, s.ktok0 + t0: s.ktok0 + t0 + TS]
                if QL >= 6:
                    P.dma("pool", dst, qrow[r][:], reads=[("qrow", r, w) for w in range(NW)])
            P.barrier()
            if getattr(self, 'a_stop', None) == 'qk':
                return

            P.sb_off = mark
            wv = P.sb([128, 8, 256], BF16, "wv")
            vrow = P.sb([128, NB, 256], BF16, "vrow")
            load_w(C_V, 256, wv, "wv")
            wvk = [("wv", 0), ("wv", 128)]
            for tb in range(NB):
                bank = tb % 4
                w = (tb * 128) // W
                for k in range(8):
                    P.mm(self.ps[bank][:, 0:256], hT[:, k, tb * 128:(tb + 1) * 128], wv[:, k, :], start=(k == 0), stop=(k == 7),
                         reads=wvk + [("hT", k, w)], writes=[self.psk(bank)])
                P.cp("act" if tb % 2 else "dve", vrow[:, tb, :], self.ps[bank][:, 0:256], reads=[self.psk(bank)], writes=[("vrow", tb)])
            P.dma("act", self.v_d[s.ktok0 + t0: s.ktok0 + t0 + TS, :].rearrange("(b p) c -> p b c", p=128), vrow[:],
                  reads=[("vrow", tb) for tb in range(NB)])
            P.barrier()
            if getattr(self, 'a_stop', None) == 'v':
                return

            P.sb_off = mark
            ubuf = [P.sb([128, TS + 2], F32, "ubuf") for _ in range(2)]
            sA = P.sb([128, TS], F32, "sA")
            sB = P.sb([128, TS], F32, "sB")
            zrow = P.sb([128, TS], BF16, "zrow")
            ztiles = [P.sb([128, NB, 128], BF16, "ztile") for _ in range(2)]
            uc = 0

            def z_transposes(j):
                ztile = ztiles[j % 2]
                for blk in range(NB):
                    bank = 6 + (blk // 8) % 2
                    pv = self.ps[bank][:].bitcast(BF16)
                    P.tr(pv[:, (blk % 8) * 128:(blk % 8 + 1) * 128], zrow[:, blk * 128:(blk + 1) * 128], self.identb[:],
                         reads=["zrow", "identb"], writes=[self.psk(bank)])
                    if blk % 8 == 7 or blk == NB - 1:
                        b0 = blk - blk % 8
                        n = blk - b0 + 1
                        P.cp("act", ztile[:, b0:b0 + n, :].rearrange("p b c -> p (b c)"), pv[:, 0:n * 128],
                             reads=[self.psk(bank)], writes=[("ztile", j % 2, b0)])
                P.dma("act", s.z[t0:t0 + TS, j * 128:(j + 1) * 128].rearrange("(b p) c -> p b c", p=128), ztile[:],
                      reads=[("ztile", j % 2, b0) for b0 in range(0, NB, 8)])

            zpend = None
            for j in range(8):
                for part, (cchunk, cm, dst, dk) in enumerate(((12 + j, j, sA, "sA"), (28 + j, 16 + j, sB, "sB"), (20 + j, 8 + j, sA, "sA"))):
                    ub = uc % 2
                    uc += 1
                    wt, wk = load_w(cchunk * 128)
                    ukeys = []
                    for w in range(NW):
                        bank = w % 4
                        self._proj(self.ps[bank][:, 0:W], wt, hT, w, W, wk, bank)
                        P.cp("act", ubuf[ub][:, 1 + w * W: 1 + (w + 1) * W], self.ps[bank][:, 0:W],
                             reads=[self.psk(bank)], writes=[("ubuf", ub, w)])
                        ukeys.append(("ubuf", ub, w))
                    proj_halo(self.ps[4][:, 0:16], wt, wk, 4)
                    P.cp("act", ubuf[ub][:, 0:TS + 2:TS + 1], self.ps[4][:, 7:9], reads=[self.psk(4)], writes=[("ubuf", ub, "h")])
                    ukeys.append(("ubuf", ub, "h"))
                    if part == 0 and zpend is not None:
                        z_transposes(zpend)
                        zpend = None
                    c0 = self.V_CW + cm
                    P.act(dst[:], ubuf[ub][:, 1:TS + 1], AF.Identity, scale=vec[:, c0 + 24:c0 + 25], bias=vec[:, self.V_CB + cm:self.V_CB + cm + 1],
                          reads=ukeys + [("vecs", l)], writes=[dk])
                    P.stt(dst[:], ubuf[ub][:, 0:TS], vec[:, c0:c0 + 1], dst[:], ALU.mult, ALU.add, reads=ukeys + [dk, ("vecs", l)], writes=[dk])
                    P.stt(dst[:], ubuf[ub][:, 2:TS + 2], vec[:, c0 + 48:c0 + 49], dst[:], ALU.mult, ALU.add, reads=ukeys + [dk, ("vecs", l)], writes=[dk])
                    if part == 1 and s.tag == "c":
                        P.tt("pool", sB[:], sA[:], sB[:], ALU.mult, reads=["sA", "sB"], writes=["sB"])
                        P.dma("pool", s.zT[j * 128:(j + 1) * 128, t0:t0 + TS], sB[:], reads=["sB"])
                    elif part == 1:
                        P.tt("pool", zrow[:], sA[:], sB[:], ALU.mult, reads=["sA", "sB"], writes=["zrow"])
                        zpend = j
                    if part == 2:
                        P.dma("pool", s.x0T[j * 128:(j + 1) * 128, t0:t0 + TS], sA[:], reads=["sA"])
            if zpend is not None:
                z_transposes(zpend)
            P.barrier()
            if getattr(self, 'a_stop', None) == 'hy':
                return

            P.sb_off = mark
            n = TS + 16
            pbuf = [P.sb([128, n], F32, "pbuf") for _ in range(2)]
            A = P.sb([128, n], F32, "pA")
            Bb = P.sb([128, n], F32, "pB")
            mT = [P.sb([128, TS], BF16, "mT") for _ in range(2)]
            prow = [P.sb([128, TS], BF16, "prow") for _ in range(2)]
            pw = P.sb([128, 2, 256], BF16, "pw")
            pwf = P.sb([128, 2, 256], F32, "pwf")
            tmp8 = P.sb([128, 8], F32, "tmp8")
            pe4 = self.pedge[:].rearrange("p (g s e) -> p g s e", g=4, s=2)
            for g in range(4):
                wsz = POOL_WINDOWS[g]
                kk = g + 1
                o = 8 + wsz // 2 - 1
                P.dma("sp", pwf[:], self.inp["pool_w"][l, g].rearrange("(i p) o -> p i o", p=128), writes=["pwf"])
                P.cp("pool", pw[:], pwf[:], reads=["pwf"], writes=["pw"])
                for i in range(2):
                    wt, wk = load_w((36 + 2 * g + i) * 128)
                    pk = []
                    for w in range(NW):
                        bank = w % 4
                        self._proj(self.ps[bank][:, 0:W], wt, hT, w, W, wk, bank)
                        P.cp("act" if w % 2 else "dve", pbuf[i][:, 8 + w * W: 8 + (w + 1) * W], self.ps[bank][:, 0:W],
                             reads=[self.psk(bank)], writes=[("pbuf", i, w)])
                        pk.append(("pbuf", i, w))
                    proj_halo(self.ps[4][:, 0:16], wt, wk, 4)
                    P.cp("dve", pbuf[i][:, 0:8], self.ps[4][:, 0:8], reads=[self.psk(4)], writes=[("pbuf", i, "h0")])
                    P.cp("dve", pbuf[i][:, TS + 8:TS + 16], self.ps[4][:, 8:16], reads=[self.psk(4)], writes=[("pbuf", i, "h1")])
                    pk += [("pbuf", i, "h0"), ("pbuf", i, "h1")]
                    u = pbuf[i]
                    P.tt("pool", A[:, 1:n], u[:, 1:n], u[:, 0:n - 1], ALU.add, reads=pk, writes=["pA"])
                    R, rk = A, "pA"
                    if kk >= 2:
                        P.tt("pool", Bb[:, 3:n], A[:, 3:n], A[:, 1:n - 2], ALU.add, reads=["pA"], writes=["pB"])
                        R, rk = Bb, "pB"
                    if kk >= 3:
                        P.tt("pool", A[:, 7:n], Bb[:, 7:n], Bb[:, 3:n - 4], ALU.add, reads=["pB"], writes=["pA"])
                        R, rk = A, "pA"
                    if kk >= 4:
                        P.tt("pool", Bb[:, 15:n], A[:, 15:n], A[:, 7:n - 8], ALU.add, reads=["pA"], writes=["pB"])
                        R, rk = Bb, "pB"
                    P.stt(mT[i][:], R[:, o:o + TS], 1.0 / wsz, u[:, 8:8 + TS], ALU.mult, ALU.subtract, reads=[rk] + pk, writes=[("mT", i)])
                    if t0 == 0:
                        P.tt("dve", tmp8[:], R[:, o:o + 8], pe4[:, g, 0, :], ALU.mult, reads=[rk, "pedge"], writes=["tmp8"])
                        P.tt("dve", mT[i][:, 0:8], tmp8[:], u[:, 8:16], ALU.subtract, reads=["tmp8"] + pk, writes=[("mT", i)])
                    if t0 + TS == T:
                        P.tt("dve", tmp8[:], R[:, o + TS - 8:o + TS], pe4[:, g, 1, :], ALU.mult, reads=[rk, "pedge"], writes=["tmp8"])
                        P.tt("dve", mT[i][:, TS - 8:TS], tmp8[:], u[:, TS:TS + 8], ALU.subtract, reads=["tmp8"] + pk, writes=[("mT", i)])
                for oc in range(2):
                    for w in range(NW):
                        bank = w % 4
                        for i in range(2):
                            P.mm(self.ps[bank][:, 0:W], pw[:, i, oc * 128:(oc + 1) * 128], mT[i][:, w * W:(w + 1) * W],
                                 start=(i == 0), stop=(i == 1), reads=["pw", ("mT", i)], writes=[self.psk(bank)])
                        cidx = self.V_PS + 2 * g + oc
                        P.act(prow[oc][:, w * W:(w + 1) * W], self.ps[bank][:, 0:W], AF.Identity, scale=vec[:, cidx:cidx + 1],
                              reads=[self.psk(bank), ("vecs", l)], writes=[("prow", oc, w)])
                    P.dma("act", s.poolT[(2 * g + oc) * 128:(2 * g + oc + 1) * 128, t0:t0 + TS], prow[oc][:],
                          reads=[("prow", oc, w) for w in range(NW)])
            P.barrier()
            if getattr(self, 'a_stop', None) == 'pool':
                return

            P.sb_off = mark
            grow = [P.sb([128, TS], BF16, "grow") for _ in range(2)]
            for gc in range(24):
                r = gc % 2
                wt, wk = load_w((44 + gc) * 128)
                for w in range(NW):
                    bank = w % 4
                    self._proj(self.ps[bank][:, 0:W], wt, hT, w, W, wk, bank)
                    P.act(grow[r][:, w * W:(w + 1) * W], self.ps[bank][:, 0:W], AF.Sigmoid, reads=[self.psk(bank)], writes=[("grow", r, w)])
                P.dma("act", s.gT[gc // 8, (gc % 8) * 128:(gc % 8 + 1) * 128, t0:t0 + TS], grow[r][:],
                      reads=[("grow", r, w) for w in range(NW)])
            P.barrier()
            if getattr(self, 'a_stop', None) == 'gate':
                return

    def stage_b(self, l, s, NK):
        P = self.P
        P.sb_reset()
        NQ, W = s.T, s.W
        NB = NK // 128
        KT = P.sb([128, 2, NK], BF16, "KT")
        V = P.sb([128, NB, 256], BF16, "V")
        for kv in range(2):
            P.dma("sp" if kv else "act", KT[:, kv, :], self.kT_d[kv * 128:(kv + 1) * 128, 0:NK], writes=[("KT", kv)])
        vsrc = self.v_d[0:NK, :].rearrange("(b p) c -> p b c", p=128)
        vk = []
        for b0 in range(0, NB, 11):
            b1 = min(NB, b0 + 11)
            P.dma("sp", V[:, b0:b1, :], vsrc[:, b0:b1, :], writes=[("V", b0)])
            vk.append(("V", b0))
        QT = [P.sb([128, 8, W], BF16, "QT") for _ in range(2)]
        attT = [P.sb([128, 8, W], BF16, "attT") for _ in range(2)]
        NPT = 8
        pT = [P.sb([128, W], BF16, "pT") for _ in range(NPT)]
        pool_tbs = [tb for tb in range(NB) if tb % 8 in (1, 4, 6)]
        dve_tbs = [tb for tb in range(NB) if tb % 8 not in (1, 4, 6)]
        rden = [P.sb([128, W], F32, "rden") for _ in range(2)]
        accD = [P.sb([128, W], F32, "accD") for _ in range(2)]
        accP = [P.sb([128, W], F32, "accP") for _ in range(2)]
        scale = float(HD) ** -0.5
        steps = [(qw, h, tb) for qw in range(NQ // W) for h in range(8) for tb in range(NB)]
        n = len(steps)

        def issue_S(i):
            qw, h, tb = steps[i]
            r = i % 4
            if h == 0 and tb == 0:
                P.dma("sp", QT[qw % 2][:], s.qT[:, qw * W:(qw + 1) * W].rearrange("(h p) t -> p h t", p=128), writes=[("QT", qw % 2)])
            P.mm(self.ps[r][:, 0:W], KT[:, h // 4, tb * 128:(tb + 1) * 128], QT[qw % 2][:, h, :], True, True,
                 reads=[("KT", h // 4), ("QT", qw % 2)], writes=[self.psk(r)])

        LA = 3
        for i in range(min(LA, n)):
            issue_S(i)
        for i, (qw, h, tb) in enumerate(steps):
            r = i % 4
            r4 = i % NPT
            kv = h // 4
            hp = h % 2
            ob = 4 + hp
            P.act(pT[r4][:], self.ps[r][:, 0:W], AF.Exp, scale=scale, reads=[self.psk(r)], writes=[("pT", r4)])
            P.mm(self.ps[ob][:, 0:W], V[:, tb, kv * 128:(kv + 1) * 128], pT[r4][:], tb == 0, tb == NB - 1,
                 reads=vk + [("pT", r4)], writes=[self.psk(ob)])
            ab = 6 + hp
            if tb in dve_tbs:
                if tb == dve_tbs[0] and tb == dve_tbs[-1]:
                    P.cp("dve", accD[hp][:], pT[r4][:], reads=[("pT", r4)], writes=[("accD", hp)])
                elif tb == dve_tbs[0]:
                    P.cp("dve", self.ps[ab][:, 0:W], pT[r4][:], reads=[("pT", r4)], writes=[self.psk(ab)])
                elif tb != dve_tbs[-1]:
                    P.tt("dve", self.ps[ab][:, 0:W], self.ps[ab][:, 0:W], pT[r4][:], ALU.add, reads=[("pT", r4), self.psk(ab)], writes=[self.psk(ab)])
                else:
                    P.tt("dve", accD[hp][:], self.ps[ab][:, 0:W], pT[r4][:], ALU.add, reads=[("pT", r4), self.psk(ab)], writes=[("accD", hp)])
            else:
                if tb == pool_tbs[0]:
                    P.cp("pool", accP[hp][:], pT[r4][:], reads=[("pT", r4)], writes=[("accP", hp)])
                else:
                    P.tt("pool", accP[hp][:], accP[hp][:], pT[r4][:], ALU.add, reads=[("pT", r4), ("accP", hp)], writes=[("accP", hp)])
            if i + LA < n:
                issue_S(i + LA)
            if tb == NB - 1:
                rd = rden[hp]
                db = ab
                P.mm(self.ps[db][:, 0:W], self.ones_f[:], accD[hp][:], True, False, reads=[("accD", hp)], writes=[self.psk(db)])
                P.mm(self.ps[db][:, 0:W], self.ones_f[:], accP[hp][:], False, True, reads=[("accP", hp)], writes=[self.psk(db)])
                P.add("dve", lambda e, rd=rd, db=db: e.reciprocal(out=rd[:], in_=self.ps[db][:, 0:W]), reads=[self.psk(db)], writes=[("rden", hp)])
                P.tt("dve", attT[qw % 2][:, h, :], self.ps[ob][:, 0:W], rd[:], ALU.mult,
                     reads=[self.psk(ob), ("rden", hp)], writes=[("attT", qw % 2, h)])
                if h == 7:
                    P.dma("pool", s.attnT[:, qw * W:(qw + 1) * W].rearrange("(h p) t -> p h t", p=128), attT[qw % 2][:],
                          reads=[("attT", qw % 2, hh) for hh in range(8)])
        P.barrier()

    def _sin_layer(self, ps_ap, fcol, bcol, hv, tmp, tmp2, out_ap, psbank, okey):
        P = self.P
        MAGIC = 12582912.0
        P.ts("dve", tmp, ps_ap, hv[:, fcol:fcol + 1], hv[:, bcol:bcol + 1], ALU.mult, ALU.add, reads=[self.psk(psbank)], writes=["sl_tmp"])
        P.ts("dve", tmp2, tmp, 1.0 / (2 * math.pi), MAGIC, ALU.mult, ALU.add, reads=["sl_tmp"], writes=["sl_tmp2"])
        P.ts("dve", tmp2, tmp2, MAGIC, -2 * math.pi, ALU.subtract, ALU.mult, reads=["sl_tmp2"], writes=["sl_tmp2"])
        P.tt("dve", tmp, tmp, tmp2, ALU.add, reads=["sl_tmp", "sl_tmp2"], writes=["sl_tmp"])
        P.act(out_ap, tmp, AF.Sin, reads=["sl_tmp"], writes=[okey])

    def stage_k(self, l, tag):
        P = self.P
        P.sb_reset()
        I = self.inp
        hv = self.hfv[l]
        Kf = self.Kf[tag]
        w1s = P.sb([33, 64], F32, "w1s")
        w2s = P.sb([64, 64], F32, "w2s")
        w3f = P.sb([64, 2048], F32, "w3f")
        w3b = P.sb([64, 2048], BF16, "w3b")
        h2T = [P.sb([64, 8192], BF16, "h2T") for _ in range(2)]
        dl = P.sb([128, 1024], F32, "dl")
        drow = P.sb([128, 1024], F32, "drow")
        nrow = P.sb([128, 1024], F32, "nrow")
        negt = P.sb([128, 128], F32, "negt")
        mask = P.sb([128, 128], F32, "mask")
        F1 = P.sb([128, 2, 128], BF16, "F1")
        P.dma("sp", w1s[:], I["hf_w1"][l], writes=["w1s"])
        P.dma("sp", w2s[:], I["hf_w2"][l], writes=["w2s"])
        P.dma("sp", w3f[:], I["hf_w3"][l], writes=["w3f"])
        P.cp("pool", w3b[:], w3f[:], reads=["w3f"], writes=["w3b"])
        P.dma("act", dl[:], self.cst["deltas"].partition_broadcast(128).rearrange("p o c -> p (o c)"), writes=["dl"])
        P.dma("act", drow[:], I["hy_d"][l].rearrange("(o c) -> o c", o=1).partition_broadcast(128).rearrange("p o c -> p (o c)"), writes=["drow"])
        P.dma("act", negt[:], self.cst[f"negt_{tag}"], writes=["negt"])
        P.dma("act", mask[:], self.cst[f"mask_{tag}"], writes=["mask"])
        P.dma("act", F1[:], self.cst["F1"], writes=["F1"])
        mark = P.sb_off
        ft = [P.sb([33, 512], F32, "ft") for _ in range(2)]
        tmp = P.sb([64, 512], F32, "sl_tmp")
        tmp2 = P.sb([64, 512], F32, "sl_tmp2")
        h1 = P.sb([64, 512], F32, "h1")
        it = 0
        for d, nm in enumerate((f"featF_{tag}", f"featR_{tag}")):
            for w in range(16):
                b = it % 2
                it += 1
                P.dma("sp", ft[b][:], self.cst[nm][:, w * 512:(w + 1) * 512], writes=[("ft", b)])
                P.mm(self.ps[0][0:64, 0:512], w1s[:], ft[b][:], True, True, reads=["w1s", ("ft", b)], writes=[self.psk(0)])
                self._sin_layer(self.ps[0][0:64, 0:512], 1, 3, hv, tmp[:], tmp2[:], h1[:], 0, "h1")
                P.mm(self.ps[1][0:64, 0:512], w2s[:], h1[:], True, True, reads=["w2s", "h1"], writes=[self.psk(1)])
                self._sin_layer(self.ps[1][0:64, 0:512], 1, 4, hv, tmp[:], tmp2[:], h2T[d][:, w * 512:(w + 1) * 512], 1, ("h2T", d))
        P.sb_off = mark
        kt = [P.sb([128, 8, 1024], BF16, "kt") for _ in range(2)]
        wn = [P.sb([128, 1024], F32, "wn") for _ in range(2)]
        sqb = [P.sb([128, 1024], BF16, "sqk") for _ in range(2)]
        Bt = [[P.sb([128, 8, 1024], BF16, "Btk") for _ in range(2)] for _ in range(2)]
        for jg in range(16):
            kb = jg % 2
            for n2i in range(8):
                n2 = jg * 8 + n2i
                i = n2 % 2
                ba = 0 if i == 0 else 2
                for ch in range(2):
                    P.mm(self.ps[ba + ch][0:64, 0:512], h2T[0][:, n2:8192:128], w3b[:, ch * 512:(ch + 1) * 512], True, True,
                         reads=[("h2T", 0), "w3b"], writes=[self.psk(ba + ch)])
                    P.mm(self.ps[ba + ch][64:128, 0:512], h2T[1][:, n2:8192:128], w3b[:, 1024 + ch * 512:1024 + (ch + 1) * 512], True, True,
                         reads=[("h2T", 1), "w3b"], writes=[self.psk(ba + ch)])
                P.act(wn[i][:], dl[:], AF.Exp, scale=negt[:, n2:n2 + 1], reads=["dl", "negt"], writes=[("wn", i)])
                P.ts("dve", wn[i][:], wn[i][:], 0.05, None, ALU.add, None, reads=[("wn", i)], writes=[("wn", i)])
                for ch in range(2):
                    P.stt(kt[kb][:, n2i, ch * 512:(ch + 1) * 512], self.ps[ba + ch][:, 0:512], mask[:, n2:n2 + 1], wn[i][:, ch * 512:(ch + 1) * 512],
                          ALU.mult, ALU.mult, reads=[self.psk(ba + ch), "mask", ("wn", i)], writes=[("kt", kb, n2i)])
                P.act(sqb[i][:], kt[kb][:, n2i, :], AF.Square, reads=[("kt", kb, n2i)], writes=[("sqk", i)])
                for ch in range(2):
                    P.mm(self.ps[6 + ch][:, 0:512], self.ones_b[:], sqb[i][:, ch * 512:(ch + 1) * 512], n2 == 0, n2 == 127,
                         reads=[("sqk", i)], writes=[self.psk(6 + ch)])
            ktf = kt[kb][:].rearrange("p a c -> p (a c)")
            for ri in range(2):
                btf = Bt[ri][kb][:].rearrange("p a c -> p (a c)")
                for cw in range(16):
                    bank = 4 + (cw % 2)
                    P.mm(self.ps[bank][:, 0:512], F1[:, ri, :], ktf[:, cw * 512:(cw + 1) * 512], True, True,
                         reads=["F1"] + [("kt", kb, q) for q in range(8)], writes=[self.psk(bank)])
                    P.cp("act" if cw % 2 else "dve", btf[:, cw * 512:(cw + 1) * 512], self.ps[bank][:, 0:512],
                         reads=[self.psk(bank)], writes=[("Btk", ri, kb, cw)])
                P.dma("pool", self.Bd[ri][:, jg * 8:(jg + 1) * 8, :], Bt[ri][kb][:],
                      reads=[("Btk", ri, kb, cw) for cw in range(16)], writes=[("Bd", ri, jg)])
        for ch in range(2):
            P.act(nrow[:, ch * 512:(ch + 1) * 512], self.ps[6 + ch][:, 0:512], AF.Ln, bias=self.epsT[:, 0:1], reads=[self.psk(6 + ch)], writes=[("nrow", ch)])
            P.act(nrow[:, ch * 512:(ch + 1) * 512], nrow[:, ch * 512:(ch + 1) * 512], AF.Exp, scale=-0.5, reads=[("nrow", ch)], writes=[("nrow", ch)])
        P.barrier()
        P.sb_off = mark
        Br = [[P.sb([128, 1024], BF16, "Brk") for _ in range(2)] for _ in range(2)]
        G = [P.sb([128, 3, 128], BF16, "Gk") for _ in range(2)]
        Kt = [[P.sb([128, 1024], BF16, "Kt") for _ in range(2)] for _ in range(2)]
        tz = [P.sb([128, 512], F32, "tz") for _ in range(2)]
        for k1 in range(128):
            b = k1 % 2
            for ri in range(2):
                P.dma("sp", Br[ri][b][:], self.Bd[ri][k1], writes=[("Brk", ri, b)])
            P.dma("sp", G[b][:], self.cst["GT"][k1], writes=[("Gk", b)])
            for ch in range(2):
                bs = 0 if (2 * k1 + ch) % 2 == 0 else 2
                cs = slice(ch * 512, (ch + 1) * 512)
                rk = [("Brk", 0, b), ("Brk", 1, b), ("Gk", b)]
                P.mm(self.ps[bs][:, 0:512], G[b][:, 0, :], Br[0][b][:, cs], True, False, reads=rk, writes=[self.psk(bs)])
                P.mm(self.ps[bs][:, 0:512], G[b][:, 2, :], Br[1][b][:, cs], False, True, reads=rk, writes=[self.psk(bs)])
                P.mm(self.ps[bs + 1][:, 0:512], G[b][:, 1, :], Br[0][b][:, cs], True, False, reads=rk, writes=[self.psk(bs + 1)])
                P.mm(self.ps[bs + 1][:, 0:512], G[b][:, 0, :], Br[1][b][:, cs], False, True, reads=rk, writes=[self.psk(bs + 1)])
                P.tt("dve", tz[ch][:], self.ps[bs][:, 0:512], nrow[:, cs], ALU.mult, reads=[self.psk(bs), ("nrow", ch)], writes=[("tz", ch)])
                P.tt("pool", Kt[0][b][:, cs], tz[ch][:], drow[:, cs], ALU.add, reads=[("tz", ch), "drow"], writes=[("Kt", 0, b, ch)])
                P.tt("dve", Kt[1][b][:, cs], self.ps[bs + 1][:, 0:512], nrow[:, cs], ALU.mult, reads=[self.psk(bs + 1), ("nrow", ch)], writes=[("Kt", 1, b, ch)])
            for ri in range(2):
                P.dma("pool", Kf[ri][k1], Kt[ri][b][:], reads=[("Kt", ri, b, 0), ("Kt", ri, b, 1)])
        P.barrier()

    def stage_h(self, l, s):
        P = self.P
        P.sb_reset()
        Kf = self.Kf[s.tag]
        nval = 64 if s.tag == "l" else s.T // 128
        F1 = P.sb([128, 2, 128], BF16, "F1")
        I2 = P.sb([128, 2, 64], BF16, "I2")
        P.dma("act", F1[:], self.cst["F1"], writes=["F1"])
        P.dma("act", I2[:], self.cst["I2"], writes=["I2"])
        mark = P.sb_off
        zt = [P.sb([64, 8, 1024], BF16, "zt") for _ in range(2)]
        Bt = [[P.sb([128, 8, 1024], BF16, "Bth") for _ in range(2)] for _ in range(2)]
        zv = s.z.rearrange("(a b) c -> a b c", b=128)
        if nval < 64:
            for b in range(2):
                P.memset("pool", zt[b][:], 0.0, writes=[("zt", b)])
        for jg in range(16):
            b = jg % 2
            P.dma("sp", zt[b][0:nval, :, :], zv[0:nval, jg * 8:(jg + 1) * 8, :], reads=[("zt", b)] if nval < 64 else [], writes=[("ztd", b)])
            ztf = zt[b][:].rearrange("p a c -> p (a c)")
            for ri in range(2):
                btf = Bt[ri][b][:].rearrange("p a c -> p (a c)")
                for cw in range(16):
                    bank = (cw % 4)
                    P.mm(self.ps[bank][:, 0:512], F1[0:64, ri, :], ztf[:, cw * 512:(cw + 1) * 512], True, True,
                         reads=["F1", ("ztd", b), ("zt", b)], writes=[self.psk(bank)])
                    P.cp("act" if cw % 2 else "dve", btf[:, cw * 512:(cw + 1) * 512], self.ps[bank][:, 0:512],
                         reads=[self.psk(bank)], writes=[("Bth", ri, b, cw)])
                P.dma("pool", self.Bd[ri][:, jg * 8:(jg + 1) * 8, :], Bt[ri][b][:],
                      reads=[("Bth", ri, b, cw) for cw in range(16)], writes=[("Bd", ri, jg)])
        P.barrier()
        P.sb_off = mark
        Br = [[P.sb([128, 1024], BF16, "Brh") for _ in range(2)] for _ in range(2)]
        Kt = [[P.sb([128, 1024], BF16, "Kth") for _ in range(2)] for _ in range(2)]
        G = [P.sb([128, 3, 128], BF16, "Gh") for _ in range(2)]
        Hh = [P.sb([128, 3, 128], BF16, "Hh") for _ in range(2)]
        Y = [[P.sb([128, 512], BF16, "Yh") for _ in range(2)] for _ in range(2)]
        tq = [[P.sb([128, 512], F32, "tq") for _ in range(4)] for _ in range(2)]
        Dt = [[P.sb([128, 1024], BF16, "Dth") for _ in range(2)] for _ in range(2)]
        def second_half(k1, ch):
            b = k1 % 2
            par = ch
            bs = 0 if par == 0 else 4
            cs = slice(ch * 512, (ch + 1) * 512)
            yk = [("Yh", 0, par), ("Yh", 1, par), ("Hh", b)]
            P.mm(self.ps[bs + 2][:, 0:512], Hh[b][:, 0, :], Y[0][par][:], True, False, reads=yk, writes=[self.psk(bs + 2)])
            P.mm(self.ps[bs + 2][:, 0:512], Hh[b][:, 1, :], Y[1][par][:], False, True, reads=yk, writes=[self.psk(bs + 2)])
            P.mm(self.ps[bs + 3][:, 0:512], Hh[b][:, 2, :], Y[0][par][:], True, False, reads=yk, writes=[self.psk(bs + 3)])
            P.mm(self.ps[bs + 3][:, 0:512], Hh[b][:, 0, :], Y[1][par][:], False, True, reads=yk, writes=[self.psk(bs + 3)])
            P.cp("act", Dt[0][b][:, cs], self.ps[bs + 2][:, 0:512], reads=[self.psk(bs + 2)], writes=[("Dth", 0, b, ch)])
            P.cp("act", Dt[1][b][:, cs], self.ps[bs + 3][:, 0:512], reads=[self.psk(bs + 3)], writes=[("Dth", 1, b, ch)])
            if ch == 1:
                for ri in range(2):
                    P.dma("act", self.Dd[ri][k1], Dt[ri][b][:], reads=[("Dth", ri, b, 0), ("Dth", ri, b, 1)])

        prev = None
        for k1 in range(128):
            b = k1 % 2
            for ri in range(2):
                P.dma("sp", Br[ri][b][:], self.Bd[ri][k1], writes=[("Brh", ri, b)])
                P.dma("sp", Kt[ri][b][:], Kf[ri][k1], writes=[("Kth", ri, b)])
            P.dma("sp", G[b][:], self.cst["GT"][k1], writes=[("Gh", b)])
            P.dma("sp", Hh[b][:], self.cst["HT"][k1], writes=[("Hh", b)])
            for ch in range(2):
                par = ch
                bs = 0 if par == 0 else 4
                cs = slice(ch * 512, (ch + 1) * 512)
                rk = [("Brh", 0, b), ("Brh", 1, b), ("Gh", b)]
                P.mm(self.ps[bs][:, 0:512], G[b][:, 0, :], Br[0][b][:, cs], True, False, reads=rk, writes=[self.psk(bs)])
                P.mm(self.ps[bs][:, 0:512], G[b][:, 2, :], Br[1][b][:, cs], False, True, reads=rk, writes=[self.psk(bs)])
                P.mm(self.ps[bs + 1][:, 0:512], G[b][:, 1, :], Br[0][b][:, cs], True, False, reads=rk, writes=[self.psk(bs + 1)])
                P.mm(self.ps[bs + 1][:, 0:512], G[b][:, 0, :], Br[1][b][:, cs], False, True, reads=rk, writes=[self.psk(bs + 1)])
                if prev is not None:
                    second_half(*prev)
                t = tq[par]
                kk = [("Kth", 0, b), ("Kth", 1, b)]
                P.tt("dve", t[0][:], self.ps[bs][:, 0:512], Kt[0][b][:, cs], ALU.mult, reads=[self.psk(bs)] + kk, writes=[("tq", par, 0)])
                P.tt("dve", t[1][:], self.ps[bs + 1][:, 0:512], Kt[1][b][:, cs], ALU.mult, reads=[self.psk(bs + 1)] + kk, writes=[("tq", par, 1)])
                P.tt("dve", t[2][:], self.ps[bs][:, 0:512], Kt[1][b][:, cs], ALU.mult, reads=[self.psk(bs)] + kk, writes=[("tq", par, 2)])
                P.tt("dve", t[3][:], self.ps[bs + 1][:, 0:512], Kt[0][b][:, cs], ALU.mult, reads=[self.psk(bs + 1)] + kk, writes=[("tq", par, 3)])
                P.tt("pool", Y[0][par][:], t[0][:], t[1][:], ALU.subtract, reads=[("tq", par, 0), ("tq", par, 1)], writes=[("Yh", 0, par)])
                P.tt("pool", Y[1][par][:], t[2][:], t[3][:], ALU.add, reads=[("tq", par, 2), ("tq", par, 3)], writes=[("Yh", 1, par)])
                prev = (k1, ch)
        second_half(*prev)
        P.barrier()
        P.sb_off = mark
        Dr = [[P.sb([128, 8, 1024], BF16, "Drh") for _ in range(2)] for _ in range(2)]
        yt = [P.sb([64, 8, 1024], F32, "yt") for _ in range(2)]
        yv = s.y.rearrange("(a b) c -> a b c", b=128)
        for jg in range(16):
            b = jg % 2
            for ri in range(2):
                P.dma("sp", Dr[ri][b][:], self.Dd[ri][:, jg * 8:(jg + 1) * 8, :], writes=[("Drh", ri, b)])
            d0 = Dr[0][b][:].rearrange("p a c -> p (a c)")
            d1 = Dr[1][b][:].rearrange("p a c -> p (a c)")
            ytf = yt[b][:].rearrange("p a c -> p (a c)")
            for cw in range(16):
                bank = cw % 4
                cs = slice(cw * 512, (cw + 1) * 512)
                P.mm(self.ps[bank][0:64, 0:512], I2[:, 0, :], d0[:, cs], True, False, reads=["I2", ("Drh", 0, b), ("Drh", 1, b)], writes=[self.psk(bank)])
                P.mm(self.ps[bank][0:64, 0:512], I2[:, 1, :], d1[:, cs], False, True, reads=["I2", ("Drh", 0, b), ("Drh", 1, b)], writes=[self.psk(bank)])
                P.cp("act" if cw % 2 else "dve", ytf[:, cs], self.ps[bank][0:64, 0:512], reads=[self.psk(bank)], writes=[("yt", b, cw)])
            P.dma("pool", yv[0:nval, jg * 8:(jg + 1) * 8, :], yt[b][0:nval, :, :], reads=[("yt", b, cw) for cw in range(16)])
        P.barrier()

    def stage_hc(self, l, s):
        P = self.P
        P.sb_reset()
        I = self.inp
        hv, vec = self.hfv[l], self.vecs[l]
        L = s.T
        w1s = P.sb([33, 64], F32, "w1s")
        w2s = P.sb([64, 64], F32, "w2s")
        w3f = P.sb([64, 2048], F32, "w3f")
        ft = P.sb([33, L], F32, "ft")
        tmp = P.sb([64, L], F32, "sl_tmp")
        tmp2 = P.sb([64, L], F32, "sl_tmp2")
        h1 = P.sb([64, L], F32, "h1")
        h2 = P.sb([64, L], F32, "h2")
        t01 = P.sb([128, L], F32, "t01")
        ndl = P.sb([128, 8], F32, "ndl")
        P.dma("sp", w1s[:], I["hf_w1"][l], writes=["w1s"])
        P.dma("sp", w2s[:], I["hf_w2"][l], writes=["w2s"])
        P.dma("sp", w3f[:], I["hf_w3"][l], writes=["w3f"])
        P.dma("act", ft[:], self.cst["featF_c"][:, 0:L], writes=["ft"])
        P.dma("act", t01[:], self.cst["t01row_c"], writes=["t01"])
        P.dma("act", ndl[:], self.cst["ndeltasT"], writes=["ndl"])
        P.mm(self.ps[0][0:64, 0:L], w1s[:], ft[:], True, True, reads=["w1s", "ft"], writes=[self.psk(0)])
        self._sin_layer(self.ps[0][0:64, 0:L], 1, 3, hv, tmp[:], tmp2[:], h1[:], 0, "h1")
        P.mm(self.ps[1][0:64, 0:L], w2s[:], h1[:], True, True, reads=["w2s", "h1"], writes=[self.psk(1)])
        self._sin_layer(self.ps[1][0:64, 0:L], 1, 4, hv, tmp[:], tmp2[:], h2[:], 1, "h2")
        NP = 3 * L - 2
        zp = [P.sb([128, NP], F32, "zp") for _ in range(2)]
        win = [P.sb([128, L], F32, "win") for _ in range(2)]
        kF = [P.sb([128, L], F32, "kF") for _ in range(2)]
        kB = [P.sb([128, L], F32, "kB") for _ in range(2)]
        junk = P.sb([128, L], F32, "junk")
        ss = [P.sb([128, 4], F32, "ss") for _ in range(2)]
        accF = [P.sb([128, L], F32, "accF") for _ in range(2)]
        accB = [P.sb([128, L], F32, "accB") for _ in range(2)]
        x0c = [P.sb([128, L], F32, "x0c") for _ in range(2)]
        hyo = [P.sb([128, L], BF16, "hyo") for _ in range(2)]
        for b in range(2):
            P.memset("pool", zp[b][:], 0.0, writes=[("zp", b)])
        for j in range(8):
            b = j % 2
            bf, bb = (2, 3) if b == 0 else (4, 5)
            P.dma("sp", zp[b][:, L - 1:2 * L - 1], s.zT[j * 128:(j + 1) * 128, :], reads=[("zp", b)], writes=[("zpd", b)])
            P.dma("act", x0c[b][:], s.x0T[j * 128:(j + 1) * 128, :], writes=[("x0c", b)])
            P.mm(self.ps[bf][:, 0:L], w3f[:, j * 128:(j + 1) * 128], h2[:], True, True, reads=["w3f", "h2"], writes=[self.psk(bf)])
            P.mm(self.ps[bb][:, 0:L], w3f[:, 1024 + j * 128:1024 + (j + 1) * 128], h2[:], True, True, reads=["w3f", "h2"], writes=[self.psk(bb)])
            P.act(win[b][:], t01[:], AF.Exp, scale=ndl[:, j:j + 1], reads=["t01", "ndl"], writes=[("win", b)])
            P.ts("pool", win[b][:], win[b][:], 1.0, 0.05, ALU.mult, ALU.add, reads=[("win", b)], writes=[("win", b)])
            P.tt("dve", kF[b][:], self.ps[bf][:, 0:L], win[b][:], ALU.mult, reads=[self.psk(bf), ("win", b)], writes=[("kF", b)])
            P.tt("dve", kB[b][:], self.ps[bb][:, 0:L], win[b][:], ALU.mult, reads=[self.psk(bb), ("win", b)], writes=[("kB", b)])
            P.memset("dve", kB[b][:, 0:1], 0.0, writes=[("kB", b)])
            P.add("act", lambda e, b=b: e.activation(out=junk[:], in_=kF[b][:], func=AF.Square, accum_out=ss[b][:, 0:1]),
                  reads=[("kF", b)], writes=[("ss", b, 0), "junk"])
            P.add("act", lambda e, b=b: e.activation(out=junk[:], in_=kB[b][:], func=AF.Square, accum_out=ss[b][:, 1:2]),
                  reads=[("kB", b)], writes=[("ss", b, 1), "junk"])
            P.tt("pool", ss[b][:, 2:3], ss[b][:, 0:1], ss[b][:, 1:2], ALU.add, reads=[("ss", b, 0), ("ss", b, 1)], writes=[("ss", b, 2)])
            P.act(ss[b][:, 2:3], ss[b][:, 2:3], AF.Ln, bias=self.epsT[:, 0:1], reads=[("ss", b, 2)], writes=[("ss", b, 2)])
            P.act(ss[b][:, 3:4], ss[b][:, 2:3], AF.Exp, scale=-0.5, reads=[("ss", b, 2)], writes=[("ss", b, 3)])
            zk = [("zp", b), ("zpd", b)]
            P.ts("dve", accF[b][:], zp[b][:, L - 1:2 * L - 1], kF[b][:, 0:1], None, ALU.mult, None, reads=zk + [("kF", b)], writes=[("accF", b)])
            P.ts("dve", accB[b][:], zp[b][:, L:2 * L], kB[b][:, 1:2], None, ALU.mult, None, reads=zk + [("kB", b)], writes=[("accB", b)])
            for m in range(1, L):
                P.stt(accF[b][:], zp[b][:, L - 1 - m:2 * L - 1 - m], kF[b][:, m:m + 1], accF[b][:], ALU.mult, ALU.add,
                      reads=[("accF", b)], writes=[("accF", b)])
                if m >= 2:
                    P.stt(accB[b][:], zp[b][:, L - 1 + m:2 * L - 1 + m], kB[b][:, m:m + 1], accB[b][:], ALU.mult, ALU.add,
                          reads=[("accB", b)], writes=[("accB", b)])
            P.tt("pool", accF[b][:], accF[b][:], accB[b][:], ALU.add, reads=[("accF", b), ("accB", b)], writes=[("accF", b)])
            P.ts("pool", accB[b][:], zp[b][:, L - 1:2 * L - 1], vec[:, self.V_HD + j:self.V_HD + j + 1], 1.0, ALU.mult, ALU.mult,
                 reads=zk + [("accB", b), ("vecs", l)], writes=[("accB", b)])
            P.stt(accF[b][:], accF[b][:], ss[b][:, 3:4], accB[b][:], ALU.mult, ALU.add, reads=[("accF", b), ("accB", b), ("ss", b, 3)], writes=[("accF", b)])
            P.tt("pool", hyo[b][:], accF[b][:], x0c[b][:], ALU.mult, reads=[("accF", b), ("x0c", b)], writes=[("hyo", b)])
            P.dma("sp", s.hyT[j * 128:(j + 1) * 128, :], hyo[b][:], reads=[("hyo", b)])
        P.barrier()

    def _load_w_resident(self, dst, src2d, nrow_chunks, ncols, key, stage, skey):
        P = self.P
        n = 0
        for c0 in range(0, ncols, 128):
            for k0 in range(0, nrow_chunks, 8):
                kn = min(8, nrow_chunks - k0)
                j = self._stg_i % len(stage)
                self._stg_i += 1
                P.dma("sp", stage[j][:, 0:kn, :],
                      src2d[k0 * 128:(k0 + kn) * 128, c0:c0 + 128].rearrange("(k p) n -> p k n", p=128), writes=[(skey, j)])
                P.cp("pool" if n % 2 else "dve", dst[:, k0:k0 + kn, c0:c0 + 128], stage[j][:, 0:kn, :], reads=[(skey, j)], writes=[(key, c0, k0)])
                n += 1
        return [[(key, c0, k0) for k0 in range(0, nrow_chunks, 8)] for c0 in range(0, ncols, 128)]

    def _layer_norm(self, rT, W, gcol, bcol, vec, l, outT, sq, stat, rkeys, okey, sqk="lnsq"):
        P = self.P
        mean, msq, var, rstd = stat
        for m in range(8):
            P.mm(self.ps[6][:, 0:W], self.ones_f[:], rT[:, m, :], m == 0, m == 7, reads=[rkeys[m]], writes=[self.psk(6)])
        for m in range(8):
            P.act(sq[:, m, :], rT[:, m, :], AF.Square, reads=[rkeys[m]], writes=[(sqk, m)])
            P.mm(self.ps[7][:, 0:W], self.ones_f[:], sq[:, m, :], m == 0, m == 7, reads=[(sqk, m)], writes=[self.psk(7)])
        P.act(mean[:, 0:W], self.ps[6][:, 0:W], AF.Copy, scale=1.0 / D, reads=[self.psk(6)], writes=["ln_mean"])
        P.tt("pool", msq[:, 0:W], mean[:, 0:W], mean[:, 0:W], ALU.mult, reads=["ln_mean"], writes=["ln_msq"])
        P.stt(var[:, 0:W], self.ps[7][:, 0:W], 1.0 / D, msq[:, 0:W], ALU.mult, ALU.subtract, reads=[self.psk(7), "ln_msq"], writes=["ln_var"])
        P.act(rstd[:, 0:W], var[:, 0:W], AF.Ln, bias=self.epsT[:, 0:1], reads=["ln_var"], writes=["ln_rstd"])
        P.act(rstd[:, 0:W], rstd[:, 0:W], AF.Exp, scale=-0.5, reads=["ln_rstd"], writes=["ln_rstd"])
        for m in range(8):
            P.tt("dve", sq[:, m, :], rT[:, m, :], mean[:, 0:W], ALU.subtract, reads=[rkeys[m], "ln_mean", (sqk, m)], writes=[(sqk, m)])
            P.tt("pool", sq[:, m, :], sq[:, m, :], rstd[:, 0:W], ALU.mult, reads=[(sqk, m), "ln_rstd"], writes=[(sqk, m)])
            P.act(outT[:, m, :], sq[:, m, :], AF.Identity, scale=vec[:, gcol + m:gcol + m + 1], bias=vec[:, bcol + m:bcol + m + 1],
                  reads=[(sqk, m), ("vecs", l)], writes=[(okey, m)])

    def stage_c(self, l, s):
        P = self.P
        P.sb_reset()
        T, W = s.T, 256
        si = s.sidx
        vec, modT = self.vecs[l], self.modT[l]
        nj = W // 128
        self._stg_i = 0
        wb = [P.sb([128, 8, D], BF16, f"wb{i}") for i in range(3)]
        wo = P.sb([128, 8, D], BF16, "wo")
        mark0 = P.sb_off
        stage = [P.sb([128, 8, 128], F32, "cst") for _ in range(3)]
        wk = []
        for i in range(3):
            wk.append(self._load_w_resident(wb[i], self.inp["w_branch"][l, i], 8, D, f"wb{i}", stage, "cst"))
        wok = self._load_w_resident(wo, self.inp["w_out"][l], 8, D, "wo", stage, "cst")
        P.barrier()
        P.sb_off = mark0
        NBUF = 2
        yt = [P.sb([128, nj, D], F32, "yt") for _ in range(NBUF)]
        x0t = [P.sb([128, 8, W], F32, "x0t") for _ in range(NBUF)]
        hyT = [P.sb([128, 8, W], BF16, "hyT") for _ in range(NBUF)]
        atT = [P.sb([128, 8, W], BF16, "atT") for _ in range(NBUF)]
        poT = [P.sb([128, 8, W], BF16, "poT") for _ in range(NBUF)]
        gt = [P.sb([128, 3, 8, W], BF16, "gt") for _ in range(NBUF)]
        xat = [P.sb([128, 8, W], F32, "xat") for _ in range(NBUF)]
        mg = [P.sb([128, 8, W], BF16, "mg") for _ in range(NBUF)]
        tA = [P.sb([128, W], F32, "tA") for _ in range(2)]
        tB = [P.sb([128, W], F32, "tB") for _ in range(2)]
        stat = [P.sb([128, W], F32, "lnst") for _ in range(4)]
        direct = s.hyT is not None

        def finish_c(p, ts_, sq):
            self._layer_norm(xat[p], W, self.V_LN, self.V_LN + 8, vec, l, xat[p], sq, stat, [("rT", p, m) for m in range(8)], ("x1T", p), ("lnsq", p))
            P.dma("act", s.xb[:, ts_].rearrange("(m p) t -> p m t", p=128), xat[p][:], reads=[(("x1T", p), m) for m in range(8)], writes=[("xat", p)])

        pend = None
        for tw in range(T // W):
            p = tw % NBUF
            ts_ = slice(tw * W, (tw + 1) * W)
            sq = yt[p][:].rearrange("p j d -> p (j d)").rearrange("p (m w) -> p m w", m=8)
            if direct:
                P.dma("sp", hyT[p][:], s.hyT[:, ts_].rearrange("(m p) t -> p m t", p=128), writes=[("hyT", p, m) for m in range(8)])
            else:
                P.dma("sp", yt[p][:], s.y[ts_, :].rearrange("(j p) d -> p j d", p=128), reads=[(("lnsq", p), m) for m in range(8)], writes=[("yt", p)])
                P.dma("sp", x0t[p][:], s.x0T[:, ts_].rearrange("(m p) t -> p m t", p=128), writes=[("x0t", p)])
            P.dma("sp", atT[p][:], s.attnT[:, ts_].rearrange("(m p) t -> p m t", p=128), writes=[("atT", p)])
            P.dma("sp", poT[p][:], s.poolT[:, ts_].rearrange("(m p) t -> p m t", p=128), writes=[("poT", p)])
            for i in range(3):
                P.dma("sp", gt[p][:, i, :, :], s.gT[i, :, ts_].rearrange("(m p) t -> p m t", p=128), writes=[("gt", p, i)])
            P.dma("sp", xat[p][:], s.xa[:, ts_].rearrange("(m p) t -> p m t", p=128), writes=[("xat", p)])
            for m in range(8):
                if direct:
                    break
                bank = m % 2
                for j in range(nj):
                    P.tr(self.ps[bank][:, j * 128:(j + 1) * 128], yt[p][:, j, m * 128:(m + 1) * 128], self.ident[:], reads=[("yt", p)], writes=[self.psk(bank)])
                P.tt("dve", hyT[p][:, m, :], self.ps[bank][:, 0:W], x0t[p][:, m, :], ALU.mult, reads=[self.psk(bank), ("x0t", p)], writes=[("hyT", p, m)])
            srcs = (hyT[p], atT[p], poT[p])
            skeys = ([("hyT", p, m) for m in range(8)], [("atT", p)], [("poT", p)])
            for mo in range(8):
                q = mo % 2
                for i in range(3):
                    bank = 2 + i
                    for k in range(8):
                        P.mm(self.ps[bank][:, 0:W], wb[i][:, k, mo * 128:(mo + 1) * 128], srcs[i][:, k, :], k == 0, k == 7,
                             reads=(skeys[i] if i else [("hyT", p, k)]), writes=[self.psk(bank)])
                P.tt("dve", tA[q][:], self.ps[2][:, 0:W], gt[p][:, 0, mo, :], ALU.mult, reads=[self.psk(2), ("gt", p, 0)], writes=[("tA", q)])
                P.tt("dve", tB[q][:], self.ps[3][:, 0:W], gt[p][:, 1, mo, :], ALU.mult, reads=[self.psk(3), ("gt", p, 1)], writes=[("tB", q)])
                P.tt("pool", tA[q][:], tA[q][:], tB[q][:], ALU.add, reads=[("tA", q), ("tB", q)], writes=[("tA", q)])
                P.tt("dve", tB[q][:], self.ps[4][:, 0:W], gt[p][:, 2, mo, :], ALU.mult, reads=[self.psk(4), ("gt", p, 2)], writes=[("tB", q)])
                P.tt("pool", mg[p][:, mo, :], tA[q][:], tB[q][:], ALU.add, reads=[("tA", q), ("tB", q)], writes=[("mg", p, mo)])
            if pend is not None:
                finish_c(*pend)
            for mo in range(8):
                bank = mo % 2
                for k in range(8):
                    P.mm(self.ps[bank][:, 0:W], wo[:, k, mo * 128:(mo + 1) * 128], mg[p][:, k, :], k == 0, k == 7,
                         reads=[("mg", p, k)], writes=[self.psk(bank)])
                P.act(xat[p][:, mo, :], xat[p][:, mo, :], AF.Copy, scale=ALPHA, reads=[("xat", p)], writes=[("xs", p, mo)])
                P.stt(xat[p][:, mo, :], self.ps[bank][:, 0:W], modT[:, si, 16 + mo:17 + mo], xat[p][:, mo, :], ALU.mult, ALU.add,
                      reads=[self.psk(bank), ("xs", p, mo), ("modT", l, si)], writes=[("rT", p, mo)])
            pend = (p, ts_, sq)
        finish_c(*pend)
        P.barrier()

    def stage_d(self, l, s, final):
        P = self.P
        P.sb_reset()
        T = s.T
        W = 256
        si = s.sidx
        vec, modT, modP = self.vecs[l], self.modT[l], self.modP[l]
        NF = DFF // 128
        self._stg_i = 0
        w1 = P.sb([128, 8, DFF], BF16, "w1")
        w3 = P.sb([128, 8, DFF], BF16, "w3")
        w2 = P.sb([128, NF, D], BF16, "w2")
        mark0 = P.sb_off
        stage = [P.sb([128, 8, 128], F32, "dst") for _ in range(3)]
        w1k = self._load_w_resident(w1, self.inp["ffn_w1"][l], 8, DFF, "w1", stage, "dst")
        w3k = self._load_w_resident(w3, self.inp["ffn_w3"][l], 8, DFF, "w3", stage, "dst")
        w2k = self._load_w_resident(w2, self.inp["ffn_w2"][l], NF, D, "w2", stage, "dst")
        P.barrier()
        P.sb_off = mark0
        x1t = [P.sb([128, 8, W], F32, "x1t") for _ in range(2)]
        h2T = [P.sb([128, 8, W], BF16, "h2T") for _ in range(2)]
        gT = P.sb([128, NF, W], BF16, "gT")
        sa = [P.sb([128, W], F32, "sa") for _ in range(2)]
        sqd = P.sb([128, 8, W], F32, "sqd")
        stat = [P.sb([128, W], F32, "lnst") for _ in range(4)]
        ot = sqd[:].rearrange("p m w -> p (m w)").rearrange("p (j d) -> p j d", j=W // 128) if final else None

        def finish_d(p, ts_):
            xt = x1t[p]
            self._layer_norm(xt, W, self.V_LN + 16, self.V_LN + 24, vec, l, xt, sqd, stat, [("rT", p, m) for m in range(8)], ("x2T", p))
            xk = [(("x2T", p), m) for m in range(8)]
            if not final:
                P.dma("act", s.xa[:, ts_].rearrange("(m p) t -> p m t", p=128), xt[:], reads=xk, writes=[("x1t", p)])
            else:
                for j in range(W // 128):
                    for half in range(2):
                        bank = half
                        for mm_ in range(4):
                            m = half * 4 + mm_
                            P.tr(self.ps[bank][:, mm_ * 128:(mm_ + 1) * 128], xt[:, m, j * 128:(j + 1) * 128], self.ident[:],
                                 reads=[(("x2T", p), m)], writes=[self.psk(bank)])
                        P.cp("act" if half else "dve", ot[:, j, half * 512:(half + 1) * 512], self.ps[bank][:, 0:512], reads=[self.psk(bank)],
                             writes=[("ot", j, half)] + ([("lnsq", m) for m in range(8)] if (j == 0 and half == 0) else []))
                P.dma("act", self.out[ts_, :].rearrange("(j p) d -> p j d", p=128), ot,
                      reads=[("ot", j, h_) for j in range(W // 128) for h_ in range(2)] + xk + [("lnsq", m) for m in range(8)],
                      writes=[("x1t", p)], final=True)

        pend = None
        for tw in range(T // W):
            p = tw % 2
            ts_ = slice(tw * W, (tw + 1) * W)
            P.dma("sp", x1t[p][:], s.xb[:, ts_].rearrange("(m p) t -> p m t", p=128), writes=[("x1t", p)])
            for m in range(8):
                P.ts("dve" if m % 2 else "pool", h2T[p][:, m, :], x1t[p][:, m, :], modP[:, si, 32 + m:33 + m], modT[:, si, 24 + m:25 + m], ALU.mult, ALU.add,
                     reads=[("x1t", p), ("modT", l, si), ("modP", l, si)], writes=[("h2T", p, m)])
            for f in range(NF):
                q = f % 2
                ba, bb = (0, 1) if q == 0 else (2, 3)
                for k in range(8):
                    P.mm(self.ps[ba][:, 0:W], w1[:, k, f * 128:(f + 1) * 128], h2T[p][:, k, :], k == 0, k == 7, reads=[("h2T", p, k)], writes=[self.psk(ba)])
                for k in range(8):
                    P.mm(self.ps[bb][:, 0:W], w3[:, k, f * 128:(f + 1) * 128], h2T[p][:, k, :], k == 0, k == 7, reads=[("h2T", p, k)], writes=[self.psk(bb)])
                P.act(sa[q][:], self.ps[ba][:, 0:W], AF.Silu, reads=[self.psk(ba)], writes=[("sa", q)])
                P.tt("dve", gT[:, f, :], self.ps[bb][:, 0:W], sa[q][:], ALU.mult, reads=[self.psk(bb), ("sa", q)], writes=[("gT", f)])
                if f == 2 and pend is not None:
                    finish_d(*pend)
                    pend = None
            for mo in range(8):
                bank = 4 + mo % 2
                for f in range(NF):
                    P.mm(self.ps[bank][:, 0:W], w2[:, f, mo * 128:(mo + 1) * 128], gT[:, f, :], f == 0, f == NF - 1,
                         reads=[("gT", f)], writes=[self.psk(bank)])
                P.act(x1t[p][:, mo, :], x1t[p][:, mo, :], AF.Copy, scale=ALPHA, reads=[("x1t", p)], writes=[("xs", p, mo)])
                P.stt(x1t[p][:, mo, :], self.ps[bank][:, 0:W], modT[:, si, 40 + mo:41 + mo], x1t[p][:, mo, :], ALU.mult, ALU.add,
                      reads=[self.psk(bank), ("xs", p, mo), ("modT", l, si)], writes=[("rT", p, mo)])
            pend = (p, ts_)
        finish_d(*pend)
        P.barrier()

    def build(self):
        st = self.stages
        def on(name):
            return st is None or name in st
        if on("p0"):
            self.stage_p0()
        if on("p1"):
            self.stage_p1()
        for l in range(DEPTH):
            if on(f"k{l}l"):
                self.stage_k(l, "l")
            if on(f"a{l}c"):
                self.stage_a(l, self.st["c"])
            if on(f"a{l}l"):
                self.stage_a(l, self.st["l"])
            if on(f"h{l}c") and l < DEPTH - 1:
                self.stage_hc(l, self.st["c"])
            if on(f"h{l}l"):
                self.stage_h(l, self.st["l"])
            if on(f"b{l}c") and l < DEPTH - 1:
                self.stage_b(l, self.st["c"], CTX)
            if on(f"b{l}l"):
                self.stage_b(l, self.st["l"], SEQ + CTX)
            if on(f"c{l}c") and l < DEPTH - 1:
                self.stage_c(l, self.st["c"])
            if on(f"d{l}c") and l < DEPTH - 1:
                self.stage_d(l, self.st["c"], False)
            if on(f"c{l}l"):
                self.stage_c(l, self.st["l"])
            if on(f"d{l}l"):
                self.stage_d(l, self.st["l"], l == DEPTH - 1)
        self.P.emit()
        return self.nc


def make_in_maps(inputs, n_cores=8):
    c = host_consts()
    shared = {k: np.ascontiguousarray(np.asarray(v, dtype=np.float32)) for k, v in inputs.items() if k not in ("x", "c", "ctx")}
    consts = {"k_" + k: v for k, v in c.items()}
    maps = []
    for b in range(n_cores):
        m = dict(shared)
        m.update(consts)
        m["x"] = np.ascontiguousarray(inputs["x"][b], dtype=np.float32)
        m["c"] = np.ascontiguousarray(inputs["c"][b], dtype=np.float32)
        m["ctx"] = np.ascontiguousarray(inputs["ctx"][b], dtype=np.float32)
        maps.append(m)
    return maps


def kernel(**inputs):
    bld = Builder()
    nc = bld.build()
    res = run_bass_kernel_spmd(nc, make_in_maps(inputs), core_ids=list(range(8)))
    return np.stack([np.asarray(r["out"], dtype=np.float32) for r in res.results], axis=0)
```

```python
import contextlib
import math
import numpy as np
import ml_dtypes
import concourse.bass as bass
import concourse.mybir as mybir
from concourse.bass_utils import run_bass_kernel_spmd

F32 = mybir.dt.float32
BF16 = mybir.dt.bfloat16
AF = mybir.ActivationFunctionType
ALU = mybir.AluOpType

D = 1024
SEQ = 8192
CTX = 256
DEPTH = 2
NH = 8
HD = 128
DFF = 2816
INW = 8704
C_Q, C_K, C_V, C_HY, C_POOL, C_GATE = 0, 1024, 1280, 1536, 4608, 5632
ALPHA = (2 * DEPTH) ** 0.25
EPS = 1e-6
NFFT = 16384
POOL_WINDOWS = (2, 4, 8, 16)


class _Op:
    __slots__ = ("eng", "fn", "deps", "dma", "marked", "cnt", "sem", "semval", "prev")


class Prog:
    COMPUTE = ("pe", "act", "dve", "pool")
    QUEUES = ("sp", "act", "pool")
    ALLENG = ("pe", "act", "dve", "pool", "sp")
    SB_BASE = 24576
    SB_LIMIT = 218 * 1024

    def __init__(self, ring=16, same_engine_sync=True):
        self.nc = bass.Bass("TRN2", target_bir_lowering=False)
        self.ops = []
        self.state = {}
        self.ring = ring
        self.same = same_engine_sync
        self.dma_count = {q: 0 for q in self.QUEUES}
        self.slot_last = {}
        self.slot_val = {}
        self.last_op = {e: None for e in self.ALLENG}
        self.sb_off = self.SB_BASE
        self.sb_mark = self.SB_BASE
        self.n_alloc = 0
        self.out_dmas = []
        self.psn = 0

    def sb(self, shape, dtype, name="t"):
        nbytes = int(np.prod(shape[1:])) * mybir.dt.size(dtype)
        nbytes_al = (nbytes + 63) // 64 * 64
        self.n_alloc += 1
        h = self.nc.alloc_sbuf_tensor_at(f"{name}_{self.n_alloc}", list(shape), dtype, offset=self.sb_off)
        self.sb_off += nbytes_al
        assert self.sb_off <= self.SB_LIMIT, f"SBUF overflow {self.sb_off} ({name})"
        return h

    def sb_persist_done(self):
        self.sb_mark = self.sb_off

    def sb_reset(self):
        self.sb_off = self.sb_mark

    def _collect(self, reads, writes):
        deps = set()
        for k in reads:
            st = self.state.get(k)
            if st is not None and st[0] is not None:
                deps.add(st[0])
        for k in writes:
            st = self.state.get(k)
            if st is not None:
                if st[0] is not None:
                    deps.add(st[0])
                deps.update(st[1].values())
                deps.update(st[2])
        return deps

    def add(self, eng, fn, reads=(), writes=(), dma=False, out=False):
        pr = [k for k in reads if isinstance(k, tuple) and k[0] == "ps"]
        if pr:
            reads = [k for k in reads if not (isinstance(k, tuple) and k[0] == "ps")]
            writes = list(writes) + pr
        i = len(self.ops)
        op = _Op()
        op.eng, op.fn, op.dma, op.marked, op.cnt, op.prev = eng, fn, dma, False, 0, None
        op.deps = self._collect(reads, writes)
        if dma:
            n = self.dma_count[eng]
            self.dma_count[eng] = n + 1
            slot = (eng, n % self.ring)
            op.prev = self.slot_last.get(slot)
            self.slot_last[slot] = i
            v = self.slot_val.get(slot, 0) + 16
            self.slot_val[slot] = v
            op.sem, op.semval = slot, v
            if out:
                self.out_dmas.append(i)
        self.ops.append(op)
        for k in reads:
            st = self.state.setdefault(k, [None, {}, []])
            if dma:
                st[2].append(i)
            else:
                st[1][eng] = i
        for k in writes:
            self.state[k] = [i, {}, []]
        self.last_op[eng] = i
        return i

    def barrier(self):
        snap = [v for v in self.last_op.values() if v is not None] + list(self.slot_last.values())
        for e in self.ALLENG:
            op = _Op()
            op.eng, op.fn, op.dma, op.marked, op.cnt, op.prev = e, None, False, False, 0, None
            op.deps = set(snap)
            self.ops.append(op)
        self.state = {}

    def dma(self, q, out, in_, reads=(), writes=(), final=False):
        return self.add(q, lambda e: e.dma_start(out=out, in_=in_), reads, writes, dma=True, out=final)

    def mm(self, out, lhsT, rhs, start, stop, reads=(), writes=()):
        return self.add("pe", lambda e: e.matmul(out, lhsT=lhsT, rhs=rhs, start=start, stop=stop), reads, writes)

    def tr(self, out, in_, ident, reads=(), writes=()):
        return self.add("pe", lambda e: e.transpose(out, in_, ident), reads, writes)

    def act(self, out, in_, func, reads=(), writes=(), bias=None, scale=None):
        kw = {}
        if bias is not None:
            kw["bias"] = bias
        if scale is not None:
            kw["scale"] = scale
        return self.add("act", lambda e: e.activation(out=out, in_=in_, func=func, **kw), reads, writes)

    def tt(self, eng, out, in0, in1, op, reads=(), writes=()):
        return self.add(eng, lambda e: e.tensor_tensor(out=out, in0=in0, in1=in1, op=op), reads, writes)

    def ts(self, eng, out, in0, s1, s2, op0, op1, reads=(), writes=()):
        if op1 is None:
            return self.add(eng, lambda e: e.tensor_scalar(out=out, in0=in0, scalar1=s1, scalar2=None, op0=op0), reads, writes)
        return self.add(eng, lambda e: e.tensor_scalar(out=out, in0=in0, scalar1=s1, scalar2=s2, op0=op0, op1=op1), reads, writes)

    def stt(self, out, in0, scalar, in1, op0, op1, reads=(), writes=()):
        return self.add("dve", lambda e: e.scalar_tensor_tensor(out=out, in0=in0, scalar=scalar, in1=in1, op0=op0, op1=op1), reads, writes)

    def cp(self, eng, out, in_, reads=(), writes=()):
        if eng == "act":
            return self.add("act", lambda e: e.copy(out=out, in_=in_), reads, writes)
        return self.add(eng, lambda e: e.tensor_copy(out=out, in_=in_), reads, writes)

    def memset(self, eng, ap, val, writes=()):
        return self.add(eng, lambda e: e.memset(ap, val), (), writes)

    def emit(self):
        nc = self.nc
        ops = self.ops
        fin = _Op()
        fin.eng, fin.fn, fin.dma, fin.marked, fin.cnt, fin.prev = "sp", None, False, False, 0, None
        fin.deps = set(self.out_dmas) | set(self.slot_last.values())
        ops.append(fin)
        for op in ops:
            for d in op.deps:
                dop = ops[d]
                if dop.dma or dop.fn is None:
                    continue
                dop.marked = True
        cnt = {e: 0 for e in self.COMPUTE}
        for op in ops:
            if op.fn is not None and not op.dma:
                if op.marked:
                    cnt[op.eng] += 1
                op.cnt = cnt[op.eng]
        per_eng = {e: [] for e in self.ALLENG}
        for op in ops:
            per_eng[op.eng].append(op)
        same = self.same

        with contextlib.ExitStack() as es:
            csem = {e: es.enter_context(nc.semaphore(f"c_{e}")) for e in self.COMPUTE}
            dsem = {}
            for q in self.QUEUES:
                for r in range(self.ring):
                    dsem[(q, r)] = es.enter_context(nc.semaphore(f"d_{q}_{r}"))
            block = es.enter_context(nc.Block())

            def run(ename, e):
                waited = {}
                for op in per_eng[ename]:
                    waits = {}
                    for d in op.deps:
                        dop = ops[d]
                        if dop.dma:
                            s = ("d", dop.sem)
                            waits[s] = max(waits.get(s, 0), dop.semval)
                        else:
                            if dop.fn is None:
                                continue
                            if dop.eng == ename:
                                if op.fn is None:
                                    continue
                                if not op.dma and (ename == "pe" or not same):
                                    continue
                            s = ("c", dop.eng)
                            waits[s] = max(waits.get(s, 0), dop.cnt)
                    if op.dma and op.prev is not None:
                        p = ops[op.prev]
                        s = ("d", p.sem)
                        waits[s] = max(waits.get(s, 0), p.semval)
                    for s, v in waits.items():
                        if v <= 0 or waited.get(s, 0) >= v:
                            continue
                        e.wait_ge(dsem[s[1]] if s[0] == "d" else csem[s[1]], v)
                        waited[s] = v
                    if op.fn is None:
                        continue
                    inst = op.fn(e)
                    if op.dma:
                        inst.then_inc(dsem[op.sem], 16)
                    elif op.marked:
                        inst.then_inc(csem[ename], 1)

            block.tensor(lambda e: run("pe", e))
            block.scalar(lambda e: run("act", e))
            block.vector(lambda e: run("dve", e))
            block.gpsimd(lambda e: run("pool", e))
            block.sync(lambda e: run("sp", e))
        return nc


def _bf(a):
    return np.asarray(a, dtype=np.float32).astype(ml_dtypes.bfloat16)


def _hy_tables(L):
    f32 = np.float32
    t_idx = np.arange(L, dtype=f32)
    t01 = (t_idx / f32(max(L - 1, 1))).astype(f32)
    bands = np.linspace(1e-4, 15.0, 16, dtype=f32)
    ang = (f32(2.0 * math.pi / L) * t_idx[:, None] * bands[None, :]).astype(f32)
    feats = np.concatenate([t01[:, None], np.cos(ang), -np.sin(ang)], axis=-1).astype(f32)
    fF = np.zeros((33, 8192), f32)
    fR = np.zeros((33, 8192), f32)
    negt = np.zeros((128, 128), f32)
    mask = np.zeros((128, 128), f32)
    fF[:, :L] = feats.T
    n = np.arange(8192)
    tF = np.where(n < L, n, 0)
    negt[:64] = -np.where(n < L, t01[tF], 0).reshape(64, 128)
    mask[:64] = (n < L).astype(f32).reshape(64, 128)
    m = 8192 - n
    valid = (m >= 1) & (m <= L - 1)
    mm = np.where(valid, m, 0)
    fR[:, valid] = feats[mm[valid]].T
    negt[64:] = -np.where(valid, t01[mm], 0).reshape(64, 128)
    mask[64:] = valid.astype(f32).reshape(64, 128)
    return fF, fR, negt, mask


def _rope_tables(T, grid_w=64, ctx=False):
    f32 = np.float32
    if ctx:
        return np.ones((128, T), f32), np.zeros((128, T), f32)
    t = np.arange(T)
    rows = (t // grid_w).astype(f32)
    cols = (t % grid_w).astype(f32)
    inv = np.power(f32(10000.0), -np.arange(32, dtype=f32) / f32(32)).astype(f32)
    C = np.zeros((128, T), f32)
    S = np.zeros((128, T), f32)
    for j in range(128):
        pos = rows if j < 64 else cols
        jj = j % 64
        ang = (pos * inv[jj % 32]).astype(f32)
        C[j] = np.cos(ang)
        S[j] = -np.sin(ang) if jj < 32 else np.sin(ang)
    return C, S


_CONST_CACHE = {}


def host_consts():
    if _CONST_CACHE:
        return _CONST_CACHE
    c = {}
    c["ident"] = np.eye(128, dtype=np.float32)
    c["identb"] = _bf(np.eye(128))
    c["ropeC_l"], c["ropeS_l"] = _rope_tables(SEQ)
    c["ropeC_c"], c["ropeS_c"] = _rope_tables(CTX, ctx=True)
    a = np.arange(128, dtype=np.float64)
    th = 2 * np.pi * np.outer(a, a) / 128.0
    c["F1"] = _bf(np.stack([np.cos(th), -np.sin(th)], axis=1))
    c["I2"] = _bf(np.stack([np.cos(th)[:, :64], -np.sin(th)[:, :64]], axis=1) / NFFT)
    k1 = a[:, None, None]
    n2 = a[None, :, None]
    k2 = a[None, None, :]
    th3 = 2 * np.pi * n2 * (k1 + 128.0 * k2) / NFFT
    G = np.stack([np.cos(th3), -np.sin(th3), np.sin(th3)], axis=2)
    c["GT"] = _bf(G)
    c["HT"] = _bf(np.transpose(G, (0, 3, 2, 1)))
    for tag, L in (("l", SEQ), ("c", CTX)):
        fF, fR, negt, mask = _hy_tables(L)
        c[f"featF_{tag}"], c[f"featR_{tag}"], c[f"negt_{tag}"], c[f"mask_{tag}"] = fF, fR, negt, mask
    lo, hi = math.log(1e-2) / 1.5, math.log(1e-2) / 0.3
    c["deltas"] = np.abs(np.linspace(lo, hi, 1024, dtype=np.float32)).reshape(1, 1024).astype(np.float32)
    edge = np.zeros((4, 2, 8), np.float32)
    Tt = 4096
    for g, w in enumerate(POOL_WINDOWS):
        before, after = w // 2, w - 1 - w // 2
        for side in range(2):
            for i in range(8):
                t = i if side == 0 else Tt - 8 + i
                cnt = min(t + after + 1, Tt) - max(t - before, 0)
                edge[g, side, i] = 1.0 / cnt
    c["pedge"] = np.broadcast_to(edge.reshape(1, 64), (128, 64)).copy()
    t01c = (np.arange(CTX, dtype=np.float32) / np.float32(CTX - 1)).astype(np.float32)
    c["t01row_c"] = np.broadcast_to(t01c.reshape(1, CTX), (128, CTX)).copy()
    c["ndeltasT"] = np.ascontiguousarray(-c["deltas"].reshape(8, 128).T)
    _CONST_CACHE.update(c)
    return c


CONST_SPECS = None


def const_specs():
    c = host_consts()
    return {k: (list(v.shape), BF16 if v.dtype == ml_dtypes.bfloat16 else F32) for k, v in c.items()}


INPUT_SHAPES = {
    "x": [SEQ, D], "c": [D], "ctx": [CTX, D], "c_ctx": [D],
    "w_ada": [DEPTH, D, 6 * D], "b_ada": [DEPTH, 6 * D], "w_in": [DEPTH, D, INW],
    "q_norm_g": [DEPTH, HD], "k_norm_g": [DEPTH, HD],
    "hy_conv_w": [DEPTH, 3, 3 * D], "hy_conv_b": [DEPTH, 3 * D],
    "hf_w1": [DEPTH, 33, 64], "hf_b1": [DEPTH, 64], "hf_freq": [DEPTH, 64],
    "hf_w2": [DEPTH, 64, 64], "hf_b2": [DEPTH, 64], "hf_w3": [DEPTH, 64, 2 * D],
    "hy_d": [DEPTH, D], "pool_w": [DEPTH, 4, 256, 256], "pool_scale": [DEPTH, D],
    "w_branch": [DEPTH, 3, D, D], "w_out": [DEPTH, D, D],
    "ln1_g": [DEPTH, D], "ln1_b": [DEPTH, D], "ln2_g": [DEPTH, D], "ln2_b": [DEPTH, D],
    "ffn_w1": [DEPTH, D, DFF], "ffn_w3": [DEPTH, D, DFF], "ffn_w2": [DEPTH, DFF, D],
}


class Stream:
    pass


class Builder:
    def __init__(self, debug_outs=(), stages=None):
        self.P = Prog()
        self.nc = self.P.nc
        self.debug_outs = set(debug_outs)
        self.stages = stages
        nc = self.nc
        self.inp = {k: nc.dram_tensor(k, shp, F32, kind="ExternalInput").ap() for k, shp in INPUT_SHAPES.items()}
        self.cst = {k: nc.dram_tensor("k_" + k, shp, dt, kind="ExternalInput").ap() for k, (shp, dt) in const_specs().items()}
        self.out = nc.dram_tensor("out", [SEQ, D], F32, kind="ExternalOutput").ap()
        self.ps = [nc.alloc_psum_tensor(f"psb{i}", [128, 512], F32) for i in range(8)]
        self.scr = {}
        self._persistent()
        self._streams()

    def dram(self, name, shape, dtype):
        kind = "ExternalOutput" if name in self.debug_outs else "Internal"
        t = self.nc.dram_tensor(name, list(shape), dtype, kind=kind).ap()
        self.scr[name] = t
        return t

    def psk(self, i):
        return ("ps", i)

    def _persistent(self):
        P = self.P
        self.ident = P.sb([128, 128], F32, "ident")
        self.identb = P.sb([128, 128], BF16, "identb")
        self.ones_f = P.sb([128, 128], F32, "ones_f")
        self.ones_b = P.sb([128, 128], BF16, "ones_b")
        self.epsT = P.sb([128, 1], F32, "epsT")
        self.vecs = [P.sb([128, 200], F32, f"vecs{l}") for l in range(DEPTH)]
        self.modT = [P.sb([128, 2, 48], F32, f"modT{l}") for l in range(DEPTH)]
        self.modP = [P.sb([128, 2, 48], F32, f"modP{l}") for l in range(DEPTH)]
        self.hfv = [P.sb([64, 8], F32, f"hfv{l}") for l in range(DEPTH)]
        self.pedge = P.sb([128, 64], F32, "pedge")
        P.sb_persist_done()

    V_QG, V_QGP, V_KG, V_KGP = 0, 1, 2, 3
    V_CW = 4
    V_CB = 76
    V_PS = 100
    V_LN = 108
    V_BA = 140
    V_HD = 188
    V_N = 196

    def _streams(self):
        self.XA = self.dram("XA", [D, SEQ], F32)
        self.XB = self.dram("XB", [D, SEQ], F32)
        self.XCA = self.dram("XCA", [D, CTX], F32)
        self.XCB = self.dram("XCB", [D, CTX], F32)
        self.kT_d = self.dram("kT_d", [256, SEQ + CTX], BF16)
        self.v_d = self.dram("v_d", [SEQ + CTX, 256], BF16)
        self.Bd = [self.dram(f"Bd{i}", [128, 128, 1024], BF16) for i in range(2)]
        self.Dd = [self.dram(f"Dd{i}", [128, 128, 1024], BF16) for i in range(2)]
        self.Kf = {t: [self.dram(f"Kf_{t}{i}", [128, 128, 1024], BF16) for i in range(2)] for t in ("l", "c")}
        self.st = {}
        for tag, T in (("l", SEQ), ("c", CTX)):
            s = Stream()
            s.tag, s.T = tag, T
            s.TS = 4096 if tag == "l" else 256
            s.W = 512 if tag == "l" else 256
            s.sidx = 0 if tag == "l" else 1
            s.xa = self.XA if tag == "l" else self.XCA
            s.xb = self.XB if tag == "l" else self.XCB
            s.ktok0 = CTX if tag == "l" else 0
            s.qT = self.dram(f"qT_{tag}", [D, T], BF16)
            s.z = self.dram(f"z_{tag}", [T, D], BF16)
            s.x0T = self.dram(f"x0T_{tag}", [D, T], F32)
            s.poolT = self.dram(f"poolT_{tag}", [D, T], BF16)
            s.gT = self.dram(f"gT_{tag}", [3, D, T], BF16)
            s.attnT = self.dram(f"attnT_{tag}", [D, T], BF16)
            s.y = self.dram(f"y_{tag}", [T, D], F32)
            s.zT = self.dram(f"zT_{tag}", [D, T], F32) if tag == "c" else None
            s.hyT = self.dram(f"hyT_{tag}", [D, T], BF16) if tag == "c" else None
            s.ropeC = self.cst[f"ropeC_{tag}"]
            s.ropeS = self.cst[f"ropeS_{tag}"]
            self.st[tag] = s

    def stage_p0(self):
        P = self.P
        P.sb_reset()
        P.dma("sp", self.ident[:], self.cst["ident"], writes=["ident"])
        P.dma("sp", self.identb[:], self.cst["identb"], writes=["identb"])
        P.memset("dve", self.ones_f[:], 1.0, writes=["ones_f"])
        P.memset("dve", self.ones_b[:], 1.0, writes=["ones_b"])
        P.memset("dve", self.epsT[:], EPS, writes=["epsT"])
        xt = [P.sb([128, 4, D], F32, "p0x") for _ in range(2)]
        xo = [P.sb([128, 8, 512], F32, "p0o") for _ in range(2)]
        it = 0
        for src, dst, T in ((self.inp["x"], self.XA, SEQ), (self.inp["ctx"], self.XCA, CTX)):
            W = min(512, T)
            nj = W // 128
            for w in range(T // W):
                b = it % 2
                it += 1
                P.dma("sp", xt[b][:, 0:nj, :], src[w * W:(w + 1) * W, :].rearrange("(j p) d -> p j d", p=128),
                      writes=[("p0x", b)])
                for m in range(8):
                    bank = m % 4
                    for j in range(nj):
                        P.tr(self.ps[bank][:, j * 128:(j + 1) * 128], xt[b][:, j, m * 128:(m + 1) * 128], self.ident[:],
                             reads=[("p0x", b), "ident"], writes=[self.psk(bank)])
                    P.cp("act" if m % 2 else "dve", xo[b][:, m, 0:W], self.ps[bank][:, 0:W],
                         reads=[self.psk(bank)], writes=[("p0o", b, m)])
                P.dma("act", dst[:, w * W:(w + 1) * W].rearrange("(m p) t -> p m t", p=128), xo[b][:, :, 0:W],
                      reads=[("p0o", b, m) for m in range(8)])
        P.barrier()

    def stage_p1(self):
        P = self.P
        P.sb_reset()
        I = self.inp
        stg = P.sb([128, 2, 128], F32, "stg")
        stgc = P.sb([16, 128], F32, "stgc")
        stg64 = P.sb([8, 64], F32, "stg64")
        scT = P.sb([128, 8, 2], F32, "scT")
        cT = P.sb([128, 16], F32, "cT")
        wa = [P.sb([128, 8, 512], F32, "wa") for _ in range(2)]
        P.dma("sp", stgc[0:8, :], I["c"].rearrange("(m p) -> m p", p=128), writes=["stgc"])
        P.dma("sp", stgc[8:16, :], I["c_ctx"].rearrange("(m p) -> m p", p=128), writes=["stgc2"])
        P.tr(self.ps[0][:, 0:16], stgc[0:16, :], self.ident[0:16, 0:16], reads=["stgc", "stgc2", "ident"], writes=[self.psk(0)])
        P.act(cT[:], self.ps[0][:, 0:16], AF.Silu, reads=[self.psk(0)], writes=["cT"])
        for s in range(2):
            P.cp("dve", scT[:, :, s], cT[:, s * 8:(s + 1) * 8], reads=["cT"], writes=[("scT", s)])
        for l in range(DEPTH):
            rows = []
            g = I["q_norm_g"][l]
            kg = I["k_norm_g"][l]
            rows.append(("full", g))
            rows.append(("perm", g))
            rows.append(("full", kg))
            rows.append(("perm", kg))
            for j in range(3):
                for m in range(24):
                    rows.append(("full", I["hy_conv_w"][l, j, m * 128:(m + 1) * 128]))
            for m in range(24):
                rows.append(("full", I["hy_conv_b"][l, m * 128:(m + 1) * 128]))
            for m in range(8):
                rows.append(("full", I["pool_scale"][l, m * 128:(m + 1) * 128]))
            for nm in ("ln1_g", "ln1_b", "ln2_g", "ln2_b"):
                for m in range(8):
                    rows.append(("full", I[nm][l, m * 128:(m + 1) * 128]))
            for m in range(48):
                rows.append(("full", I["b_ada"][l, m * 128:(m + 1) * 128]))
            for m in range(8):
                rows.append(("full", I["hy_d"][l, m * 128:(m + 1) * 128]))
            assert len(rows) == self.V_N
            def ld(r0, ap2d, n):
                grp, rr = divmod(r0, 128)
                assert rr + n <= 128
                P.dma("sp", stg[rr:rr + n, grp, :], ap2d, writes=[("stg", r0)])
                return ("stg", r0)
            keys = []
            for ri, (kind, ap) in enumerate(rows[:4]):
                grp, rr = divmod(ri, 128)
                if kind == "full":
                    P.dma("sp", stg[rr:rr + 1, grp, :], ap.rearrange("(o n) -> o n", o=1), writes=[("stg", ri)])
                else:
                    for q4, src0 in enumerate((32, 0, 96, 64)):
                        P.dma("sp", stg[rr:rr + 1, grp, q4 * 32:(q4 + 1) * 32],
                              ap[src0:src0 + 32].rearrange("(o n) -> o n", o=1), writes=[("stg", ri, q4)])
                        keys.append(("stg", ri, q4))
                keys.append(("stg", ri))
            keys.append(ld(4, I["hy_conv_w"][l].rearrange("j (m p) -> (j m) p", p=128), 72))
            keys.append(ld(76, I["hy_conv_b"][l].rearrange("(m p) -> m p", p=128), 24))
            keys.append(ld(100, I["pool_scale"][l].rearrange("(m p) -> m p", p=128), 8))
            for qi, nm in enumerate(("ln1_g", "ln1_b")):
                keys.append(ld(108 + qi * 8, I[nm][l].rearrange("(m p) -> m p", p=128), 8))
            keys.append(ld(124, I["ln2_g"][l, 0:512].rearrange("(m p) -> m p", p=128), 4))
            keys.append(ld(128, I["ln2_g"][l, 512:1024].rearrange("(m p) -> m p", p=128), 4))
            keys.append(ld(132, I["ln2_b"][l].rearrange("(m p) -> m p", p=128), 8))
            keys.append(ld(140, I["b_ada"][l].rearrange("(m p) -> m p", p=128), 48))
            keys.append(ld(188, I["hy_d"][l].rearrange("(m p) -> m p", p=128), 8))
            P.tr(self.ps[1][:, 0:128], stg[:, 0, :], self.ident[:], reads=keys + ["ident"], writes=[self.psk(1)])
            P.tr(self.ps[1][:, 128:128 + 68], stg[0:68, 1, :], self.ident[0:68, 0:68], reads=keys + ["ident"], writes=[self.psk(1)])
            P.cp("dve", self.vecs[l][:, 0:196], self.ps[1][:, 0:196], reads=[self.psk(1)], writes=[("vecs", l)])
            for ci, nm in enumerate(("hf_b1", "hf_freq", "hf_b2")):
                P.dma("sp", stg64[ci:ci + 1, :], I[nm][l].rearrange("(o n) -> o n", o=1), writes=[("stg64", ci)])
            P.tr(self.ps[2][0:64, 0:3], stg64[0:3, :], self.ident[0:3, 0:3],
                 reads=[("stg64", ci) for ci in range(3)] + ["ident"], writes=[self.psk(2)])
            P.cp("dve", self.hfv[l][:, 0:3], self.ps[2][0:64, 0:3], reads=[self.psk(2)], writes=[("hfv", l)])
            P.tt("dve", self.hfv[l][:, 3:4], self.hfv[l][:, 0:1], self.hfv[l][:, 1:2], ALU.mult, reads=[("hfv", l)], writes=[("hfv3", l)])
            P.tt("dve", self.hfv[l][:, 4:5], self.hfv[l][:, 2:3], self.hfv[l][:, 1:2], ALU.mult, reads=[("hfv", l)], writes=[("hfv4", l)])
            bank = 3
            for cg in range(12):
                b = cg % 2
                P.dma("sp" if cg % 2 else "act", wa[b][:], I["w_ada"][l][:, cg * 512:(cg + 1) * 512].rearrange("(k p) n -> p k n", p=128),
                      writes=[("wa", b)])
                for mm in range(4):
                    m = cg * 4 + mm
                    for k in range(8):
                        P.mm(self.ps[bank][:, m * 2:m * 2 + 2], wa[b][:, k, mm * 128:(mm + 1) * 128], scT[:, k, :],
                             start=(k == 0), stop=(k == 7), reads=[("wa", b), ("scT", 0), ("scT", 1)], writes=[self.psk(bank)])
            psv = self.ps[bank][:, 0:96].rearrange("p (m s) -> p m s", s=2)
            for s in range(2):
                P.tt("dve", self.modT[l][:, s, :], psv[:, :, s], self.vecs[l][:, self.V_BA:self.V_BA + 48], ALU.add,
                     reads=[self.psk(bank), ("vecs", l)], writes=[("modT", l, s)])
                P.ts("dve", self.modP[l][:, s, :], self.modT[l][:, s, :], 1.0, None, ALU.add, None,
                     reads=[("modT", l, s)], writes=[("modP", l, s)])
            if "dbg_mod" in self.debug_outs:
                if l == 0:
                    self.dbg_mod = self.dram("dbg_mod", [DEPTH, 128, 96], F32)
                    self.dbg_vec = self.dram("dbg_vec", [DEPTH, 128, 192], F32)
                P.dma("sp", self.dbg_mod[l], self.modT[l][:].rearrange("p s m -> p (s m)"), reads=[("modT", l, 0), ("modT", l, 1)])
                P.dma("sp", self.dbg_vec[l], self.vecs[l][:, 0:192], reads=[("vecs", l)])
        P.barrier()

    def _proj(self, ps_ap, wt, hT, w, W, keys_w, bank, col0=0, ncol=None):
        P = self.P
        for k in range(8):
            P.mm(ps_ap, wt[:, k, :], hT[:, k, col0 + w * W: col0 + w * W + (ncol or W)],
                 start=(k == 0), stop=(k == 7), reads=[keys_w, ("hT", k, w)], writes=[self.psk(bank)])

    def stage_a(self, l, s):
        P = self.P
        P.sb_reset()
        T, TS, W = s.T, s.TS, s.W
        NW = TS // W
        NB = TS // 128
        si = s.sidx
        vec, modT, modP = self.vecs[l], self.modT[l], self.modP[l]
        w_in = self.inp["w_in"][l]
        hT = P.sb([128, 8, TS + 16], BF16, "hT")
        wring = [P.sb([128, 8, 128], BF16, "wr") for _ in range(4)]
        wcount = [0]

        wstage = [P.sb([128, 8, 128], F32, "wst") for _ in range(3)]
        scount = [0]

        def load_w(col0, ncols=128, buf=None, key=None):
            if buf is None:
                i = wcount[0] % 4
                wcount[0] += 1
                buf, key = wring[i], ("wr", i)
            for c in range(0, ncols, 128):
                j = scount[0] % 3
                scount[0] += 1
                P.dma("sp", wstage[j][:], w_in[:, col0 + c:col0 + c + 128].rearrange("(k p) n -> p k n", p=128),
                      writes=[("wst", j)])
                P.cp("dve", buf[:, :, c:c + 128], wstage[j][:], reads=[("wst", j)], writes=[key if ncols == 128 else (key, c)])
            return buf, key

        mark = P.sb_off
        P.dma("sp", self.pedge[:], self.cst["pedge"], writes=["pedge"])
        for sti in range(T // TS):
            t0 = sti * TS
            P.sb_off = mark
            xs = [P.sb([128, 8, W], F32, "xs") for _ in range(2)]
            hal = P.sb([128, 8, 16], F32, "hal")
            for w in range(NW):
                b = w % 2
                P.dma("sp", xs[b][:], s.xa[:, t0 + w * W: t0 + (w + 1) * W].rearrange("(m p) t -> p m t", p=128), writes=[("xs", b)])
                for m in range(8):
                    eng = ("act", "dve", "pool")[m % 3]
                    o = hT[:, m, w * W:(w + 1) * W]
                    if eng == "act":
                        P.act(o, xs[b][:, m, :], AF.Identity, scale=modP[:, si, 8 + m:9 + m], bias=modT[:, si, m:m + 1],
                              reads=[("xs", b), ("modT", l, si), ("modP", l, si)], writes=[("hT", m, w)])
                    else:
                        P.ts(eng, o, xs[b][:, m, :], modP[:, si, 8 + m:9 + m], modT[:, si, m:m + 1], ALU.mult, ALU.add,
                             reads=[("xs", b), ("modT", l, si), ("modP", l, si)], writes=[("hT", m, w)])
            hk = []
            for side, (a, b_) in enumerate(((t0 - 8, t0), (t0 + TS, t0 + TS + 8))):
                if a >= 0 and b_ <= T:
                    P.dma("sp", hal[:, :, side * 8:(side + 1) * 8], s.xa[:, a:b_].rearrange("(m p) t -> p m t", p=128), writes=[("hal", side)])
                    for m in range(8):
                        P.ts("dve", hT[:, m, TS + side * 8: TS + side * 8 + 8], hal[:, m, side * 8:(side + 1) * 8],
                             modP[:, si, 8 + m:9 + m], modT[:, si, m:m + 1], ALU.mult, ALU.add,
                             reads=[("hal", side), ("modT", l, si), ("modP", l, si)], writes=[("hT", m, "h%d" % side)])
                else:
                    for m in range(8):
                        P.memset("dve", hT[:, m, TS + side * 8: TS + side * 8 + 8], 0.0, writes=[("hT", m, "h%d" % side)])
            P.barrier()
            if getattr(self, 'a_stop', None) == 'ph0':
                return

            def proj_halo(ps_ap, wt, wkey, bank):
                for k in range(8):
                    P.mm(ps_ap, wt[:, k, :], hT[:, k, TS:TS + 16], start=(k == 0), stop=(k == 7),
                         reads=[wkey, ("hT", k, "h0"), ("hT", k, "h1")], writes=[self.psk(bank)])

            P.sb_off = mark
            rC = P.sb([128, TS], F32, "rC")
            rS = P.sb([128, TS], F32, "rS")
            P.dma("sp", rC[:], s.ropeC[:, t0:t0 + TS], writes=["rC"])
            P.dma("sp", rS[:], s.ropeS[:, t0:t0 + TS], writes=["rS"])
            wp = [P.sb([128, 8, 128], BF16, "wp") for _ in range(2)]
            sqb = [P.sb([128, W], F32, "sqb") for _ in range(2)]
            rs = [P.sb([128, W], F32, "rs") for _ in range(2)]
            t1 = [P.sb([128, W], F32, "t1") for _ in range(2)]
            t2 = [P.sb([128, W], F32, "t2") for _ in range(2)]
            qrow = [P.sb([128, TS], BF16, "qrow") for _ in range(2)]
            it = 0
            for hc in range(10):
                wq, wk = load_w(hc * 128)
                pb = hc % 2
                for q4, src0 in enumerate((32, 0, 96, 64)):
                    P.cp("pool", wp[pb][:, :, q4 * 32:(q4 + 1) * 32], wq[:, :, src0:src0 + 32], reads=[wk], writes=[("wp", pb, q4)])
                wpk = [("wp", pb, q4) for q4 in range(4)]
                gcol = self.V_QG if hc < 8 else self.V_KG
                r = hc % 2
                for w in range(NW):
                    i = it % 2
                    it += 1
                    bq, bp, bs = (0, 1, 4) if i == 0 else (2, 3, 5)
                    QL = 9
                    if QL < 2:
                        continue
                    self._proj(self.ps[bq][:, 0:W], wq, hT, w, W, wk, bq)
                    for k in range(8):
                        P.mm(self.ps[bp][:, 0:W], wp[pb][:, k, :], hT[:, k, w * W:(w + 1) * W], start=(k == 0), stop=(k == 7),
                             reads=wpk + [("hT", k, w)], writes=[self.psk(bp)])
                    if QL < 3:
                        continue
                    P.act(sqb[i][:], self.ps[bq][:, 0:W], AF.Square, reads=[self.psk(bq)], writes=[("sqb", i)])
                    P.mm(self.ps[bs][:, 0:W], self.ones_f[:], sqb[i][:], start=True, stop=True,
                         reads=["ones_f", ("sqb", i)], writes=[self.psk(bs)])
                    P.act(rs[i][:], self.ps[bs][:, 0:W], AF.Ln, scale=1.0 / 128.0, bias=self.epsT[:, 0:1],
                          reads=[self.psk(bs), "epsT"], writes=[("rs", i)])
                    P.act(rs[i][:], rs[i][:], AF.Exp, scale=-0.5, reads=[("rs", i)], writes=[("rs", i)])
                    if QL < 4:
                        continue
                    P.stt(t1[i][:], self.ps[bq][:, 0:W], vec[:, gcol:gcol + 1], rC[:, w * W:(w + 1) * W], ALU.mult, ALU.mult,
                          reads=[self.psk(bq), ("vecs", l), "rC"], writes=[("t1", i)])
                    P.stt(t2[i][:], self.ps[bp][:, 0:W], vec[:, gcol + 1:gcol + 2], rS[:, w * W:(w + 1) * W], ALU.mult, ALU.mult,
                          reads=[self.psk(bp), ("vecs", l), "rS"], writes=[("t2", i)])
                    if QL < 5:
                        continue
                    P.tt("pool", t1[i][:], t1[i][:], t2[i][:], ALU.add, reads=[("t1", i), ("t2", i)], writes=[("t1", i)])
                    P.tt("pool", qrow[r][:, w * W:(w + 1) * W], t1[i][:], rs[i][:], ALU.mult,
                         reads=[("t1", i), ("rs", i)], writes=[("qrow", r, w)])
                if hc < 8:
                    dst = s.qT[hc * 128:(hc + 1) * 128, t0:t0 + TS]
                else:
                    dst = self.kT_d[(hc - 8) * 128:(hc - 7) * 128, s.ktok0 + t0: s.ktok0 + t0 + TS]
                if QL >= 6:
                    P.dma("pool", dst, qrow[r][:], reads=[("qrow", r, w) for w in range(NW)])
            P.barrier()
            if getattr(self, 'a_stop', None) == 'qk':
                return

            P.sb_off = mark
            wv = P.sb([128, 8, 256], BF16, "wv")
            vrow = P.sb([128, NB, 256], BF16, "vrow")
            load_w(C_V, 256, wv, "wv")
            wvk = [("wv", 0), ("wv", 128)]
            for tb in range(NB):
                bank = tb % 4
                w = (tb * 128) // W
                for k in range(8):
                    P.mm(self.ps[bank][:, 0:256], hT[:, k, tb * 128:(tb + 1) * 128], wv[:, k, :], start=(k == 0), stop=(k == 7),
                         reads=wvk + [("hT", k, w)], writes=[self.psk(bank)])
                P.cp("act" if tb % 2 else "dve", vrow[:, tb, :], self.ps[bank][:, 0:256], reads=[self.psk(bank)], writes=[("vrow", tb)])
            P.dma("act", self.v_d[s.ktok0 + t0: s.ktok0 + t0 + TS, :].rearrange("(b p) c -> p b c", p=128), vrow[:],
                  reads=[("vrow", tb) for tb in range(NB)])
            P.barrier()
            if getattr(self, 'a_stop', None) == 'v':
                return

            P.sb_off = mark
            ubuf = [P.sb([128, TS + 2], F32, "ubuf") for _ in range(2)]
            sA = P.sb([128, TS], F32, "sA")
            sB = P.sb([128, TS], F32, "sB")
            zrow = P.sb([128, TS], BF16, "zrow")
            ztiles = [P.sb([128, NB, 128], BF16, "ztile") for _ in range(2)]
            uc = 0

            def z_transposes(j):
                ztile = ztiles[j % 2]
                for blk in range(NB):
                    bank = 6 + (blk // 8) % 2
                    pv = self.ps[bank][:].bitcast(BF16)
                    P.tr(pv[:, (blk % 8) * 128:(blk % 8 + 1) * 128], zrow[:, blk * 128:(blk + 1) * 128], self.identb[:],
                         reads=["zrow", "identb"], writes=[self.psk(bank)])
                    if blk % 8 == 7 or blk == NB - 1:
                        b0 = blk - blk % 8
                        n = blk - b0 + 1
                        P.cp("act", ztile[:, b0:b0 + n, :].rearrange("p b c -> p (b c)"), pv[:, 0:n * 128],
                             reads=[self.psk(bank)], writes=[("ztile", j % 2, b0)])
                P.dma("act", s.z[t0:t0 + TS, j * 128:(j + 1) * 128].rearrange("(b p) c -> p b c", p=128), ztile[:],
                      reads=[("ztile", j % 2, b0) for b0 in range(0, NB, 8)])

            zpend = None
            for j in range(8):
                for part, (cchunk, cm, dst, dk) in enumerate(((12 + j, j, sA, "sA"), (28 + j, 16 + j, sB, "sB"), (20 + j, 8 + j, sA, "sA"))):
                    ub = uc % 2
                    uc += 1
                    wt, wk = load_w(cchunk * 128)
                    ukeys = []
                    for w in range(NW):
                        bank = w % 4
                        self._proj(self.ps[bank][:, 0:W], wt, hT, w, W, wk, bank)
                        P.cp("act", ubuf[ub][:, 1 + w * W: 1 + (w + 1) * W], self.ps[bank][:, 0:W],
                             reads=[self.psk(bank)], writes=[("ubuf", ub, w)])
                        ukeys.append(("ubuf", ub, w))
                    proj_halo(self.ps[4][:, 0:16], wt, wk, 4)
                    P.cp("act", ubuf[ub][:, 0:TS + 2:TS + 1], self.ps[4][:, 7:9], reads=[self.psk(4)], writes=[("ubuf", ub, "h")])
                    ukeys.append(("ubuf", ub, "h"))
                    if part == 0 and zpend is not None:
                        z_transposes(zpend)
                        zpend = None
                    c0 = self.V_CW + cm
                    P.act(dst[:], ubuf[ub][:, 1:TS + 1], AF.Identity, scale=vec[:, c0 + 24:c0 + 25], bias=vec[:, self.V_CB + cm:self.V_CB + cm + 1],
                          reads=ukeys + [("vecs", l)], writes=[dk])
                    P.stt(dst[:], ubuf[ub][:, 0:TS], vec[:, c0:c0 + 1], dst[:], ALU.mult, ALU.add, reads=ukeys + [dk, ("vecs", l)], writes=[dk])
                    P.stt(dst[:], ubuf[ub][:, 2:TS + 2], vec[:, c0 + 48:c0 + 49], dst[:], ALU.mult, ALU.add, reads=ukeys + [dk, ("vecs", l)], writes=[dk])
                    if part == 1 and s.tag == "c":
                        P.tt("pool", sB[:], sA[:], sB[:], ALU.mult, reads=["sA", "sB"], writes=["sB"])
                        P.dma("pool", s.zT[j * 128:(j + 1) * 128, t0:t0 + TS], sB[:], reads=["sB"])
                    elif part == 1:
                        P.tt("pool", zrow[:], sA[:], sB[:], ALU.mult, reads=["sA", "sB"], writes=["zrow"])
                        zpend = j
                    if part == 2:
                        P.dma("pool", s.x0T[j * 128:(j + 1) * 128, t0:t0 + TS], sA[:], reads=["sA"])
            if zpend is not None:
                z_transposes(zpend)
            P.barrier()
            if getattr(self, 'a_stop', None) == 'hy':
                return

            P.sb_off = mark
            n = TS + 16
            pbuf = [P.sb([128, n], F32, "pbuf") for _ in range(2)]
            A = P.sb([128, n], F32, "pA")
            Bb = P.sb([128, n], F32, "pB")
            mT = [P.sb([128, TS], BF16, "mT") for _ in range(2)]
            prow = [P.sb([128, TS], BF16, "prow") for _ in range(2)]
            pw = P.sb([128, 2, 256], BF16, "pw")
            pwf = P.sb([128, 2, 256], F32, "pwf")
            tmp8 = P.sb([128, 8], F32, "tmp8")
            pe4 = self.pedge[:].rearrange("p (g s e) -> p g s e", g=4, s=2)
            for g in range(4):
                wsz = POOL_WINDOWS[g]
                kk = g + 1
                o = 8 + wsz // 2 - 1
                P.dma("sp", pwf[:], self.inp["pool_w"][l, g].rearrange("(i p) o -> p i o", p=128), writes=["pwf"])
                P.cp("pool", pw[:], pwf[:], reads=["pwf"], writes=["pw"])
                for i in range(2):
                    wt, wk = load_w((36 + 2 * g + i) * 128)
                    pk = []
                    for w in range(NW):
                        bank = w % 4
                        self._proj(self.ps[bank][:, 0:W], wt, hT, w, W, wk, bank)
                        P.cp("act" if w % 2 else "dve", pbuf[i][:, 8 + w * W: 8 + (w + 1) * W], self.ps[bank][:, 0:W],
                             reads=[self.psk(bank)], writes=[("pbuf", i, w)])
                        pk.append(("pbuf", i, w))
                    proj_halo(self.ps[4][:, 0:16], wt, wk, 4)
                    P.cp("dve", pbuf[i][:, 0:8], self.ps[4][:, 0:8], reads=[self.psk(4)], writes=[("pbuf", i, "h0")])
                    P.cp("dve", pbuf[i][:, TS + 8:TS + 16], self.ps[4][:, 8:16], reads=[self.psk(4)], writes=[("pbuf", i, "h1")])
                    pk += [("pbuf", i, "h0"), ("pbuf", i, "h1")]
                    u = pbuf[i]
                    ce = "pool"
                    P.tt(ce, A[:, 1:n], u[:, 1:n], u[:, 0:n - 1], ALU.add, reads=pk, writes=["pA"])
                    R, rk = A, "pA"
                    if kk >= 2:
                        P.tt(ce, Bb[:, 3:n], A[:, 3:n], A[:, 1:n - 2], ALU.add, reads=["pA"], writes=["pB"])
                        R, rk = Bb, "pB"
                    if kk >= 3:
                        P.tt("dve", A[:, 7:n], Bb[:, 7:n], Bb[:, 3:n - 4], ALU.add, reads=["pB"], writes=["pA"])
                        R, rk = A, "pA"
                    if kk >= 4:
                        P.tt("dve", Bb[:, 15:n], A[:, 15:n], A[:, 7:n - 8], ALU.add, reads=["pA"], writes=["pB"])
                        R, rk = Bb, "pB"
                    P.stt(mT[i][:], R[:, o:o + TS], 1.0 / wsz, u[:, 8:8 + TS], ALU.mult, ALU.subtract, reads=[rk] + pk, writes=[("mT", i)])
                    if t0 == 0:
                        P.tt("dve", tmp8[:], R[:, o:o + 8], pe4[:, g, 0, :], ALU.mult, reads=[rk, "pedge"], writes=["tmp8"])
                        P.tt("dve", mT[i][:, 0:8], tmp8[:], u[:, 8:16], ALU.subtract, reads=["tmp8"] + pk, writes=[("mT", i)])
                    if t0 + TS == T:
                        P.tt("dve", tmp8[:], R[:, o + TS - 8:o + TS], pe4[:, g, 1, :], ALU.mult, reads=[rk, "pedge"], writes=["tmp8"])
                        P.tt("dve", mT[i][:, TS - 8:TS], tmp8[:], u[:, TS:TS + 8], ALU.subtract, reads=["tmp8"] + pk, writes=[("mT", i)])
                for oc in range(2):
                    for w in range(NW):
                        bank = w % 4
                        for i in range(2):
                            P.mm(self.ps[bank][:, 0:W], pw[:, i, oc * 128:(oc + 1) * 128], mT[i][:, w * W:(w + 1) * W],
                                 start=(i == 0), stop=(i == 1), reads=["pw", ("mT", i)], writes=[self.psk(bank)])
                        cidx = self.V_PS + 2 * g + oc
                        P.act(prow[oc][:, w * W:(w + 1) * W], self.ps[bank][:, 0:W], AF.Identity, scale=vec[:, cidx:cidx + 1],
                              reads=[self.psk(bank), ("vecs", l)], writes=[("prow", oc, w)])
                    P.dma("act", s.poolT[(2 * g + oc) * 128:(2 * g + oc + 1) * 128, t0:t0 + TS], prow[oc][:],
                          reads=[("prow", oc, w) for w in range(NW)])
            P.barrier()
            if getattr(self, 'a_stop', None) == 'pool':
                return

            P.sb_off = mark
            grow = [P.sb([128, TS], BF16, "grow") for _ in range(2)]
            for gc in range(24):
                r = gc % 2
                wt, wk = load_w((44 + gc) * 128)
                for w in range(NW):
                    bank = w % 4
                    self._proj(self.ps[bank][:, 0:W], wt, hT, w, W, wk, bank)
                    P.act(grow[r][:, w * W:(w + 1) * W], self.ps[bank][:, 0:W], AF.Sigmoid, reads=[self.psk(bank)], writes=[("grow", r, w)])
                P.dma("act", s.gT[gc // 8, (gc % 8) * 128:(gc % 8 + 1) * 128, t0:t0 + TS], grow[r][:],
                      reads=[("grow", r, w) for w in range(NW)])
            P.barrier()
            if getattr(self, 'a_stop', None) == 'gate':
                return

    def stage_b(self, l, s, NK):
        P = self.P
        P.sb_reset()
        NQ, W = s.T, s.W
        NB = NK // 128
        KT = P.sb([128, 2, NK], BF16, "KT")
        V = P.sb([128, NB, 256], BF16, "V")
        for kv in range(2):
            P.dma("sp" if kv else "act", KT[:, kv, :], self.kT_d[kv * 128:(kv + 1) * 128, 0:NK], writes=[("KT", kv)])
        vsrc = self.v_d[0:NK, :].rearrange("(b p) c -> p b c", p=128)
        vk = []
        for b0 in range(0, NB, 11):
            b1 = min(NB, b0 + 11)
            P.dma("sp", V[:, b0:b1, :], vsrc[:, b0:b1, :], writes=[("V", b0)])
            vk.append(("V", b0))
        QT = [P.sb([128, 8, W], BF16, "QT") for _ in range(2)]
        attT = [P.sb([128, 8, W], BF16, "attT") for _ in range(2)]
        NPT = 8
        pT = [P.sb([128, W], BF16, "pT") for _ in range(NPT)]
        pool_tbs = [tb for tb in range(NB) if tb % 8 in (1, 4, 6)]
        dve_tbs = [tb for tb in range(NB) if tb % 8 not in (1, 4, 6)]
        rden = [P.sb([128, W], F32, "rden") for _ in range(2)]
        accD = [P.sb([128, W], F32, "accD") for _ in range(2)]
        accP = [P.sb([128, W], F32, "accP") for _ in range(2)]
        scale = float(HD) ** -0.5
        steps = [(qw, h, tb) for qw in range(NQ // W) for h in range(8) for tb in range(NB)]
        n = len(steps)

        def issue_S(i):
            qw, h, tb = steps[i]
            r = i % 4
            if h == 0 and tb == 0:
                P.dma("sp", QT[qw % 2][:], s.qT[:, qw * W:(qw + 1) * W].rearrange("(h p) t -> p h t", p=128), writes=[("QT", qw % 2)])
            P.mm(self.ps[r][:, 0:W], KT[:, h // 4, tb * 128:(tb + 1) * 128], QT[qw % 2][:, h, :], True, True,
                 reads=[("KT", h // 4), ("QT", qw % 2)], writes=[self.psk(r)])

        LA = 3
        for i in range(min(LA, n)):
            issue_S(i)
        for i, (qw, h, tb) in enumerate(steps):
            r = i % 4
            r4 = i % NPT
            kv = h // 4
            hp = h % 2
            ob = 4 + hp
            P.act(pT[r4][:], self.ps[r][:, 0:W], AF.Exp, scale=scale, reads=[self.psk(r)], writes=[("pT", r4)])
            P.mm(self.ps[ob][:, 0:W], V[:, tb, kv * 128:(kv + 1) * 128], pT[r4][:], tb == 0, tb == NB - 1,
                 reads=vk + [("pT", r4)], writes=[self.psk(ob)])
            ab = 6 + hp
            if tb in dve_tbs:
                if tb == dve_tbs[0] and tb == dve_tbs[-1]:
                    P.cp("dve", accD[hp][:], pT[r4][:], reads=[("pT", r4)], writes=[("accD", hp)])
                elif tb == dve_tbs[0]:
                    P.cp("dve", self.ps[ab][:, 0:W], pT[r4][:], reads=[("pT", r4)], writes=[self.psk(ab)])
                elif tb != dve_tbs[-1]:
                    P.tt("dve", self.ps[ab][:, 0:W], self.ps[ab][:, 0:W], pT[r4][:], ALU.add, reads=[("pT", r4), self.psk(ab)], writes=[self.psk(ab)])
                else:
                    P.tt("dve", accD[hp][:], self.ps[ab][:, 0:W], pT[r4][:], ALU.add, reads=[("pT", r4), self.psk(ab)], writes=[("accD", hp)])
            else:
                if tb == pool_tbs[0]:
                    P.cp("pool", accP[hp][:], pT[r4][:], reads=[("pT", r4)], writes=[("accP", hp)])
                else:
                    P.tt("pool", accP[hp][:], accP[hp][:], pT[r4][:], ALU.add, reads=[("pT", r4), ("accP", hp)], writes=[("accP", hp)])
            if i + LA < n:
                issue_S(i + LA)
            if tb == NB - 1:
                rd = rden[hp]
                db = ab
                P.mm(self.ps[db][:, 0:W], self.ones_f[:], accD[hp][:], True, False, reads=[("accD", hp)], writes=[self.psk(db)])
                P.mm(self.ps[db][:, 0:W], self.ones_f[:], accP[hp][:], False, True, reads=[("accP", hp)], writes=[self.psk(db)])
                P.add("dve", lambda e, rd=rd, db=db: e.reciprocal(out=rd[:], in_=self.ps[db][:, 0:W]), reads=[self.psk(db)], writes=[("rden", hp)])
                P.tt("dve", attT[qw % 2][:, h, :], self.ps[ob][:, 0:W], rd[:], ALU.mult,
                     reads=[self.psk(ob), ("rden", hp)], writes=[("attT", qw % 2, h)])
                if h == 7:
                    P.dma("pool", s.attnT[:, qw * W:(qw + 1) * W].rearrange("(h p) t -> p h t", p=128), attT[qw % 2][:],
                          reads=[("attT", qw % 2, hh) for hh in range(8)])
        P.barrier()

    def _sin_layer(self, ps_ap, fcol, bcol, hv, tmp, tmp2, out_ap, psbank, okey, sfx=0):
        P = self.P
        MAGIC = 12582912.0
        k1, k2 = ("sl_tmp", sfx), ("sl_tmp2", sfx)
        P.ts("dve", tmp, ps_ap, hv[:, fcol:fcol + 1], hv[:, bcol:bcol + 1], ALU.mult, ALU.add, reads=[self.psk(psbank)], writes=[k1])
        P.ts("dve", tmp2, tmp, 1.0 / (2 * math.pi), MAGIC, ALU.mult, ALU.add, reads=[k1], writes=[k2])
        P.ts("dve", tmp2, tmp2, MAGIC, -2 * math.pi, ALU.subtract, ALU.mult, reads=[k2], writes=[k2])
        P.tt("dve", tmp, tmp, tmp2, ALU.add, reads=[k1, k2], writes=[k1])
        P.act(out_ap, tmp, AF.Sin, reads=[k1], writes=[okey])

    def stage_k(self, l, tag):
        P = self.P
        P.sb_reset()
        I = self.inp
        hv = self.hfv[l]
        Kf = self.Kf[tag]
        w1s = P.sb([33, 64], F32, "w1s")
        w2s = P.sb([64, 64], F32, "w2s")
        w3f = P.sb([64, 2048], F32, "w3f")
        w3b = P.sb([64, 2048], BF16, "w3b")
        h2T = [P.sb([64, 8192], BF16, "h2T") for _ in range(2)]
        dl = P.sb([128, 1024], F32, "dl")
        drow = P.sb([128, 1024], F32, "drow")
        nrow = P.sb([128, 1024], F32, "nrow")
        negt = P.sb([128, 128], F32, "negt")
        mask = P.sb([128, 128], F32, "mask")
        F1 = P.sb([128, 2, 128], BF16, "F1")
        P.dma("sp", w1s[:], I["hf_w1"][l], writes=["w1s"])
        P.dma("sp", w2s[:], I["hf_w2"][l], writes=["w2s"])
        P.dma("sp", w3f[:], I["hf_w3"][l], writes=["w3f"])
        P.cp("pool", w3b[:], w3f[:], reads=["w3f"], writes=["w3b"])
        P.dma("act", dl[:], self.cst["deltas"].partition_broadcast(128).rearrange("p o c -> p (o c)"), writes=["dl"])
        P.dma("act", drow[:], I["hy_d"][l].rearrange("(o c) -> o c", o=1).partition_broadcast(128).rearrange("p o c -> p (o c)"), writes=["drow"])
        P.dma("act", negt[:], self.cst[f"negt_{tag}"], writes=["negt"])
        P.dma("act", mask[:], self.cst[f"mask_{tag}"], writes=["mask"])
        P.dma("act", F1[:], self.cst["F1"], writes=["F1"])
        mark = P.sb_off
        ft = [P.sb([33, 512], F32, "ft") for _ in range(2)]
        tmpA = [P.sb([64, 512], F32, "sl_tmp") for _ in range(4)]
        tmpB = [P.sb([64, 512], F32, "sl_tmp2") for _ in range(4)]
        h1s = [P.sb([64, 512], F32, "h1") for _ in range(2)]
        it = 0
        for d, nm in enumerate((f"featF_{tag}", f"featR_{tag}")):
            for w in range(16):
                b = it % 2
                it += 1
                h1 = h1s[b]
                P.dma("sp", ft[b][:], self.cst[nm][:, w * 512:(w + 1) * 512], writes=[("ft", b)])
                P.mm(self.ps[b][0:64, 0:512], w1s[:], ft[b][:], True, True, reads=["w1s", ("ft", b)], writes=[self.psk(b)])
                self._sin_layer(self.ps[b][0:64, 0:512], 1, 3, hv, tmpA[b][:], tmpB[b][:], h1[:], b, ("h1", b), sfx=b)
                P.mm(self.ps[2 + b][0:64, 0:512], w2s[:], h1[:], True, True, reads=["w2s", ("h1", b)], writes=[self.psk(2 + b)])
                self._sin_layer(self.ps[2 + b][0:64, 0:512], 1, 4, hv, tmpA[2 + b][:], tmpB[2 + b][:], h2T[d][:, w * 512:(w + 1) * 512], 2 + b, ("h2T", d), sfx=2 + b)
        P.sb_off = mark
        kt = [P.sb([128, 8, 1024], BF16, "kt") for _ in range(2)]
        wn = [P.sb([128, 1024], F32, "wn") for _ in range(2)]
        sqb = [P.sb([128, 1024], BF16, "sqk") for _ in range(2)]
        Bt = [[P.sb([128, 8, 1024], BF16, "Btk") for _ in range(2)] for _ in range(2)]
        for jg in range(16):
            kb = jg % 2
            for n2i in range(8):
                n2 = jg * 8 + n2i
                i = n2 % 2
                ba = 0 if i == 0 else 2
                for ch in range(2):
                    P.mm(self.ps[ba + ch][0:64, 0:512], h2T[0][:, n2:8192:128], w3b[:, ch * 512:(ch + 1) * 512], True, True,
                         reads=[("h2T", 0), "w3b"], writes=[self.psk(ba + ch)])
                    P.mm(self.ps[ba + ch][64:128, 0:512], h2T[1][:, n2:8192:128], w3b[:, 1024 + ch * 512:1024 + (ch + 1) * 512], True, True,
                         reads=[("h2T", 1), "w3b"], writes=[self.psk(ba + ch)])
                P.act(wn[i][:], dl[:], AF.Exp, scale=negt[:, n2:n2 + 1], reads=["dl", "negt"], writes=[("wn", i)])
                P.ts("dve", wn[i][:], wn[i][:], 0.05, None, ALU.add, None, reads=[("wn", i)], writes=[("wn", i)])
                for ch in range(2):
                    P.stt(kt[kb][:, n2i, ch * 512:(ch + 1) * 512], self.ps[ba + ch][:, 0:512], mask[:, n2:n2 + 1], wn[i][:, ch * 512:(ch + 1) * 512],
                          ALU.mult, ALU.mult, reads=[self.psk(ba + ch), "mask", ("wn", i)], writes=[("kt", kb, n2i)])
                P.act(sqb[i][:], kt[kb][:, n2i, :], AF.Square, reads=[("kt", kb, n2i)], writes=[("sqk", i)])
                for ch in range(2):
                    P.mm(self.ps[6 + ch][:, 0:512], self.ones_b[:], sqb[i][:, ch * 512:(ch + 1) * 512], n2 == 0, n2 == 127,
                         reads=[("sqk", i)], writes=[self.psk(6 + ch)])
            ktf = kt[kb][:].rearrange("p a c -> p (a c)")
            for ri in range(2):
                btf = Bt[ri][kb][:].rearrange("p a c -> p (a c)")
                for cw in range(16):
                    bank = 4 + (cw % 2)
                    P.mm(self.ps[bank][:, 0:512], F1[:, ri, :], ktf[:, cw * 512:(cw + 1) * 512], True, True,
                         reads=["F1"] + [("kt", kb, q) for q in range(8)], writes=[self.psk(bank)])
                    P.cp("act" if cw % 2 else "dve", btf[:, cw * 512:(cw + 1) * 512], self.ps[bank][:, 0:512],
                         reads=[self.psk(bank)], writes=[("Btk", ri, kb, cw)])
                P.dma("pool", self.Bd[ri][:, jg * 8:(jg + 1) * 8, :], Bt[ri][kb][:],
                      reads=[("Btk", ri, kb, cw) for cw in range(16)], writes=[("Bd", ri, jg)])
        for ch in range(2):
            P.act(nrow[:, ch * 512:(ch + 1) * 512], self.ps[6 + ch][:, 0:512], AF.Ln, bias=self.epsT[:, 0:1], reads=[self.psk(6 + ch)], writes=[("nrow", ch)])
            P.act(nrow[:, ch * 512:(ch + 1) * 512], nrow[:, ch * 512:(ch + 1) * 512], AF.Exp, scale=-0.5, reads=[("nrow", ch)], writes=[("nrow", ch)])
        P.barrier()
        P.sb_off = mark
        Br = [[P.sb([128, 1024], BF16, "Brk") for _ in range(2)] for _ in range(2)]
        G = [P.sb([128, 3, 128], BF16, "Gk") for _ in range(2)]
        Kt = [[P.sb([128, 1024], BF16, "Kt") for _ in range(2)] for _ in range(2)]
        tz = [P.sb([128, 512], F32, "tz") for _ in range(2)]
        for k1 in range(128):
            b = k1 % 2
            for ri in range(2):
                P.dma("sp", Br[ri][b][:], self.Bd[ri][k1], writes=[("Brk", ri, b)])
            P.dma("sp", G[b][:], self.cst["GT"][k1], writes=[("Gk", b)])
            for ch in range(2):
                bs = 0 if (2 * k1 + ch) % 2 == 0 else 2
                cs = slice(ch * 512, (ch + 1) * 512)
                rk = [("Brk", 0, b), ("Brk", 1, b), ("Gk", b)]
                P.mm(self.ps[bs][:, 0:512], G[b][:, 0, :], Br[0][b][:, cs], True, False, reads=rk, writes=[self.psk(bs)])
                P.mm(self.ps[bs][:, 0:512], G[b][:, 2, :], Br[1][b][:, cs], False, True, reads=rk, writes=[self.psk(bs)])
                P.mm(self.ps[bs + 1][:, 0:512], G[b][:, 1, :], Br[0][b][:, cs], True, False, reads=rk, writes=[self.psk(bs + 1)])
                P.mm(self.ps[bs + 1][:, 0:512], G[b][:, 0, :], Br[1][b][:, cs], False, True, reads=rk, writes=[self.psk(bs + 1)])
                P.tt("dve", tz[ch][:], self.ps[bs][:, 0:512], nrow[:, cs], ALU.mult, reads=[self.psk(bs), ("nrow", ch)], writes=[("tz", ch)])
                P.tt("pool", Kt[0][b][:, cs], tz[ch][:], drow[:, cs], ALU.add, reads=[("tz", ch), "drow"], writes=[("Kt", 0, b, ch)])
                P.tt("dve", Kt[1][b][:, cs], self.ps[bs + 1][:, 0:512], nrow[:, cs], ALU.mult, reads=[self.psk(bs + 1), ("nrow", ch)], writes=[("Kt", 1, b, ch)])
            for ri in range(2):
                P.dma("pool", Kf[ri][k1], Kt[ri][b][:], reads=[("Kt", ri, b, 0), ("Kt", ri, b, 1)])
        P.barrier()

    def stage_h(self, l, s):
        P = self.P
        P.sb_reset()
        Kf = self.Kf[s.tag]
        nval = 64 if s.tag == "l" else s.T // 128
        F1 = P.sb([128, 2, 128], BF16, "F1")
        I2 = P.sb([128, 2, 64], BF16, "I2")
        P.dma("act", F1[:], self.cst["F1"], writes=["F1"])
        P.dma("act", I2[:], self.cst["I2"], writes=["I2"])
        mark = P.sb_off
        zt = [P.sb([64, 8, 1024], BF16, "zt") for _ in range(2)]
        Bt = [[P.sb([128, 8, 1024], BF16, "Bth") for _ in range(2)] for _ in range(2)]
        zv = s.z.rearrange("(a b) c -> a b c", b=128)
        if nval < 64:
            for b in range(2):
                P.memset("pool", zt[b][:], 0.0, writes=[("zt", b)])
        for jg in range(16):
            b = jg % 2
            P.dma("sp", zt[b][0:nval, :, :], zv[0:nval, jg * 8:(jg + 1) * 8, :], reads=[("zt", b)] if nval < 64 else [], writes=[("ztd", b)])
            ztf = zt[b][:].rearrange("p a c -> p (a c)")
            for ri in range(2):
                btf = Bt[ri][b][:].rearrange("p a c -> p (a c)")
                for cw in range(16):
                    bank = (cw % 4)
                    P.mm(self.ps[bank][:, 0:512], F1[0:64, ri, :], ztf[:, cw * 512:(cw + 1) * 512], True, True,
                         reads=["F1", ("ztd", b), ("zt", b)], writes=[self.psk(bank)])
                    P.cp("act" if cw % 2 else "dve", btf[:, cw * 512:(cw + 1) * 512], self.ps[bank][:, 0:512],
                         reads=[self.psk(bank)], writes=[("Bth", ri, b, cw)])
                P.dma("pool", self.Bd[ri][:, jg * 8:(jg + 1) * 8, :], Bt[ri][b][:],
                      reads=[("Bth", ri, b, cw) for cw in range(16)], writes=[("Bd", ri, jg)])
        P.barrier()
        P.sb_off = mark
        Br = [[P.sb([128, 1024], BF16, "Brh") for _ in range(2)] for _ in range(2)]
        Kt = [[P.sb([128, 1024], BF16, "Kth") for _ in range(2)] for _ in range(2)]
        G = [P.sb([128, 3, 128], BF16, "Gh") for _ in range(2)]
        Hh = [P.sb([128, 3, 128], BF16, "Hh") for _ in range(2)]
        Y = [[P.sb([128, 512], BF16, "Yh") for _ in range(2)] for _ in range(2)]
        tq = [[P.sb([128, 512], F32, "tq") for _ in range(4)] for _ in range(2)]
        Dt = [[P.sb([128, 1024], BF16, "Dth") for _ in range(2)] for _ in range(2)]
        def second_half(k1, ch):
            b = k1 % 2
            par = ch
            bs = 0 if par == 0 else 4
            cs = slice(ch * 512, (ch + 1) * 512)
            yk = [("Yh", 0, par), ("Yh", 1, par), ("Hh", b)]
            P.mm(self.ps[bs + 2][:, 0:512], Hh[b][:, 0, :], Y[0][par][:], True, False, reads=yk, writes=[self.psk(bs + 2)])
            P.mm(self.ps[bs + 2][:, 0:512], Hh[b][:, 1, :], Y[1][par][:], False, True, reads=yk, writes=[self.psk(bs + 2)])
            P.mm(self.ps[bs + 3][:, 0:512], Hh[b][:, 2, :], Y[0][par][:], True, False, reads=yk, writes=[self.psk(bs + 3)])
            P.mm(self.ps[bs + 3][:, 0:512], Hh[b][:, 0, :], Y[1][par][:], False, True, reads=yk, writes=[self.psk(bs + 3)])
            P.cp("act", Dt[0][b][:, cs], self.ps[bs + 2][:, 0:512], reads=[self.psk(bs + 2)], writes=[("Dth", 0, b, ch)])
            P.cp("act", Dt[1][b][:, cs], self.ps[bs + 3][:, 0:512], reads=[self.psk(bs + 3)], writes=[("Dth", 1, b, ch)])
            if ch == 1:
                for ri in range(2):
                    P.dma("act", self.Dd[ri][k1], Dt[ri][b][:], reads=[("Dth", ri, b, 0), ("Dth", ri, b, 1)])

        prev = None
        for k1 in range(128):
            b = k1 % 2
            for ri in range(2):
                P.dma("sp", Br[ri][b][:], self.Bd[ri][k1], writes=[("Brh", ri, b)])
                P.dma("sp", Kt[ri][b][:], Kf[ri][k1], writes=[("Kth", ri, b)])
            P.dma("sp", G[b][:], self.cst["GT"][k1], writes=[("Gh", b)])
            P.dma("sp", Hh[b][:], self.cst["HT"][k1], writes=[("Hh", b)])
            for ch in range(2):
                par = ch
                bs = 0 if par == 0 else 4
                cs = slice(ch * 512, (ch + 1) * 512)
                rk = [("Brh", 0, b), ("Brh", 1, b), ("Gh", b)]
                P.mm(self.ps[bs][:, 0:512], G[b][:, 0, :], Br[0][b][:, cs], True, False, reads=rk, writes=[self.psk(bs)])
                P.mm(self.ps[bs][:, 0:512], G[b][:, 2, :], Br[1][b][:, cs], False, True, reads=rk, writes=[self.psk(bs)])
                P.mm(self.ps[bs + 1][:, 0:512], G[b][:, 1, :], Br[0][b][:, cs], True, False, reads=rk, writes=[self.psk(bs + 1)])
                P.mm(self.ps[bs + 1][:, 0:512], G[b][:, 0, :], Br[1][b][:, cs], False, True, reads=rk, writes=[self.psk(bs + 1)])
                if prev is not None:
                    second_half(*prev)
                t = tq[par]
                kk = [("Kth", 0, b), ("Kth", 1, b)]
                P.tt("dve", t[0][:], self.ps[bs][:, 0:512], Kt[0][b][:, cs], ALU.mult, reads=[self.psk(bs)] + kk, writes=[("tq", par, 0)])
                P.tt("dve", t[1][:], self.ps[bs + 1][:, 0:512], Kt[1][b][:, cs], ALU.mult, reads=[self.psk(bs + 1)] + kk, writes=[("tq", par, 1)])
                P.tt("dve", t[2][:], self.ps[bs][:, 0:512], Kt[1][b][:, cs], ALU.mult, reads=[self.psk(bs)] + kk, writes=[("tq", par, 2)])
                P.tt("dve", t[3][:], self.ps[bs + 1][:, 0:512], Kt[0][b][:, cs], ALU.mult, reads=[self.psk(bs + 1)] + kk, writes=[("tq", par, 3)])
                P.tt("pool", Y[0][par][:], t[0][:], t[1][:], ALU.subtract, reads=[("tq", par, 0), ("tq", par, 1)], writes=[("Yh", 0, par)])
                P.tt("pool", Y[1][par][:], t[2][:], t[3][:], ALU.add, reads=[("tq", par, 2), ("tq", par, 3)], writes=[("Yh", 1, par)])
                prev = (k1, ch)
        second_half(*prev)
        P.barrier()
        P.sb_off = mark
        Dr = [[P.sb([128, 8, 1024], BF16, "Drh") for _ in range(2)] for _ in range(2)]
        yt = [P.sb([64, 8, 1024], F32, "yt") for _ in range(2)]
        yv = s.y.rearrange("(a b) c -> a b c", b=128)
        for jg in range(16):
            b = jg % 2
            for ri in range(2):
                P.dma("sp", Dr[ri][b][:], self.Dd[ri][:, jg * 8:(jg + 1) * 8, :], writes=[("Drh", ri, b)])
            d0 = Dr[0][b][:].rearrange("p a c -> p (a c)")
            d1 = Dr[1][b][:].rearrange("p a c -> p (a c)")
            ytf = yt[b][:].rearrange("p a c -> p (a c)")
            for cw in range(16):
                bank = cw % 4
                cs = slice(cw * 512, (cw + 1) * 512)
                P.mm(self.ps[bank][0:64, 0:512], I2[:, 0, :], d0[:, cs], True, False, reads=["I2", ("Drh", 0, b), ("Drh", 1, b)], writes=[self.psk(bank)])
                P.mm(self.ps[bank][0:64, 0:512], I2[:, 1, :], d1[:, cs], False, True, reads=["I2", ("Drh", 0, b), ("Drh", 1, b)], writes=[self.psk(bank)])
                P.cp("act" if cw % 2 else "dve", ytf[:, cs], self.ps[bank][0:64, 0:512], reads=[self.psk(bank)], writes=[("yt", b, cw)])
            P.dma("pool", yv[0:nval, jg * 8:(jg + 1) * 8, :], yt[b][0:nval, :, :], reads=[("yt", b, cw) for cw in range(16)])
        P.barrier()

    def stage_hc(self, l, s):
        P = self.P
        P.sb_reset()
        I = self.inp
        hv, vec = self.hfv[l], self.vecs[l]
        L = s.T
        w1s = P.sb([33, 64], F32, "w1s")
        w2s = P.sb([64, 64], F32, "w2s")
        w3f = P.sb([64, 2048], F32, "w3f")
        ft = P.sb([33, L], F32, "ft")
        tmp = P.sb([64, L], F32, "sl_tmp")
        tmp2 = P.sb([64, L], F32, "sl_tmp2")
        h1 = P.sb([64, L], F32, "h1")
        h2 = P.sb([64, L], F32, "h2")
        t01 = P.sb([128, L], F32, "t01")
        ndl = P.sb([128, 8], F32, "ndl")
        P.dma("sp", w1s[:], I["hf_w1"][l], writes=["w1s"])
        P.dma("sp", w2s[:], I["hf_w2"][l], writes=["w2s"])
        P.dma("sp", w3f[:], I["hf_w3"][l], writes=["w3f"])
        P.dma("act", ft[:], self.cst["featF_c"][:, 0:L], writes=["ft"])
        P.dma("act", t01[:], self.cst["t01row_c"], writes=["t01"])
        P.dma("act", ndl[:], self.cst["ndeltasT"], writes=["ndl"])
        P.mm(self.ps[0][0:64, 0:L], w1s[:], ft[:], True, True, reads=["w1s", "ft"], writes=[self.psk(0)])
        self._sin_layer(self.ps[0][0:64, 0:L], 1, 3, hv, tmp[:], tmp2[:], h1[:], 0, "h1")
        P.mm(self.ps[1][0:64, 0:L], w2s[:], h1[:], True, True, reads=["w2s", "h1"], writes=[self.psk(1)])
        self._sin_layer(self.ps[1][0:64, 0:L], 1, 4, hv, tmp[:], tmp2[:], h2[:], 1, "h2")
        NP = 3 * L - 2
        zp = [P.sb([128, NP], F32, "zp") for _ in range(2)]
        win = [P.sb([128, L], F32, "win") for _ in range(2)]
        kF = [P.sb([128, L], F32, "kF") for _ in range(2)]
        kB = [P.sb([128, L], F32, "kB") for _ in range(2)]
        junk = P.sb([128, L], F32, "junk")
        ss = [P.sb([128, 4], F32, "ss") for _ in range(2)]
        accF = [P.sb([128, L], F32, "accF") for _ in range(2)]
        accB = [P.sb([128, L], F32, "accB") for _ in range(2)]
        x0c = [P.sb([128, L], F32, "x0c") for _ in range(2)]
        hyo = [P.sb([128, L], BF16, "hyo") for _ in range(2)]
        for b in range(2):
            P.memset("pool", zp[b][:], 0.0, writes=[("zp", b)])
        for j in range(8):
            b = j % 2
            bf, bb = (2, 3) if b == 0 else (4, 5)
            P.dma("sp", zp[b][:, L - 1:2 * L - 1], s.zT[j * 128:(j + 1) * 128, :], reads=[("zp", b)], writes=[("zpd", b)])
            P.dma("act", x0c[b][:], s.x0T[j * 128:(j + 1) * 128, :], writes=[("x0c", b)])
            P.mm(self.ps[bf][:, 0:L], w3f[:, j * 128:(j + 1) * 128], h2[:], True, True, reads=["w3f", "h2"], writes=[self.psk(bf)])
            P.mm(self.ps[bb][:, 0:L], w3f[:, 1024 + j * 128:1024 + (j + 1) * 128], h2[:], True, True, reads=["w3f", "h2"], writes=[self.psk(bb)])
            P.act(win[b][:], t01[:], AF.Exp, scale=ndl[:, j:j + 1], reads=["t01", "ndl"], writes=[("win", b)])
            P.ts("pool", win[b][:], win[b][:], 1.0, 0.05, ALU.mult, ALU.add, reads=[("win", b)], writes=[("win", b)])
            P.tt("dve", kF[b][:], self.ps[bf][:, 0:L], win[b][:], ALU.mult, reads=[self.psk(bf), ("win", b)], writes=[("kF", b)])
            P.tt("dve", kB[b][:], self.ps[bb][:, 0:L], win[b][:], ALU.mult, reads=[self.psk(bb), ("win", b)], writes=[("kB", b)])
            P.memset("dve", kB[b][:, 0:1], 0.0, writes=[("kB", b)])
            P.add("act", lambda e, b=b: e.activation(out=junk[:], in_=kF[b][:], func=AF.Square, accum_out=ss[b][:, 0:1]),
                  reads=[("kF", b)], writes=[("ss", b, 0), "junk"])
            P.add("act", lambda e, b=b: e.activation(out=junk[:], in_=kB[b][:], func=AF.Square, accum_out=ss[b][:, 1:2]),
                  reads=[("kB", b)], writes=[("ss", b, 1), "junk"])
            P.tt("pool", ss[b][:, 2:3], ss[b][:, 0:1], ss[b][:, 1:2], ALU.add, reads=[("ss", b, 0), ("ss", b, 1)], writes=[("ss", b, 2)])
            P.act(ss[b][:, 2:3], ss[b][:, 2:3], AF.Ln, bias=self.epsT[:, 0:1], reads=[("ss", b, 2)], writes=[("ss", b, 2)])
            P.act(ss[b][:, 3:4], ss[b][:, 2:3], AF.Exp, scale=-0.5, reads=[("ss", b, 2)], writes=[("ss", b, 3)])
            zk = [("zp", b), ("zpd", b)]
            P.ts("dve", accF[b][:], zp[b][:, L - 1:2 * L - 1], kF[b][:, 0:1], None, ALU.mult, None, reads=zk + [("kF", b)], writes=[("accF", b)])
            P.ts("dve", accB[b][:], zp[b][:, L:2 * L], kB[b][:, 1:2], None, ALU.mult, None, reads=zk + [("kB", b)], writes=[("accB", b)])
            for m in range(1, L):
                P.stt(accF[b][:], zp[b][:, L - 1 - m:2 * L - 1 - m], kF[b][:, m:m + 1], accF[b][:], ALU.mult, ALU.add,
                      reads=[("accF", b)], writes=[("accF", b)])
                if m >= 2:
                    P.stt(accB[b][:], zp[b][:, L - 1 + m:2 * L - 1 + m], kB[b][:, m:m + 1], accB[b][:], ALU.mult, ALU.add,
                          reads=[("accB", b)], writes=[("accB", b)])
            P.tt("pool", accF[b][:], accF[b][:], accB[b][:], ALU.add, reads=[("accF", b), ("accB", b)], writes=[("accF", b)])
            P.ts("pool", accB[b][:], zp[b][:, L - 1:2 * L - 1], vec[:, self.V_HD + j:self.V_HD + j + 1], 1.0, ALU.mult, ALU.mult,
                 reads=zk + [("accB", b), ("vecs", l)], writes=[("accB", b)])
            P.stt(accF[b][:], accF[b][:], ss[b][:, 3:4], accB[b][:], ALU.mult, ALU.add, reads=[("accF", b), ("accB", b), ("ss", b, 3)], writes=[("accF", b)])
            P.tt("pool", hyo[b][:], accF[b][:], x0c[b][:], ALU.mult, reads=[("accF", b), ("x0c", b)], writes=[("hyo", b)])
            P.dma("sp", s.hyT[j * 128:(j + 1) * 128, :], hyo[b][:], reads=[("hyo", b)])
        P.barrier()

    def _load_w_resident(self, dst, src2d, nrow_chunks, ncols, key, stage, skey):
        P = self.P
        n = 0
        for c0 in range(0, ncols, 128):
            for k0 in range(0, nrow_chunks, 8):
                kn = min(8, nrow_chunks - k0)
                j = self._stg_i % len(stage)
                self._stg_i += 1
                P.dma("sp", stage[j][:, 0:kn, :],
                      src2d[k0 * 128:(k0 + kn) * 128, c0:c0 + 128].rearrange("(k p) n -> p k n", p=128), writes=[(skey, j)])
                P.cp("pool" if n % 2 else "dve", dst[:, k0:k0 + kn, c0:c0 + 128], stage[j][:, 0:kn, :], reads=[(skey, j)], writes=[(key, c0, k0)])
                n += 1
        return [[(key, c0, k0) for k0 in range(0, nrow_chunks, 8)] for c0 in range(0, ncols, 128)]

    def _layer_norm(self, rT, W, gcol, bcol, vec, l, outT, sq, stat, rkeys, okey, sqk="lnsq"):
        P = self.P
        mean, msq, var, rstd = stat
        for m in range(8):
            P.mm(self.ps[6][:, 0:W], self.ones_f[:], rT[:, m, :], m == 0, m == 7, reads=[rkeys[m]], writes=[self.psk(6)])
        for m in range(8):
            P.act(sq[:, m, :], rT[:, m, :], AF.Square, reads=[rkeys[m]], writes=[(sqk, m)])
            P.mm(self.ps[7][:, 0:W], self.ones_f[:], sq[:, m, :], m == 0, m == 7, reads=[(sqk, m)], writes=[self.psk(7)])
        P.act(mean[:, 0:W], self.ps[6][:, 0:W], AF.Copy, scale=1.0 / D, reads=[self.psk(6)], writes=["ln_mean"])
        P.tt("pool", msq[:, 0:W], mean[:, 0:W], mean[:, 0:W], ALU.mult, reads=["ln_mean"], writes=["ln_msq"])
        P.stt(var[:, 0:W], self.ps[7][:, 0:W], 1.0 / D, msq[:, 0:W], ALU.mult, ALU.subtract, reads=[self.psk(7), "ln_msq"], writes=["ln_var"])
        P.act(rstd[:, 0:W], var[:, 0:W], AF.Ln, bias=self.epsT[:, 0:1], reads=["ln_var"], writes=["ln_rstd"])
        P.act(rstd[:, 0:W], rstd[:, 0:W], AF.Exp, scale=-0.5, reads=["ln_rstd"], writes=["ln_rstd"])
        for m in range(8):
            P.tt("dve", sq[:, m, :], rT[:, m, :], mean[:, 0:W], ALU.subtract, reads=[rkeys[m], "ln_mean", (sqk, m)], writes=[(sqk, m)])
            P.tt("pool", sq[:, m, :], sq[:, m, :], rstd[:, 0:W], ALU.mult, reads=[(sqk, m), "ln_rstd"], writes=[(sqk, m)])
            P.act(outT[:, m, :], sq[:, m, :], AF.Identity, scale=vec[:, gcol + m:gcol + m + 1], bias=vec[:, bcol + m:bcol + m + 1],
                  reads=[(sqk, m), ("vecs", l)], writes=[(okey, m)])

    def stage_c(self, l, s):
        P = self.P
        P.sb_reset()
        T, W = s.T, 256
        si = s.sidx
        vec, modT = self.vecs[l], self.modT[l]
        nj = W // 128
        self._stg_i = 0
        wb = [P.sb([128, 8, D], BF16, f"wb{i}") for i in range(3)]
        wo = P.sb([128, 8, D], BF16, "wo")
        mark0 = P.sb_off
        stage = [P.sb([128, 8, 128], F32, "cst") for _ in range(3)]
        wk = []
        for i in range(3):
            wk.append(self._load_w_resident(wb[i], self.inp["w_branch"][l, i], 8, D, f"wb{i}", stage, "cst"))
        wok = self._load_w_resident(wo, self.inp["w_out"][l], 8, D, "wo", stage, "cst")
        P.barrier()
        P.sb_off = mark0
        NBUF = 2
        yt = [P.sb([128, nj, D], F32, "yt") for _ in range(NBUF)]
        x0t = [P.sb([128, 8, W], F32, "x0t") for _ in range(NBUF)]
        hyT = [P.sb([128, 8, W], BF16, "hyT") for _ in range(NBUF)]
        atT = [P.sb([128, 8, W], BF16, "atT") for _ in range(NBUF)]
        poT = [P.sb([128, 8, W], BF16, "poT") for _ in range(NBUF)]
        gt = [P.sb([128, 3, 8, W], BF16, "gt") for _ in range(NBUF)]
        xat = [P.sb([128, 8, W], F32, "xat") for _ in range(NBUF)]
        mg = [P.sb([128, 8, W], BF16, "mg") for _ in range(NBUF)]
        tA = [P.sb([128, W], F32, "tA") for _ in range(2)]
        tB = [P.sb([128, W], F32, "tB") for _ in range(2)]
        stat = [P.sb([128, W], F32, "lnst") for _ in range(4)]
        direct = s.hyT is not None

        def finish_c(p, ts_, sq):
            self._layer_norm(xat[p], W, self.V_LN, self.V_LN + 8, vec, l, xat[p], sq, stat, [("rT", p, m) for m in range(8)], ("x1T", p), ("lnsq", p))
            P.dma("act", s.xb[:, ts_].rearrange("(m p) t -> p m t", p=128), xat[p][:], reads=[(("x1T", p), m) for m in range(8)], writes=[("xat", p)])

        pend = None
        for tw in range(T // W):
            p = tw % NBUF
            ts_ = slice(tw * W, (tw + 1) * W)
            sq = yt[p][:].rearrange("p j d -> p (j d)").rearrange("p (m w) -> p m w", m=8)
            if direct:
                P.dma("sp", hyT[p][:], s.hyT[:, ts_].rearrange("(m p) t -> p m t", p=128), writes=[("hyT", p, m) for m in range(8)])
            else:
                P.dma("sp", yt[p][:], s.y[ts_, :].rearrange("(j p) d -> p j d", p=128), reads=[(("lnsq", p), m) for m in range(8)], writes=[("yt", p)])
                P.dma("sp", x0t[p][:], s.x0T[:, ts_].rearrange("(m p) t -> p m t", p=128), writes=[("x0t", p)])
            P.dma("sp", atT[p][:], s.attnT[:, ts_].rearrange("(m p) t -> p m t", p=128), writes=[("atT", p)])
            P.dma("sp", poT[p][:], s.poolT[:, ts_].rearrange("(m p) t -> p m t", p=128), writes=[("poT", p)])
            for i in range(3):
                P.dma("sp", gt[p][:, i, :, :], s.gT[i, :, ts_].rearrange("(m p) t -> p m t", p=128), writes=[("gt", p, i)])
            P.dma("sp", xat[p][:], s.xa[:, ts_].rearrange("(m p) t -> p m t", p=128), writes=[("xat", p)])
            for m in range(8):
                if direct:
                    break
                bank = m % 2
                for j in range(nj):
                    P.tr(self.ps[bank][:, j * 128:(j + 1) * 128], yt[p][:, j, m * 128:(m + 1) * 128], self.ident[:], reads=[("yt", p)], writes=[self.psk(bank)])
                P.tt("dve", hyT[p][:, m, :], self.ps[bank][:, 0:W], x0t[p][:, m, :], ALU.mult, reads=[self.psk(bank), ("x0t", p)], writes=[("hyT", p, m)])
            srcs = (hyT[p], atT[p], poT[p])
            skeys = ([("hyT", p, m) for m in range(8)], [("atT", p)], [("poT", p)])
            for mo in range(8):
                q = mo % 2
                for i in range(3):
                    bank = 2 + i
                    for k in range(8):
                        P.mm(self.ps[bank][:, 0:W], wb[i][:, k, mo * 128:(mo + 1) * 128], srcs[i][:, k, :], k == 0, k == 7,
                             reads=(skeys[i] if i else [("hyT", p, k)]), writes=[self.psk(bank)])
                P.tt("dve", tA[q][:], self.ps[2][:, 0:W], gt[p][:, 0, mo, :], ALU.mult, reads=[self.psk(2), ("gt", p, 0)], writes=[("tA", q)])
                P.tt("dve", tB[q][:], self.ps[3][:, 0:W], gt[p][:, 1, mo, :], ALU.mult, reads=[self.psk(3), ("gt", p, 1)], writes=[("tB", q)])
                P.tt("pool", tA[q][:], tA[q][:], tB[q][:], ALU.add, reads=[("tA", q), ("tB", q)], writes=[("tA", q)])
                P.tt("dve", tB[q][:], self.ps[4][:, 0:W], gt[p][:, 2, mo, :], ALU.mult, reads=[self.psk(4), ("gt", p, 2)], writes=[("tB", q)])
                P.tt("pool", mg[p][:, mo, :], tA[q][:], tB[q][:], ALU.add, reads=[("tA", q), ("tB", q)], writes=[("mg", p, mo)])
            if pend is not None:
                finish_c(*pend)
            for mo in range(8):
                bank = mo % 2
                for k in range(8):
                    P.mm(self.ps[bank][:, 0:W], wo[:, k, mo * 128:(mo + 1) * 128], mg[p][:, k, :], k == 0, k == 7,
                         reads=[("mg", p, k)], writes=[self.psk(bank)])
                P.act(xat[p][:, mo, :], xat[p][:, mo, :], AF.Copy, scale=ALPHA, reads=[("xat", p)], writes=[("xs", p, mo)])
                P.stt(xat[p][:, mo, :], self.ps[bank][:, 0:W], modT[:, si, 16 + mo:17 + mo], xat[p][:, mo, :], ALU.mult, ALU.add,
                      reads=[self.psk(bank), ("xs", p, mo), ("modT", l, si)], writes=[("rT", p, mo)])
            pend = (p, ts_, sq)
        finish_c(*pend)
        P.barrier()

    def stage_d(self, l, s, final):
        P = self.P
        P.sb_reset()
        T = s.T
        W = 256
        si = s.sidx
        vec, modT, modP = self.vecs[l], self.modT[l], self.modP[l]
        NF = DFF // 128
        self._stg_i = 0
        w1 = P.sb([128, 8, DFF], BF16, "w1")
        w3 = P.sb([128, 8, DFF], BF16, "w3")
        w2 = P.sb([128, NF, D], BF16, "w2")
        mark0 = P.sb_off
        stage = [P.sb([128, 8, 128], F32, "dst") for _ in range(3)]
        w1k = self._load_w_resident(w1, self.inp["ffn_w1"][l], 8, DFF, "w1", stage, "dst")
        w3k = self._load_w_resident(w3, self.inp["ffn_w3"][l], 8, DFF, "w3", stage, "dst")
        w2k = self._load_w_resident(w2, self.inp["ffn_w2"][l], NF, D, "w2", stage, "dst")
        P.barrier()
        P.sb_off = mark0
        x1t = [P.sb([128, 8, W], F32, "x1t") for _ in range(2)]
        h2T = [P.sb([128, 8, W], BF16, "h2T") for _ in range(2)]
        gT = P.sb([128, NF, W], BF16, "gT")
        sa = [P.sb([128, W], F32, "sa") for _ in range(2)]
        sqd = P.sb([128, 8, W], F32, "sqd")
        stat = [P.sb([128, W], F32, "lnst") for _ in range(4)]
        ot = sqd[:].rearrange("p m w -> p (m w)").rearrange("p (j d) -> p j d", j=W // 128) if final else None

        def finish_d(p, ts_):
            xt = x1t[p]
            self._layer_norm(xt, W, self.V_LN + 16, self.V_LN + 24, vec, l, xt, sqd, stat, [("rT", p, m) for m in range(8)], ("x2T", p))
            xk = [(("x2T", p), m) for m in range(8)]
            if not final:
                P.dma("act", s.xa[:, ts_].rearrange("(m p) t -> p m t", p=128), xt[:], reads=xk, writes=[("x1t", p)])
            else:
                for j in range(W // 128):
                    for half in range(2):
                        bank = half
                        for mm_ in range(4):
                            m = half * 4 + mm_
                            P.tr(self.ps[bank][:, mm_ * 128:(mm_ + 1) * 128], xt[:, m, j * 128:(j + 1) * 128], self.ident[:],
                                 reads=[(("x2T", p), m)], writes=[self.psk(bank)])
                        P.cp("act" if half else "dve", ot[:, j, half * 512:(half + 1) * 512], self.ps[bank][:, 0:512], reads=[self.psk(bank)],
                             writes=[("ot", j, half)] + ([("lnsq", m) for m in range(8)] if (j == 0 and half == 0) else []))
                P.dma("act", self.out[ts_, :].rearrange("(j p) d -> p j d", p=128), ot,
                      reads=[("ot", j, h_) for j in range(W // 128) for h_ in range(2)] + xk + [("lnsq", m) for m in range(8)],
                      writes=[("x1t", p)], final=True)

        pend = None
        for tw in range(T // W):
            p = tw % 2
            ts_ = slice(tw * W, (tw + 1) * W)
            P.dma("sp", x1t[p][:], s.xb[:, ts_].rearrange("(m p) t -> p m t", p=128), writes=[("x1t", p)])
            for m in range(8):
                P.ts("dve" if m % 2 else "pool", h2T[p][:, m, :], x1t[p][:, m, :], modP[:, si, 32 + m:33 + m], modT[:, si, 24 + m:25 + m], ALU.mult, ALU.add,
                     reads=[("x1t", p), ("modT", l, si), ("modP", l, si)], writes=[("h2T", p, m)])
            for f in range(NF):
                q = f % 2
                ba, bb = (0, 1) if q == 0 else (2, 3)
                for k in range(8):
                    P.mm(self.ps[ba][:, 0:W], w1[:, k, f * 128:(f + 1) * 128], h2T[p][:, k, :], k == 0, k == 7, reads=[("h2T", p, k)], writes=[self.psk(ba)])
                for k in range(8):
                    P.mm(self.ps[bb][:, 0:W], w3[:, k, f * 128:(f + 1) * 128], h2T[p][:, k, :], k == 0, k == 7, reads=[("h2T", p, k)], writes=[self.psk(bb)])
                P.act(sa[q][:], self.ps[ba][:, 0:W], AF.Silu, reads=[self.psk(ba)], writes=[("sa", q)])
                P.tt("dve", gT[:, f, :], self.ps[bb][:, 0:W], sa[q][:], ALU.mult, reads=[self.psk(bb), ("sa", q)], writes=[("gT", f)])
                if f == 2 and pend is not None:
                    finish_d(*pend)
                    pend = None
            for mo in range(8):
                bank = 4 + mo % 2
                for f in range(NF):
                    P.mm(self.ps[bank][:, 0:W], w2[:, f, mo * 128:(mo + 1) * 128], gT[:, f, :], f == 0, f == NF - 1,
                         reads=[("gT", f)], writes=[self.psk(bank)])
                P.act(x1t[p][:, mo, :], x1t[p][:, mo, :], AF.Copy, scale=ALPHA, reads=[("x1t", p)], writes=[("xs", p, mo)])
                P.stt(x1t[p][:, mo, :], self.ps[bank][:, 0:W], modT[:, si, 40 + mo:41 + mo], x1t[p][:, mo, :], ALU.mult, ALU.add,
                      reads=[self.psk(bank), ("xs", p, mo), ("modT", l, si)], writes=[("rT", p, mo)])
            pend = (p, ts_)
        finish_d(*pend)
        P.barrier()

    def build(self):
        st = self.stages
        def on(name):
            return st is None or name in st
        if on("p0"):
            self.stage_p0()
        if on("p1"):
            self.stage_p1()
        for l in range(DEPTH):
            if on(f"k{l}l"):
                self.stage_k(l, "l")
            if on(f"a{l}c"):
                self.stage_a(l, self.st["c"])
            if on(f"a{l}l"):
                self.stage_a(l, self.st["l"])
            if on(f"h{l}c") and l < DEPTH - 1:
                self.stage_hc(l, self.st["c"])
            if on(f"h{l}l"):
                self.stage_h(l, self.st["l"])
            if on(f"b{l}c") and l < DEPTH - 1:
                self.stage_b(l, self.st["c"], CTX)
            if on(f"b{l}l"):
                self.stage_b(l, self.st["l"], SEQ + CTX)
            if on(f"c{l}c") and l < DEPTH - 1:
                self.stage_c(l, self.st["c"])
            if on(f"d{l}c") and l < DEPTH - 1:
                self.stage_d(l, self.st["c"], False)
            if on(f"c{l}l"):
                self.stage_c(l, self.st["l"])
            if on(f"d{l}l"):
                self.stage_d(l, self.st["l"], l == DEPTH - 1)
        self.P.emit()
        return self.nc


def make_in_maps(inputs, n_cores=8):
    c = host_consts()
    shared = {k: np.ascontiguousarray(np.asarray(v, dtype=np.float32)) for k, v in inputs.items() if k not in ("x", "c", "ctx")}
    consts = {"k_" + k: v for k, v in c.items()}
    maps = []
    for b in range(n_cores):
        m = dict(shared)
        m.update(consts)
        m["x"] = np.ascontiguousarray(inputs["x"][b], dtype=np.float32)
        m["c"] = np.ascontiguousarray(inputs["c"][b], dtype=np.float32)
        m["ctx"] = np.ascontiguousarray(inputs["ctx"][b], dtype=np.float32)
        maps.append(m)
    return maps


def kernel(**inputs):
    bld = Builder()
    nc = bld.build()
    res = run_bass_kernel_spmd(nc, make_in_maps(inputs), core_ids=list(range(8)))
    return np.stack([np.asarray(r["out"], dtype=np.float32) for r in res.results], axis=0)
```
